# Optimizing a Trainium2 kernel written in Bass

```python
import jax, jax.numpy as jnp
from jax import lax
import numpy as np

D_MODEL = 1024
BATCH = 8
SEQ = 2048
DEPTH = 4

GRID_W = 64
CTX_LEN = 256
D_FF = 2816
N_MOD = 9
EPS = 1e-6
ROPE_THETA = 10000.0
BLOCK_Q = 128

GQA_HEADS = 6
GQA_KV_HEADS = 2
GQA_HEAD_DIM = 64
GQA_Q = GQA_HEADS * GQA_HEAD_DIM
GQA_KV = GQA_KV_HEADS * GQA_HEAD_DIM
GQA_IN = GQA_Q + 2 * GQA_KV
CONV_DIM = 256
CONV_GROUPS = 4
CONV_WIDTH = 3
CONV_IN = 3 * CONV_DIM
MLA_HEADS = 6
MLA_Q_LORA = 384
MLA_KV_LORA = 256
MLA_NOPE = 64
MLA_ROPE = 32
MLA_V = 64
MLA_IN = MLA_Q_LORA + MLA_KV_LORA + MLA_ROPE

D_MIX = GQA_Q + CONV_DIM + MLA_HEADS * MLA_V
IN_COLS = GQA_IN + CONV_IN + MLA_IN

kernel_name = "hybrid_gqa_conv_mla_macaron_dit"


def rms_norm(x, g):
    xf = x.astype(jnp.float32)
    y = xf * lax.rsqrt(jnp.mean(xf * xf, axis=-1, keepdims=True) + EPS)
    return (y * g.astype(jnp.float32)).astype(x.dtype)


def modulate(h, shift, scale):
    return h * (1 + scale) + shift


def swiglu(h, wg, wu, wd):
    return (jax.nn.silu(h @ wg) * (h @ wu)) @ wd


def half_ffn(h, g, shift, scale, gate, wg, wu, wd):
    return h + 0.5 * gate * swiglu(modulate(rms_norm(h, g), shift, scale), wg, wu, wd)


def rope_tables(row, col, dim):
    half = dim // 2
    inv = 1.0 / (ROPE_THETA ** (jnp.arange(0, half, 2, dtype=jnp.float32) / half))
    ar = row[:, None] * inv[None, :]
    ac = col[:, None] * inv[None, :]
    ang = jnp.concatenate([ar, ar, ac, ac], axis=-1)
    return jnp.cos(ang), jnp.sin(ang)


def apply_rope(x, cos, sin):
    x0, x1, x2, x3 = jnp.split(x, 4, axis=-1)
    rot = jnp.concatenate([-x1, x0, -x3, x2], axis=-1)
    return (x * cos + rot * sin).astype(x.dtype)


def block_attention(q, k, v):
    b, hk, g, sq, dk = q.shape
    nb = sq // BLOCK_Q
    scale = dk ** -0.5
    qb = jnp.moveaxis(q.reshape(b, hk, g, nb, BLOCK_Q, dk), 3, 0)

    def one_block(qi):
        s = jnp.einsum('bhgqd,bhkd->bhgqk', qi, k).astype(jnp.float32) * scale
        p = jax.nn.softmax(s, axis=-1).astype(v.dtype)
        return jnp.einsum('bhgqk,bhkd->bhgqd', p, v)

    o = lax.map(one_block, qb)
    return jnp.moveaxis(o, 0, 3).reshape(b, hk, g, sq, v.shape[-1])


def merge_heads(o):
    b, hk, g, s, d = o.shape
    return o.transpose(0, 3, 1, 2, 4).reshape(b, s, hk * g * d)


def gqa_heads(p, g_q, g_k):
    b, s, _ = p.shape
    q, k, v = jnp.split(p, [GQA_Q, GQA_Q + GQA_KV], axis=-1)
    q = q.reshape(b, s, GQA_KV_HEADS, GQA_HEADS // GQA_KV_HEADS, GQA_HEAD_DIM).transpose(0, 2, 3, 1, 4)
    k = k.reshape(b, s, GQA_KV_HEADS, GQA_HEAD_DIM).transpose(0, 2, 1, 3)
    v = v.reshape(b, s, GQA_KV_HEADS, GQA_HEAD_DIM).transpose(0, 2, 1, 3)
    return rms_norm(q, g_q), rms_norm(k, g_k), v


def gqa_mix(p_lat, p_ctx, g_q, g_k, cos, sin, need_ctx):
    q_l, k_l, v_l = gqa_heads(p_lat, g_q, g_k)
    q_c, k_c, v_c = gqa_heads(p_ctx, g_q, g_k)
    q_l = apply_rope(q_l, cos, sin)
    k_l = apply_rope(k_l, cos, sin)
    o_l = block_attention(q_l, jnp.concatenate([k_c, k_l], axis=2), jnp.concatenate([v_c, v_l], axis=2))
    o_c = merge_heads(block_attention(q_c, k_c, v_c)) if need_ctx else None
    return merge_heads(o_l), o_c


def short_conv(p, w, bias):
    x_in, b_gate, c_gate = jnp.split(p, 3, axis=-1)
    u = c_gate * x_in
    s = u.shape[1]
    pad = CONV_WIDTH // 2
    up = jnp.pad(u, ((0, 0), (pad, pad), (0, 0)))
    y = sum(up[:, j:j + s] * w[j] for j in range(CONV_WIDTH)) + bias
    return b_gate * y


def mla_heads(p, g_cq, g_ckv, w_uq, w_ukv, g_qn, g_kn, g_qr, g_kr):
    b, s, _ = p.shape
    cq, ckv, kr = jnp.split(p, [MLA_Q_LORA, MLA_Q_LORA + MLA_KV_LORA], axis=-1)
    q = (rms_norm(cq, g_cq) @ w_uq).reshape(b, s, MLA_HEADS, MLA_NOPE + MLA_ROPE).transpose(0, 2, 1, 3)
    kv = (rms_norm(ckv, g_ckv) @ w_ukv).reshape(b, s, MLA_HEADS, MLA_NOPE + MLA_V).transpose(0, 2, 1, 3)
    q_nope, q_rope = jnp.split(q, [MLA_NOPE], axis=-1)
    k_nope, v = jnp.split(kv, [MLA_NOPE], axis=-1)
    q_nope = rms_norm(q_nope, g_qn)
    q_rope = rms_norm(q_rope, g_qr)
    k_nope = rms_norm(k_nope, g_kn)
    k_rope = rms_norm(kr, g_kr)[:, None]
    return q_nope, q_rope, k_nope, k_rope, v


def mla_assemble(q_nope, q_rope, k_nope, k_rope):
    q = jnp.concatenate([q_nope, q_rope], axis=-1)[:, :, None]
    k = jnp.concatenate([k_nope, jnp.broadcast_to(k_rope, k_nope.shape[:-1] + (MLA_ROPE,))], axis=-1)
    return q, k


def mla_mix(p_lat, p_ctx, g_cq, g_ckv, w_uq, w_ukv, g_qn, g_kn, g_qr, g_kr, cos, sin, need_ctx):
    qn_l, qr_l, kn_l, kr_l, v_l = mla_heads(p_lat, g_cq, g_ckv, w_uq, w_ukv, g_qn, g_kn, g_qr, g_kr)
    qn_c, qr_c, kn_c, kr_c, v_c = mla_heads(p_ctx, g_cq, g_ckv, w_uq, w_ukv, g_qn, g_kn, g_qr, g_kr)
    q_l, k_l = mla_assemble(qn_l, apply_rope(qr_l, cos, sin), kn_l, apply_rope(kr_l, cos, sin))
    q_c, k_c = mla_assemble(qn_c, qr_c, kn_c, kr_c)
    o_l = block_attention(q_l, jnp.concatenate([k_c, k_l], axis=2), jnp.concatenate([v_c, v_l], axis=2))
    o_c = merge_heads(block_attention(q_c, k_c, v_c)) if need_ctx else None
    return merge_heads(o_l), o_c


def token_mixing(h_lat, h_ctx, w_in, w_out, gqa_g_q, gqa_g_k, conv_w, conv_b,
                 mla_g_cq, mla_g_ckv, mla_w_uq, mla_w_ukv, mla_g_qn, mla_g_kn, mla_g_qr, mla_g_kr,
                 cos_a, sin_a, cos_m, sin_m, need_ctx):
    splits = [GQA_IN, GQA_IN + CONV_IN]
    a_l, s_l, m_l = jnp.split(h_lat @ w_in, splits, axis=-1)
    a_c, s_c, m_c = jnp.split(h_ctx @ w_in, splits, axis=-1)
    ya_l, ya_c = gqa_mix(a_l, a_c, gqa_g_q, gqa_g_k, cos_a, sin_a, need_ctx)
    ym_l, ym_c = mla_mix(m_l, m_c, mla_g_cq, mla_g_ckv, mla_w_uq, mla_w_ukv,
                         mla_g_qn, mla_g_kn, mla_g_qr, mla_g_kr, cos_m, sin_m, need_ctx)
    y_l = jnp.concatenate([ya_l, short_conv(s_l, conv_w, conv_b), ym_l], axis=-1) @ w_out
    y_c = None
    if need_ctx:
        y_c = jnp.concatenate([ya_c, short_conv(s_c, conv_w, conv_b), ym_c], axis=-1) @ w_out
    return y_l, y_c


def setup_inputs(seed: int = 0) -> dict:
    key = jax.random.key(seed)
    ks = list(jax.random.split(key, 32))
    it = iter(ks)
    d = D_MODEL

    def nrm(shape, std):
        return std * jax.random.normal(next(it), shape, jnp.float32)

    return {
        "x": nrm((BATCH, SEQ, d), 1.0),
        "c": nrm((BATCH, d), 1.0),
        "ctx": nrm((BATCH, CTX_LEN, d), 1.0),
        "c_ctx": nrm((d,), 1.0),
        "w_mod": nrm((DEPTH, d, N_MOD * d), 0.5 * d ** -0.5),
        "b_mod": nrm((DEPTH, N_MOD * d), 0.02),
        "g_norm": 1.0 + nrm((DEPTH, 3, d), 0.02),
        "ffn_w_gate": nrm((DEPTH, 2, d, D_FF), d ** -0.5),
        "ffn_w_up": nrm((DEPTH, 2, d, D_FF), d ** -0.5),
        "ffn_w_down": nrm((DEPTH, 2, D_FF, d), D_FF ** -0.5),
        "w_in": nrm((DEPTH, d, IN_COLS), d ** -0.5),
        "w_out": nrm((DEPTH, D_MIX, d), D_MIX ** -0.5),
        "gqa_g_q": 1.0 + nrm((DEPTH, GQA_HEAD_DIM), 0.02),
        "gqa_g_k": 1.0 + nrm((DEPTH, GQA_HEAD_DIM), 0.02),
        "conv_w": nrm((DEPTH, CONV_WIDTH, CONV_DIM), CONV_WIDTH ** -0.5),
        "conv_b": nrm((DEPTH, CONV_DIM), 0.02),
        "mla_g_cq": 1.0 + nrm((DEPTH, MLA_Q_LORA), 0.02),
        "mla_g_ckv": 1.0 + nrm((DEPTH, MLA_KV_LORA), 0.02),
        "mla_w_uq": nrm((DEPTH, MLA_Q_LORA, MLA_HEADS * (MLA_NOPE + MLA_ROPE)), MLA_Q_LORA ** -0.5),
        "mla_w_ukv": nrm((DEPTH, MLA_KV_LORA, MLA_HEADS * (MLA_NOPE + MLA_V)), MLA_KV_LORA ** -0.5),
        "mla_g_qn": 1.0 + nrm((DEPTH, MLA_NOPE), 0.02),
        "mla_g_kn": 1.0 + nrm((DEPTH, MLA_NOPE), 0.02),
        "mla_g_qr": 1.0 + nrm((DEPTH, MLA_ROPE), 0.02),
        "mla_g_kr": 1.0 + nrm((DEPTH, MLA_ROPE), 0.02),
    }


def reference(x, c, ctx, c_ctx, w_mod, b_mod, g_norm, ffn_w_gate, ffn_w_up, ffn_w_down,
              w_in, w_out, gqa_g_q, gqa_g_k, conv_w, conv_b, mla_g_cq, mla_g_ckv,
              mla_w_uq, mla_w_ukv, mla_g_qn, mla_g_kn, mla_g_qr, mla_g_kr):
    n_lat = x.shape[1]
    rows = n_lat // GRID_W
    row = jnp.repeat(jnp.arange(rows, dtype=jnp.float32), GRID_W)
    col = jnp.tile(jnp.arange(GRID_W, dtype=jnp.float32), rows)
    cos_a, sin_a = rope_tables(row, col, GQA_HEAD_DIM)
    cos_m, sin_m = rope_tables(row, col, MLA_ROPE)
    s_lat = jax.nn.silu(c)[:, None, :]
    s_ctx = jax.nn.silu(c_ctx)
    for l in range(DEPTH):
        need_ctx = l < DEPTH - 1
        ml = jnp.split(s_lat @ w_mod[l] + b_mod[l], N_MOD, axis=-1)
        mc = jnp.split(s_ctx @ w_mod[l] + b_mod[l], N_MOD, axis=-1)
        x = half_ffn(x, g_norm[l, 0], ml[0], ml[1], ml[2], ffn_w_gate[l, 0], ffn_w_up[l, 0], ffn_w_down[l, 0])
        ctx = half_ffn(ctx, g_norm[l, 0], mc[0], mc[1], mc[2], ffn_w_gate[l, 0], ffn_w_up[l, 0], ffn_w_down[l, 0])
        h_lat = modulate(rms_norm(x, g_norm[l, 1]), ml[3], ml[4])
        h_ctx = modulate(rms_norm(ctx, g_norm[l, 1]), mc[3], mc[4])
        y_l, y_c = token_mixing(h_lat, h_ctx, w_in[l], w_out[l], gqa_g_q[l], gqa_g_k[l], conv_w[l], conv_b[l],
                                mla_g_cq[l], mla_g_ckv[l], mla_w_uq[l], mla_w_ukv[l],
                                mla_g_qn[l], mla_g_kn[l], mla_g_qr[l], mla_g_kr[l],
                                cos_a, sin_a, cos_m, sin_m, need_ctx)
        x = x + ml[5] * y_l
        x = half_ffn(x, g_norm[l, 2], ml[6], ml[7], ml[8], ffn_w_gate[l, 1], ffn_w_up[l, 1], ffn_w_down[l, 1])
        if need_ctx:
            ctx = ctx + mc[5] * y_c
            ctx = half_ffn(ctx, g_norm[l, 2], mc[6], mc[7], mc[8], ffn_w_gate[l, 1], ffn_w_up[l, 1], ffn_w_down[l, 1])
    return x
```

```python
import numpy as np
from contextlib import ExitStack
import concourse.bass as bass
import concourse.mybir as mybir
from concourse.bass_utils import run_bass_kernel_spmd

F32 = mybir.dt.float32
BF16 = mybir.dt.bfloat16
U8 = mybir.dt.uint8
ALU = mybir.AluOpType
AF = mybir.ActivationFunctionType
AX = mybir.AxisListType

D = 1024
DEPTH = 4
NT = 18
NLT = 16
T = NT * 128
DFF = 2816
NFC = 22
EPS = 1e-6
KC = 8
NGAIN = 320
FF_GROUPS = [4, 4, 4, 4, 4, 2]


class Rec:
    ENG = ("pe", "act", "dve", "pool", "sp")

    def __init__(self):
        self.ops = {e: [] for e in self.ENG}
        self.res = {}
        self.lane_cnt = {}
        self.epoch_starts = {e: [0] for e in self.ENG}
        self.pending = {e: [] for e in self.ENG}

    def new_epoch(self):
        for e in self.ENG:
            self.epoch_starts[e].append(len(self.ops[e]))

    def _deps(self, reads, writes):
        d = []
        for k in reads:
            st = self.res.get(k)
            if st is not None and st[0] is not None:
                d.append(st[0])
        for k in writes:
            st = self.res.get(k)
            if st is not None:
                if st[0] is not None:
                    d.append(st[0])
                for kk, v in st[1].items():
                    d.append(kk + (v,))
        return d

    def _commit(self, tok, reads, writes):
        for k in reads:
            st = self.res.get(k)
            if st is None:
                st = [None, {}]
                self.res[k] = st
            key = tok[:2]
            if st[1].get(key, -1) < tok[2]:
                st[1][key] = tok[2]
        for k in writes:
            self.res[k] = [tok, {}]

    def op(self, eng, fn, reads=(), writes=(), sync_self=False):
        deps = self._deps(reads, writes) + self.pending[eng]
        self.pending[eng] = []
        idx = len(self.ops[eng])
        tok = ("e", eng, idx)
        self.ops[eng].append({"fn": fn, "deps": deps, "lane": None, "ss": sync_self})
        self._commit(tok, reads, writes)
        return tok

    def dma(self, eng, lane, fn, reads=(), writes=()):
        deps = self._deps(reads, writes) + self.pending[eng]
        self.pending[eng] = []
        self.lane_cnt[lane] = self.lane_cnt.get(lane, 0) + 1
        tok = ("d", lane, self.lane_cnt[lane])
        self.ops[eng].append({"fn": fn, "deps": deps, "lane": lane, "ss": True})
        self._commit(tok, reads, writes)
        return tok

    def barrier(self):
        last = []
        for e in self.ENG:
            for i in range(len(self.ops[e]) - 1, -1, -1):
                if self.ops[e][i]["lane"] is None:
                    last.append(("e", e, i))
                    break
        for l, c in self.lane_cnt.items():
            last.append(("d", l, c))
        for e in self.ENG:
            self.pending[e] = self.pending[e] + list(last)

    def finalize(self):
        self.signal = {e: [False] * len(self.ops[e]) for e in self.ENG}
        for e in self.ENG:
            for i, op in enumerate(self.ops[e]):
                for tok in op["deps"]:
                    if tok[0] == "e" and (tok[1] != e or op["ss"] or tok[2] == i - 1):
                        self.signal[tok[1]][tok[2]] = True
        self.sigval = {}
        self.epoch_of = {}
        for e in self.ENG:
            starts = self.epoch_starts[e]
            vals = [None] * len(self.ops[e])
            eps = [0] * len(self.ops[e])
            ep = 0
            cnt = 0
            for i in range(len(self.ops[e])):
                while ep + 1 < len(starts) and i >= starts[ep + 1]:
                    ep += 1
                    cnt = 0
                if self.signal[e][i]:
                    cnt += 1
                    vals[i] = cnt
                eps[i] = ep
            self.sigval[e] = vals
            self.epoch_of[e] = eps
        self.n_epochs = max(len(s) for s in self.epoch_starts.values())

    def emit(self, eng, e, esems, lsems):
        waited = {}
        for i, op in enumerate(self.ops[eng]):
            need = {}
            for tok in op["deps"]:
                if tok[0] == "e":
                    if tok[1] == eng and not op["ss"] and tok[2] != i - 1:
                        continue
                    key = ("e", tok[1], self.epoch_of[tok[1]][tok[2]])
                    val = self.sigval[tok[1]][tok[2]]
                else:
                    key = ("d", tok[1])
                    val = 16 * tok[2]
                if need.get(key, 0) < val:
                    need[key] = val
            for key, val in need.items():
                if waited.get(key, 0) >= val:
                    continue
                if key[0] == "e":
                    later = [k for k in waited if k[0] == "e" and k[1] == key[1] and k[2] > key[2]]
                    if later:
                        continue
                    e.wait_ge(esems[(key[1], key[2])], val)
                else:
                    e.wait_ge(lsems[key[1]], val)
                waited[key] = val
            inst = op["fn"](e)
            if inst is None:
                continue
            if op["lane"] is not None:
                inst.then_inc(lsems[op["lane"]], 16)
            elif self.signal[eng][i]:
                inst.then_inc(esems[(eng, self.epoch_of[eng][i])], 1)


Q_ORDER = [0, 3, 1, 4, 2, 5]


def _rope_np(dim):
    rows = 2048 // 64
    row = np.repeat(np.arange(rows, dtype=np.float32), 64)
    col = np.tile(np.arange(64, dtype=np.float32), rows)
    half = dim // 2
    inv = (1.0 / (10000.0 ** (np.arange(0, half, 2, dtype=np.float32) / half))).astype(np.float32)
    ar = row[:, None] * inv[None, :]
    ac = col[:, None] * inv[None, :]
    ang = np.concatenate([ar, ar, ac, ac], axis=-1).astype(np.float32)
    cos = np.cos(ang).astype(np.float32)
    sin = np.sin(ang).astype(np.float32)
    q = dim // 4
    sgn = np.concatenate([-np.ones(q), np.ones(q), -np.ones(q), np.ones(q)]).astype(np.float32)
    sin = sin * sgn[None, :]
    cos = np.ascontiguousarray(cos.reshape(16, 128, dim).transpose(1, 0, 2))
    sin = np.ascontiguousarray(sin.reshape(16, 128, dim).transpose(1, 0, 2))
    return cos, sin


def prep_shared(inp, n_layers):
    L = n_layers
    f = lambda a: np.ascontiguousarray(a, dtype=np.float32)
    sh = {}
    wm = inp["w_mod"][:L]
    sh["wmod"] = f(wm.reshape(L, KC, 128, 18, 512).transpose(0, 3, 2, 1, 4))
    sh["bmodT"] = f(inp["b_mod"][:L].reshape(L, 72, 128).transpose(2, 0, 1))
    sh["gn"] = f(inp["g_norm"][:L].reshape(L, 3, KC, 128).transpose(3, 0, 1, 2))
    wg = inp["ffn_w_gate"][:L].reshape(L, 2, KC, 128, NFC, 128)
    wu = inp["ffn_w_up"][:L].reshape(L, 2, KC, 128, NFC, 128)
    wgu = np.stack([wg, wu], axis=0)
    sh["wgu"] = f(wgu.transpose(1, 2, 5, 4, 0, 3, 6))
    sh["wd"] = f(inp["ffn_w_down"][:L])
    win = inp["w_in"][:L]
    qcols = np.concatenate([np.arange(64 * h, 64 * h + 64) for h in Q_ORDER])
    kside = np.concatenate([np.arange(384, 512), np.arange(512, 640), np.arange(1792, 2048),
                            np.arange(2048, 2080),
                            np.arange(640, 896), np.arange(1152, 1408), np.arange(896, 1152)])
    qside = np.concatenate([qcols, np.arange(1408, 1792)])
    sh["wink"] = f(win[:, :, kside].reshape(L, KC, 128, 1312).transpose(0, 2, 1, 3))
    sh["winq"] = f(win[:, :, qside].reshape(L, KC, 128, 768).transpose(0, 2, 1, 3))
    sh["wuq"] = f(inp["mla_w_uq"][:L].reshape(L, 3, 128, 576).transpose(0, 2, 1, 3))
    sh["wukv"] = f(inp["mla_w_ukv"][:L].reshape(L, 2, 128, 768).transpose(0, 2, 1, 3))
    rows = []
    for pr in range(3):
        for hf in range(2):
            h = Q_ORDER[2 * pr + hf]
            rows.append(np.arange(64 * h, 64 * h + 64))
    rows.append(np.arange(384, 640))
    rows.append(np.arange(640, 1024))
    rows = np.concatenate(rows)
    wo = inp["w_out"][:L][:, rows, :]
    sh["wout"] = f(wo.reshape(L, KC, 128, 2, 512).transpose(0, 3, 2, 1, 4))
    sh["gains"] = f(np.concatenate([inp["gqa_g_q"][:L], inp["gqa_g_k"][:L], inp["mla_g_qn"][:L],
                                    inp["mla_g_kn"][:L], inp["mla_g_qr"][:L], inp["mla_g_kr"][:L]], axis=1))
    cw = inp["conv_w"][:L]
    cb = inp["conv_b"][:L]
    cp = np.concatenate([cw, cb[:, None, :]], axis=1)
    sh["convp"] = f(cp.reshape(L, 4, 2, 128).transpose(3, 0, 2, 1))
    gl = np.concatenate([inp["mla_g_cq"][:L].reshape(L, 3, 128), inp["mla_g_ckv"][:L].reshape(L, 2, 128)], axis=1)
    sh["glat"] = f(gl.transpose(2, 0, 1))
    ca, sa = _rope_np(64)
    cm, sm = _rope_np(32)
    sh["ropeA"] = f(np.stack([ca, sa], axis=1))
    sh["ropeM"] = f(np.stack([cm, sm], axis=1))
    return sh


def prep_core(inp, b):
    xin = np.concatenate([inp["x"][b], inp["ctx"][b]], axis=0)
    cc = np.stack([inp["c"][b], inp["c_ctx"]], axis=-1)
    ccT = np.ascontiguousarray(cc.reshape(KC, 128, 2).transpose(1, 0, 2), dtype=np.float32)
    return {"xin": np.ascontiguousarray(xin, dtype=np.float32), "ccT": ccT}


def build_program(n_layers=DEPTH, stop_after=None):
    L = n_layers
    nc = bass.Bass("TRN2", target_bir_lowering=False)
    dt_in = lambda name, shape: nc.dram_tensor(name, list(shape), F32, kind="ExternalInput").ap()
    xin = dt_in("xin", [T, D])
    ccT_d = dt_in("ccT", [128, KC, 2])
    wmod_d = dt_in("wmod", [L, 18, 128, KC, 512])
    bmodT_d = dt_in("bmodT", [128, L, 72])
    gn_d = dt_in("gn", [128, L, 3, KC])
    wgu_d = dt_in("wgu", [L, 2, NFC, 128, 2, KC, 128])
    wd_d = dt_in("wd", [L, 2, DFF, D])
    wink_d = dt_in("wink", [L, 128, KC, 1312])
    winq_d = dt_in("winq", [L, 128, KC, 768])
    wuq_d = dt_in("wuq", [L, 128, 3, 576])
    wukv_d = dt_in("wukv", [L, 128, 2, 768])
    wout_d = dt_in("wout", [L, 2, 128, KC, 512])
    gains_d = dt_in("gains", [L, NGAIN])
    convp_d = dt_in("convp", [128, L, 2, 4])
    glat_d = dt_in("glat", [128, L, 5])
    ropeA_d = dt_in("ropeA", [128, 2, 16, 64])
    ropeM_d = dt_in("ropeM", [128, 2, 16, 32])
    out_d = nc.dram_tensor("out", [2048, D], F32, kind="ExternalOutput").ap()

    R = Rec()
    es = ExitStack()
    sb = lambda name, shape, dt: es.enter_context(nc.sbuf_tensor(name, list(shape), dt))
    xs = sb("xs", [128, NT, D], F32)
    ropeA = sb("ropeA_s", [128, 2, 16, 64], BF16)
    ropeM = sb("ropeM_s", [128, 2, 16, 32], BF16)
    ident = sb("ident", [128, 128], BF16)
    identf = sb("identf", [128, 128], F32)
    ones_bf = sb("ones_bf", [128, 128], BF16)
    ccs = sb("ccs", [128, KC, 2], F32)
    s2 = sb("s2", [128, KC, 2], BF16)
    gn = sb("gn_s", [128, L, 3, KC], F32)
    bmodT = sb("bmodT_s", [128, L, 72], F32)
    convp = sb("convp_s", [128, L, 2, 4], F32)
    glat = sb("glat_s", [128, L, 5], F32)
    gains = sb("gains_s", [128, NGAIN], F32)
    modT = sb("modT", [128, 72, 2], F32)
    Amod = sb("Amod", [128, KC, 2], F32)
    Bmod = sb("Bmod", [128, KC, 2], F32)
    gate_bc = sb("gate_bc", [128, 2, D], BF16)
    epsb = sb("epsb", [128, 1], F32)
    st_ss = sb("st_ss", [128, 64], F32)
    st_r = sb("st_r", [128, 64], F32)
    ARENA_BYTES = 118 * 1024
    arena = sb("arena", [128, ARENA_BYTES], U8)
    ps = es.enter_context(nc.psum_tensor("ps", [128, 8, 512], F32))

    class Ar:
        pos = 0

    def aalloc(nbytes):
        a0 = (Ar.pos + 31) // 32 * 32
        Ar.pos = a0 + nbytes
        assert Ar.pos <= ARENA_BYTES, ("arena overflow", Ar.pos)
        return a0

    def aview(a0, dt, shape):
        esz = 2 if dt == BF16 else 4
        n = int(np.prod(shape))
        v = arena[:, a0:a0 + n * esz].bitcast(dt)
        if len(shape) == 1:
            return v
        names = " ".join("a%d" % i for i in range(len(shape)))
        kw = {"a%d" % i: shape[i] for i in range(1, len(shape))}
        return v.rearrange("p (%s) -> p %s" % (names, names), **kw)

    def anew(dt, shape):
        esz = 2 if dt == BF16 else 4
        return aview(aalloc(int(np.prod(shape)) * esz), dt, shape)

    psb = lambda b: ps[:, b, :].bitcast(BF16)

    for i in range(6):
        R.dma("sp", "xin%d" % i, lambda e, i=i: e.dma_start(
            out=xs[:, 3 * i:3 * i + 3, :], in_=xin[384 * i:384 * (i + 1), :].rearrange("(t p) d -> p t d", p=128)),
            writes=[("x", 3 * i), ("x", 3 * i + 1), ("x", 3 * i + 2)])
    R.dma("sp", "cst0", lambda e: e.dma_start(out=ccs[:], in_=ccT_d), writes=["ccs"])
    R.dma("sp", "cst1", lambda e: e.dma_start(out=gn[:], in_=gn_d), writes=["gn"])
    R.dma("sp", "cst2", lambda e: e.dma_start(out=bmodT[:], in_=bmodT_d), writes=["bmodT"])
    R.dma("sp", "cst3", lambda e: e.dma_start(out=convp[:], in_=convp_d), writes=["convp"])
    R.dma("sp", "cst4", lambda e: e.dma_start(out=glat[:], in_=glat_d), writes=["glat"])
    R.dma("pool", "cstA", lambda e: e.dma_start(out=ropeA[:], in_=ropeA_d), writes=["ropeA"])
    R.dma("pool", "cstM", lambda e: e.dma_start(out=ropeM[:], in_=ropeM_d), writes=["ropeM"])
    R.op("pool", lambda e: e.memset(identf[:], 0.0), writes=["identf"])
    R.op("pool", lambda e: e.affine_select(out=identf[:], in_=identf[:], pattern=[[-1, 128]],
                                           compare_op=ALU.not_equal, fill=1.0, base=0, channel_multiplier=1),
         reads=["identf"], writes=["identf"])
    R.op("pool", lambda e: e.tensor_copy(out=ident[:], in_=identf[:]), reads=["identf"], writes=["ident"])
    R.op("pool", lambda e: e.memset(ones_bf[:], 1.0), writes=["ones"])
    R.op("pool", lambda e: e.memset(epsb[:], EPS), writes=["eps"])
    R.op("act", lambda e: e.activation(out=s2[:], in_=ccs[:], func=AF.Silu), reads=["ccs"], writes=["s2"])

    def rstd_from_ss(n, scale, key):
        R.op("act", lambda e: e.activation(out=st_r[:, 0:n], in_=st_ss[:, 0:n], func=AF.Sqrt, scale=scale, bias=epsb[:]),
             reads=[("ss", key), "eps"], writes=[("sr", key)])
        R.op("dve", lambda e: e.reciprocal(out=st_r[:, 0:n], in_=st_r[:, 0:n]), reads=[("sr", key)], writes=[("sr", key)])

    def modulation(l):
        mark = Ar.pos
        ring = [anew(BF16, [KC, 512]) for _ in range(2)]
        for c in range(18):
            s = c % 2
            R.dma("pool", "wm%d" % s, lambda e, c=c, s=s: e.dma_start(out=ring[s][:], in_=wmod_d[l, c]),
                  writes=[("wmring", s)])

            def mm(e, c=c, s=s):
                inst = None
                for fc in range(4):
                    col = (c * 4 + fc) * 2
                    for kc in range(KC):
                        inst = e.matmul(ps[:, 0, col:col + 2], ring[s][:, kc, fc * 128:(fc + 1) * 128], s2[:, kc, :],
                                        start=(kc == 0), stop=(kc == KC - 1))
                return inst
            R.op("pe", mm, reads=[("wmring", s), "s2"], writes=[("ps", 0)])
        R.op("dve", lambda e: e.tensor_tensor(
            out=modT[:], in0=ps[:, 0, 0:144].rearrange("p (c r) -> p c r", r=2),
            in1=bmodT[:, l, :].unsqueeze(2).broadcast_to([128, 72, 2]), op=ALU.add),
            reads=[("ps", 0), "bmodT"], writes=["modT"])
        R.barrier()
        Ar.pos = mark

    def sub_modulation(l, j, gate_mult):
        c_shift, c_scale, c_gate = (3 * j) * 8, (3 * j + 1) * 8, (3 * j + 2) * 8
        R.op("dve", lambda e: e.tensor_scalar(out=Amod[:], in0=modT[:, c_scale:c_scale + 8, :], scalar1=1.0, scalar2=None,
                                              op0=ALU.add), reads=["modT"], writes=["Amod"])
        R.op("dve", lambda e: e.tensor_tensor(out=Amod[:], in0=Amod[:], in1=gn[:, l, j, :].unsqueeze(2).broadcast_to([128, KC, 2]),
                                              op=ALU.mult), reads=["Amod", "gn"], writes=["Amod"])
        R.op("dve", lambda e: e.tensor_copy(out=Bmod[:], in_=modT[:, c_shift:c_shift + 8, :]), reads=["modT"], writes=["Bmod"])
        mark = Ar.pos
        rep = anew(BF16, [KC, 128])
        for r in range(2):
            R.op("dve", lambda e, r=r: e.tensor_scalar(
                out=rep[:], in0=modT[:, c_gate:c_gate + 8, r:r + 1].broadcast_to([128, KC, 128]),
                scalar1=gate_mult, scalar2=None, op0=ALU.mult), reads=["modT"], writes=["rep"])

            def tr(e):
                inst = None
                for kc in range(KC):
                    inst = e.transpose(psb(1)[:, kc * 128:(kc + 1) * 128], rep[:, kc, :], ident[:])
                return inst
            R.op("pe", tr, reads=["rep", "ident"], writes=[("ps", 1)])
            R.op("act", lambda e, r=r: e.copy(out=gate_bc[:, r, :], in_=psb(1)[:, 0:1024]), reads=[("ps", 1)],
                 writes=[("gate", r)])
        R.barrier()
        Ar.pos = mark

    def emit_hT(t, dst, dst_key, xn, xn_key, psbank, stat_col):
        r = 0 if t < NLT else 1
        R.op("act", lambda e: e.activation(out=xn[:], in_=xs[:, t, :], func=AF.Square, accum_out=st_ss[:, stat_col:stat_col + 1]),
             reads=[("x", t)], writes=[xn_key, ("ss", "h%d" % stat_col)])
        R.op("act", lambda e: e.activation(out=st_r[:, stat_col:stat_col + 1], in_=st_ss[:, stat_col:stat_col + 1], func=AF.Sqrt,
                                           scale=1.0 / D, bias=epsb[:]),
             reads=[("ss", "h%d" % stat_col), "eps"], writes=[("sr", "h%d" % stat_col)], sync_self=True)
        R.op("dve", lambda e: e.reciprocal(out=st_r[:, stat_col:stat_col + 1], in_=st_r[:, stat_col:stat_col + 1]),
             reads=[("sr", "h%d" % stat_col)], writes=[("sr", "h%d" % stat_col)])
        R.op("dve", lambda e: e.tensor_scalar(out=xn[:], in0=xs[:, t, :], scalar1=st_r[:, stat_col:stat_col + 1], scalar2=None,
                                              op0=ALU.mult), reads=[("x", t), ("sr", "h%d" % stat_col), xn_key], writes=[xn_key], sync_self=True)

        def tr(e):
            inst = None
            for kc in range(KC):
                inst = e.transpose(psb(psbank)[:, kc * 128:(kc + 1) * 128], xn[:, kc * 128:(kc + 1) * 128], ident[:])
            return inst
        R.op("pe", tr, reads=[xn_key, "ident"], writes=[("ps", psbank)])
        for kc in range(KC):
            R.op("act", lambda e, kc=kc: e.activation(out=dst[:, kc, :], in_=psb(psbank)[:, kc * 128:(kc + 1) * 128],
                                                      func=AF.Identity, scale=Amod[:, kc, r:r + 1], bias=Bmod[:, kc, r:r + 1]),
                 reads=[("ps", psbank), "Amod", "Bmod"], writes=[dst_key])

    def ffn(l, f, j, tiles):
        ntl = len(tiles)
        sub_modulation(l, j, 0.5)
        mark = Ar.pos
        hT = anew(BF16, [KC, T])
        aT = anew(BF16, [4, T])
        gu_ring = [anew(BF16, [2, KC, 128]) for _ in range(3)]
        d_ring = [anew(BF16, [D]) for _ in range(8)]
        sil = [anew(BF16, [512]) for _ in range(2)]
        xn = [anew(BF16, [D]) for _ in range(2)]
        for i, t in enumerate(tiles):
            emit_hT(t, hT[:, :, t * 128:(t + 1) * 128], ("hT", t), xn[i % 2], ("xn", i % 2), i % 2, i % 2)
        tgs = []
        i = 0
        while i < ntl:
            n = min(4, ntl - i)
            tgs.append((tiles[i], n))
            i += n
        cbase = 0
        gu_cnt = 0
        d_cnt = 0
        ps_g = [2, 3]
        ps_u = [4, 5]
        gu_i = 0
        y_i = 0
        for gsz in FF_GROUPS:
            for ci in range(gsz):
                c = cbase + ci
                s = gu_cnt % 3
                R.dma("pool", "gu%d" % s, lambda e, c=c, s=s: e.dma_start(out=gu_ring[s][:], in_=wgu_d[l, f, c]),
                      writes=[("guring", s)])
                sd = d_cnt % 8
                R.dma("pool", "wd%d" % sd, lambda e, c=c, sd=sd: e.dma_start(out=d_ring[sd][:], in_=wd_d[l, f, c * 128:(c + 1) * 128, :]),
                      writes=[("dring", sd)])
                for (t0, n) in tgs:
                    ntok = n * 128
                    tok0 = t0 * 128
                    bg = ps_g[gu_i % 2]
                    bu = ps_u[gu_i % 2]
                    sl = sil[gu_i % 2]
                    gu_i += 1
                    hkeys = [("hT", t0 + k) for k in range(n)]

                    def mm(e, s=s, bg=bg, bu=bu, tok0=tok0, ntok=ntok):
                        inst = None
                        for which, bank in ((0, bg), (1, bu)):
                            for kc in range(KC):
                                inst = e.matmul(ps[:, bank, 0:ntok], gu_ring[s][:, which, kc, :], hT[:, kc, tok0:tok0 + ntok],
                                                start=(kc == 0), stop=(kc == KC - 1))
                        return inst
                    R.op("pe", mm, reads=[("guring", s)] + hkeys, writes=[("ps", bg), ("ps", bu)])
                    R.op("act", lambda e, bg=bg, sl=sl, ntok=ntok: e.activation(out=sl[:, 0:ntok], in_=ps[:, bg, 0:ntok], func=AF.Silu),
                         reads=[("ps", bg)], writes=[("sil", id(sl))])
                    R.op("dve", lambda e, bu=bu, sl=sl, ci=ci, tok0=tok0, ntok=ntok: e.tensor_tensor(
                        out=aT[:, ci, tok0:tok0 + ntok], in0=ps[:, bu, 0:ntok], in1=sl[:, 0:ntok], op=ALU.mult),
                        reads=[("ps", bu), ("sil", id(sl))], writes=[("aT", ci, t0 + k) for k in range(n)])
                gu_cnt += 1
                d_cnt += 1
            dslots = [(d_cnt - gsz + ci) % 8 for ci in range(gsz)]
            for t in tiles:
                r = 0 if t < NLT else 1
                b0 = 6 if (y_i % 2 == 0) else 0
                y_i += 1

                def mmd(e, t=t, b0=b0, dslots=dslots, gsz=gsz):
                    inst = None
                    for hd in range(2):
                        for ci in range(gsz):
                            inst = e.matmul(ps[:, b0 + hd, :], aT[:, ci, t * 128:(t + 1) * 128],
                                            d_ring[dslots[ci]][:, hd * 512:(hd + 1) * 512],
                                            start=(ci == 0), stop=(ci == gsz - 1))
                    return inst
                R.op("pe", mmd, reads=[("aT", ci, t) for ci in range(gsz)] + [("dring", sd) for sd in dslots],
                     writes=[("ps", b0), ("ps", b0 + 1)])
                yv = ps[:, b0:b0 + 2, :]
                R.op("dve", lambda e, yv=yv, r=r: e.tensor_tensor(out=yv, in0=yv, in1=gate_bc[:, r, :].rearrange("p (a b) -> p a b", a=2),
                                                                  op=ALU.mult),
                     reads=[("ps", b0), ("ps", b0 + 1), ("gate", r)], writes=[("ps", b0), ("ps", b0 + 1)])
                R.op("dve", lambda e, yv=yv, t=t: e.tensor_tensor(out=xs[:, t, :].rearrange("p (a b) -> p a b", a=2),
                                                                  in0=xs[:, t, :].rearrange("p (a b) -> p a b", a=2), in1=yv, op=ALU.add),
                     reads=[("ps", b0), ("ps", b0 + 1), ("x", t)], writes=[("x", t)])
            cbase += gsz
        R.barrier()
        Ar.pos = mark

    def rope_apply(eng, src, dst, cs, t, H, dim, key_r, key_w, tmp1, tmp2):
        q = dim // 4
        cosb = cs[:, 0, t, :].unsqueeze(1).broadcast_to([128, H, dim])
        R.op(eng, lambda e: e.tensor_tensor(out=tmp1, in0=src, in1=cosb, op=ALU.mult), reads=key_r + ["rope"], writes=[("rt1", id(tmp1))])
        s4 = src.rearrange("p h (a b c) -> p h a b c", a=2, b=2)
        t4 = tmp2.rearrange("p h (a b c) -> p h a b c", a=2, b=2)
        sn = cs[:, 1, t, :].rearrange("p (a b c) -> p a b c", a=2, b=2)
        for bsel in range(2):
            R.op(eng, lambda e, bsel=bsel: e.tensor_tensor(
                out=t4[:, :, :, bsel, :], in0=s4[:, :, :, 1 - bsel, :],
                in1=sn[:, :, bsel, :].unsqueeze(1).broadcast_to([128, H, 2, q]), op=ALU.mult),
                reads=key_r + ["rope"], writes=[("rt2", id(tmp2), bsel)])
        R.op(eng, lambda e: e.tensor_tensor(out=dst, in0=tmp1, in1=tmp2, op=ALU.add),
             reads=[("rt1", id(tmp1)), ("rt2", id(tmp2), 0), ("rt2", id(tmp2), 1)], writes=key_w)

    def sumsq(src, H, dd, col0, scr, key_r, key_w):
        sv = scr[:, 0:H * dd].rearrange("p (h d) -> p h d", h=H)
        R.op("dve", lambda e: e.tensor_tensor(out=sv, in0=src, in1=src, op=ALU.mult), reads=key_r, writes=["sqscr"])
        R.op("dve", lambda e: e.tensor_reduce(out=st_ss[:, col0:col0 + H], in_=sv, axis=AX.X, op=ALU.add),
             reads=["sqscr"], writes=key_w)

    def mixer(l, need_ctx):
        sub_modulation(l, 1, 1.0)
        mark0 = Ar.pos
        kT_g = anew(BF16, [T])
        V_g = anew(BF16, [NT, 128])
        kT_m = anew(BF16, [6, T])
        V_m = anew(BF16, [NT, 384])
        NCV = 2050 + 258
        bT = anew(BF16, [2, NCV])
        gsc = anew(F32, [NGAIN])
        markU = Ar.pos
        uT = anew(BF16, [2, NCV])
        R.dma("sp", "gains", lambda e: e.dma_start(out=gains[:], in_=gains_d[l].partition_broadcast(128)), writes=["gains"])
        R.op("dve", lambda e: e.tensor_copy(out=gsc[:], in_=gains[:]), reads=["gains"], writes=["gsc"])
        R.op("dve", lambda e: e.tensor_scalar(out=gsc[:, 0:64], in0=gains[:, 0:64], scalar1=64.0 ** -0.5, scalar2=None, op0=ALU.mult),
             reads=["gains", "gsc"], writes=["gsc"])
        R.op("dve", lambda e: e.tensor_scalar(out=gsc[:, 128:192], in0=gains[:, 128:192], scalar1=96.0 ** -0.5, scalar2=None, op0=ALU.mult),
             reads=["gains", "gsc"], writes=["gsc"])
        R.op("dve", lambda e: e.tensor_scalar(out=gsc[:, 256:288], in0=gains[:, 256:288], scalar1=96.0 ** -0.5, scalar2=None, op0=ALU.mult),
             reads=["gains", "gsc"], writes=["gsc"])
        g_q, g_k, g_qn, g_kn, g_qr, g_kr = (gsc[:, 0:64], gsc[:, 64:128], gsc[:, 128:192], gsc[:, 192:256],
                                            gsc[:, 256:288], gsc[:, 288:320])
        R.op("pool", lambda e: e.memset(uT[:], 0.0), writes=["uT"])

        markK = Ar.pos
        wK = anew(BF16, [KC, 1312])
        wukv = anew(BF16, [2, 768])
        hTr = [anew(BF16, [KC, 128]) for _ in range(2)]
        xn = [anew(BF16, [D]) for _ in range(2)]
        kraw = anew(F32, [544])
        kvraw = anew(F32, [768])
        scr = anew(F32, [768])
        tA = anew(F32, [384])
        tB = anew(F32, [384])
        tC = anew(F32, [384])
        kf = anew(BF16, [128])
        ckvb = anew(BF16, [256])
        ckvT = anew(BF16, [2, 128])
        kfull = anew(BF16, [6, 96])
        cgt = anew(F32, [2, 128])
        R.dma("pool", "wK", lambda e: e.dma_start(out=wK[:], in_=wink_d[l]), writes=["wK"])
        R.dma("pool", "wukv", lambda e: e.dma_start(out=wukv[:], in_=wukv_d[l]), writes=["wukv"])
        for c in range(2):
            R.op("dve", lambda e, c=c: e.tensor_scalar(out=wukv[:, c, :], in0=wukv[:, c, :], scalar1=glat[:, l, 3 + c:4 + c], scalar2=None,
                                                       op0=ALU.mult), reads=["wukv", "glat"], writes=["wukv"])
        for t in range(NT):
            lat = t < NLT
            h = hTr[t % 2]
            hk = ("hTr", t % 2)
            emit_hT(t, h, hk, xn[t % 2], ("xn", t % 2), 6, t % 2)

            def mmA(e, h=h):
                inst = None
                for kc in range(KC):
                    inst = e.matmul(ps[:, 0, :], h[:, kc, :], wK[:, kc, 0:512], start=(kc == 0), stop=(kc == KC - 1))
                for kc in range(KC):
                    inst = e.matmul(ps[:, 1, 0:32], h[:, kc, :], wK[:, kc, 512:544], start=(kc == 0), stop=(kc == KC - 1))
                return inst
            R.op("pe", mmA, reads=[hk, "wK"], writes=[("ps", 0), ("ps", 1)])

            def mmC(e, h=h):
                inst = None
                for cc in range(6):
                    bank, off = (2, cc * 128) if cc < 4 else (3, (cc - 4) * 128)
                    for kc in range(KC):
                        inst = e.matmul(ps[:, bank, off:off + 128], wK[:, kc, 544 + cc * 128:544 + (cc + 1) * 128], h[:, kc, :],
                                        start=(kc == 0), stop=(kc == KC - 1))
                return inst
            R.op("pe", mmC, reads=[hk, "wK"], writes=[("ps", 2), ("ps", 3)])
            R.op("act", lambda e: e.copy(out=kraw[:, 0:512], in_=ps[:, 0, :]), reads=[("ps", 0)], writes=["kraw"])
            R.op("act", lambda e: e.copy(out=kraw[:, 512:544], in_=ps[:, 1, 0:32]), reads=[("ps", 1), "kraw"], writes=["kraw"])
            R.op("act", lambda e, t=t: e.copy(out=V_g[:, t, :], in_=kraw[:, 128:256]), reads=["kraw"], writes=[("Vg", t)])
            R.op("act", lambda e: e.copy(out=ckvb[:], in_=kraw[:, 256:512]), reads=["kraw"], writes=["ckvb"])
            k3 = kraw[:, 0:128].rearrange("p (h d) -> p h d", h=2)
            sumsq(k3, 2, 64, 8, scr, ["kraw"], [("ss", "k")])
            sumsq(kraw[:, 256:512].unsqueeze(1), 1, 256, 10, scr, ["kraw"], [("ss", "ckv")])
            sumsq(kraw[:, 512:544].unsqueeze(1), 1, 32, 11, scr, ["kraw"], [("ss", "kr")])
            R.op("act", lambda e: e.activation(out=st_r[:, 8:10], in_=st_ss[:, 8:10], func=AF.Sqrt, scale=1.0 / 64, bias=epsb[:]),
                 reads=[("ss", "k"), "eps"], writes=[("sr", "k")])
            R.op("act", lambda e: e.activation(out=st_r[:, 10:11], in_=st_ss[:, 10:11], func=AF.Sqrt, scale=1.0 / 256, bias=epsb[:]),
                 reads=[("ss", "ckv"), "eps"], writes=[("sr", "ckv")])
            R.op("act", lambda e: e.activation(out=st_r[:, 11:12], in_=st_ss[:, 11:12], func=AF.Sqrt, scale=1.0 / 32, bias=epsb[:]),
                 reads=[("ss", "kr"), "eps"], writes=[("sr", "kr")])
            R.op("dve", lambda e: e.reciprocal(out=st_r[:, 8:12], in_=st_r[:, 8:12]),
                 reads=[("sr", "k"), ("sr", "ckv"), ("sr", "kr")], writes=[("sr", "k"), ("sr", "ckv"), ("sr", "kr")])
            kn = tA[:, 0:128].rearrange("p (h d) -> p h d", h=2)
            for hh in range(2):
                R.op("dve", lambda e, hh=hh: e.scalar_tensor_tensor(out=kn[:, hh, :], in0=k3[:, hh, :], scalar=st_r[:, 8 + hh:9 + hh], in1=g_k,
                                                                    op0=ALU.mult, op1=ALU.mult),
                     reads=["kraw", ("sr", "k"), "gsc"], writes=[("kn", hh)], sync_self=True)
            kf3 = kf[:].rearrange("p (h d) -> p h d", h=2)
            if lat:
                rope_apply("dve", kn, kf3, ropeA, t, 2, 64, [("kn", 0), ("kn", 1)], ["kf"],
                           tB[:, 0:128].rearrange("p (h d) -> p h d", h=2), tC[:, 0:128].rearrange("p (h d) -> p h d", h=2))
            else:
                R.op("dve", lambda e: e.tensor_copy(out=kf3, in_=kn), reads=[("kn", 0), ("kn", 1)], writes=["kf"])
            R.op("pe", lambda e: e.transpose(psb(7)[:, 0:128], kf[:], ident[:]), reads=["kf", "ident"], writes=[("ps", 7)])
            R.op("act", lambda e, t=t: e.copy(out=kT_g[:, t * 128:(t + 1) * 128], in_=psb(7)[:, 0:128]), reads=[("ps", 7)],
                 writes=[("kTg", t)])
            def trc(e):
                inst = None
                for c in range(2):
                    inst = e.transpose(psb(7)[:, 128 + c * 128:256 + c * 128], ckvb[:, c * 128:(c + 1) * 128], ident[:])
                return inst
            R.op("pe", trc, reads=["ckvb", "ident"], writes=[("ps", 7)])
            R.op("act", lambda e: e.copy(out=ckvT[:].rearrange("p a b -> p (a b)"), in_=psb(7)[:, 128:384]), reads=[("ps", 7)],
                 writes=["ckvT"])

            def mmkv(e):
                inst = None
                for (bank, c0, n) in ((4, 0, 512), (5, 512, 256)):
                    for c in range(2):
                        inst = e.matmul(ps[:, bank, 0:n], ckvT[:, c, :], wukv[:, c, c0:c0 + n], start=(c == 0), stop=(c == 1))
                return inst
            R.op("pe", mmkv, reads=["ckvT", "wukv"], writes=[("ps", 4), ("ps", 5)])
            R.op("act", lambda e: e.copy(out=kvraw[:, 0:512], in_=ps[:, 4, :]), reads=[("ps", 4)], writes=["kvraw"])
            R.op("act", lambda e: e.copy(out=kvraw[:, 512:768], in_=ps[:, 5, 0:256]), reads=[("ps", 5), "kvraw"], writes=["kvraw"])
            kv3 = kvraw[:].rearrange("p (h d) -> p h d", h=6)
            sumsq(kv3[:, :, 0:64], 6, 64, 16, scr, ["kvraw"], [("ss", "kn")])
            R.op("dve", lambda e: e.tensor_tensor(out=st_ss[:, 12:13], in0=st_r[:, 10:11], in1=st_r[:, 10:11], op=ALU.mult),
                 reads=[("sr", "ckv")], writes=[("ss", "b2")])
            R.op("dve", lambda e: e.tensor_scalar(out=st_ss[:, 16:22], in0=st_ss[:, 16:22], scalar1=st_ss[:, 12:13], scalar2=None, op0=ALU.mult),
                 reads=[("ss", "kn"), ("ss", "b2")], writes=[("ss", "kn")], sync_self=True)
            R.op("act", lambda e: e.activation(out=st_r[:, 16:22], in_=st_ss[:, 16:22], func=AF.Sqrt, scale=1.0 / 64, bias=epsb[:]),
                 reads=[("ss", "kn"), "eps"], writes=[("sr", "kn")])
            R.op("dve", lambda e: e.reciprocal(out=st_r[:, 16:22], in_=st_r[:, 16:22]), reads=[("sr", "kn")], writes=[("sr", "kn")])
            R.op("dve", lambda e: e.tensor_scalar(out=st_r[:, 16:22], in0=st_r[:, 16:22], scalar1=st_r[:, 10:11], scalar2=None, op0=ALU.mult),
                 reads=[("sr", "kn"), ("sr", "ckv")], writes=[("sr", "kn")], sync_self=True)
            for hh in range(6):
                R.op("dve", lambda e, hh=hh: e.scalar_tensor_tensor(out=kfull[:, hh, 0:64], in0=kv3[:, hh, 0:64], scalar=st_r[:, 16 + hh:17 + hh],
                                                                    in1=g_kn, op0=ALU.mult, op1=ALU.mult),
                     reads=["kvraw", ("sr", "kn"), "gsc"], writes=[("kfull", hh)], sync_self=(hh == 0))
            R.op("dve", lambda e, t=t: e.tensor_scalar(out=V_m[:, t, :].rearrange("p (h d) -> p h d", h=6), in0=kv3[:, :, 64:128],
                                                       scalar1=st_r[:, 10:11], scalar2=None, op0=ALU.mult),
                 reads=["kvraw", ("sr", "ckv")], writes=[("Vm", t)])
            krn = tA[:, 128:160].unsqueeze(1)
            R.op("dve", lambda e: e.scalar_tensor_tensor(out=tA[:, 128:160], in0=kraw[:, 512:544], scalar=st_r[:, 11:12], in1=g_kr,
                                                         op0=ALU.mult, op1=ALU.mult), reads=["kraw", ("sr", "kr"), "gsc"], writes=["krn"])
            krf = tA[:, 160:192].unsqueeze(1)
            if lat:
                rope_apply("dve", krn, krf, ropeM, t, 1, 32, ["krn"], ["krf"], tB[:, 128:160].unsqueeze(1), tC[:, 128:160].unsqueeze(1))
                src_kr = krf
                krk = "krf"
            else:
                src_kr = krn
                krk = "krn"
            R.op("dve", lambda e, src_kr=src_kr: e.tensor_copy(out=kfull[:, :, 64:96], in_=src_kr.broadcast_to([128, 6, 32])),
                 reads=[krk], writes=[("kfull", "r")])

            def trk(e):
                inst = None
                for hh in range(6):
                    inst = e.transpose(psb(6)[0:96, hh * 128:(hh + 1) * 128], kfull[:, hh, :], ident[:])
                return inst
            R.op("pe", trk, reads=[("kfull", hh) for hh in range(6)] + [("kfull", "r"), "ident"], writes=[("ps", 6)])
            R.op("act", lambda e, t=t: e.copy(out=kT_m[0:96, :, t * 128:(t + 1) * 128],
                                              in_=psb(6)[0:96, 0:768].rearrange("p (h n) -> p h n", h=6)),
                 reads=[("ps", 6)], writes=[("kTm", t)])
            pos = (1 + t * 128) if lat else (2051 + (t - NLT) * 128)
            R.op("act", lambda e: e.copy(out=cgt[:].rearrange("p a b -> p (a b)"), in_=ps[:, 2, 256:512]), reads=[("ps", 2)], writes=["cgt"])
            R.op("dve", lambda e, pos=pos: e.tensor_tensor(out=uT[:, :, pos:pos + 128], in0=ps[:, 2, 0:256].rearrange("p (a b) -> p a b", a=2),
                                                           in1=cgt[:], op=ALU.mult), reads=[("ps", 2), "cgt", "uT"], writes=["uT"])
            R.op("act", lambda e, pos=pos: e.copy(out=bT[:, :, pos:pos + 128], in_=ps[:, 3, 0:256].rearrange("p (a b) -> p a b", a=2)),
                 reads=[("ps", 3)], writes=["bT"])
        cvt = scr[:, 0:512]
        for ch in range(2):
            segs = [(1 + 512 * i, 512) for i in range(4)] + [(2051, 256)]
            for (p0, n) in segs:
                R.op("dve", lambda e, ch=ch, p0=p0, n=n: e.tensor_scalar(out=cvt[:, 0:n], in0=uT[:, ch, p0:p0 + n], scalar1=convp[:, l, ch, 1:2],
                                                                         scalar2=convp[:, l, ch, 3:4], op0=ALU.mult, op1=ALU.add),
                     reads=["uT", "convp"], writes=["cvt"])
                R.op("dve", lambda e, ch=ch, p0=p0, n=n: e.scalar_tensor_tensor(out=cvt[:, 0:n], in0=uT[:, ch, p0 - 1:p0 - 1 + n],
                                                                                scalar=convp[:, l, ch, 0:1], in1=cvt[:, 0:n], op0=ALU.mult, op1=ALU.add),
                     reads=["uT", "convp", "cvt"], writes=["cvt"])
                R.op("dve", lambda e, ch=ch, p0=p0, n=n: e.scalar_tensor_tensor(out=cvt[:, 0:n], in0=uT[:, ch, p0 + 1:p0 + 1 + n],
                                                                                scalar=convp[:, l, ch, 2:3], in1=cvt[:, 0:n], op0=ALU.mult, op1=ALU.add),
                     reads=["uT", "convp", "cvt"], writes=["cvt"])
                R.op("dve", lambda e, ch=ch, p0=p0, n=n: e.tensor_tensor(out=bT[:, ch, p0:p0 + n], in0=bT[:, ch, p0:p0 + n], in1=cvt[:, 0:n],
                                                                         op=ALU.mult), reads=["bT", "cvt"], writes=["bT"])
        R.barrier()
        Ar.pos = markU

        wQ = anew(BF16, [KC, 768])
        wuq = anew(BF16, [3, 576])
        wo_b = anew(BF16, [KC, 512])
        hTq = [anew(BF16, [KC, 128])] * 2
        xnq = [anew(BF16, [D])] * 2
        qraw = anew(F32, [768])
        qmraw = qraw[:, 0:576]
        scrq = anew(F32, [576])
        tAq = anew(F32, [384])
        tBq = anew(F32, [384])
        tCq = anew(F32, [384])
        qf = anew(BF16, [384])
        cqb = anew(BF16, [384])
        cqT = anew(BF16, [3, 128])
        qmfull = anew(BF16, [6, 96])
        qT_g = anew(BF16, [3, 512])
        qT_m = anew(BF16, [6, 512])
        PT = [anew(BF16, [512]) for _ in range(2)]
        mixT = anew(BF16, [6, 512])
        rden = scrq[:, 0:512]
        R.dma("pool", "wQ", lambda e: e.dma_start(out=wQ[:], in_=winq_d[l]), writes=["wQ"])
        R.dma("pool", "wuq", lambda e: e.dma_start(out=wuq[:], in_=wuq_d[l]), writes=["wuq"])
        for c in range(3):
            R.op("dve", lambda e, c=c: e.tensor_scalar(out=wuq[:, c, :], in0=wuq[:, c, :], scalar1=glat[:, l, c:c + 1], scalar2=None,
                                                       op0=ALU.mult), reads=["wuq", "glat"], writes=["wuq"])
        groups = [(4 * g, 4) for g in range(4)]
        if need_ctx:
            groups.append((16, 2))
        pt_i = 0
        st_i = 0
        o_i = 0
        for (t0, ntile) in groups:
            lat = t0 < NLT
            nq = ntile * 128
            r = 0 if lat else 1
            key_tiles = list(range(NT)) if lat else [16, 17]
            for lt in range(ntile):
                t = t0 + lt
                h = hTq[0]
                hk = ("hTr", 0)
                emit_hT(t, h, hk, xnq[0], ("xn", 0), 6, t % 2)

                def mmQ(e, h=h):
                    inst = None
                    for (bank, c0) in ((4, 0), (5, 384)):
                        for kc in range(KC):
                            inst = e.matmul(ps[:, bank, 0:384], h[:, kc, :], wQ[:, kc, c0:c0 + 384], start=(kc == 0), stop=(kc == KC - 1))
                    return inst
                R.op("pe", mmQ, reads=[hk, "wQ"], writes=[("ps", 4), ("ps", 5)])
                R.op("act", lambda e: e.copy(out=qraw[:, 0:384], in_=ps[:, 4, 0:384]), reads=[("ps", 4)], writes=["qraw"])
                R.op("act", lambda e: e.copy(out=qraw[:, 384:768], in_=ps[:, 5, 0:384]), reads=[("ps", 5), "qraw"], writes=["qraw"])
                R.op("act", lambda e: e.copy(out=cqb[:], in_=qraw[:, 384:768]), reads=["qraw"], writes=["cqb"])
                q3 = qraw[:, 0:384].rearrange("p (h d) -> p h d", h=6)
                sumsq(q3, 6, 64, 24, scrq, ["qraw"], [("ss", "q")])
                sumsq(qraw[:, 384:768].unsqueeze(1), 1, 384, 30, scrq, ["qraw"], [("ss", "cq")])
                R.op("act", lambda e: e.activation(out=st_r[:, 24:30], in_=st_ss[:, 24:30], func=AF.Sqrt, scale=1.0 / 64, bias=epsb[:]),
                     reads=[("ss", "q"), "eps"], writes=[("sr", "q")])
                R.op("act", lambda e: e.activation(out=st_r[:, 30:31], in_=st_ss[:, 30:31], func=AF.Sqrt, scale=1.0 / 384, bias=epsb[:]),
                     reads=[("ss", "cq"), "eps"], writes=[("sr", "cq")])
                R.op("dve", lambda e: e.reciprocal(out=st_r[:, 24:31], in_=st_r[:, 24:31]), reads=[("sr", "q"), ("sr", "cq")],
                     writes=[("sr", "q"), ("sr", "cq")])
                qn = tAq[:].rearrange("p (h d) -> p h d", h=6)
                for hh in range(6):
                    R.op("dve", lambda e, hh=hh: e.scalar_tensor_tensor(out=qn[:, hh, :], in0=q3[:, hh, :], scalar=st_r[:, 24 + hh:25 + hh], in1=g_q,
                                                                        op0=ALU.mult, op1=ALU.mult),
                         reads=["qraw", ("sr", "q"), "gsc"], writes=[("qn", hh)], sync_self=(hh == 0))
                qf3 = qf[:].rearrange("p (h d) -> p h d", h=6)
                qnk = [("qn", hh) for hh in range(6)]
                if lat:
                    rope_apply("dve", qn, qf3, ropeA, t, 6, 64, qnk, ["qf"], tBq[:].rearrange("p (h d) -> p h d", h=6),
                               tCq[:].rearrange("p (h d) -> p h d", h=6))
                else:
                    R.op("dve", lambda e: e.tensor_copy(out=qf3, in_=qn), reads=qnk, writes=["qf"])

                def trq(e):
                    inst = None
                    for pr in range(3):
                        inst = e.transpose(psb(7)[:, pr * 128:(pr + 1) * 128], qf[:, pr * 128:(pr + 1) * 128], ident[:])
                    for c in range(3):
                        inst = e.transpose(psb(7)[:, 384 + c * 128:512 + c * 128], cqb[:, c * 128:(c + 1) * 128], ident[:])
                    return inst
                R.op("pe", trq, reads=["qf", "cqb", "ident"], writes=[("ps", 7)])
                R.op("act", lambda e, lt=lt: e.copy(out=qT_g[:, :, lt * 128:(lt + 1) * 128], in_=psb(7)[:, 0:384].rearrange("p (a b) -> p a b", a=3)),
                     reads=[("ps", 7)], writes=[("qTg", lt)])
                R.op("act", lambda e: e.copy(out=cqT[:].rearrange("p a b -> p (a b)"), in_=psb(7)[:, 384:768]), reads=[("ps", 7)], writes=["cqT"])

                def mmuq(e):
                    inst = None
                    for (bank, c0) in ((4, 0), (5, 288)):
                        for c in range(3):
                            inst = e.matmul(ps[:, bank, 0:288], cqT[:, c, :], wuq[:, c, c0:c0 + 288], start=(c == 0), stop=(c == 2))
                    return inst
                R.op("pe", mmuq, reads=["cqT", "wuq"], writes=[("ps", 4), ("ps", 5)])
                R.op("act", lambda e: e.copy(out=qmraw[:, 0:288], in_=ps[:, 4, 0:288]), reads=[("ps", 4)], writes=["qraw"])
                R.op("act", lambda e: e.copy(out=qmraw[:, 288:576], in_=ps[:, 5, 0:288]), reads=[("ps", 5), "qraw"], writes=["qraw"])
                qm3 = qmraw[:].rearrange("p (h d) -> p h d", h=6)
                sumsq(qm3[:, :, 0:64], 6, 64, 32, scrq, ["qraw"], [("ss", "qmn")])
                sumsq(qm3[:, :, 64:96], 6, 32, 38, scrq, ["qraw"], [("ss", "qmr")])
                R.op("dve", lambda e: e.tensor_tensor(out=st_ss[:, 31:32], in0=st_r[:, 30:31], in1=st_r[:, 30:31], op=ALU.mult),
                     reads=[("sr", "cq")], writes=[("ss", "a2")])
                R.op("dve", lambda e: e.tensor_scalar(out=st_ss[:, 32:44], in0=st_ss[:, 32:44], scalar1=st_ss[:, 31:32], scalar2=None, op0=ALU.mult),
                     reads=[("ss", "qmn"), ("ss", "qmr"), ("ss", "a2")], writes=[("ss", "qmn"), ("ss", "qmr")], sync_self=True)
                R.op("act", lambda e: e.activation(out=st_r[:, 32:38], in_=st_ss[:, 32:38], func=AF.Sqrt, scale=1.0 / 64, bias=epsb[:]),
                     reads=[("ss", "qmn"), "eps"], writes=[("sr", "qmn")])
                R.op("act", lambda e: e.activation(out=st_r[:, 38:44], in_=st_ss[:, 38:44], func=AF.Sqrt, scale=1.0 / 32, bias=epsb[:]),
                     reads=[("ss", "qmr"), "eps"], writes=[("sr", "qmr")])
                R.op("dve", lambda e: e.reciprocal(out=st_r[:, 32:44], in_=st_r[:, 32:44]), reads=[("sr", "qmn"), ("sr", "qmr")],
                     writes=[("sr", "qmn"), ("sr", "qmr")])
                R.op("dve", lambda e: e.tensor_scalar(out=st_r[:, 32:44], in0=st_r[:, 32:44], scalar1=st_r[:, 30:31], scalar2=None, op0=ALU.mult),
                     reads=[("sr", "qmn"), ("sr", "qmr"), ("sr", "cq")], writes=[("sr", "qmn"), ("sr", "qmr")], sync_self=True)
                qr = tAq[:, 0:192].rearrange("p (h d) -> p h d", h=6)
                for hh in range(6):
                    R.op("dve", lambda e, hh=hh: e.scalar_tensor_tensor(out=qmfull[:, hh, 0:64], in0=qm3[:, hh, 0:64], scalar=st_r[:, 32 + hh:33 + hh],
                                                                        in1=g_qn, op0=ALU.mult, op1=ALU.mult),
                         reads=["qraw", ("sr", "qmn"), "gsc"], writes=[("qmfull", hh)], sync_self=(hh == 0))
                    R.op("dve", lambda e, hh=hh: e.scalar_tensor_tensor(out=qr[:, hh, :], in0=qm3[:, hh, 64:96], scalar=st_r[:, 38 + hh:39 + hh],
                                                                        in1=g_qr, op0=ALU.mult, op1=ALU.mult),
                         reads=["qraw", ("sr", "qmr"), "gsc", "qf"] + qnk, writes=[("qr", hh)])
                qrk = [("qr", hh) for hh in range(6)]
                if lat:
                    rope_apply("dve", qr, qmfull[:, :, 64:96], ropeM, t, 6, 32, qrk, [("qmfull", "r")],
                               tBq[:, 0:192].rearrange("p (h d) -> p h d", h=6), tCq[:, 0:192].rearrange("p (h d) -> p h d", h=6))
                else:
                    R.op("dve", lambda e: e.tensor_copy(out=qmfull[:, :, 64:96], in_=qr), reads=qrk, writes=[("qmfull", "r")])

                def trqm(e):
                    inst = None
                    for hh in range(6):
                        inst = e.transpose(psb(7)[0:96, hh * 128:(hh + 1) * 128], qmfull[:, hh, :], ident[:])
                    return inst
                R.op("pe", trqm, reads=[("qmfull", hh) for hh in range(6)] + [("qmfull", "r"), "ident"], writes=[("ps", 7)])
                R.op("act", lambda e, lt=lt: e.copy(out=qT_m[0:96, :, lt * 128:(lt + 1) * 128],
                                                    in_=psb(7)[0:96, 0:768].rearrange("p (h n) -> p h n", h=6)),
                     reads=[("ps", 7)], writes=[("qTm", lt)])
            qgk = [("qTg", lt) for lt in range(ntile)]
            qmk = [("qTm", lt) for lt in range(ntile)]
            for slot in range(12):
                gqa = slot < 6
                if gqa:
                    pr, hf = slot // 2, slot % 2
                    chunk = pr
                else:
                    hm = slot - 6
                    pr, hf = hm // 2, hm % 2
                    chunk = 3 + pr
                bo = 2 if (o_i % 2 == 0) else 4
                bd = bo + 1
                o_i += 1
                nk = len(key_tiles)
                for ki, kt in enumerate(key_tiles):
                    bs = st_i % 2
                    st_i += 1
                    pt = PT[pt_i % 2]
                    ptk = ("PT", pt_i % 2)
                    pt_i += 1
                    if gqa:
                        R.op("pe", lambda e, bs=bs, hf=hf, kt=kt, pr=pr, nq=nq: e.matmul(
                            ps[:, bs, 0:nq], kT_g[64 * hf:64 * hf + 64, kt * 128:(kt + 1) * 128], qT_g[64 * hf:64 * hf + 64, pr, 0:nq],
                            start=True, stop=True), reads=[("kTg", kt)] + qgk, writes=[("ps", bs)])
                    else:
                        R.op("pe", lambda e, bs=bs, hm=hm, kt=kt, nq=nq: e.matmul(
                            ps[:, bs, 0:nq], kT_m[0:96, hm, kt * 128:(kt + 1) * 128], qT_m[0:96, hm, 0:nq],
                            start=True, stop=True), reads=[("kTm", kt)] + qmk, writes=[("ps", bs)])
                    R.op("act", lambda e, bs=bs, pt=pt, nq=nq: e.activation(out=pt[:, 0:nq], in_=ps[:, bs, 0:nq], func=AF.Exp),
                         reads=[("ps", bs)], writes=[ptk])
                    if gqa:
                        vap = V_g[:, kt, :]
                        vk = ("Vg", kt)
                    else:
                        vap = V_m[:, kt, pr * 128:(pr + 1) * 128]
                        vk = ("Vm", kt)

                    def mmpv(e, vap=vap, pt=pt, bo=bo, bd=bd, ki=ki, nk=nk, nq=nq):
                        e.matmul(ps[:, bo, 0:nq], vap, pt[:, 0:nq], start=(ki == 0), stop=(ki == nk - 1))
                        return e.matmul(ps[:, bd, 0:nq], ones_bf[:], pt[:, 0:nq], start=(ki == 0), stop=(ki == nk - 1))
                    R.op("pe", mmpv, reads=[vk, ptk, "ones"], writes=[("ps", bo), ("ps", bd)])
                p0 = 64 * hf
                R.op("dve", lambda e, bd=bd, p0=p0, nq=nq: e.reciprocal(out=rden[p0:p0 + 64, 0:nq], in_=ps[p0:p0 + 64, bd, 0:nq]),
                     reads=[("ps", bd)], writes=["rden"])
                R.op("dve", lambda e, bo=bo, p0=p0, chunk=chunk, nq=nq: e.tensor_tensor(out=mixT[p0:p0 + 64, chunk, 0:nq], in0=ps[p0:p0 + 64, bo, 0:nq],
                                                                                in1=rden[p0:p0 + 64, 0:nq], op=ALU.mult),
                     reads=[("ps", bo), "rden"], writes=[("mixT", chunk, hf)])
            mixk = [("mixT", c, hf) for c in range(6) for hf in range(2)]
            for hd in range(2):
                R.dma("pool", "wo", lambda e, hd=hd: e.dma_start(out=wo_b[:], in_=wout_d[l, hd]), writes=["wo"])
                for lt in range(ntile):
                    t = t0 + lt
                    pos = (1 + t * 128) if lat else (2051 + (t - NLT) * 128)
                    bk = 6 + (lt % 2)

                    def mmo(e, lt=lt, pos=pos, bk=bk):
                        inst = None
                        for c in range(8):
                            if c < 3:
                                lh = mixT[:, c, lt * 128:(lt + 1) * 128]
                            elif c < 5:
                                lh = bT[:, c - 3, pos:pos + 128]
                            else:
                                lh = mixT[:, c - 2, lt * 128:(lt + 1) * 128]
                            inst = e.matmul(ps[:, bk, :], lh, wo_b[:, c, :], start=(c == 0), stop=(c == 7))
                        return inst
                    R.op("pe", mmo, reads=mixk + ["bT", "wo"], writes=[("ps", bk)])
                    R.op("dve", lambda e, bk=bk, r=r, hd=hd: e.tensor_tensor(out=ps[:, bk, :], in0=ps[:, bk, :],
                                                                             in1=gate_bc[:, r, hd * 512:(hd + 1) * 512], op=ALU.mult),
                         reads=[("ps", bk), ("gate", r)], writes=[("ps", bk)])
                    R.op("dve", lambda e, bk=bk, t=t, hd=hd: e.tensor_tensor(out=xs[:, t, hd * 512:(hd + 1) * 512],
                                                                             in0=xs[:, t, hd * 512:(hd + 1) * 512], in1=ps[:, bk, :], op=ALU.add),
                         reads=[("ps", bk), ("x", t)], writes=[("x", t)])
        R.barrier()
        Ar.pos = mark0

    done = False
    for l in range(L):
        if l > 0:
            R.new_epoch()
        need_ctx = l < DEPTH - 1
        modulation(l)
        ffn(l, 0, 0, list(range(NT)))
        if stop_after == (l, "ffn1"):
            break
        mixer(l, need_ctx)
        if stop_after == (l, "mix"):
            break
        ffn(l, 1, 2, list(range(NT)) if need_ctx else list(range(NLT)))
        if stop_after == (l, "ffn2"):
            break

    for i in range(4):
        R.dma("sp", "out", lambda e, i=i: e.dma_start(
            out=out_d[512 * i:512 * (i + 1), :].rearrange("(t p) d -> p t d", p=128), in_=xs[:, 4 * i:4 * i + 4, :]),
            reads=[("x", 4 * i + k) for k in range(4)])
    R.op("sp", lambda e: None, reads=[], writes=[("x", k) for k in range(NLT)])

    R.finalize()
    esems = {}
    for e in Rec.ENG:
        for ep in range(R.n_epochs):
            esems[(e, ep)] = es.enter_context(nc.semaphore("s_%s_%d" % (e, ep)))
    lsems = {ln: es.enter_context(nc.semaphore("l_%s" % ln)) for ln in R.lane_cnt}
    block = es.enter_context(nc.Block())

    @block.tensor
    def _(e):
        R.emit("pe", e, esems, lsems)

    @block.scalar
    def _(e):
        R.emit("act", e, esems, lsems)

    @block.vector
    def _(e):
        R.emit("dve", e, esems, lsems)

    @block.gpsimd
    def _(e):
        R.emit("pool", e, esems, lsems)

    @block.sync
    def _(e):
        R.emit("sp", e, esems, lsems)

    es.close()
    return nc


_CACHE = {}


def kernel(**inputs):
    inp = {k: np.asarray(v) for k, v in inputs.items()}
    if "nc" not in _CACHE:
        _CACHE["nc"] = build_program(DEPTH)
    nc = _CACHE["nc"]
    sh = prep_shared(inp, DEPTH)
    in_maps = []
    for b in range(8):
        m = dict(sh)
        m.update(prep_core(inp, b))
        in_maps.append(m)
    res = run_bass_kernel_spmd(nc, in_maps, core_ids=list(range(8)))
    out = np.stack([np.asarray(r["out"]) for r in res.results], axis=0)
    return out.astype(np.float32)
```

```python
import numpy as np
from contextlib import ExitStack
import concourse.bass as bass
import concourse.mybir as mybir
from concourse.bass_utils import run_bass_kernel_spmd

F32 = mybir.dt.float32
BF16 = mybir.dt.bfloat16
U8 = mybir.dt.uint8
ALU = mybir.AluOpType
AF = mybir.ActivationFunctionType
AX = mybir.AxisListType

D = 1024
DEPTH = 4
NT = 18
NLT = 16
T = NT * 128
DFF = 2816
NFC = 22
EPS = 1e-6
KC = 8
NGAIN = 320
FF_GROUPS = [4, 4, 4, 4, 4, 2]


class Rec:
    ENG = ("pe", "act", "dve", "pool", "sp")

    def __init__(self):
        self.ops = {e: [] for e in self.ENG}
        self.res = {}
        self.lane_cnt = {}
        self.epoch_starts = {e: [0] for e in self.ENG}
        self.pending = {e: [] for e in self.ENG}

    def new_epoch(self):
        for e in self.ENG:
            self.epoch_starts[e].append(len(self.ops[e]))

    def _deps(self, reads, writes):
        d = []
        for k in reads:
            st = self.res.get(k)
            if st is not None and st[0] is not None:
                d.append(st[0])
        for k in writes:
            st = self.res.get(k)
            if st is not None:
                if st[0] is not None:
                    d.append(st[0])
                for kk, v in st[1].items():
                    d.append(kk + (v,))
        return d

    def _commit(self, tok, reads, writes):
        for k in reads:
            st = self.res.get(k)
            if st is None:
                st = [None, {}]
                self.res[k] = st
            key = tok[:2]
            if st[1].get(key, -1) < tok[2]:
                st[1][key] = tok[2]
        for k in writes:
            self.res[k] = [tok, {}]

    def op(self, eng, fn, reads=(), writes=(), sync_self=False):
        deps = self._deps(reads, writes) + self.pending[eng]
        self.pending[eng] = []
        idx = len(self.ops[eng])
        tok = ("e", eng, idx)
        self.ops[eng].append({"fn": fn, "deps": deps, "lane": None, "ss": sync_self})
        self._commit(tok, reads, writes)
        return tok

    def dma(self, eng, lane, fn, reads=(), writes=()):
        deps = self._deps(reads, writes) + self.pending[eng]
        self.pending[eng] = []
        self.lane_cnt[lane] = self.lane_cnt.get(lane, 0) + 1
        tok = ("d", lane, self.lane_cnt[lane])
        self.ops[eng].append({"fn": fn, "deps": deps, "lane": lane, "ss": True})
        self._commit(tok, reads, writes)
        return tok

    def barrier(self):
        last = []
        for e in self.ENG:
            for i in range(len(self.ops[e]) - 1, -1, -1):
                if self.ops[e][i]["lane"] is None:
                    last.append(("e", e, i))
                    break
        for l, c in self.lane_cnt.items():
            last.append(("d", l, c))
        for e in self.ENG:
            self.pending[e] = self.pending[e] + list(last)

    def finalize(self):
        self.signal = {e: [False] * len(self.ops[e]) for e in self.ENG}
        for e in self.ENG:
            for i, op in enumerate(self.ops[e]):
                for tok in op["deps"]:
                    if tok[0] == "e" and (tok[1] != e or op["ss"] or tok[2] == i - 1):
                        self.signal[tok[1]][tok[2]] = True
        self.sigval = {}
        self.epoch_of = {}
        for e in self.ENG:
            starts = self.epoch_starts[e]
            vals = [None] * len(self.ops[e])
            eps = [0] * len(self.ops[e])
            ep = 0
            cnt = 0
            for i in range(len(self.ops[e])):
                while ep + 1 < len(starts) and i >= starts[ep + 1]:
                    ep += 1
                    cnt = 0
                if self.signal[e][i]:
                    cnt += 1
                    vals[i] = cnt
                eps[i] = ep
            self.sigval[e] = vals
            self.epoch_of[e] = eps
        self.n_epochs = max(len(s) for s in self.epoch_starts.values())

    def emit(self, eng, e, esems, lsems):
        waited = {}
        for i, op in enumerate(self.ops[eng]):
            need = {}
            for tok in op["deps"]:
                if tok[0] == "e":
                    if tok[1] == eng and not op["ss"] and tok[2] != i - 1:
                        continue
                    key = ("e", tok[1], self.epoch_of[tok[1]][tok[2]])
                    val = self.sigval[tok[1]][tok[2]]
                else:
                    key = ("d", tok[1])
                    val = 16 * tok[2]
                if need.get(key, 0) < val:
                    need[key] = val
            for key, val in need.items():
                if waited.get(key, 0) >= val:
                    continue
                if key[0] == "e":
                    later = [k for k in waited if k[0] == "e" and k[1] == key[1] and k[2] > key[2]]
                    if later:
                        continue
                    e.wait_ge(esems[(key[1], key[2])], val)
                else:
                    e.wait_ge(lsems[key[1]], val)
                waited[key] = val
            inst = op["fn"](e)
            if inst is None:
                continue
            if op["lane"] is not None:
                inst.then_inc(lsems[op["lane"]], 16)
            elif self.signal[eng][i]:
                inst.then_inc(esems[(eng, self.epoch_of[eng][i])], 1)


Q_ORDER = [0, 3, 1, 4, 2, 5]


def _rope_np(dim):
    rows = 2048 // 64
    row = np.repeat(np.arange(rows, dtype=np.float32), 64)
    col = np.tile(np.arange(64, dtype=np.float32), rows)
    half = dim // 2
    inv = (1.0 / (10000.0 ** (np.arange(0, half, 2, dtype=np.float32) / half))).astype(np.float32)
    ar = row[:, None] * inv[None, :]
    ac = col[:, None] * inv[None, :]
    ang = np.concatenate([ar, ar, ac, ac], axis=-1).astype(np.float32)
    cos = np.cos(ang).astype(np.float32)
    sin = np.sin(ang).astype(np.float32)
    q = dim // 4
    sgn = np.concatenate([-np.ones(q), np.ones(q), -np.ones(q), np.ones(q)]).astype(np.float32)
    sin = sin * sgn[None, :]
    cos = np.ascontiguousarray(cos.reshape(16, 128, dim).transpose(1, 0, 2))
    sin = np.ascontiguousarray(sin.reshape(16, 128, dim).transpose(1, 0, 2))
    return cos, sin


def prep_shared(inp, n_layers):
    L = n_layers
    f = lambda a: np.ascontiguousarray(a, dtype=np.float32)
    sh = {}
    wm = inp["w_mod"][:L]
    sh["wmod"] = f(wm.reshape(L, KC, 128, 18, 512).transpose(0, 3, 2, 1, 4))
    sh["bmodT"] = f(inp["b_mod"][:L].reshape(L, 72, 128).transpose(2, 0, 1))
    sh["gn"] = f(inp["g_norm"][:L].reshape(L, 3, KC, 128).transpose(3, 0, 1, 2))
    wg = inp["ffn_w_gate"][:L].reshape(L, 2, KC, 128, NFC, 128)
    wu = inp["ffn_w_up"][:L].reshape(L, 2, KC, 128, NFC, 128)
    wgu = np.stack([wg, wu], axis=0)
    sh["wgu"] = f(wgu.transpose(1, 2, 5, 4, 0, 3, 6))
    sh["wd"] = f(inp["ffn_w_down"][:L])
    win = inp["w_in"][:L]
    qcols = np.concatenate([np.arange(64 * h, 64 * h + 64) for h in Q_ORDER])
    kside = np.concatenate([np.arange(384, 512), np.arange(512, 640), np.arange(1792, 2048),
                            np.arange(2048, 2080),
                            np.arange(640, 896), np.arange(1152, 1408), np.arange(896, 1152)])
    qside = np.concatenate([qcols, np.arange(1408, 1792)])
    sh["wink"] = f(win[:, :, kside].reshape(L, KC, 128, 1312).transpose(0, 2, 1, 3))
    sh["winq"] = f(win[:, :, qside].reshape(L, KC, 128, 768).transpose(0, 2, 1, 3))
    sh["wuq"] = f(inp["mla_w_uq"][:L].reshape(L, 3, 128, 576).transpose(0, 2, 1, 3))
    sh["wukv"] = f(inp["mla_w_ukv"][:L].reshape(L, 2, 128, 768).transpose(0, 2, 1, 3))
    rows = []
    for pr in range(3):
        for hf in range(2):
            h = Q_ORDER[2 * pr + hf]
            rows.append(np.arange(64 * h, 64 * h + 64))
    rows.append(np.arange(384, 640))
    rows.append(np.arange(640, 1024))
    rows = np.concatenate(rows)
    wo = inp["w_out"][:L][:, rows, :]
    sh["wout"] = f(wo.reshape(L, KC, 128, 2, 512).transpose(0, 3, 2, 1, 4))
    sh["gains"] = f(np.concatenate([inp["gqa_g_q"][:L], inp["gqa_g_k"][:L], inp["mla_g_qn"][:L],
                                    inp["mla_g_kn"][:L], inp["mla_g_qr"][:L], inp["mla_g_kr"][:L]], axis=1))
    cw = inp["conv_w"][:L]
    cb = inp["conv_b"][:L]
    cp = np.concatenate([cw, cb[:, None, :]], axis=1)
    sh["convp"] = f(cp.reshape(L, 4, 2, 128).transpose(3, 0, 2, 1))
    gl = np.concatenate([inp["mla_g_cq"][:L].reshape(L, 3, 128), inp["mla_g_ckv"][:L].reshape(L, 2, 128)], axis=1)
    sh["glat"] = f(gl.transpose(2, 0, 1))
    ca, sa = _rope_np(64)
    cm, sm = _rope_np(32)
    sh["ropeA"] = f(np.stack([ca, sa], axis=1))
    sh["ropeM"] = f(np.stack([cm, sm], axis=1))
    return sh


def prep_core(inp, b):
    xin = np.concatenate([inp["x"][b], inp["ctx"][b]], axis=0)
    cc = np.stack([inp["c"][b], inp["c_ctx"]], axis=-1)
    ccT = np.ascontiguousarray(cc.reshape(KC, 128, 2).transpose(1, 0, 2), dtype=np.float32)
    return {"xin": np.ascontiguousarray(xin, dtype=np.float32), "ccT": ccT}


def build_program(n_layers=DEPTH, stop_after=None):
    L = n_layers
    nc = bass.Bass("TRN2", target_bir_lowering=False)
    dt_in = lambda name, shape: nc.dram_tensor(name, list(shape), F32, kind="ExternalInput").ap()
    xin = dt_in("xin", [T, D])
    ccT_d = dt_in("ccT", [128, KC, 2])
    wmod_d = dt_in("wmod", [L, 18, 128, KC, 512])
    bmodT_d = dt_in("bmodT", [128, L, 72])
    gn_d = dt_in("gn", [128, L, 3, KC])
    wgu_d = dt_in("wgu", [L, 2, NFC, 128, 2, KC, 128])
    wd_d = dt_in("wd", [L, 2, DFF, D])
    wink_d = dt_in("wink", [L, 128, KC, 1312])
    winq_d = dt_in("winq", [L, 128, KC, 768])
    wuq_d = dt_in("wuq", [L, 128, 3, 576])
    wukv_d = dt_in("wukv", [L, 128, 2, 768])
    wout_d = dt_in("wout", [L, 2, 128, KC, 512])
    gains_d = dt_in("gains", [L, NGAIN])
    convp_d = dt_in("convp", [128, L, 2, 4])
    glat_d = dt_in("glat", [128, L, 5])
    ropeA_d = dt_in("ropeA", [128, 2, 16, 64])
    ropeM_d = dt_in("ropeM", [128, 2, 16, 32])
    out_d = nc.dram_tensor("out", [2048, D], F32, kind="ExternalOutput").ap()

    R = Rec()
    es = ExitStack()
    sb = lambda name, shape, dt: es.enter_context(nc.sbuf_tensor(name, list(shape), dt))
    xs = sb("xs", [128, NT, D], F32)
    ropeA = sb("ropeA_s", [128, 2, 16, 64], BF16)
    ropeM = sb("ropeM_s", [128, 2, 16, 32], BF16)
    ident = sb("ident", [128, 128], BF16)
    identf = sb("identf", [128, 128], F32)
    ones_bf = sb("ones_bf", [128, 128], BF16)
    ccs = sb("ccs", [128, KC, 2], F32)
    s2 = sb("s2", [128, KC, 2], BF16)
    gn = sb("gn_s", [128, L, 3, KC], F32)
    bmodT = sb("bmodT_s", [128, L, 72], F32)
    convp = sb("convp_s", [128, L, 2, 4], F32)
    glat = sb("glat_s", [128, L, 5], F32)
    gains = sb("gains_s", [128, NGAIN], F32)
    modT = sb("modT", [128, 72, 2], F32)
    Amod = sb("Amod", [128, KC, 2], F32)
    Bmod = sb("Bmod", [128, KC, 2], F32)
    gate_bc = sb("gate_bc", [128, 2, D], BF16)
    epsb = sb("epsb", [128, 1], F32)
    st_ss = sb("st_ss", [128, 64], F32)
    st_r = sb("st_r", [128, 64], F32)
    ARENA_BYTES = 120 * 1024
    arena = sb("arena", [128, ARENA_BYTES], U8)
    ps = es.enter_context(nc.psum_tensor("ps", [128, 8, 512], F32))

    class Ar:
        pos = 0

    def aalloc(nbytes):
        a0 = (Ar.pos + 31) // 32 * 32
        Ar.pos = a0 + nbytes
        assert Ar.pos <= ARENA_BYTES, ("arena overflow", Ar.pos)
        return a0

    def aview(a0, dt, shape):
        esz = 2 if dt == BF16 else 4
        n = int(np.prod(shape))
        v = arena[:, a0:a0 + n * esz].bitcast(dt)
        if len(shape) == 1:
            return v
        names = " ".join("a%d" % i for i in range(len(shape)))
        kw = {"a%d" % i: shape[i] for i in range(1, len(shape))}
        return v.rearrange("p (%s) -> p %s" % (names, names), **kw)

    def anew(dt, shape):
        esz = 2 if dt == BF16 else 4
        return aview(aalloc(int(np.prod(shape)) * esz), dt, shape)

    psb = lambda b: ps[:, b, :].bitcast(BF16)

    for i in range(6):
        R.dma("sp", "xin%d" % i, lambda e, i=i: e.dma_start(
            out=xs[:, 3 * i:3 * i + 3, :], in_=xin[384 * i:384 * (i + 1), :].rearrange("(t p) d -> p t d", p=128)),
            writes=[("x", 3 * i), ("x", 3 * i + 1), ("x", 3 * i + 2)])
    R.dma("sp", "cst0", lambda e: e.dma_start(out=ccs[:], in_=ccT_d), writes=["ccs"])
    R.dma("sp", "cst1", lambda e: e.dma_start(out=gn[:], in_=gn_d), writes=["gn"])
    R.dma("sp", "cst2", lambda e: e.dma_start(out=bmodT[:], in_=bmodT_d), writes=["bmodT"])
    R.dma("sp", "cst3", lambda e: e.dma_start(out=convp[:], in_=convp_d), writes=["convp"])
    R.dma("sp", "cst4", lambda e: e.dma_start(out=glat[:], in_=glat_d), writes=["glat"])
    R.dma("pool", "cstA", lambda e: e.dma_start(out=ropeA[:], in_=ropeA_d), writes=["ropeA"])
    R.dma("pool", "cstM", lambda e: e.dma_start(out=ropeM[:], in_=ropeM_d), writes=["ropeM"])
    R.op("pool", lambda e: e.memset(identf[:], 0.0), writes=["identf"])
    R.op("pool", lambda e: e.affine_select(out=identf[:], in_=identf[:], pattern=[[-1, 128]],
                                           compare_op=ALU.not_equal, fill=1.0, base=0, channel_multiplier=1),
         reads=["identf"], writes=["identf"])
    R.op("pool", lambda e: e.tensor_copy(out=ident[:], in_=identf[:]), reads=["identf"], writes=["ident"])
    R.op("pool", lambda e: e.memset(ones_bf[:], 1.0), writes=["ones"])
    R.op("pool", lambda e: e.memset(epsb[:], EPS), writes=["eps"])
    R.op("act", lambda e: e.activation(out=s2[:], in_=ccs[:], func=AF.Silu), reads=["ccs"], writes=["s2"])

    def rstd_from_ss(n, scale, key):
        R.op("act", lambda e: e.activation(out=st_r[:, 0:n], in_=st_ss[:, 0:n], func=AF.Sqrt, scale=scale, bias=epsb[:]),
             reads=[("ss", key), "eps"], writes=[("sr", key)])
        R.op("dve", lambda e: e.reciprocal(out=st_r[:, 0:n], in_=st_r[:, 0:n]), reads=[("sr", key)], writes=[("sr", key)])

    def modulation(l):
        mark = Ar.pos
        ring = [anew(BF16, [KC, 512]) for _ in range(2)]
        for c in range(18):
            s = c % 2
            R.dma("pool", "wm%d" % s, lambda e, c=c, s=s: e.dma_start(out=ring[s][:], in_=wmod_d[l, c]),
                  writes=[("wmring", s)])

            def mm(e, c=c, s=s):
                inst = None
                for fc in range(4):
                    col = (c * 4 + fc) * 2
                    for kc in range(KC):
                        inst = e.matmul(ps[:, 0, col:col + 2], ring[s][:, kc, fc * 128:(fc + 1) * 128], s2[:, kc, :],
                                        start=(kc == 0), stop=(kc == KC - 1))
                return inst
            R.op("pe", mm, reads=[("wmring", s), "s2"], writes=[("ps", 0)])
        R.op("dve", lambda e: e.tensor_tensor(
            out=modT[:], in0=ps[:, 0, 0:144].rearrange("p (c r) -> p c r", r=2),
            in1=bmodT[:, l, :].unsqueeze(2).broadcast_to([128, 72, 2]), op=ALU.add),
            reads=[("ps", 0), "bmodT"], writes=["modT"])
        R.barrier()
        Ar.pos = mark

    def sub_modulation(l, j, gate_mult):
        c_shift, c_scale, c_gate = (3 * j) * 8, (3 * j + 1) * 8, (3 * j + 2) * 8
        R.op("dve", lambda e: e.tensor_scalar(out=Amod[:], in0=modT[:, c_scale:c_scale + 8, :], scalar1=1.0, scalar2=None,
                                              op0=ALU.add), reads=["modT"], writes=["Amod"])
        R.op("dve", lambda e: e.tensor_tensor(out=Amod[:], in0=Amod[:], in1=gn[:, l, j, :].unsqueeze(2).broadcast_to([128, KC, 2]),
                                              op=ALU.mult), reads=["Amod", "gn"], writes=["Amod"])
        R.op("dve", lambda e: e.tensor_copy(out=Bmod[:], in_=modT[:, c_shift:c_shift + 8, :]), reads=["modT"], writes=["Bmod"])
        mark = Ar.pos
        rep = anew(BF16, [KC, 128])
        for r in range(2):
            R.op("dve", lambda e, r=r: e.tensor_scalar(
                out=rep[:], in0=modT[:, c_gate:c_gate + 8, r:r + 1].broadcast_to([128, KC, 128]),
                scalar1=gate_mult, scalar2=None, op0=ALU.mult), reads=["modT"], writes=["rep"])

            def tr(e):
                inst = None
                for kc in range(KC):
                    inst = e.transpose(psb(1)[:, kc * 128:(kc + 1) * 128], rep[:, kc, :], ident[:])
                return inst
            R.op("pe", tr, reads=["rep", "ident"], writes=[("ps", 1)])
            R.op("act", lambda e, r=r: e.copy(out=gate_bc[:, r, :], in_=psb(1)[:, 0:1024]), reads=[("ps", 1)],
                 writes=[("gate", r)])
        R.barrier()
        Ar.pos = mark

    def emit_hT(t, dst, dst_key, xn, xn_key, psbank, stat_col):
        r = 0 if t < NLT else 1
        R.op("act", lambda e: e.activation(out=xn[:], in_=xs[:, t, :], func=AF.Square, accum_out=st_ss[:, stat_col:stat_col + 1]),
             reads=[("x", t)], writes=[xn_key, ("ss", "h%d" % stat_col)])
        R.op("act", lambda e: e.activation(out=st_r[:, stat_col:stat_col + 1], in_=st_ss[:, stat_col:stat_col + 1], func=AF.Sqrt,
                                           scale=1.0 / D, bias=epsb[:]),
             reads=[("ss", "h%d" % stat_col), "eps"], writes=[("sr", "h%d" % stat_col)], sync_self=True)
        R.op("dve", lambda e: e.reciprocal(out=st_r[:, stat_col:stat_col + 1], in_=st_r[:, stat_col:stat_col + 1]),
             reads=[("sr", "h%d" % stat_col)], writes=[("sr", "h%d" % stat_col)])
        R.op("dve", lambda e: e.tensor_scalar(out=xn[:], in0=xs[:, t, :], scalar1=st_r[:, stat_col:stat_col + 1], scalar2=None,
                                              op0=ALU.mult), reads=[("x", t), ("sr", "h%d" % stat_col), xn_key], writes=[xn_key], sync_self=True)

        def tr(e):
            inst = None
            for kc in range(KC):
                inst = e.transpose(psb(psbank)[:, kc * 128:(kc + 1) * 128], xn[:, kc * 128:(kc + 1) * 128], ident[:])
            return inst
        R.op("pe", tr, reads=[xn_key, "ident"], writes=[("ps", psbank)])
        for kc in range(KC):
            R.op("act", lambda e, kc=kc: e.activation(out=dst[:, kc, :], in_=psb(psbank)[:, kc * 128:(kc + 1) * 128],
                                                      func=AF.Identity, scale=Amod[:, kc, r:r + 1], bias=Bmod[:, kc, r:r + 1]),
                 reads=[("ps", psbank), "Amod", "Bmod"], writes=[dst_key])

    def ffn(l, f, j, tiles):
        ntl = len(tiles)
        sub_modulation(l, j, 0.5)
        mark = Ar.pos
        hT = anew(BF16, [KC, T])
        aT = anew(BF16, [4, T])
        gu_ring = [anew(BF16, [2, KC, 128]) for _ in range(3)]
        d_ring = [anew(BF16, [D]) for _ in range(8)]
        sil = [anew(BF16, [512]) for _ in range(2)]
        xn = [anew(BF16, [D]) for _ in range(2)]
        for i, t in enumerate(tiles):
            emit_hT(t, hT[:, :, t * 128:(t + 1) * 128], ("hT", t), xn[i % 2], ("xn", i % 2), i % 2, i % 2)
        tgs = []
        i = 0
        while i < ntl:
            n = min(4, ntl - i)
            tgs.append((tiles[i], n))
            i += n
        cbase = 0
        gu_cnt = 0
        d_cnt = 0
        ps_g = [2, 3]
        ps_u = [4, 5]
        gu_i = 0
        y_i = 0
        for gsz in FF_GROUPS:
            for ci in range(gsz):
                c = cbase + ci
                s = gu_cnt % 3
                R.dma("pool", "gu%d" % s, lambda e, c=c, s=s: e.dma_start(out=gu_ring[s][:], in_=wgu_d[l, f, c]),
                      writes=[("guring", s)])
                sd = d_cnt % 8
                R.dma("pool", "wd%d" % sd, lambda e, c=c, sd=sd: e.dma_start(out=d_ring[sd][:], in_=wd_d[l, f, c * 128:(c + 1) * 128, :]),
                      writes=[("dring", sd)])
                for (t0, n) in tgs:
                    ntok = n * 128
                    tok0 = t0 * 128
                    bg = ps_g[gu_i % 2]
                    bu = ps_u[gu_i % 2]
                    sl = sil[gu_i % 2]
                    gu_i += 1
                    hkeys = [("hT", t0 + k) for k in range(n)]

                    def mm(e, s=s, bg=bg, bu=bu, tok0=tok0, ntok=ntok):
                        inst = None
                        for which, bank in ((0, bg), (1, bu)):
                            for kc in range(KC):
                                inst = e.matmul(ps[:, bank, 0:ntok], gu_ring[s][:, which, kc, :], hT[:, kc, tok0:tok0 + ntok],
                                                start=(kc == 0), stop=(kc == KC - 1))
                        return inst
                    R.op("pe", mm, reads=[("guring", s)] + hkeys, writes=[("ps", bg), ("ps", bu)])
                    R.op("act", lambda e, bg=bg, sl=sl, ntok=ntok: e.activation(out=sl[:, 0:ntok], in_=ps[:, bg, 0:ntok], func=AF.Silu),
                         reads=[("ps", bg)], writes=[("sil", id(sl))])
                    R.op("dve", lambda e, bu=bu, sl=sl, ci=ci, tok0=tok0, ntok=ntok: e.tensor_tensor(
                        out=aT[:, ci, tok0:tok0 + ntok], in0=ps[:, bu, 0:ntok], in1=sl[:, 0:ntok], op=ALU.mult),
                        reads=[("ps", bu), ("sil", id(sl))], writes=[("aT", ci, t0 + k) for k in range(n)])
                gu_cnt += 1
                d_cnt += 1
            dslots = [(d_cnt - gsz + ci) % 8 for ci in range(gsz)]
            for t in tiles:
                r = 0 if t < NLT else 1
                b0 = 6 if (y_i % 2 == 0) else 0
                y_i += 1

                def mmd(e, t=t, b0=b0, dslots=dslots, gsz=gsz):
                    inst = None
                    for hd in range(2):
                        for ci in range(gsz):
                            inst = e.matmul(ps[:, b0 + hd, :], aT[:, ci, t * 128:(t + 1) * 128],
                                            d_ring[dslots[ci]][:, hd * 512:(hd + 1) * 512],
                                            start=(ci == 0), stop=(ci == gsz - 1))
                    return inst
                R.op("pe", mmd, reads=[("aT", ci, t) for ci in range(gsz)] + [("dring", sd) for sd in dslots],
                     writes=[("ps", b0), ("ps", b0 + 1)])
                yv = ps[:, b0:b0 + 2, :]
                R.op("dve", lambda e, yv=yv, r=r: e.tensor_tensor(out=yv, in0=yv, in1=gate_bc[:, r, :].rearrange("p (a b) -> p a b", a=2),
                                                                  op=ALU.mult),
                     reads=[("ps", b0), ("ps", b0 + 1), ("gate", r)], writes=[("ps", b0), ("ps", b0 + 1)])
                R.op("dve", lambda e, yv=yv, t=t: e.tensor_tensor(out=xs[:, t, :].rearrange("p (a b) -> p a b", a=2),
                                                                  in0=xs[:, t, :].rearrange("p (a b) -> p a b", a=2), in1=yv, op=ALU.add),
                     reads=[("ps", b0), ("ps", b0 + 1), ("x", t)], writes=[("x", t)])
            cbase += gsz
        R.barrier()
        Ar.pos = mark

    def rope_apply(eng, src, dst, cs, t, H, dim, key_r, key_w, tmp1, tmp2):
        q = dim // 4
        cosb = cs[:, 0, t, :].unsqueeze(1).broadcast_to([128, H, dim])
        R.op(eng, lambda e: e.tensor_tensor(out=tmp1, in0=src, in1=cosb, op=ALU.mult), reads=key_r + ["rope"], writes=[("rt1", id(tmp1))])
        s4 = src.rearrange("p h (a b c) -> p h a b c", a=2, b=2)
        t4 = tmp2.rearrange("p h (a b c) -> p h a b c", a=2, b=2)
        sn = cs[:, 1, t, :].rearrange("p (a b c) -> p a b c", a=2, b=2)
        for bsel in range(2):
            R.op(eng, lambda e, bsel=bsel: e.tensor_tensor(
                out=t4[:, :, :, bsel, :], in0=s4[:, :, :, 1 - bsel, :],
                in1=sn[:, :, bsel, :].unsqueeze(1).broadcast_to([128, H, 2, q]), op=ALU.mult),
                reads=key_r + ["rope"], writes=[("rt2", id(tmp2), bsel)])
        R.op(eng, lambda e: e.tensor_tensor(out=dst, in0=tmp1, in1=tmp2, op=ALU.add),
             reads=[("rt1", id(tmp1)), ("rt2", id(tmp2), 0), ("rt2", id(tmp2), 1)], writes=key_w)

    def sumsq(src, H, dd, col0, scr, key_r, key_w):
        sv = scr[:, 0:H * dd].rearrange("p (h d) -> p h d", h=H)
        R.op("dve", lambda e: e.tensor_tensor(out=sv, in0=src, in1=src, op=ALU.mult), reads=key_r, writes=["sqscr"])
        R.op("dve", lambda e: e.tensor_reduce(out=st_ss[:, col0:col0 + H], in_=sv, axis=AX.X, op=ALU.add),
             reads=["sqscr"], writes=key_w)

    def mixer(l, need_ctx):
        sub_modulation(l, 1, 1.0)
        mark0 = Ar.pos
        kT_g = anew(BF16, [T])
        V_g = anew(BF16, [NT, 128])
        kT_m = anew(BF16, [6, T])
        V_m = anew(BF16, [NT, 384])
        NCV = 2050 + 258
        bT = anew(BF16, [2, NCV])
        gsc = anew(F32, [NGAIN])
        markU = Ar.pos
        uT = anew(BF16, [2, NCV])
        R.dma("sp", "gains", lambda e: e.dma_start(out=gains[:], in_=gains_d[l].partition_broadcast(128)), writes=["gains"])
        R.op("dve", lambda e: e.tensor_copy(out=gsc[:], in_=gains[:]), reads=["gains"], writes=["gsc"])
        R.op("dve", lambda e: e.tensor_scalar(out=gsc[:, 0:64], in0=gains[:, 0:64], scalar1=64.0 ** -0.5, scalar2=None, op0=ALU.mult),
             reads=["gains", "gsc"], writes=["gsc"])
        R.op("dve", lambda e: e.tensor_scalar(out=gsc[:, 128:192], in0=gains[:, 128:192], scalar1=96.0 ** -0.5, scalar2=None, op0=ALU.mult),
             reads=["gains", "gsc"], writes=["gsc"])
        R.op("dve", lambda e: e.tensor_scalar(out=gsc[:, 256:288], in0=gains[:, 256:288], scalar1=96.0 ** -0.5, scalar2=None, op0=ALU.mult),
             reads=["gains", "gsc"], writes=["gsc"])
        g_q, g_k, g_qn, g_kn, g_qr, g_kr = (gsc[:, 0:64], gsc[:, 64:128], gsc[:, 128:192], gsc[:, 192:256],
                                            gsc[:, 256:288], gsc[:, 288:320])
        R.op("pool", lambda e: e.memset(uT[:], 0.0), writes=["uT"])

        markK = Ar.pos
        wK = anew(BF16, [KC, 1312])
        wukv = anew(BF16, [2, 768])
        hTr = [anew(BF16, [KC, 128]) for _ in range(2)]
        xn = [anew(BF16, [D]) for _ in range(2)]
        kraw = anew(F32, [544])
        kvraw = anew(F32, [768])
        scr = anew(F32, [768])
        tA = anew(F32, [384])
        tB = anew(F32, [384])
        tC = anew(F32, [384])
        kf = anew(BF16, [128])
        ckvb = anew(BF16, [256])
        ckvT = anew(BF16, [2, 128])
        kfull = anew(BF16, [6, 96])
        cgt = anew(F32, [2, 128])
        R.dma("pool", "wK", lambda e: e.dma_start(out=wK[:], in_=wink_d[l]), writes=["wK"])
        R.dma("pool", "wukv", lambda e: e.dma_start(out=wukv[:], in_=wukv_d[l]), writes=["wukv"])
        for c in range(2):
            R.op("dve", lambda e, c=c: e.tensor_scalar(out=wukv[:, c, :], in0=wukv[:, c, :], scalar1=glat[:, l, 3 + c:4 + c], scalar2=None,
                                                       op0=ALU.mult), reads=["wukv", "glat"], writes=["wukv"])
        for t in range(NT):
            lat = t < NLT
            h = hTr[t % 2]
            hk = ("hTr", t % 2)
            emit_hT(t, h, hk, xn[t % 2], ("xn", t % 2), 6, t % 2)

            def mmA(e, h=h):
                inst = None
                for kc in range(KC):
                    inst = e.matmul(ps[:, 0, :], h[:, kc, :], wK[:, kc, 0:512], start=(kc == 0), stop=(kc == KC - 1))
                for kc in range(KC):
                    inst = e.matmul(ps[:, 1, 0:32], h[:, kc, :], wK[:, kc, 512:544], start=(kc == 0), stop=(kc == KC - 1))
                return inst
            R.op("pe", mmA, reads=[hk, "wK"], writes=[("ps", 0), ("ps", 1)])

            def mmC(e, h=h):
                inst = None
                for cc in range(6):
                    bank, off = (2, cc * 128) if cc < 4 else (3, (cc - 4) * 128)
                    for kc in range(KC):
                        inst = e.matmul(ps[:, bank, off:off + 128], wK[:, kc, 544 + cc * 128:544 + (cc + 1) * 128], h[:, kc, :],
                                        start=(kc == 0), stop=(kc == KC - 1))
                return inst
            R.op("pe", mmC, reads=[hk, "wK"], writes=[("ps", 2), ("ps", 3)])
            R.op("act", lambda e: e.copy(out=kraw[:, 0:512], in_=ps[:, 0, :]), reads=[("ps", 0)], writes=["kraw"])
            R.op("act", lambda e: e.copy(out=kraw[:, 512:544], in_=ps[:, 1, 0:32]), reads=[("ps", 1), "kraw"], writes=["kraw"])
            R.op("act", lambda e, t=t: e.copy(out=V_g[:, t, :], in_=kraw[:, 128:256]), reads=["kraw"], writes=[("Vg", t)])
            R.op("act", lambda e: e.copy(out=ckvb[:], in_=kraw[:, 256:512]), reads=["kraw"], writes=["ckvb"])
            k3 = kraw[:, 0:128].rearrange("p (h d) -> p h d", h=2)
            sumsq(k3, 2, 64, 8, scr, ["kraw"], [("ss", "k")])
            sumsq(kraw[:, 256:512].unsqueeze(1), 1, 256, 10, scr, ["kraw"], [("ss", "ckv")])
            sumsq(kraw[:, 512:544].unsqueeze(1), 1, 32, 11, scr, ["kraw"], [("ss", "kr")])
            R.op("act", lambda e: e.activation(out=st_r[:, 8:10], in_=st_ss[:, 8:10], func=AF.Sqrt, scale=1.0 / 64, bias=epsb[:]),
                 reads=[("ss", "k"), "eps"], writes=[("sr", "k")])
            R.op("act", lambda e: e.activation(out=st_r[:, 10:11], in_=st_ss[:, 10:11], func=AF.Sqrt, scale=1.0 / 256, bias=epsb[:]),
                 reads=[("ss", "ckv"), "eps"], writes=[("sr", "ckv")])
            R.op("act", lambda e: e.activation(out=st_r[:, 11:12], in_=st_ss[:, 11:12], func=AF.Sqrt, scale=1.0 / 32, bias=epsb[:]),
                 reads=[("ss", "kr"), "eps"], writes=[("sr", "kr")])
            R.op("dve", lambda e: e.reciprocal(out=st_r[:, 8:12], in_=st_r[:, 8:12]),
                 reads=[("sr", "k"), ("sr", "ckv"), ("sr", "kr")], writes=[("sr", "k"), ("sr", "ckv"), ("sr", "kr")])
            kn = tA[:, 0:128].rearrange("p (h d) -> p h d", h=2)
            for hh in range(2):
                R.op("dve", lambda e, hh=hh: e.scalar_tensor_tensor(out=kn[:, hh, :], in0=k3[:, hh, :], scalar=st_r[:, 8 + hh:9 + hh], in1=g_k,
                                                                    op0=ALU.mult, op1=ALU.mult),
                     reads=["kraw", ("sr", "k"), "gsc"], writes=[("kn", hh)], sync_self=True)
            kf3 = kf[:].rearrange("p (h d) -> p h d", h=2)
            if lat:
                rope_apply("dve", kn, kf3, ropeA, t, 2, 64, [("kn", 0), ("kn", 1)], ["kf"],
                           tB[:, 0:128].rearrange("p (h d) -> p h d", h=2), tC[:, 0:128].rearrange("p (h d) -> p h d", h=2))
            else:
                R.op("dve", lambda e: e.tensor_copy(out=kf3, in_=kn), reads=[("kn", 0), ("kn", 1)], writes=["kf"])
            R.op("pe", lambda e: e.transpose(psb(7)[:, 0:128], kf[:], ident[:]), reads=["kf", "ident"], writes=[("ps", 7)])
            R.op("act", lambda e, t=t: e.copy(out=kT_g[:, t * 128:(t + 1) * 128], in_=psb(7)[:, 0:128]), reads=[("ps", 7)],
                 writes=[("kTg", t)])
            def trc(e):
                inst = None
                for c in range(2):
                    inst = e.transpose(psb(7)[:, 128 + c * 128:256 + c * 128], ckvb[:, c * 128:(c + 1) * 128], ident[:])
                return inst
            R.op("pe", trc, reads=["ckvb", "ident"], writes=[("ps", 7)])
            R.op("act", lambda e: e.copy(out=ckvT[:].rearrange("p a b -> p (a b)"), in_=psb(7)[:, 128:384]), reads=[("ps", 7)],
                 writes=["ckvT"])

            def mmkv(e):
                inst = None
                for (bank, c0, n) in ((4, 0, 512), (5, 512, 256)):
                    for c in range(2):
                        inst = e.matmul(ps[:, bank, 0:n], ckvT[:, c, :], wukv[:, c, c0:c0 + n], start=(c == 0), stop=(c == 1))
                return inst
            R.op("pe", mmkv, reads=["ckvT", "wukv"], writes=[("ps", 4), ("ps", 5)])
            R.op("act", lambda e: e.copy(out=kvraw[:, 0:512], in_=ps[:, 4, :]), reads=[("ps", 4)], writes=["kvraw"])
            R.op("act", lambda e: e.copy(out=kvraw[:, 512:768], in_=ps[:, 5, 0:256]), reads=[("ps", 5), "kvraw"], writes=["kvraw"])
            kv3 = kvraw[:].rearrange("p (h d) -> p h d", h=6)
            sumsq(kv3[:, :, 0:64], 6, 64, 16, scr, ["kvraw"], [("ss", "kn")])
            R.op("dve", lambda e: e.tensor_tensor(out=st_ss[:, 12:13], in0=st_r[:, 10:11], in1=st_r[:, 10:11], op=ALU.mult),
                 reads=[("sr", "ckv")], writes=[("ss", "b2")])
            R.op("dve", lambda e: e.tensor_scalar(out=st_ss[:, 16:22], in0=st_ss[:, 16:22], scalar1=st_ss[:, 12:13], scalar2=None, op0=ALU.mult),
                 reads=[("ss", "kn"), ("ss", "b2")], writes=[("ss", "kn")], sync_self=True)
            R.op("act", lambda e: e.activation(out=st_r[:, 16:22], in_=st_ss[:, 16:22], func=AF.Sqrt, scale=1.0 / 64, bias=epsb[:]),
                 reads=[("ss", "kn"), "eps"], writes=[("sr", "kn")])
            R.op("dve", lambda e: e.reciprocal(out=st_r[:, 16:22], in_=st_r[:, 16:22]), reads=[("sr", "kn")], writes=[("sr", "kn")])
            R.op("dve", lambda e: e.tensor_scalar(out=st_r[:, 16:22], in0=st_r[:, 16:22], scalar1=st_r[:, 10:11], scalar2=None, op0=ALU.mult),
                 reads=[("sr", "kn"), ("sr", "ckv")], writes=[("sr", "kn")], sync_self=True)
            for hh in range(6):
                R.op("dve", lambda e, hh=hh: e.scalar_tensor_tensor(out=kfull[:, hh, 0:64], in0=kv3[:, hh, 0:64], scalar=st_r[:, 16 + hh:17 + hh],
                                                                    in1=g_kn, op0=ALU.mult, op1=ALU.mult),
                     reads=["kvraw", ("sr", "kn"), "gsc"], writes=[("kfull", hh)], sync_self=(hh == 0))
            R.op("dve", lambda e, t=t: e.tensor_scalar(out=V_m[:, t, :].rearrange("p (h d) -> p h d", h=6), in0=kv3[:, :, 64:128],
                                                       scalar1=st_r[:, 10:11], scalar2=None, op0=ALU.mult),
                 reads=["kvraw", ("sr", "ckv")], writes=[("Vm", t)])
            krn = tA[:, 128:160].unsqueeze(1)
            R.op("dve", lambda e: e.scalar_tensor_tensor(out=tA[:, 128:160], in0=kraw[:, 512:544], scalar=st_r[:, 11:12], in1=g_kr,
                                                         op0=ALU.mult, op1=ALU.mult), reads=["kraw", ("sr", "kr"), "gsc"], writes=["krn"])
            krf = tA[:, 160:192].unsqueeze(1)
            if lat:
                rope_apply("dve", krn, krf, ropeM, t, 1, 32, ["krn"], ["krf"], tB[:, 128:160].unsqueeze(1), tC[:, 128:160].unsqueeze(1))
                src_kr = krf
                krk = "krf"
            else:
                src_kr = krn
                krk = "krn"
            R.op("dve", lambda e, src_kr=src_kr: e.tensor_copy(out=kfull[:, :, 64:96], in_=src_kr.broadcast_to([128, 6, 32])),
                 reads=[krk], writes=[("kfull", "r")])

            def trk(e):
                inst = None
                for hh in range(6):
                    inst = e.transpose(psb(6)[0:96, hh * 128:(hh + 1) * 128], kfull[:, hh, :], ident[:])
                return inst
            R.op("pe", trk, reads=[("kfull", hh) for hh in range(6)] + [("kfull", "r"), "ident"], writes=[("ps", 6)])
            R.op("act", lambda e, t=t: e.copy(out=kT_m[0:96, :, t * 128:(t + 1) * 128],
                                              in_=psb(6)[0:96, 0:768].rearrange("p (h n) -> p h n", h=6)),
                 reads=[("ps", 6)], writes=[("kTm", t)])
            pos = (1 + t * 128) if lat else (2051 + (t - NLT) * 128)
            R.op("act", lambda e: e.copy(out=cgt[:].rearrange("p a b -> p (a b)"), in_=ps[:, 2, 256:512]), reads=[("ps", 2)], writes=["cgt"])
            R.op("dve", lambda e, pos=pos: e.tensor_tensor(out=uT[:, :, pos:pos + 128], in0=ps[:, 2, 0:256].rearrange("p (a b) -> p a b", a=2),
                                                           in1=cgt[:], op=ALU.mult), reads=[("ps", 2), "cgt", "uT"], writes=["uT"])
            R.op("act", lambda e, pos=pos: e.copy(out=bT[:, :, pos:pos + 128], in_=ps[:, 3, 0:256].rearrange("p (a b) -> p a b", a=2)),
                 reads=[("ps", 3)], writes=["bT"])
        cvt = scr[:, 0:512]
        for ch in range(2):
            segs = [(1 + 512 * i, 512) for i in range(4)] + [(2051, 256)]
            for (p0, n) in segs:
                R.op("dve", lambda e, ch=ch, p0=p0, n=n: e.tensor_scalar(out=cvt[:, 0:n], in0=uT[:, ch, p0:p0 + n], scalar1=convp[:, l, ch, 1:2],
                                                                         scalar2=convp[:, l, ch, 3:4], op0=ALU.mult, op1=ALU.add),
                     reads=["uT", "convp"], writes=["cvt"])
                R.op("dve", lambda e, ch=ch, p0=p0, n=n: e.scalar_tensor_tensor(out=cvt[:, 0:n], in0=uT[:, ch, p0 - 1:p0 - 1 + n],
                                                                                scalar=convp[:, l, ch, 0:1], in1=cvt[:, 0:n], op0=ALU.mult, op1=ALU.add),
                     reads=["uT", "convp", "cvt"], writes=["cvt"])
                R.op("dve", lambda e, ch=ch, p0=p0, n=n: e.scalar_tensor_tensor(out=cvt[:, 0:n], in0=uT[:, ch, p0 + 1:p0 + 1 + n],
                                                                                scalar=convp[:, l, ch, 2:3], in1=cvt[:, 0:n], op0=ALU.mult, op1=ALU.add),
                     reads=["uT", "convp", "cvt"], writes=["cvt"])
                R.op("dve", lambda e, ch=ch, p0=p0, n=n: e.tensor_tensor(out=bT[:, ch, p0:p0 + n], in0=bT[:, ch, p0:p0 + n], in1=cvt[:, 0:n],
                                                                         op=ALU.mult), reads=["bT", "cvt"], writes=["bT"])
        R.barrier()
        Ar.pos = markU

        wQ = anew(BF16, [KC, 768])
        wuq = anew(BF16, [3, 576])
        wo_b = anew(BF16, [KC, 512])
        hTq = [anew(BF16, [KC, 128])] * 2
        xnq = [anew(BF16, [D])] * 2
        qraw = anew(F32, [768])
        qmraw = qraw[:, 0:576]
        scrq = anew(F32, [576])
        tAq = anew(F32, [384])
        tBq = anew(F32, [384])
        tCq = anew(F32, [384])
        qf = anew(BF16, [384])
        cqb = anew(BF16, [384])
        cqT = anew(BF16, [3, 128])
        qmfull = anew(BF16, [6, 96])
        qT_g = anew(BF16, [3, 512])
        qT_m = anew(BF16, [6, 512])
        PT = [anew(BF16, [512]) for _ in range(4)]
        mixT = anew(BF16, [6, 512])
        rden = scrq[:, 0:512]
        R.dma("pool", "wQ", lambda e: e.dma_start(out=wQ[:], in_=winq_d[l]), writes=["wQ"])
        R.dma("pool", "wuq", lambda e: e.dma_start(out=wuq[:], in_=wuq_d[l]), writes=["wuq"])
        for c in range(3):
            R.op("dve", lambda e, c=c: e.tensor_scalar(out=wuq[:, c, :], in0=wuq[:, c, :], scalar1=glat[:, l, c:c + 1], scalar2=None,
                                                       op0=ALU.mult), reads=["wuq", "glat"], writes=["wuq"])
        groups = [(4 * g, 4) for g in range(4)]
        if need_ctx:
            groups.append((16, 2))
        pt_i = 0
        st_i = 0
        o_i = 0
        for (t0, ntile) in groups:
            lat = t0 < NLT
            nq = ntile * 128
            r = 0 if lat else 1
            key_tiles = list(range(NT)) if lat else [16, 17]
            for lt in range(ntile):
                t = t0 + lt
                h = hTq[0]
                hk = ("hTr", 0)
                emit_hT(t, h, hk, xnq[0], ("xn", 0), 6, t % 2)

                def mmQ(e, h=h):
                    inst = None
                    for (bank, c0) in ((4, 0), (5, 384)):
                        for kc in range(KC):
                            inst = e.matmul(ps[:, bank, 0:384], h[:, kc, :], wQ[:, kc, c0:c0 + 384], start=(kc == 0), stop=(kc == KC - 1))
                    return inst
                R.op("pe", mmQ, reads=[hk, "wQ"], writes=[("ps", 4), ("ps", 5)])
                R.op("act", lambda e: e.copy(out=qraw[:, 0:384], in_=ps[:, 4, 0:384]), reads=[("ps", 4)], writes=["qraw"])
                R.op("act", lambda e: e.copy(out=qraw[:, 384:768], in_=ps[:, 5, 0:384]), reads=[("ps", 5), "qraw"], writes=["qraw"])
                R.op("act", lambda e: e.copy(out=cqb[:], in_=qraw[:, 384:768]), reads=["qraw"], writes=["cqb"])
                q3 = qraw[:, 0:384].rearrange("p (h d) -> p h d", h=6)
                sumsq(q3, 6, 64, 24, scrq, ["qraw"], [("ss", "q")])
                sumsq(qraw[:, 384:768].unsqueeze(1), 1, 384, 30, scrq, ["qraw"], [("ss", "cq")])
                R.op("act", lambda e: e.activation(out=st_r[:, 24:30], in_=st_ss[:, 24:30], func=AF.Sqrt, scale=1.0 / 64, bias=epsb[:]),
                     reads=[("ss", "q"), "eps"], writes=[("sr", "q")])
                R.op("act", lambda e: e.activation(out=st_r[:, 30:31], in_=st_ss[:, 30:31], func=AF.Sqrt, scale=1.0 / 384, bias=epsb[:]),
                     reads=[("ss", "cq"), "eps"], writes=[("sr", "cq")])
                R.op("dve", lambda e: e.reciprocal(out=st_r[:, 24:31], in_=st_r[:, 24:31]), reads=[("sr", "q"), ("sr", "cq")],
                     writes=[("sr", "q"), ("sr", "cq")])
                qn = tAq[:].rearrange("p (h d) -> p h d", h=6)
                for hh in range(6):
                    R.op("dve", lambda e, hh=hh: e.scalar_tensor_tensor(out=qn[:, hh, :], in0=q3[:, hh, :], scalar=st_r[:, 24 + hh:25 + hh], in1=g_q,
                                                                        op0=ALU.mult, op1=ALU.mult),
                         reads=["qraw", ("sr", "q"), "gsc"], writes=[("qn", hh)], sync_self=(hh == 0))
                qf3 = qf[:].rearrange("p (h d) -> p h d", h=6)
                qnk = [("qn", hh) for hh in range(6)]
                if lat:
                    rope_apply("dve", qn, qf3, ropeA, t, 6, 64, qnk, ["qf"], tBq[:].rearrange("p (h d) -> p h d", h=6),
                               tCq[:].rearrange("p (h d) -> p h d", h=6))
                else:
                    R.op("dve", lambda e: e.tensor_copy(out=qf3, in_=qn), reads=qnk, writes=["qf"])

                def trq(e):
                    inst = None
                    for pr in range(3):
                        inst = e.transpose(psb(7)[:, pr * 128:(pr + 1) * 128], qf[:, pr * 128:(pr + 1) * 128], ident[:])
                    for c in range(3):
                        inst = e.transpose(psb(7)[:, 384 + c * 128:512 + c * 128], cqb[:, c * 128:(c + 1) * 128], ident[:])
                    return inst
                R.op("pe", trq, reads=["qf", "cqb", "ident"], writes=[("ps", 7)])
                R.op("act", lambda e, lt=lt: e.copy(out=qT_g[:, :, lt * 128:(lt + 1) * 128], in_=psb(7)[:, 0:384].rearrange("p (a b) -> p a b", a=3)),
                     reads=[("ps", 7)], writes=[("qTg", lt)])
                R.op("act", lambda e: e.copy(out=cqT[:].rearrange("p a b -> p (a b)"), in_=psb(7)[:, 384:768]), reads=[("ps", 7)], writes=["cqT"])

                def mmuq(e):
                    inst = None
                    for (bank, c0) in ((4, 0), (5, 288)):
                        for c in range(3):
                            inst = e.matmul(ps[:, bank, 0:288], cqT[:, c, :], wuq[:, c, c0:c0 + 288], start=(c == 0), stop=(c == 2))
                    return inst
                R.op("pe", mmuq, reads=["cqT", "wuq"], writes=[("ps", 4), ("ps", 5)])
                R.op("act", lambda e: e.copy(out=qmraw[:, 0:288], in_=ps[:, 4, 0:288]), reads=[("ps", 4)], writes=["qraw"])
                R.op("act", lambda e: e.copy(out=qmraw[:, 288:576], in_=ps[:, 5, 0:288]), reads=[("ps", 5), "qraw"], writes=["qraw"])
                qm3 = qmraw[:].rearrange("p (h d) -> p h d", h=6)
                sumsq(qm3[:, :, 0:64], 6, 64, 32, scrq, ["qraw"], [("ss", "qmn")])
                sumsq(qm3[:, :, 64:96], 6, 32, 38, scrq, ["qraw"], [("ss", "qmr")])
                R.op("dve", lambda e: e.tensor_tensor(out=st_ss[:, 31:32], in0=st_r[:, 30:31], in1=st_r[:, 30:31], op=ALU.mult),
                     reads=[("sr", "cq")], writes=[("ss", "a2")])
                R.op("dve", lambda e: e.tensor_scalar(out=st_ss[:, 32:44], in0=st_ss[:, 32:44], scalar1=st_ss[:, 31:32], scalar2=None, op0=ALU.mult),
                     reads=[("ss", "qmn"), ("ss", "qmr"), ("ss", "a2")], writes=[("ss", "qmn"), ("ss", "qmr")], sync_self=True)
                R.op("act", lambda e: e.activation(out=st_r[:, 32:38], in_=st_ss[:, 32:38], func=AF.Sqrt, scale=1.0 / 64, bias=epsb[:]),
                     reads=[("ss", "qmn"), "eps"], writes=[("sr", "qmn")])
                R.op("act", lambda e: e.activation(out=st_r[:, 38:44], in_=st_ss[:, 38:44], func=AF.Sqrt, scale=1.0 / 32, bias=epsb[:]),
                     reads=[("ss", "qmr"), "eps"], writes=[("sr", "qmr")])
                R.op("dve", lambda e: e.reciprocal(out=st_r[:, 32:44], in_=st_r[:, 32:44]), reads=[("sr", "qmn"), ("sr", "qmr")],
                     writes=[("sr", "qmn"), ("sr", "qmr")])
                R.op("dve", lambda e: e.tensor_scalar(out=st_r[:, 32:44], in0=st_r[:, 32:44], scalar1=st_r[:, 30:31], scalar2=None, op0=ALU.mult),
                     reads=[("sr", "qmn"), ("sr", "qmr"), ("sr", "cq")], writes=[("sr", "qmn"), ("sr", "qmr")], sync_self=True)
                qr = tAq[:, 0:192].rearrange("p (h d) -> p h d", h=6)
                for hh in range(6):
                    R.op("dve", lambda e, hh=hh: e.scalar_tensor_tensor(out=qmfull[:, hh, 0:64], in0=qm3[:, hh, 0:64], scalar=st_r[:, 32 + hh:33 + hh],
                                                                        in1=g_qn, op0=ALU.mult, op1=ALU.mult),
                         reads=["qraw", ("sr", "qmn"), "gsc"], writes=[("qmfull", hh)], sync_self=(hh == 0))
                    R.op("dve", lambda e, hh=hh: e.scalar_tensor_tensor(out=qr[:, hh, :], in0=qm3[:, hh, 64:96], scalar=st_r[:, 38 + hh:39 + hh],
                                                                        in1=g_qr, op0=ALU.mult, op1=ALU.mult),
                         reads=["qraw", ("sr", "qmr"), "gsc", "qf"] + qnk, writes=[("qr", hh)])
                qrk = [("qr", hh) for hh in range(6)]
                if lat:
                    rope_apply("dve", qr, qmfull[:, :, 64:96], ropeM, t, 6, 32, qrk, [("qmfull", "r")],
                               tBq[:, 0:192].rearrange("p (h d) -> p h d", h=6), tCq[:, 0:192].rearrange("p (h d) -> p h d", h=6))
                else:
                    R.op("dve", lambda e: e.tensor_copy(out=qmfull[:, :, 64:96], in_=qr), reads=qrk, writes=[("qmfull", "r")])

                def trqm(e):
                    inst = None
                    for hh in range(6):
                        inst = e.transpose(psb(7)[0:96, hh * 128:(hh + 1) * 128], qmfull[:, hh, :], ident[:])
                    return inst
                R.op("pe", trqm, reads=[("qmfull", hh) for hh in range(6)] + [("qmfull", "r"), "ident"], writes=[("ps", 7)])
                R.op("act", lambda e, lt=lt: e.copy(out=qT_m[0:96, :, lt * 128:(lt + 1) * 128],
                                                    in_=psb(7)[0:96, 0:768].rearrange("p (h n) -> p h n", h=6)),
                     reads=[("ps", 7)], writes=[("qTm", lt)])
            qgk = [("qTg", lt) for lt in range(ntile)]
            qmk = [("qTm", lt) for lt in range(ntile)]
            items = [(slot, ki, kt) for slot in range(12) for ki, kt in enumerate(key_tiles)]
            nk = len(key_tiles)
            LA = 3
            ST_BANKS = [0, 1, 6, 7]

            def slot_info(slot):
                if slot < 6:
                    pr, hf = slot // 2, slot % 2
                    return True, pr, hf, pr, slot
                hm = slot - 6
                pr, hf = hm // 2, hm % 2
                return False, pr, hf, 3 + pr, hm

            def emit_st(i, nq=nq):
                slot, ki, kt = items[i]
                gqa, pr, hf, chunk, hm = slot_info(slot)
                bs = ST_BANKS[i % 4]
                pt = PT[i % 4]
                ptk = ("PT", i % 4)
                if gqa:
                    R.op("pe", lambda e, bs=bs, hf=hf, kt=kt, pr=pr, nq=nq: e.matmul(
                        ps[:, bs, 0:nq], kT_g[64 * hf:64 * hf + 64, kt * 128:(kt + 1) * 128], qT_g[64 * hf:64 * hf + 64, pr, 0:nq],
                        start=True, stop=True), reads=[("kTg", kt)] + qgk, writes=[("ps", bs)])
                else:
                    R.op("pe", lambda e, bs=bs, hm=hm, kt=kt, nq=nq: e.matmul(
                        ps[:, bs, 0:nq], kT_m[0:96, hm, kt * 128:(kt + 1) * 128], qT_m[0:96, hm, 0:nq],
                        start=True, stop=True), reads=[("kTm", kt)] + qmk, writes=[("ps", bs)])
                R.op("act", lambda e, bs=bs, pt=pt, nq=nq: e.activation(out=pt[:, 0:nq], in_=ps[:, bs, 0:nq], func=AF.Exp),
                     reads=[("ps", bs)], writes=[ptk])

            def emit_pv(i, nq=nq):
                slot, ki, kt = items[i]
                gqa, pr, hf, chunk, hm = slot_info(slot)
                pt = PT[i % 4]
                ptk = ("PT", i % 4)
                bo = 2 if (slot % 2 == 0) else 4
                bd = bo + 1
                if gqa:
                    vap = V_g[:, kt, :]
                    vk = ("Vg", kt)
                else:
                    vap = V_m[:, kt, pr * 128:(pr + 1) * 128]
                    vk = ("Vm", kt)

                def mmpv(e, vap=vap, pt=pt, bo=bo, bd=bd, ki=ki, nk=nk, nq=nq):
                    e.matmul(ps[:, bo, 0:nq], vap, pt[:, 0:nq], start=(ki == 0), stop=(ki == nk - 1))
                    return e.matmul(ps[:, bd, 0:nq], ones_bf[:], pt[:, 0:nq], start=(ki == 0), stop=(ki == nk - 1))
                R.op("pe", mmpv, reads=[vk, ptk, "ones"], writes=[("ps", bo), ("ps", bd)])
                if ki == nk - 1:
                    p0 = 64 * hf
                    R.op("dve", lambda e, bd=bd, p0=p0, nq=nq: e.reciprocal(out=rden[p0:p0 + 64, 0:nq], in_=ps[p0:p0 + 64, bd, 0:nq]),
                         reads=[("ps", bd)], writes=["rden"])
                    R.op("dve", lambda e, bo=bo, p0=p0, chunk=chunk, nq=nq: e.tensor_tensor(
                        out=mixT[p0:p0 + 64, chunk, 0:nq], in0=ps[p0:p0 + 64, bo, 0:nq], in1=rden[p0:p0 + 64, 0:nq], op=ALU.mult),
                        reads=[("ps", bo), "rden"], writes=[("mixT", chunk, hf)])

            for i in range(len(items) + LA):
                if i < len(items):
                    emit_st(i)
                if i >= LA:
                    emit_pv(i - LA)
            mixk = [("mixT", c, hf) for c in range(6) for hf in range(2)]
            for hd in range(2):
                R.dma("pool", "wo", lambda e, hd=hd: e.dma_start(out=wo_b[:], in_=wout_d[l, hd]), writes=["wo"])
                for lt in range(ntile):
                    t = t0 + lt
                    pos = (1 + t * 128) if lat else (2051 + (t - NLT) * 128)
                    bk = 6 + (lt % 2)

                    def mmo(e, lt=lt, pos=pos, bk=bk):
                        inst = None
                        for c in range(8):
                            if c < 3:
                                lh = mixT[:, c, lt * 128:(lt + 1) * 128]
                            elif c < 5:
                                lh = bT[:, c - 3, pos:pos + 128]
                            else:
                                lh = mixT[:, c - 2, lt * 128:(lt + 1) * 128]
                            inst = e.matmul(ps[:, bk, :], lh, wo_b[:, c, :], start=(c == 0), stop=(c == 7))
                        return inst
                    R.op("pe", mmo, reads=mixk + ["bT", "wo"], writes=[("ps", bk)])
                    R.op("dve", lambda e, bk=bk, r=r, hd=hd: e.tensor_tensor(out=ps[:, bk, :], in0=ps[:, bk, :],
                                                                             in1=gate_bc[:, r, hd * 512:(hd + 1) * 512], op=ALU.mult),
                         reads=[("ps", bk), ("gate", r)], writes=[("ps", bk)])
                    R.op("dve", lambda e, bk=bk, t=t, hd=hd: e.tensor_tensor(out=xs[:, t, hd * 512:(hd + 1) * 512],
                                                                             in0=xs[:, t, hd * 512:(hd + 1) * 512], in1=ps[:, bk, :], op=ALU.add),
                         reads=[("ps", bk), ("x", t)], writes=[("x", t)])
        R.barrier()
        Ar.pos = mark0

    done = False
    for l in range(L):
        if l > 0:
            R.new_epoch()
        need_ctx = l < DEPTH - 1
        modulation(l)
        ffn(l, 0, 0, list(range(NT)))
        if stop_after == (l, "ffn1"):
            break
        mixer(l, need_ctx)
        if stop_after == (l, "mix"):
            break
        ffn(l, 1, 2, list(range(NT)) if need_ctx else list(range(NLT)))
        if stop_after == (l, "ffn2"):
            break

    for i in range(4):
        R.dma("sp", "out", lambda e, i=i: e.dma_start(
            out=out_d[512 * i:512 * (i + 1), :].rearrange("(t p) d -> p t d", p=128), in_=xs[:, 4 * i:4 * i + 4, :]),
            reads=[("x", 4 * i + k) for k in range(4)])
    R.op("sp", lambda e: None, reads=[], writes=[("x", k) for k in range(NLT)])

    R.finalize()
    esems = {}
    for e in Rec.ENG:
        for ep in range(R.n_epochs):
            esems[(e, ep)] = es.enter_context(nc.semaphore("s_%s_%d" % (e, ep)))
    lsems = {ln: es.enter_context(nc.semaphore("l_%s" % ln)) for ln in R.lane_cnt}
    block = es.enter_context(nc.Block())

    @block.tensor
    def _(e):
        R.emit("pe", e, esems, lsems)

    @block.scalar
    def _(e):
        R.emit("act", e, esems, lsems)

    @block.vector
    def _(e):
        R.emit("dve", e, esems, lsems)

    @block.gpsimd
    def _(e):
        R.emit("pool", e, esems, lsems)

    @block.sync
    def _(e):
        R.emit("sp", e, esems, lsems)

    es.close()
    return nc


_CACHE = {}


def kernel(**inputs):
    inp = {k: np.asarray(v) for k, v in inputs.items()}
    if "nc" not in _CACHE:
        _CACHE["nc"] = build_program(DEPTH)
    nc = _CACHE["nc"]
    sh = prep_shared(inp, DEPTH)
    in_maps = []
    for b in range(8):
        m = dict(sh)
        m.update(prep_core(inp, b))
        in_maps.append(m)
    res = run_bass_kernel_spmd(nc, in_maps, core_ids=list(range(8)))
    out = np.stack([np.asarray(r["out"]) for r in res.results], axis=0)
    return out.astype(np.float32)
```

```python
import numpy as np
from contextlib import ExitStack
import concourse.bass as bass
import concourse.mybir as mybir
from concourse.bass_utils import run_bass_kernel_spmd

F32 = mybir.dt.float32
BF16 = mybir.dt.bfloat16
U8 = mybir.dt.uint8
ALU = mybir.AluOpType
AF = mybir.ActivationFunctionType
AX = mybir.AxisListType

D = 1024
DEPTH = 4
NT = 18
NLT = 16
T = NT * 128
DFF = 2816
NFC = 22
EPS = 1e-6
KC = 8
NGAIN = 320
FF_GROUPS = [4, 4, 4, 4, 4, 2]


class Rec:
    ENG = ("pe", "act", "dve", "pool", "sp")

    def __init__(self):
        self.ops = {e: [] for e in self.ENG}
        self.res = {}
        self.lane_cnt = {}
        self.epoch_starts = {e: [0] for e in self.ENG}
        self.pending = {e: [] for e in self.ENG}

    def new_epoch(self):
        for e in self.ENG:
            self.epoch_starts[e].append(len(self.ops[e]))

    def _deps(self, reads, writes):
        d = []
        for k in reads:
            st = self.res.get(k)
            if st is not None and st[0] is not None:
                d.append(st[0])
        for k in writes:
            st = self.res.get(k)
            if st is not None:
                if st[0] is not None:
                    d.append(st[0])
                for kk, v in st[1].items():
                    d.append(kk + (v,))
        return d

    def _commit(self, tok, reads, writes):
        for k in reads:
            st = self.res.get(k)
            if st is None:
                st = [None, {}]
                self.res[k] = st
            key = tok[:2]
            if st[1].get(key, -1) < tok[2]:
                st[1][key] = tok[2]
        for k in writes:
            self.res[k] = [tok, {}]

    def op(self, eng, fn, reads=(), writes=(), sync_self=False):
        deps = self._deps(reads, writes) + self.pending[eng]
        self.pending[eng] = []
        idx = len(self.ops[eng])
        tok = ("e", eng, idx)
        self.ops[eng].append({"fn": fn, "deps": deps, "lane": None, "ss": sync_self})
        self._commit(tok, reads, writes)
        return tok

    def dma(self, eng, lane, fn, reads=(), writes=()):
        deps = self._deps(reads, writes) + self.pending[eng]
        self.pending[eng] = []
        self.lane_cnt[lane] = self.lane_cnt.get(lane, 0) + 1
        tok = ("d", lane, self.lane_cnt[lane])
        self.ops[eng].append({"fn": fn, "deps": deps, "lane": lane, "ss": True})
        self._commit(tok, reads, writes)
        return tok

    def barrier(self):
        last = []
        for e in self.ENG:
            for i in range(len(self.ops[e]) - 1, -1, -1):
                if self.ops[e][i]["lane"] is None:
                    last.append(("e", e, i))
                    break
        for l, c in self.lane_cnt.items():
            last.append(("d", l, c))
        for e in self.ENG:
            self.pending[e] = self.pending[e] + list(last)

    def finalize(self):
        self.signal = {e: [False] * len(self.ops[e]) for e in self.ENG}
        for e in self.ENG:
            for i, op in enumerate(self.ops[e]):
                for tok in op["deps"]:
                    if tok[0] == "e" and (tok[1] != e or op["ss"] or tok[2] == i - 1):
                        self.signal[tok[1]][tok[2]] = True
        self.sigval = {}
        self.epoch_of = {}
        for e in self.ENG:
            starts = self.epoch_starts[e]
            vals = [None] * len(self.ops[e])
            eps = [0] * len(self.ops[e])
            ep = 0
            cnt = 0
            for i in range(len(self.ops[e])):
                while ep + 1 < len(starts) and i >= starts[ep + 1]:
                    ep += 1
                    cnt = 0
                if self.signal[e][i]:
                    cnt += 1
                    vals[i] = cnt
                eps[i] = ep
            self.sigval[e] = vals
            self.epoch_of[e] = eps
        self.n_epochs = max(len(s) for s in self.epoch_starts.values())

    def emit(self, eng, e, esems, lsems):
        waited = {}
        for i, op in enumerate(self.ops[eng]):
            need = {}
            for tok in op["deps"]:
                if tok[0] == "e":
                    if tok[1] == eng and not op["ss"] and tok[2] != i - 1:
                        continue
                    key = ("e", tok[1], self.epoch_of[tok[1]][tok[2]])
                    val = self.sigval[tok[1]][tok[2]]
                else:
                    key = ("d", tok[1])
                    val = 16 * tok[2]
                if need.get(key, 0) < val:
                    need[key] = val
            for key, val in need.items():
                if waited.get(key, 0) >= val:
                    continue
                if key[0] == "e":
                    later = [k for k in waited if k[0] == "e" and k[1] == key[1] and k[2] > key[2]]
                    if later:
                        continue
                    e.wait_ge(esems[(key[1], key[2])], val)
                else:
                    e.wait_ge(lsems[key[1]], val)
                waited[key] = val
            inst = op["fn"](e)
            if inst is None:
                continue
            if op["lane"] is not None:
                inst.then_inc(lsems[op["lane"]], 16)
            elif self.signal[eng][i]:
                inst.then_inc(esems[(eng, self.epoch_of[eng][i])], 1)


Q_ORDER = [0, 3, 1, 4, 2, 5]


def _rope_np(dim):
    rows = 2048 // 64
    row = np.repeat(np.arange(rows, dtype=np.float64), 64)
    col = np.tile(np.arange(64, dtype=np.float64), rows)
    half = dim // 2
    inv = 1.0 / (10000.0 ** (np.arange(0, half, 2, dtype=np.float64) / half))
    ar = row[:, None] * inv[None, :]
    ac = col[:, None] * inv[None, :]
    ang = np.concatenate([ar, ar, ac, ac], axis=-1)
    cos = np.cos(ang).astype(np.float32)
    sin = np.sin(ang).astype(np.float32)
    q = dim // 4
    sgn = np.concatenate([-np.ones(q), np.ones(q), -np.ones(q), np.ones(q)]).astype(np.float32)
    sin = sin * sgn[None, :]
    cos = np.ascontiguousarray(cos.reshape(16, 128, dim).transpose(1, 0, 2))
    sin = np.ascontiguousarray(sin.reshape(16, 128, dim).transpose(1, 0, 2))
    return cos, sin


def prep_shared(inp, n_layers):
    L = n_layers
    f = lambda a: np.ascontiguousarray(a, dtype=np.float32)
    sh = {}
    wm = inp["w_mod"][:L]
    sh["wmod"] = f(wm.reshape(L, KC, 128, 18, 512).transpose(0, 3, 2, 1, 4))
    sh["bmodT"] = f(inp["b_mod"][:L].reshape(L, 72, 128).transpose(2, 0, 1))
    sh["gn"] = f(inp["g_norm"][:L].reshape(L, 3, KC, 128).transpose(3, 0, 1, 2))
    wg = inp["ffn_w_gate"][:L].reshape(L, 2, KC, 128, NFC, 128)
    wu = inp["ffn_w_up"][:L].reshape(L, 2, KC, 128, NFC, 128)
    wgu = np.stack([wg, wu], axis=0)
    sh["wgu"] = f(wgu.transpose(1, 2, 5, 4, 0, 3, 6))
    sh["wd"] = f(inp["ffn_w_down"][:L])
    win = inp["w_in"][:L]
    qcols = np.concatenate([np.arange(64 * h, 64 * h + 64) for h in Q_ORDER])
    kside = np.concatenate([np.arange(384, 512), np.arange(512, 640), np.arange(1792, 2048),
                            np.arange(2048, 2080),
                            np.arange(640, 896), np.arange(1152, 1408), np.arange(896, 1152)])
    qside = np.concatenate([qcols, np.arange(1408, 1792)])
    sh["wink"] = f(win[:, :, kside].reshape(L, KC, 128, 1312).transpose(0, 2, 1, 3))
    sh["winq"] = f(win[:, :, qside].reshape(L, KC, 128, 768).transpose(0, 2, 1, 3))
    sh["wuq"] = f(inp["mla_w_uq"][:L].reshape(L, 3, 128, 576).transpose(0, 2, 1, 3))
    sh["wukv"] = f(inp["mla_w_ukv"][:L].reshape(L, 2, 128, 768).transpose(0, 2, 1, 3))
    rows = []
    for pr in range(3):
        for hf in range(2):
            h = Q_ORDER[2 * pr + hf]
            rows.append(np.arange(64 * h, 64 * h + 64))
    rows.append(np.arange(384, 640))
    rows.append(np.arange(640, 1024))
    rows = np.concatenate(rows)
    wo = inp["w_out"][:L][:, rows, :]
    sh["wout"] = f(wo.reshape(L, KC, 128, 2, 512).transpose(0, 3, 2, 1, 4))
    sh["gains"] = f(np.concatenate([inp["gqa_g_q"][:L], inp["gqa_g_k"][:L], inp["mla_g_qn"][:L],
                                    inp["mla_g_kn"][:L], inp["mla_g_qr"][:L], inp["mla_g_kr"][:L]], axis=1))
    cw = inp["conv_w"][:L]
    cb = inp["conv_b"][:L]
    cp = np.concatenate([cw, cb[:, None, :]], axis=1)
    sh["convp"] = f(cp.reshape(L, 4, 2, 128).transpose(3, 0, 2, 1))
    gl = np.concatenate([inp["mla_g_cq"][:L].reshape(L, 3, 128), inp["mla_g_ckv"][:L].reshape(L, 2, 128)], axis=1)
    sh["glat"] = f(gl.transpose(2, 0, 1))
    ca, sa = _rope_np(64)
    cm, sm = _rope_np(32)
    sh["ropeA"] = f(np.stack([ca, sa], axis=1))
    sh["ropeM"] = f(np.stack([cm, sm], axis=1))
    return sh


def prep_core(inp, b):
    xin = np.concatenate([inp["x"][b], inp["ctx"][b]], axis=0)
    cc = np.stack([inp["c"][b], inp["c_ctx"]], axis=-1)
    ccT = np.ascontiguousarray(cc.reshape(KC, 128, 2).transpose(1, 0, 2), dtype=np.float32)
    return {"xin": np.ascontiguousarray(xin, dtype=np.float32), "ccT": ccT}


def build_program(n_layers=DEPTH, stop_after=None):
    L = n_layers
    nc = bass.Bass("TRN2", target_bir_lowering=False)
    dt_in = lambda name, shape: nc.dram_tensor(name, list(shape), F32, kind="ExternalInput").ap()
    xin = dt_in("xin", [T, D])
    ccT_d = dt_in("ccT", [128, KC, 2])
    wmod_d = dt_in("wmod", [L, 18, 128, KC, 512])
    bmodT_d = dt_in("bmodT", [128, L, 72])
    gn_d = dt_in("gn", [128, L, 3, KC])
    wgu_d = dt_in("wgu", [L, 2, NFC, 128, 2, KC, 128])
    wd_d = dt_in("wd", [L, 2, DFF, D])
    wink_d = dt_in("wink", [L, 128, KC, 1312])
    winq_d = dt_in("winq", [L, 128, KC, 768])
    wuq_d = dt_in("wuq", [L, 128, 3, 576])
    wukv_d = dt_in("wukv", [L, 128, 2, 768])
    wout_d = dt_in("wout", [L, 2, 128, KC, 512])
    gains_d = dt_in("gains", [L, NGAIN])
    convp_d = dt_in("convp", [128, L, 2, 4])
    glat_d = dt_in("glat", [128, L, 5])
    ropeA_d = dt_in("ropeA", [128, 2, 16, 64])
    ropeM_d = dt_in("ropeM", [128, 2, 16, 32])
    out_d = nc.dram_tensor("out", [2048, D], F32, kind="ExternalOutput").ap()

    R = Rec()
    es = ExitStack()
    sb = lambda name, shape, dt: es.enter_context(nc.sbuf_tensor(name, list(shape), dt))
    xs = sb("xs", [128, NT, D], F32)
    ropeA = sb("ropeA_s", [128, 2, 16, 64], BF16)
    ropeM = sb("ropeM_s", [128, 2, 16, 32], BF16)
    ident = sb("ident", [128, 128], BF16)
    identf = sb("identf", [128, 128], F32)
    ones_bf = sb("ones_bf", [128, 128], BF16)
    ccs = sb("ccs", [128, KC, 2], F32)
    s2 = sb("s2", [128, KC, 2], BF16)
    gn = sb("gn_s", [128, L, 3, KC], F32)
    bmodT = sb("bmodT_s", [128, L, 72], F32)
    convp = sb("convp_s", [128, L, 2, 4], F32)
    glat = sb("glat_s", [128, L, 5], F32)
    gains = sb("gains_s", [128, NGAIN], F32)
    modT = sb("modT", [128, 72, 2], F32)
    Amod = sb("Amod", [128, KC, 2], F32)
    Bmod = sb("Bmod", [128, KC, 2], F32)
    gate_bc = sb("gate_bc", [128, 2, D], BF16)
    epsb = sb("epsb", [128, 1], F32)
    st_ss = sb("st_ss", [128, 64], F32)
    st_r = sb("st_r", [128, 64], F32)
    ARENA_BYTES = 120 * 1024
    arena = sb("arena", [128, ARENA_BYTES], U8)
    ps = es.enter_context(nc.psum_tensor("ps", [128, 8, 512], F32))

    class Ar:
        pos = 0

    def aalloc(nbytes):
        a0 = (Ar.pos + 31) // 32 * 32
        Ar.pos = a0 + nbytes
        assert Ar.pos <= ARENA_BYTES, ("arena overflow", Ar.pos)
        return a0

    def aview(a0, dt, shape):
        esz = 2 if dt == BF16 else 4
        n = int(np.prod(shape))
        v = arena[:, a0:a0 + n * esz].bitcast(dt)
        if len(shape) == 1:
            return v
        names = " ".join("a%d" % i for i in range(len(shape)))
        kw = {"a%d" % i: shape[i] for i in range(1, len(shape))}
        return v.rearrange("p (%s) -> p %s" % (names, names), **kw)

    def anew(dt, shape):
        esz = 2 if dt == BF16 else 4
        return aview(aalloc(int(np.prod(shape)) * esz), dt, shape)

    psb = lambda b: ps[:, b, :].bitcast(BF16)

    for i in range(6):
        R.dma("sp", "xin%d" % i, lambda e, i=i: e.dma_start(
            out=xs[:, 3 * i:3 * i + 3, :], in_=xin[384 * i:384 * (i + 1), :].rearrange("(t p) d -> p t d", p=128)),
            writes=[("x", 3 * i), ("x", 3 * i + 1), ("x", 3 * i + 2)])
    R.dma("sp", "cst0", lambda e: e.dma_start(out=ccs[:], in_=ccT_d), writes=["ccs"])
    R.dma("sp", "cst1", lambda e: e.dma_start(out=gn[:], in_=gn_d), writes=["gn"])
    R.dma("sp", "cst2", lambda e: e.dma_start(out=bmodT[:], in_=bmodT_d), writes=["bmodT"])
    R.dma("sp", "cst3", lambda e: e.dma_start(out=convp[:], in_=convp_d), writes=["convp"])
    R.dma("sp", "cst4", lambda e: e.dma_start(out=glat[:], in_=glat_d), writes=["glat"])
    R.dma("pool", "cstA", lambda e: e.dma_start(out=ropeA[:], in_=ropeA_d), writes=["ropeA"])
    R.dma("pool", "cstM", lambda e: e.dma_start(out=ropeM[:], in_=ropeM_d), writes=["ropeM"])
    R.op("pool", lambda e: e.memset(identf[:], 0.0), writes=["identf"])
    R.op("pool", lambda e: e.affine_select(out=identf[:], in_=identf[:], pattern=[[-1, 128]],
                                           compare_op=ALU.not_equal, fill=1.0, base=0, channel_multiplier=1),
         reads=["identf"], writes=["identf"])
    R.op("pool", lambda e: e.tensor_copy(out=ident[:], in_=identf[:]), reads=["identf"], writes=["ident"])
    R.op("pool", lambda e: e.memset(ones_bf[:], 1.0), writes=["ones"])
    R.op("pool", lambda e: e.memset(epsb[:], EPS), writes=["eps"])
    R.op("act", lambda e: e.activation(out=s2[:], in_=ccs[:], func=AF.Silu), reads=["ccs"], writes=["s2"])

    def rstd_from_ss(n, scale, key):
        R.op("act", lambda e: e.activation(out=st_r[:, 0:n], in_=st_ss[:, 0:n], func=AF.Sqrt, scale=scale, bias=epsb[:]),
             reads=[("ss", key), "eps"], writes=[("sr", key)])
        R.op("dve", lambda e: e.reciprocal(out=st_r[:, 0:n], in_=st_r[:, 0:n]), reads=[("sr", key)], writes=[("sr", key)])

    def modulation(l):
        mark = Ar.pos
        ring = [anew(BF16, [KC, 512]) for _ in range(2)]
        for c in range(18):
            s = c % 2
            R.dma("pool", "wm%d" % s, lambda e, c=c, s=s: e.dma_start(out=ring[s][:], in_=wmod_d[l, c]),
                  writes=[("wmring", s)])

            def mm(e, c=c, s=s):
                inst = None
                for fc in range(4):
                    col = (c * 4 + fc) * 2
                    for kc in range(KC):
                        inst = e.matmul(ps[:, 0, col:col + 2], ring[s][:, kc, fc * 128:(fc + 1) * 128], s2[:, kc, :],
                                        start=(kc == 0), stop=(kc == KC - 1))
                return inst
            R.op("pe", mm, reads=[("wmring", s), "s2"], writes=[("ps", 0)])
        R.op("dve", lambda e: e.tensor_tensor(
            out=modT[:], in0=ps[:, 0, 0:144].rearrange("p (c r) -> p c r", r=2),
            in1=bmodT[:, l, :].unsqueeze(2).broadcast_to([128, 72, 2]), op=ALU.add),
            reads=[("ps", 0), "bmodT"], writes=["modT"])
        R.barrier()
        Ar.pos = mark

    def sub_modulation(l, j, gate_mult):
        c_shift, c_scale, c_gate = (3 * j) * 8, (3 * j + 1) * 8, (3 * j + 2) * 8
        R.op("dve", lambda e: e.tensor_scalar(out=Amod[:], in0=modT[:, c_scale:c_scale + 8, :], scalar1=1.0, scalar2=None,
                                              op0=ALU.add), reads=["modT"], writes=["Amod"])
        R.op("dve", lambda e: e.tensor_tensor(out=Amod[:], in0=Amod[:], in1=gn[:, l, j, :].unsqueeze(2).broadcast_to([128, KC, 2]),
                                              op=ALU.mult), reads=["Amod", "gn"], writes=["Amod"])
        R.op("dve", lambda e: e.tensor_copy(out=Bmod[:], in_=modT[:, c_shift:c_shift + 8, :]), reads=["modT"], writes=["Bmod"])
        mark = Ar.pos
        rep = anew(BF16, [KC, 128])
        for r in range(2):
            R.op("dve", lambda e, r=r: e.tensor_scalar(
                out=rep[:], in0=modT[:, c_gate:c_gate + 8, r:r + 1].broadcast_to([128, KC, 128]),
                scalar1=gate_mult, scalar2=None, op0=ALU.mult), reads=["modT"], writes=["rep"])

            def tr(e):
                inst = None
                for kc in range(KC):
                    inst = e.transpose(psb(1)[:, kc * 128:(kc + 1) * 128], rep[:, kc, :], ident[:])
                return inst
            R.op("pe", tr, reads=["rep", "ident"], writes=[("ps", 1)])
            R.op("act", lambda e, r=r: e.copy(out=gate_bc[:, r, :], in_=psb(1)[:, 0:1024]), reads=[("ps", 1)],
                 writes=[("gate", r)])
        R.barrier()
        Ar.pos = mark

    def emit_hT(t, dst, dst_key, xn, xn_key, psbank, stat_col):
        r = 0 if t < NLT else 1
        R.op("act", lambda e: e.activation(out=xn[:], in_=xs[:, t, :], func=AF.Square, accum_out=st_ss[:, stat_col:stat_col + 1]),
             reads=[("x", t)], writes=[xn_key, ("ss", "h%d" % stat_col)])
        R.op("act", lambda e: e.activation(out=st_r[:, stat_col:stat_col + 1], in_=st_ss[:, stat_col:stat_col + 1], func=AF.Sqrt,
                                           scale=1.0 / D, bias=epsb[:]),
             reads=[("ss", "h%d" % stat_col), "eps"], writes=[("sr", "h%d" % stat_col)], sync_self=True)
        R.op("dve", lambda e: e.reciprocal(out=st_r[:, stat_col:stat_col + 1], in_=st_r[:, stat_col:stat_col + 1]),
             reads=[("sr", "h%d" % stat_col)], writes=[("sr", "h%d" % stat_col)])
        R.op("dve", lambda e: e.tensor_scalar(out=xn[:], in0=xs[:, t, :], scalar1=st_r[:, stat_col:stat_col + 1], scalar2=None,
                                              op0=ALU.mult), reads=[("x", t), ("sr", "h%d" % stat_col), xn_key], writes=[xn_key], sync_self=True)

        def tr(e):
            inst = None
            for kc in range(KC):
                inst = e.transpose(psb(psbank)[:, kc * 128:(kc + 1) * 128], xn[:, kc * 128:(kc + 1) * 128], ident[:])
            return inst
        R.op("pe", tr, reads=[xn_key, "ident"], writes=[("ps", psbank)])
        for kc in range(KC):
            R.op("act", lambda e, kc=kc: e.activation(out=dst[:, kc, :], in_=psb(psbank)[:, kc * 128:(kc + 1) * 128],
                                                      func=AF.Identity, scale=Amod[:, kc, r:r + 1], bias=Bmod[:, kc, r:r + 1]),
                 reads=[("ps", psbank), "Amod", "Bmod"], writes=[dst_key])

    def ffn(l, f, j, tiles):
        ntl = len(tiles)
        sub_modulation(l, j, 0.5)
        mark = Ar.pos
        hT = anew(BF16, [KC, T])
        aT = anew(BF16, [4, T])
        gu_ring = [anew(BF16, [2, KC, 128]) for _ in range(3)]
        d_ring = [anew(BF16, [D]) for _ in range(8)]
        sil = [anew(BF16, [512]) for _ in range(2)]
        xn = [anew(BF16, [D]) for _ in range(2)]
        for i, t in enumerate(tiles):
            emit_hT(t, hT[:, :, t * 128:(t + 1) * 128], ("hT", t), xn[i % 2], ("xn", i % 2), i % 2, i % 2)
        tgs = []
        i = 0
        while i < ntl:
            n = min(4, ntl - i)
            tgs.append((tiles[i], n))
            i += n
        cbase = 0
        gu_cnt = 0
        d_cnt = 0
        ps_g = [2, 3]
        ps_u = [4, 5]
        gu_i = 0
        y_i = 0
        for gsz in FF_GROUPS:
            for ci in range(gsz):
                c = cbase + ci
                s = gu_cnt % 3
                R.dma("pool", "gu%d" % s, lambda e, c=c, s=s: e.dma_start(out=gu_ring[s][:], in_=wgu_d[l, f, c]),
                      writes=[("guring", s)])
                sd = d_cnt % 8
                R.dma("pool", "wd%d" % sd, lambda e, c=c, sd=sd: e.dma_start(out=d_ring[sd][:], in_=wd_d[l, f, c * 128:(c + 1) * 128, :]),
                      writes=[("dring", sd)])
                for (t0, n) in tgs:
                    ntok = n * 128
                    tok0 = t0 * 128
                    bg = ps_g[gu_i % 2]
                    bu = ps_u[gu_i % 2]
                    sl = sil[gu_i % 2]
                    gu_i += 1
                    hkeys = [("hT", t0 + k) for k in range(n)]

                    def mm(e, s=s, bg=bg, bu=bu, tok0=tok0, ntok=ntok):
                        inst = None
                        for which, bank in ((0, bg), (1, bu)):
                            for kc in range(KC):
                                inst = e.matmul(ps[:, bank, 0:ntok], gu_ring[s][:, which, kc, :], hT[:, kc, tok0:tok0 + ntok],
                                                start=(kc == 0), stop=(kc == KC - 1))
                        return inst
                    R.op("pe", mm, reads=[("guring", s)] + hkeys, writes=[("ps", bg), ("ps", bu)])
                    R.op("act", lambda e, bg=bg, sl=sl, ntok=ntok: e.activation(out=sl[:, 0:ntok], in_=ps[:, bg, 0:ntok], func=AF.Silu),
                         reads=[("ps", bg)], writes=[("sil", id(sl))])
                    R.op("dve", lambda e, bu=bu, sl=sl, ci=ci, tok0=tok0, ntok=ntok: e.tensor_tensor(
                        out=aT[:, ci, tok0:tok0 + ntok], in0=ps[:, bu, 0:ntok], in1=sl[:, 0:ntok], op=ALU.mult),
                        reads=[("ps", bu), ("sil", id(sl))], writes=[("aT", ci, t0 + k) for k in range(n)])
                gu_cnt += 1
                d_cnt += 1
            dslots = [(d_cnt - gsz + ci) % 8 for ci in range(gsz)]
            for t in tiles:
                r = 0 if t < NLT else 1
                b0 = 6 if (y_i % 2 == 0) else 0
                y_i += 1

                def mmd(e, t=t, b0=b0, dslots=dslots, gsz=gsz):
                    inst = None
                    for hd in range(2):
                        for ci in range(gsz):
                            inst = e.matmul(ps[:, b0 + hd, :], aT[:, ci, t * 128:(t + 1) * 128],
                                            d_ring[dslots[ci]][:, hd * 512:(hd + 1) * 512],
                                            start=(ci == 0), stop=(ci == gsz - 1))
                    return inst
                R.op("pe", mmd, reads=[("aT", ci, t) for ci in range(gsz)] + [("dring", sd) for sd in dslots],
                     writes=[("ps", b0), ("ps", b0 + 1)])
                yv = ps[:, b0:b0 + 2, :]
                R.op("dve", lambda e, yv=yv, r=r: e.tensor_tensor(out=yv, in0=yv, in1=gate_bc[:, r, :].rearrange("p (a b) -> p a b", a=2),
                                                                  op=ALU.mult),
                     reads=[("ps", b0), ("ps", b0 + 1), ("gate", r)], writes=[("ps", b0), ("ps", b0 + 1)])
                R.op("dve", lambda e, yv=yv, t=t: e.tensor_tensor(out=xs[:, t, :].rearrange("p (a b) -> p a b", a=2),
                                                                  in0=xs[:, t, :].rearrange("p (a b) -> p a b", a=2), in1=yv, op=ALU.add),
                     reads=[("ps", b0), ("ps", b0 + 1), ("x", t)], writes=[("x", t)])
            cbase += gsz
        R.barrier()
        Ar.pos = mark

    def rope_apply(eng, src, dst, cs, t, H, dim, key_r, key_w, tmp1, tmp2):
        q = dim // 4
        cosb = cs[:, 0, t, :].unsqueeze(1).broadcast_to([128, H, dim])
        R.op(eng, lambda e: e.tensor_tensor(out=tmp1, in0=src, in1=cosb, op=ALU.mult), reads=key_r + ["rope"], writes=[("rt1", id(tmp1))])
        s4 = src.rearrange("p h (a b c) -> p h a b c", a=2, b=2)
        t4 = tmp2.rearrange("p h (a b c) -> p h a b c", a=2, b=2)
        sn = cs[:, 1, t, :].rearrange("p (a b c) -> p a b c", a=2, b=2)
        for bsel in range(2):
            R.op(eng, lambda e, bsel=bsel: e.tensor_tensor(
                out=t4[:, :, :, bsel, :], in0=s4[:, :, :, 1 - bsel, :],
                in1=sn[:, :, bsel, :].unsqueeze(1).broadcast_to([128, H, 2, q]), op=ALU.mult),
                reads=key_r + ["rope"], writes=[("rt2", id(tmp2), bsel)])
        R.op(eng, lambda e: e.tensor_tensor(out=dst, in0=tmp1, in1=tmp2, op=ALU.add),
             reads=[("rt1", id(tmp1)), ("rt2", id(tmp2), 0), ("rt2", id(tmp2), 1)], writes=key_w)

    def sumsq(src, H, dd, col0, scr, key_r, key_w):
        sv = scr[:, 0:H * dd].rearrange("p (h d) -> p h d", h=H)
        R.op("dve", lambda e: e.tensor_tensor(out=sv, in0=src, in1=src, op=ALU.mult), reads=key_r, writes=["sqscr"])
        R.op("dve", lambda e: e.tensor_reduce(out=st_ss[:, col0:col0 + H], in_=sv, axis=AX.X, op=ALU.add),
             reads=["sqscr"], writes=key_w)

    def mixer(l, need_ctx):
        sub_modulation(l, 1, 1.0)
        mark0 = Ar.pos
        kT_g = anew(BF16, [T])
        V_g = anew(BF16, [NT, 128])
        kT_m = anew(BF16, [6, T])
        V_m = anew(BF16, [NT, 384])
        NCV = 2050 + 258
        bT = anew(BF16, [2, NCV])
        gsc = anew(F32, [NGAIN])
        markU = Ar.pos
        uT = anew(BF16, [2, NCV])
        R.dma("sp", "gains", lambda e: e.dma_start(out=gains[:], in_=gains_d[l].partition_broadcast(128)), writes=["gains"])
        R.op("dve", lambda e: e.tensor_copy(out=gsc[:], in_=gains[:]), reads=["gains"], writes=["gsc"])
        R.op("dve", lambda e: e.tensor_scalar(out=gsc[:, 0:64], in0=gains[:, 0:64], scalar1=64.0 ** -0.5, scalar2=None, op0=ALU.mult),
             reads=["gains", "gsc"], writes=["gsc"])
        R.op("dve", lambda e: e.tensor_scalar(out=gsc[:, 128:192], in0=gains[:, 128:192], scalar1=96.0 ** -0.5, scalar2=None, op0=ALU.mult),
             reads=["gains", "gsc"], writes=["gsc"])
        R.op("dve", lambda e: e.tensor_scalar(out=gsc[:, 256:288], in0=gains[:, 256:288], scalar1=96.0 ** -0.5, scalar2=None, op0=ALU.mult),
             reads=["gains", "gsc"], writes=["gsc"])
        g_q, g_k, g_qn, g_kn, g_qr, g_kr = (gsc[:, 0:64], gsc[:, 64:128], gsc[:, 128:192], gsc[:, 192:256],
                                            gsc[:, 256:288], gsc[:, 288:320])
        R.op("pool", lambda e: e.memset(uT[:], 0.0), writes=["uT"])

        markK = Ar.pos
        wK = anew(BF16, [KC, 1312])
        wukv = anew(BF16, [2, 768])
        hTr = [anew(BF16, [KC, 128]) for _ in range(2)]
        xn = [anew(BF16, [D]) for _ in range(2)]
        kraw = anew(F32, [544])
        kvraw = anew(F32, [768])
        scr = anew(F32, [768])
        tA = anew(F32, [384])
        tB = anew(F32, [384])
        tC = anew(F32, [384])
        kf = anew(BF16, [128])
        ckvb = anew(BF16, [256])
        ckvT = anew(BF16, [2, 128])
        kfull = anew(BF16, [6, 96])
        cgt = anew(F32, [2, 128])
        R.dma("pool", "wK", lambda e: e.dma_start(out=wK[:], in_=wink_d[l]), writes=["wK"])
        R.dma("pool", "wukv", lambda e: e.dma_start(out=wukv[:], in_=wukv_d[l]), writes=["wukv"])
        for c in range(2):
            R.op("dve", lambda e, c=c: e.tensor_scalar(out=wukv[:, c, :], in0=wukv[:, c, :], scalar1=glat[:, l, 3 + c:4 + c], scalar2=None,
                                                       op0=ALU.mult), reads=["wukv", "glat"], writes=["wukv"])
        for t in range(NT):
            lat = t < NLT
            h = hTr[t % 2]
            hk = ("hTr", t % 2)
            emit_hT(t, h, hk, xn[t % 2], ("xn", t % 2), 6, t % 2)

            def mmA(e, h=h):
                inst = None
                for kc in range(KC):
                    inst = e.matmul(ps[:, 0, :], h[:, kc, :], wK[:, kc, 0:512], start=(kc == 0), stop=(kc == KC - 1))
                for kc in range(KC):
                    inst = e.matmul(ps[:, 1, 0:32], h[:, kc, :], wK[:, kc, 512:544], start=(kc == 0), stop=(kc == KC - 1))
                return inst
            R.op("pe", mmA, reads=[hk, "wK"], writes=[("ps", 0), ("ps", 1)])

            def mmC(e, h=h):
                inst = None
                for cc in range(6):
                    bank, off = (2, cc * 128) if cc < 4 else (3, (cc - 4) * 128)
                    for kc in range(KC):
                        inst = e.matmul(ps[:, bank, off:off + 128], wK[:, kc, 544 + cc * 128:544 + (cc + 1) * 128], h[:, kc, :],
                                        start=(kc == 0), stop=(kc == KC - 1))
                return inst
            R.op("pe", mmC, reads=[hk, "wK"], writes=[("ps", 2), ("ps", 3)])
            R.op("act", lambda e: e.copy(out=kraw[:, 0:512], in_=ps[:, 0, :]), reads=[("ps", 0)], writes=["kraw"])
            R.op("act", lambda e: e.copy(out=kraw[:, 512:544], in_=ps[:, 1, 0:32]), reads=[("ps", 1), "kraw"], writes=["kraw"])
            R.op("act", lambda e, t=t: e.copy(out=V_g[:, t, :], in_=kraw[:, 128:256]), reads=["kraw"], writes=[("Vg", t)])
            R.op("act", lambda e: e.copy(out=ckvb[:], in_=kraw[:, 256:512]), reads=["kraw"], writes=["ckvb"])
            k3 = kraw[:, 0:128].rearrange("p (h d) -> p h d", h=2)
            sumsq(k3, 2, 64, 8, scr, ["kraw"], [("ss", "k")])
            sumsq(kraw[:, 256:512].unsqueeze(1), 1, 256, 10, scr, ["kraw"], [("ss", "ckv")])
            sumsq(kraw[:, 512:544].unsqueeze(1), 1, 32, 11, scr, ["kraw"], [("ss", "kr")])
            R.op("act", lambda e: e.activation(out=st_r[:, 8:10], in_=st_ss[:, 8:10], func=AF.Sqrt, scale=1.0 / 64, bias=epsb[:]),
                 reads=[("ss", "k"), "eps"], writes=[("sr", "k")])
            R.op("act", lambda e: e.activation(out=st_r[:, 10:11], in_=st_ss[:, 10:11], func=AF.Sqrt, scale=1.0 / 256, bias=epsb[:]),
                 reads=[("ss", "ckv"), "eps"], writes=[("sr", "ckv")])
            R.op("act", lambda e: e.activation(out=st_r[:, 11:12], in_=st_ss[:, 11:12], func=AF.Sqrt, scale=1.0 / 32, bias=epsb[:]),
                 reads=[("ss", "kr"), "eps"], writes=[("sr", "kr")])
            R.op("dve", lambda e: e.reciprocal(out=st_r[:, 8:12], in_=st_r[:, 8:12]),
                 reads=[("sr", "k"), ("sr", "ckv"), ("sr", "kr")], writes=[("sr", "k"), ("sr", "ckv"), ("sr", "kr")])
            kn = tA[:, 0:128].rearrange("p (h d) -> p h d", h=2)
            for hh in range(2):
                R.op("dve", lambda e, hh=hh: e.scalar_tensor_tensor(out=kn[:, hh, :], in0=k3[:, hh, :], scalar=st_r[:, 8 + hh:9 + hh], in1=g_k,
                                                                    op0=ALU.mult, op1=ALU.mult),
                     reads=["kraw", ("sr", "k"), "gsc"], writes=[("kn", hh)], sync_self=True)
            kf3 = kf[:].rearrange("p (h d) -> p h d", h=2)
            if lat:
                rope_apply("dve", kn, kf3, ropeA, t, 2, 64, [("kn", 0), ("kn", 1)], ["kf"],
                           tB[:, 0:128].rearrange("p (h d) -> p h d", h=2), tC[:, 0:128].rearrange("p (h d) -> p h d", h=2))
            else:
                R.op("dve", lambda e: e.tensor_copy(out=kf3, in_=kn), reads=[("kn", 0), ("kn", 1)], writes=["kf"])
            R.op("pe", lambda e: e.transpose(psb(7)[:, 0:128], kf[:], ident[:]), reads=["kf", "ident"], writes=[("ps", 7)])
            R.op("act", lambda e, t=t: e.copy(out=kT_g[:, t * 128:(t + 1) * 128], in_=psb(7)[:, 0:128]), reads=[("ps", 7)],
                 writes=[("kTg", t)])
            def trc(e):
                inst = None
                for c in range(2):
                    inst = e.transpose(psb(7)[:, 128 + c * 128:256 + c * 128], ckvb[:, c * 128:(c + 1) * 128], ident[:])
                return inst
            R.op("pe", trc, reads=["ckvb", "ident"], writes=[("ps", 7)])
            R.op("act", lambda e: e.copy(out=ckvT[:].rearrange("p a b -> p (a b)"), in_=psb(7)[:, 128:384]), reads=[("ps", 7)],
                 writes=["ckvT"])

            def mmkv(e):
                inst = None
                for (bank, c0, n) in ((4, 0, 512), (5, 512, 256)):
                    for c in range(2):
                        inst = e.matmul(ps[:, bank, 0:n], ckvT[:, c, :], wukv[:, c, c0:c0 + n], start=(c == 0), stop=(c == 1))
                return inst
            R.op("pe", mmkv, reads=["ckvT", "wukv"], writes=[("ps", 4), ("ps", 5)])
            R.op("act", lambda e: e.copy(out=kvraw[:, 0:512], in_=ps[:, 4, :]), reads=[("ps", 4)], writes=["kvraw"])
            R.op("act", lambda e: e.copy(out=kvraw[:, 512:768], in_=ps[:, 5, 0:256]), reads=[("ps", 5), "kvraw"], writes=["kvraw"])
            kv3 = kvraw[:].rearrange("p (h d) -> p h d", h=6)
            sumsq(kv3[:, :, 0:64], 6, 64, 16, scr, ["kvraw"], [("ss", "kn")])
            R.op("dve", lambda e: e.tensor_tensor(out=st_ss[:, 12:13], in0=st_r[:, 10:11], in1=st_r[:, 10:11], op=ALU.mult),
                 reads=[("sr", "ckv")], writes=[("ss", "b2")])
            R.op("dve", lambda e: e.tensor_scalar(out=st_ss[:, 16:22], in0=st_ss[:, 16:22], scalar1=st_ss[:, 12:13], scalar2=None, op0=ALU.mult),
                 reads=[("ss", "kn"), ("ss", "b2")], writes=[("ss", "kn")], sync_self=True)
            R.op("act", lambda e: e.activation(out=st_r[:, 16:22], in_=st_ss[:, 16:22], func=AF.Sqrt, scale=1.0 / 64, bias=epsb[:]),
                 reads=[("ss", "kn"), "eps"], writes=[("sr", "kn")])
            R.op("dve", lambda e: e.reciprocal(out=st_r[:, 16:22], in_=st_r[:, 16:22]), reads=[("sr", "kn")], writes=[("sr", "kn")])
            R.op("dve", lambda e: e.tensor_scalar(out=st_r[:, 16:22], in0=st_r[:, 16:22], scalar1=st_r[:, 10:11], scalar2=None, op0=ALU.mult),
                 reads=[("sr", "kn"), ("sr", "ckv")], writes=[("sr", "kn")], sync_self=True)
            for hh in range(6):
                R.op("dve", lambda e, hh=hh: e.scalar_tensor_tensor(out=kfull[:, hh, 0:64], in0=kv3[:, hh, 0:64], scalar=st_r[:, 16 + hh:17 + hh],
                                                                    in1=g_kn, op0=ALU.mult, op1=ALU.mult),
                     reads=["kvraw", ("sr", "kn"), "gsc"], writes=[("kfull", hh)], sync_self=(hh == 0))
            R.op("dve", lambda e, t=t: e.tensor_scalar(out=V_m[:, t, :].rearrange("p (h d) -> p h d", h=6), in0=kv3[:, :, 64:128],
                                                       scalar1=st_r[:, 10:11], scalar2=None, op0=ALU.mult),
                 reads=["kvraw", ("sr", "ckv")], writes=[("Vm", t)])
            krn = tA[:, 128:160].unsqueeze(1)
            R.op("dve", lambda e: e.scalar_tensor_tensor(out=tA[:, 128:160], in0=kraw[:, 512:544], scalar=st_r[:, 11:12], in1=g_kr,
                                                         op0=ALU.mult, op1=ALU.mult), reads=["kraw", ("sr", "kr"), "gsc"], writes=["krn"])
            krf = tA[:, 160:192].unsqueeze(1)
            if lat:
                rope_apply("dve", krn, krf, ropeM, t, 1, 32, ["krn"], ["krf"], tB[:, 128:160].unsqueeze(1), tC[:, 128:160].unsqueeze(1))
                src_kr = krf
                krk = "krf"
            else:
                src_kr = krn
                krk = "krn"
            R.op("dve", lambda e, src_kr=src_kr: e.tensor_copy(out=kfull[:, :, 64:96], in_=src_kr.broadcast_to([128, 6, 32])),
                 reads=[krk], writes=[("kfull", "r")])

            def trk(e):
                inst = None
                for hh in range(6):
                    inst = e.transpose(psb(6)[0:96, hh * 128:(hh + 1) * 128], kfull[:, hh, :], ident[:])
                return inst
            R.op("pe", trk, reads=[("kfull", hh) for hh in range(6)] + [("kfull", "r"), "ident"], writes=[("ps", 6)])
            R.op("act", lambda e, t=t: e.copy(out=kT_m[0:96, :, t * 128:(t + 1) * 128],
                                              in_=psb(6)[0:96, 0:768].rearrange("p (h n) -> p h n", h=6)),
                 reads=[("ps", 6)], writes=[("kTm", t)])
            pos = (1 + t * 128) if lat else (2051 + (t - NLT) * 128)
            R.op("act", lambda e: e.copy(out=cgt[:].rearrange("p a b -> p (a b)"), in_=ps[:, 2, 256:512]), reads=[("ps", 2)], writes=["cgt"])
            R.op("dve", lambda e, pos=pos: e.tensor_tensor(out=uT[:, :, pos:pos + 128], in0=ps[:, 2, 0:256].rearrange("p (a b) -> p a b", a=2),
                                                           in1=cgt[:], op=ALU.mult), reads=[("ps", 2), "cgt", "uT"], writes=["uT"])
            R.op("act", lambda e, pos=pos: e.copy(out=bT[:, :, pos:pos + 128], in_=ps[:, 3, 0:256].rearrange("p (a b) -> p a b", a=2)),
                 reads=[("ps", 3)], writes=["bT"])
        cvt = scr[:, 0:512]
        for ch in range(2):
            segs = [(1 + 512 * i, 512) for i in range(4)] + [(2051, 256)]
            for (p0, n) in segs:
                R.op("dve", lambda e, ch=ch, p0=p0, n=n: e.tensor_scalar(out=cvt[:, 0:n], in0=uT[:, ch, p0:p0 + n], scalar1=convp[:, l, ch, 1:2],
                                                                         scalar2=convp[:, l, ch, 3:4], op0=ALU.mult, op1=ALU.add),
                     reads=["uT", "convp"], writes=["cvt"])
                R.op("dve", lambda e, ch=ch, p0=p0, n=n: e.scalar_tensor_tensor(out=cvt[:, 0:n], in0=uT[:, ch, p0 - 1:p0 - 1 + n],
                                                                                scalar=convp[:, l, ch, 0:1], in1=cvt[:, 0:n], op0=ALU.mult, op1=ALU.add),
                     reads=["uT", "convp", "cvt"], writes=["cvt"])
                R.op("dve", lambda e, ch=ch, p0=p0, n=n: e.scalar_tensor_tensor(out=cvt[:, 0:n], in0=uT[:, ch, p0 + 1:p0 + 1 + n],
                                                                                scalar=convp[:, l, ch, 2:3], in1=cvt[:, 0:n], op0=ALU.mult, op1=ALU.add),
                     reads=["uT", "convp", "cvt"], writes=["cvt"])
                R.op("dve", lambda e, ch=ch, p0=p0, n=n: e.tensor_tensor(out=bT[:, ch, p0:p0 + n], in0=bT[:, ch, p0:p0 + n], in1=cvt[:, 0:n],
                                                                         op=ALU.mult), reads=["bT", "cvt"], writes=["bT"])
        R.barrier()
        Ar.pos = markU

        wQ = anew(BF16, [KC, 768])
        wuq = anew(BF16, [3, 576])
        wo_b = anew(BF16, [KC, 512])
        hTq = [anew(BF16, [KC, 128])] * 2
        xnq = [anew(BF16, [D])] * 2
        qraw = anew(F32, [768])
        qmraw = qraw[:, 0:576]
        scrq = anew(F32, [576])
        tAq = anew(F32, [384])
        tBq = anew(F32, [384])
        tCq = anew(F32, [384])
        qf = anew(BF16, [384])
        cqb = anew(BF16, [384])
        cqT = anew(BF16, [3, 128])
        qmfull = anew(BF16, [6, 96])
        qT_g = anew(BF16, [3, 512])
        qT_m = anew(BF16, [6, 512])
        PT2 = [anew(BF16, [2, 512]) for _ in range(2)]
        mixT = anew(BF16, [6, 512])
        rden = scrq[:, 0:512]
        R.dma("pool", "wQ", lambda e: e.dma_start(out=wQ[:], in_=winq_d[l]), writes=["wQ"])
        R.dma("pool", "wuq", lambda e: e.dma_start(out=wuq[:], in_=wuq_d[l]), writes=["wuq"])
        for c in range(3):
            R.op("dve", lambda e, c=c: e.tensor_scalar(out=wuq[:, c, :], in0=wuq[:, c, :], scalar1=glat[:, l, c:c + 1], scalar2=None,
                                                       op0=ALU.mult), reads=["wuq", "glat"], writes=["wuq"])
        groups = [(4 * g, 4) for g in range(4)]
        if need_ctx:
            groups.append((16, 2))
        pt_i = 0
        st_i = 0
        o_i = 0
        for (t0, ntile) in groups:
            lat = t0 < NLT
            nq = ntile * 128
            r = 0 if lat else 1
            key_tiles = list(range(NT)) if lat else [16, 17]
            for lt in range(ntile):
                t = t0 + lt
                h = hTq[0]
                hk = ("hTr", 0)
                emit_hT(t, h, hk, xnq[0], ("xn", 0), 6, t % 2)

                def mmQ(e, h=h):
                    inst = None
                    for (bank, c0) in ((4, 0), (5, 384)):
                        for kc in range(KC):
                            inst = e.matmul(ps[:, bank, 0:384], h[:, kc, :], wQ[:, kc, c0:c0 + 384], start=(kc == 0), stop=(kc == KC - 1))
                    return inst
                R.op("pe", mmQ, reads=[hk, "wQ"], writes=[("ps", 4), ("ps", 5)])
                R.op("act", lambda e: e.copy(out=qraw[:, 0:384], in_=ps[:, 4, 0:384]), reads=[("ps", 4)], writes=["qraw"])
                R.op("act", lambda e: e.copy(out=qraw[:, 384:768], in_=ps[:, 5, 0:384]), reads=[("ps", 5), "qraw"], writes=["qraw"])
                R.op("act", lambda e: e.copy(out=cqb[:], in_=qraw[:, 384:768]), reads=["qraw"], writes=["cqb"])
                q3 = qraw[:, 0:384].rearrange("p (h d) -> p h d", h=6)
                sumsq(q3, 6, 64, 24, scrq, ["qraw"], [("ss", "q")])
                sumsq(qraw[:, 384:768].unsqueeze(1), 1, 384, 30, scrq, ["qraw"], [("ss", "cq")])
                R.op("act", lambda e: e.activation(out=st_r[:, 24:30], in_=st_ss[:, 24:30], func=AF.Sqrt, scale=1.0 / 64, bias=epsb[:]),
                     reads=[("ss", "q"), "eps"], writes=[("sr", "q")])
                R.op("act", lambda e: e.activation(out=st_r[:, 30:31], in_=st_ss[:, 30:31], func=AF.Sqrt, scale=1.0 / 384, bias=epsb[:]),
                     reads=[("ss", "cq"), "eps"], writes=[("sr", "cq")])
                R.op("dve", lambda e: e.reciprocal(out=st_r[:, 24:31], in_=st_r[:, 24:31]), reads=[("sr", "q"), ("sr", "cq")],
                     writes=[("sr", "q"), ("sr", "cq")])
                qn = tAq[:].rearrange("p (h d) -> p h d", h=6)
                for hh in range(6):
                    R.op("dve", lambda e, hh=hh: e.scalar_tensor_tensor(out=qn[:, hh, :], in0=q3[:, hh, :], scalar=st_r[:, 24 + hh:25 + hh], in1=g_q,
                                                                        op0=ALU.mult, op1=ALU.mult),
                         reads=["qraw", ("sr", "q"), "gsc"], writes=[("qn", hh)], sync_self=(hh == 0))
                qf3 = qf[:].rearrange("p (h d) -> p h d", h=6)
                qnk = [("qn", hh) for hh in range(6)]
                if lat:
                    rope_apply("dve", qn, qf3, ropeA, t, 6, 64, qnk, ["qf"], tBq[:].rearrange("p (h d) -> p h d", h=6),
                               tCq[:].rearrange("p (h d) -> p h d", h=6))
                else:
                    R.op("dve", lambda e: e.tensor_copy(out=qf3, in_=qn), reads=qnk, writes=["qf"])

                def trq(e):
                    inst = None
                    for pr in range(3):
                        inst = e.transpose(psb(7)[:, pr * 128:(pr + 1) * 128], qf[:, pr * 128:(pr + 1) * 128], ident[:])
                    for c in range(3):
                        inst = e.transpose(psb(7)[:, 384 + c * 128:512 + c * 128], cqb[:, c * 128:(c + 1) * 128], ident[:])
                    return inst
                R.op("pe", trq, reads=["qf", "cqb", "ident"], writes=[("ps", 7)])
                R.op("act", lambda e, lt=lt: e.copy(out=qT_g[:, :, lt * 128:(lt + 1) * 128], in_=psb(7)[:, 0:384].rearrange("p (a b) -> p a b", a=3)),
                     reads=[("ps", 7)], writes=[("qTg", lt)])
                R.op("act", lambda e: e.copy(out=cqT[:].rearrange("p a b -> p (a b)"), in_=psb(7)[:, 384:768]), reads=[("ps", 7)], writes=["cqT"])

                def mmuq(e):
                    inst = None
                    for (bank, c0) in ((4, 0), (5, 288)):
                        for c in range(3):
                            inst = e.matmul(ps[:, bank, 0:288], cqT[:, c, :], wuq[:, c, c0:c0 + 288], start=(c == 0), stop=(c == 2))
                    return inst
                R.op("pe", mmuq, reads=["cqT", "wuq"], writes=[("ps", 4), ("ps", 5)])
                R.op("act", lambda e: e.copy(out=qmraw[:, 0:288], in_=ps[:, 4, 0:288]), reads=[("ps", 4)], writes=["qraw"])
                R.op("act", lambda e: e.copy(out=qmraw[:, 288:576], in_=ps[:, 5, 0:288]), reads=[("ps", 5), "qraw"], writes=["qraw"])
                qm3 = qmraw[:].rearrange("p (h d) -> p h d", h=6)
                sumsq(qm3[:, :, 0:64], 6, 64, 32, scrq, ["qraw"], [("ss", "qmn")])
                sumsq(qm3[:, :, 64:96], 6, 32, 38, scrq, ["qraw"], [("ss", "qmr")])
                R.op("dve", lambda e: e.tensor_tensor(out=st_ss[:, 31:32], in0=st_r[:, 30:31], in1=st_r[:, 30:31], op=ALU.mult),
                     reads=[("sr", "cq")], writes=[("ss", "a2")])
                R.op("dve", lambda e: e.tensor_scalar(out=st_ss[:, 32:44], in0=st_ss[:, 32:44], scalar1=st_ss[:, 31:32], scalar2=None, op0=ALU.mult),
                     reads=[("ss", "qmn"), ("ss", "qmr"), ("ss", "a2")], writes=[("ss", "qmn"), ("ss", "qmr")], sync_self=True)
                R.op("act", lambda e: e.activation(out=st_r[:, 32:38], in_=st_ss[:, 32:38], func=AF.Sqrt, scale=1.0 / 64, bias=epsb[:]),
                     reads=[("ss", "qmn"), "eps"], writes=[("sr", "qmn")])
                R.op("act", lambda e: e.activation(out=st_r[:, 38:44], in_=st_ss[:, 38:44], func=AF.Sqrt, scale=1.0 / 32, bias=epsb[:]),
                     reads=[("ss", "qmr"), "eps"], writes=[("sr", "qmr")])
                R.op("dve", lambda e: e.reciprocal(out=st_r[:, 32:44], in_=st_r[:, 32:44]), reads=[("sr", "qmn"), ("sr", "qmr")],
                     writes=[("sr", "qmn"), ("sr", "qmr")])
                R.op("dve", lambda e: e.tensor_scalar(out=st_r[:, 32:44], in0=st_r[:, 32:44], scalar1=st_r[:, 30:31], scalar2=None, op0=ALU.mult),
                     reads=[("sr", "qmn"), ("sr", "qmr"), ("sr", "cq")], writes=[("sr", "qmn"), ("sr", "qmr")], sync_self=True)
                qr = tAq[:, 0:192].rearrange("p (h d) -> p h d", h=6)
                for hh in range(6):
                    R.op("dve", lambda e, hh=hh: e.scalar_tensor_tensor(out=qmfull[:, hh, 0:64], in0=qm3[:, hh, 0:64], scalar=st_r[:, 32 + hh:33 + hh],
                                                                        in1=g_qn, op0=ALU.mult, op1=ALU.mult),
                         reads=["qraw", ("sr", "qmn"), "gsc"], writes=[("qmfull", hh)], sync_self=(hh == 0))
                    R.op("dve", lambda e, hh=hh: e.scalar_tensor_tensor(out=qr[:, hh, :], in0=qm3[:, hh, 64:96], scalar=st_r[:, 38 + hh:39 + hh],
                                                                        in1=g_qr, op0=ALU.mult, op1=ALU.mult),
                         reads=["qraw", ("sr", "qmr"), "gsc", "qf"] + qnk, writes=[("qr", hh)])
                qrk = [("qr", hh) for hh in range(6)]
                if lat:
                    rope_apply("dve", qr, qmfull[:, :, 64:96], ropeM, t, 6, 32, qrk, [("qmfull", "r")],
                               tBq[:, 0:192].rearrange("p (h d) -> p h d", h=6), tCq[:, 0:192].rearrange("p (h d) -> p h d", h=6))
                else:
                    R.op("dve", lambda e: e.tensor_copy(out=qmfull[:, :, 64:96], in_=qr), reads=qrk, writes=[("qmfull", "r")])

                def trqm(e):
                    inst = None
                    for hh in range(6):
                        inst = e.transpose(psb(7)[0:96, hh * 128:(hh + 1) * 128], qmfull[:, hh, :], ident[:])
                    return inst
                R.op("pe", trqm, reads=[("qmfull", hh) for hh in range(6)] + [("qmfull", "r"), "ident"], writes=[("ps", 7)])
                R.op("act", lambda e, lt=lt: e.copy(out=qT_m[0:96, :, lt * 128:(lt + 1) * 128],
                                                    in_=psb(7)[0:96, 0:768].rearrange("p (h n) -> p h n", h=6)),
                     reads=[("ps", 7)], writes=[("qTm", lt)])
            qgk = [("qTg", lt) for lt in range(ntile)]
            qmk = [("qTm", lt) for lt in range(ntile)]
            npair = len(key_tiles) // 2
            items = [(slot, kj, key_tiles[2 * kj], key_tiles[2 * kj + 1]) for slot in range(12) for kj in range(npair)]
            LA = 1
            ST_PAIRS = [0, 6]

            def slot_info(slot):
                if slot < 6:
                    pr, hf = slot // 2, slot % 2
                    return True, pr, hf, pr, slot
                hm = slot - 6
                pr, hf = hm // 2, hm % 2
                return False, pr, hf, 3 + pr, hm

            def emit_st(i, nq=nq):
                slot, kj, kta, ktb = items[i]
                gqa, pr, hf, chunk, hm = slot_info(slot)
                b0 = ST_PAIRS[i % 2]
                pt = PT2[i % 2]
                ptk = ("PT", i % 2)

                def mmst(e, b0=b0, gqa=gqa, pr=pr, hf=hf, hm=hm, kta=kta, ktb=ktb, nq=nq):
                    inst = None
                    for k, kt in enumerate((kta, ktb)):
                        if gqa:
                            inst = e.matmul(ps[:, b0 + k, 0:nq], kT_g[64 * hf:64 * hf + 64, kt * 128:(kt + 1) * 128],
                                            qT_g[64 * hf:64 * hf + 64, pr, 0:nq], start=True, stop=True)
                        else:
                            inst = e.matmul(ps[:, b0 + k, 0:nq], kT_m[0:96, hm, kt * 128:(kt + 1) * 128], qT_m[0:96, hm, 0:nq],
                                            start=True, stop=True)
                    return inst
                kk = [("kTg", kta), ("kTg", ktb)] + qgk if gqa else [("kTm", kta), ("kTm", ktb)] + qmk
                R.op("pe", mmst, reads=kk, writes=[("ps", b0), ("ps", b0 + 1)])
                R.op("act", lambda e, b0=b0, pt=pt, nq=nq: e.activation(out=pt[:, :, 0:nq], in_=ps[:, b0:b0 + 2, 0:nq], func=AF.Exp),
                     reads=[("ps", b0), ("ps", b0 + 1)], writes=[ptk])

            def emit_pv(i, nq=nq):
                slot, kj, kta, ktb = items[i]
                gqa, pr, hf, chunk, hm = slot_info(slot)
                pt = PT2[i % 2]
                ptk = ("PT", i % 2)
                bo = 2 if (slot % 2 == 0) else 4
                bd = bo + 1
                if gqa:
                    vaps = [V_g[:, kta, :], V_g[:, ktb, :]]
                    vk = [("Vg", kta), ("Vg", ktb)]
                else:
                    vaps = [V_m[:, kta, pr * 128:(pr + 1) * 128], V_m[:, ktb, pr * 128:(pr + 1) * 128]]
                    vk = [("Vm", kta), ("Vm", ktb)]

                def mmpv(e, vaps=vaps, pt=pt, bo=bo, bd=bd, kj=kj, nq=nq, npair=npair):
                    inst = None
                    for k in range(2):
                        first = (kj == 0 and k == 0)
                        last = (kj == npair - 1 and k == 1)
                        e.matmul(ps[:, bo, 0:nq], vaps[k], pt[:, k, 0:nq], start=first, stop=last)
                        inst = e.matmul(ps[:, bd, 0:nq], ones_bf[:], pt[:, k, 0:nq], start=first, stop=last)
                    return inst
                R.op("pe", mmpv, reads=vk + [ptk, "ones"], writes=[("ps", bo), ("ps", bd)])
                if kj == npair - 1:
                    p0 = 64 * hf
                    R.op("dve", lambda e, bd=bd, p0=p0, nq=nq: e.reciprocal(out=rden[p0:p0 + 64, 0:nq], in_=ps[p0:p0 + 64, bd, 0:nq]),
                         reads=[("ps", bd)], writes=["rden"])
                    R.op("dve", lambda e, bo=bo, p0=p0, chunk=chunk, nq=nq: e.tensor_tensor(
                        out=mixT[p0:p0 + 64, chunk, 0:nq], in0=ps[p0:p0 + 64, bo, 0:nq], in1=rden[p0:p0 + 64, 0:nq], op=ALU.mult),
                        reads=[("ps", bo), "rden"], writes=[("mixT", chunk, hf)])

            for i in range(len(items) + LA):
                if i < len(items):
                    emit_st(i)
                if i >= LA:
                    emit_pv(i - LA)
            mixk = [("mixT", c, hf) for c in range(6) for hf in range(2)]
            for hd in range(2):
                R.dma("pool", "wo", lambda e, hd=hd: e.dma_start(out=wo_b[:], in_=wout_d[l, hd]), writes=["wo"])
                for lt in range(ntile):
                    t = t0 + lt
                    pos = (1 + t * 128) if lat else (2051 + (t - NLT) * 128)
                    bk = 6 + (lt % 2)

                    def mmo(e, lt=lt, pos=pos, bk=bk):
                        inst = None
                        for c in range(8):
                            if c < 3:
                                lh = mixT[:, c, lt * 128:(lt + 1) * 128]
                            elif c < 5:
                                lh = bT[:, c - 3, pos:pos + 128]
                            else:
                                lh = mixT[:, c - 2, lt * 128:(lt + 1) * 128]
                            inst = e.matmul(ps[:, bk, :], lh, wo_b[:, c, :], start=(c == 0), stop=(c == 7))
                        return inst
                    R.op("pe", mmo, reads=mixk + ["bT", "wo"], writes=[("ps", bk)])
                    R.op("dve", lambda e, bk=bk, r=r, hd=hd: e.tensor_tensor(out=ps[:, bk, :], in0=ps[:, bk, :],
                                                                             in1=gate_bc[:, r, hd * 512:(hd + 1) * 512], op=ALU.mult),
                         reads=[("ps", bk), ("gate", r)], writes=[("ps", bk)])
                    R.op("dve", lambda e, bk=bk, t=t, hd=hd: e.tensor_tensor(out=xs[:, t, hd * 512:(hd + 1) * 512],
                                                                             in0=xs[:, t, hd * 512:(hd + 1) * 512], in1=ps[:, bk, :], op=ALU.add),
                         reads=[("ps", bk), ("x", t)], writes=[("x", t)])
        R.barrier()
        Ar.pos = mark0

    done = False
    for l in range(L):
        if l > 0:
            R.new_epoch()
        need_ctx = l < DEPTH - 1
        modulation(l)
        ffn(l, 0, 0, list(range(NT)))
        if stop_after == (l, "ffn1"):
            break
        mixer(l, need_ctx)
        if stop_after == (l, "mix"):
            break
        ffn(l, 1, 2, list(range(NT)) if need_ctx else list(range(NLT)))
        if stop_after == (l, "ffn2"):
            break

    for i in range(4):
        R.dma("sp", "out", lambda e, i=i: e.dma_start(
            out=out_d[512 * i:512 * (i + 1), :].rearrange("(t p) d -> p t d", p=128), in_=xs[:, 4 * i:4 * i + 4, :]),
            reads=[("x", 4 * i + k) for k in range(4)])
    R.op("sp", lambda e: None, reads=[], writes=[("x", k) for k in range(NLT)])

    R.finalize()
    esems = {}
    for e in Rec.ENG:
        for ep in range(R.n_epochs):
            esems[(e, ep)] = es.enter_context(nc.semaphore("s_%s_%d" % (e, ep)))
    lsems = {ln: es.enter_context(nc.semaphore("l_%s" % ln)) for ln in R.lane_cnt}
    block = es.enter_context(nc.Block())

    @block.tensor
    def _(e):
        R.emit("pe", e, esems, lsems)

    @block.scalar
    def _(e):
        R.emit("act", e, esems, lsems)

    @block.vector
    def _(e):
        R.emit("dve", e, esems, lsems)

    @block.gpsimd
    def _(e):
        R.emit("pool", e, esems, lsems)

    @block.sync
    def _(e):
        R.emit("sp", e, esems, lsems)

    es.close()
    return nc


_CACHE = {}


def kernel(**inputs):
    inp = {k: np.asarray(v) for k, v in inputs.items()}
    if "nc" not in _CACHE:
        _CACHE["nc"] = build_program(DEPTH)
    nc = _CACHE["nc"]
    sh = prep_shared(inp, DEPTH)
    in_maps = []
    for b in range(8):
        m = dict(sh)
        m.update(prep_core(inp, b))
        in_maps.append(m)
    res = run_bass_kernel_spmd(nc, in_maps, core_ids=list(range(8)))
    out = np.stack([np.asarray(r["out"]) for r in res.results], axis=0)
    return out.astype(np.float32)
```

```python
import numpy as np
from contextlib import ExitStack
import concourse.bass as bass
import concourse.mybir as mybir
from concourse.bass_utils import run_bass_kernel_spmd

F32 = mybir.dt.float32
BF16 = mybir.dt.bfloat16
U8 = mybir.dt.uint8
ALU = mybir.AluOpType
AF = mybir.ActivationFunctionType
AX = mybir.AxisListType

D = 1024
DEPTH = 4
NT = 18
NLT = 16
T = NT * 128
DFF = 2816
NFC = 22
EPS = 1e-6
KC = 8
NGAIN = 320
FF_GROUPS = [4, 4, 4, 4, 4, 2]


class Rec:
    ENG = ("pe", "act", "dve", "pool", "sp")

    def __init__(self):
        self.ops = {e: [] for e in self.ENG}
        self.res = {}
        self.lane_cnt = {}
        self.epoch_starts = {e: [0] for e in self.ENG}
        self.pending = {e: [] for e in self.ENG}
        self.cap = None

    def capture_start(self):
        self.cap = []

    def capture_end(self):
        c = self.cap
        self.cap = None
        return c

    def replay(self, item):
        if item[0] == "op":
            self.op(item[1], item[2], item[3], item[4], item[5])
        else:
            self.dma(item[1], item[2], item[3], item[4], item[5])

    def replay_zipped(self, streams, frac=0.5):
        if not streams:
            return
        H = max(1, int(frac * max(len(st) for st in streams)))
        keyed = []
        for k, st in enumerate(streams):
            for j, it in enumerate(st):
                keyed.append((k * H + j, k, j, it))
        keyed.sort(key=lambda z: (z[0], z[1], z[2]))
        for _, _, _, it in keyed:
            self.replay(it)

    def new_epoch(self):
        for e in self.ENG:
            self.epoch_starts[e].append(len(self.ops[e]))

    def _deps(self, reads, writes):
        d = []
        for k in reads:
            st = self.res.get(k)
            if st is not None and st[0] is not None:
                d.append(st[0])
        for k in writes:
            st = self.res.get(k)
            if st is not None:
                if st[0] is not None:
                    d.append(st[0])
                for kk, v in st[1].items():
                    d.append(kk + (v,))
        return d

    def _commit(self, tok, reads, writes):
        for k in reads:
            st = self.res.get(k)
            if st is None:
                st = [None, {}]
                self.res[k] = st
            key = tok[:2]
            if st[1].get(key, -1) < tok[2]:
                st[1][key] = tok[2]
        for k in writes:
            self.res[k] = [tok, {}]

    def op(self, eng, fn, reads=(), writes=(), sync_self=False):
        if self.cap is not None:
            self.cap.append(("op", eng, fn, tuple(reads), tuple(writes), sync_self))
            return None
        deps = self._deps(reads, writes) + self.pending[eng]
        self.pending[eng] = []
        idx = len(self.ops[eng])
        tok = ("e", eng, idx)
        self.ops[eng].append({"fn": fn, "deps": deps, "lane": None, "ss": sync_self})
        self._commit(tok, reads, writes)
        return tok

    def dma(self, eng, lane, fn, reads=(), writes=()):
        if self.cap is not None:
            self.cap.append(("dma", eng, lane, fn, tuple(reads), tuple(writes)))
            return None
        deps = self._deps(reads, writes) + self.pending[eng]
        self.pending[eng] = []
        self.lane_cnt[lane] = self.lane_cnt.get(lane, 0) + 1
        tok = ("d", lane, self.lane_cnt[lane])
        self.ops[eng].append({"fn": fn, "deps": deps, "lane": lane, "ss": True})
        self._commit(tok, reads, writes)
        return tok

    def barrier(self):
        last = []
        for e in self.ENG:
            for i in range(len(self.ops[e]) - 1, -1, -1):
                if self.ops[e][i]["lane"] is None:
                    last.append(("e", e, i))
                    break
        for l, c in self.lane_cnt.items():
            last.append(("d", l, c))
        for e in self.ENG:
            self.pending[e] = self.pending[e] + list(last)

    def finalize(self):
        self.signal = {e: [False] * len(self.ops[e]) for e in self.ENG}
        for e in self.ENG:
            for i, op in enumerate(self.ops[e]):
                for tok in op["deps"]:
                    if tok[0] == "e" and (tok[1] != e or op["ss"] or tok[2] >= i - 2):
                        self.signal[tok[1]][tok[2]] = True
        self.sigval = {}
        self.epoch_of = {}
        for e in self.ENG:
            starts = self.epoch_starts[e]
            vals = [None] * len(self.ops[e])
            eps = [0] * len(self.ops[e])
            ep = 0
            cnt = 0
            for i in range(len(self.ops[e])):
                while ep + 1 < len(starts) and i >= starts[ep + 1]:
                    ep += 1
                    cnt = 0
                if self.signal[e][i]:
                    cnt += 1
                    vals[i] = cnt
                eps[i] = ep
            self.sigval[e] = vals
            self.epoch_of[e] = eps
        self.n_epochs = max(len(s) for s in self.epoch_starts.values())

    def emit(self, eng, e, esems, lsems):
        waited = {}
        for i, op in enumerate(self.ops[eng]):
            need = {}
            for tok in op["deps"]:
                if tok[0] == "e":
                    if tok[1] == eng and not op["ss"] and tok[2] < i - 2:
                        continue
                    key = ("e", tok[1], self.epoch_of[tok[1]][tok[2]])
                    val = self.sigval[tok[1]][tok[2]]
                else:
                    key = ("d", tok[1])
                    val = 16 * tok[2]
                if need.get(key, 0) < val:
                    need[key] = val
            for key, val in need.items():
                if waited.get(key, 0) >= val:
                    continue
                if key[0] == "e":
                    later = [k for k in waited if k[0] == "e" and k[1] == key[1] and k[2] > key[2]]
                    if later:
                        continue
                    e.wait_ge(esems[(key[1], key[2])], val)
                else:
                    e.wait_ge(lsems[key[1]], val)
                waited[key] = val
            inst = op["fn"](e)
            if inst is None:
                continue
            if op["lane"] is not None:
                inst.then_inc(lsems[op["lane"]], 16)
            elif self.signal[eng][i]:
                inst.then_inc(esems[(eng, self.epoch_of[eng][i])], 1)


Q_ORDER = [0, 3, 1, 4, 2, 5]


def _rope_np(dim):
    rows = 2048 // 64
    row = np.repeat(np.arange(rows, dtype=np.float64), 64)
    col = np.tile(np.arange(64, dtype=np.float64), rows)
    half = dim // 2
    inv = 1.0 / (10000.0 ** (np.arange(0, half, 2, dtype=np.float64) / half))
    ar = row[:, None] * inv[None, :]
    ac = col[:, None] * inv[None, :]
    ang = np.concatenate([ar, ar, ac, ac], axis=-1)
    cos = np.cos(ang).astype(np.float32)
    sin = np.sin(ang).astype(np.float32)
    q = dim // 4
    sgn = np.concatenate([-np.ones(q), np.ones(q), -np.ones(q), np.ones(q)]).astype(np.float32)
    sin = sin * sgn[None, :]
    cos = np.ascontiguousarray(cos.reshape(16, 128, dim).transpose(1, 0, 2))
    sin = np.ascontiguousarray(sin.reshape(16, 128, dim).transpose(1, 0, 2))
    return cos, sin


def prep_shared(inp, n_layers):
    L = n_layers
    f = lambda a: np.ascontiguousarray(a, dtype=np.float32)
    sh = {}
    wm = inp["w_mod"][:L]
    sh["wmod"] = f(wm.reshape(L, KC, 128, 18, 512).transpose(0, 3, 2, 1, 4))
    sh["bmodT"] = f(inp["b_mod"][:L].reshape(L, 72, 128).transpose(2, 0, 1))
    sh["gn"] = f(inp["g_norm"][:L].reshape(L, 3, KC, 128).transpose(3, 0, 1, 2))
    wg = inp["ffn_w_gate"][:L].reshape(L, 2, KC, 128, NFC, 128)
    wu = inp["ffn_w_up"][:L].reshape(L, 2, KC, 128, NFC, 128)
    wgu = np.stack([wg, wu], axis=0)
    sh["wgu"] = f(wgu.transpose(1, 2, 5, 4, 0, 3, 6))
    sh["wd"] = f(inp["ffn_w_down"][:L])
    win = inp["w_in"][:L]
    qcols = np.concatenate([np.arange(64 * h, 64 * h + 64) for h in Q_ORDER])
    kside = np.concatenate([np.arange(384, 512), np.arange(512, 640), np.arange(1792, 2048),
                            np.arange(2048, 2080),
                            np.arange(640, 896), np.arange(1152, 1408), np.arange(896, 1152)])
    qside = np.concatenate([qcols, np.arange(1408, 1792)])
    sh["wink"] = f(win[:, :, kside].reshape(L, KC, 128, 1312).transpose(0, 2, 1, 3))
    sh["winq"] = f(win[:, :, qside].reshape(L, KC, 128, 768).transpose(0, 2, 1, 3))
    sh["wuq"] = f(inp["mla_w_uq"][:L].reshape(L, 3, 128, 576).transpose(0, 2, 1, 3))
    sh["wukv"] = f(inp["mla_w_ukv"][:L].reshape(L, 2, 128, 768).transpose(0, 2, 1, 3))
    rows = []
    for pr in range(3):
        for hf in range(2):
            h = Q_ORDER[2 * pr + hf]
            rows.append(np.arange(64 * h, 64 * h + 64))
    rows.append(np.arange(384, 640))
    rows.append(np.arange(640, 1024))
    rows = np.concatenate(rows)
    wo = inp["w_out"][:L][:, rows, :]
    sh["wout"] = f(wo.reshape(L, KC, 128, 2, 512).transpose(0, 3, 2, 1, 4))
    sh["gains"] = f(np.concatenate([inp["gqa_g_q"][:L], inp["gqa_g_k"][:L], inp["mla_g_qn"][:L],
                                    inp["mla_g_kn"][:L], inp["mla_g_qr"][:L], inp["mla_g_kr"][:L]], axis=1))
    cw = inp["conv_w"][:L]
    cb = inp["conv_b"][:L]
    cp = np.concatenate([cw, cb[:, None, :]], axis=1)
    sh["convp"] = f(cp.reshape(L, 4, 2, 128).transpose(3, 0, 2, 1))
    gl = np.concatenate([inp["mla_g_cq"][:L].reshape(L, 3, 128), inp["mla_g_ckv"][:L].reshape(L, 2, 128)], axis=1)
    sh["glat"] = f(gl.transpose(2, 0, 1))
    ca, sa = _rope_np(64)
    cm, sm = _rope_np(32)
    sh["ropeA"] = f(np.stack([ca, sa], axis=1))
    sh["ropeM"] = f(np.stack([cm, sm], axis=1))
    return sh


def prep_core(inp, b):
    xin = np.concatenate([inp["x"][b], inp["ctx"][b]], axis=0)
    cc = np.stack([inp["c"][b], inp["c_ctx"]], axis=-1)
    ccT = np.ascontiguousarray(cc.reshape(KC, 128, 2).transpose(1, 0, 2), dtype=np.float32)
    return {"xin": np.ascontiguousarray(xin, dtype=np.float32), "ccT": ccT}


def build_program(n_layers=DEPTH, stop_after=None):
    L = n_layers
    nc = bass.Bass("TRN2", target_bir_lowering=False)
    dt_in = lambda name, shape: nc.dram_tensor(name, list(shape), F32, kind="ExternalInput").ap()
    xin = dt_in("xin", [T, D])
    ccT_d = dt_in("ccT", [128, KC, 2])
    wmod_d = dt_in("wmod", [L, 18, 128, KC, 512])
    bmodT_d = dt_in("bmodT", [128, L, 72])
    gn_d = dt_in("gn", [128, L, 3, KC])
    wgu_d = dt_in("wgu", [L, 2, NFC, 128, 2, KC, 128])
    wd_d = dt_in("wd", [L, 2, DFF, D])
    wink_d = dt_in("wink", [L, 128, KC, 1312])
    winq_d = dt_in("winq", [L, 128, KC, 768])
    wuq_d = dt_in("wuq", [L, 128, 3, 576])
    wukv_d = dt_in("wukv", [L, 128, 2, 768])
    wout_d = dt_in("wout", [L, 2, 128, KC, 512])
    gains_d = dt_in("gains", [L, NGAIN])
    convp_d = dt_in("convp", [128, L, 2, 4])
    glat_d = dt_in("glat", [128, L, 5])
    ropeA_d = dt_in("ropeA", [128, 2, 16, 64])
    ropeM_d = dt_in("ropeM", [128, 2, 16, 32])
    out_d = nc.dram_tensor("out", [2048, D], F32, kind="ExternalOutput").ap()

    R = Rec()
    es = ExitStack()
    sb = lambda name, shape, dt: es.enter_context(nc.sbuf_tensor(name, list(shape), dt))
    xs = sb("xs", [128, NT, D], F32)
    ropeA = sb("ropeA_s", [128, 2, 16, 64], BF16)
    ropeM = sb("ropeM_s", [128, 2, 16, 32], BF16)
    ident = sb("ident", [128, 128], BF16)
    identf = sb("identf", [128, 128], F32)
    ones_bf = sb("ones_bf", [128, 128], BF16)
    ccs = sb("ccs", [128, KC, 2], F32)
    s2 = sb("s2", [128, KC, 2], BF16)
    gn = sb("gn_s", [128, L, 3, KC], F32)
    bmodT = sb("bmodT_s", [128, L, 72], F32)
    convp = sb("convp_s", [128, L, 2, 4], F32)
    glat = sb("glat_s", [128, L, 5], F32)
    gains = sb("gains_s", [128, NGAIN], F32)
    modT = sb("modT", [128, 72, 2], F32)
    Amod = sb("Amod", [128, KC, 2], F32)
    Bmod = sb("Bmod", [128, KC, 2], F32)
    gate_bc = sb("gate_bc", [128, 2, D], BF16)
    epsb = sb("epsb", [128, 1], F32)
    st_ss = sb("st_ss", [128, 64], F32)
    st_r = sb("st_r", [128, 64], F32)
    ARENA_BYTES = 120 * 1024
    arena = sb("arena", [128, ARENA_BYTES], U8)
    ps = es.enter_context(nc.psum_tensor("ps", [128, 8, 512], F32))

    class Ar:
        pos = 0

    def aalloc(nbytes):
        a0 = (Ar.pos + 31) // 32 * 32
        Ar.pos = a0 + nbytes
        assert Ar.pos <= ARENA_BYTES, ("arena overflow", Ar.pos)
        return a0

    def aview(a0, dt, shape):
        esz = 2 if dt == BF16 else 4
        n = int(np.prod(shape))
        v = arena[:, a0:a0 + n * esz].bitcast(dt)
        if len(shape) == 1:
            return v
        names = " ".join("a%d" % i for i in range(len(shape)))
        kw = {"a%d" % i: shape[i] for i in range(1, len(shape))}
        return v.rearrange("p (%s) -> p %s" % (names, names), **kw)

    def anew(dt, shape):
        esz = 2 if dt == BF16 else 4
        return aview(aalloc(int(np.prod(shape)) * esz), dt, shape)

    psb = lambda b: ps[:, b, :].bitcast(BF16)

    for i in range(6):
        R.dma("sp", "xin%d" % i, lambda e, i=i: e.dma_start(
            out=xs[:, 3 * i:3 * i + 3, :], in_=xin[384 * i:384 * (i + 1), :].rearrange("(t p) d -> p t d", p=128)),
            writes=[("x", 3 * i), ("x", 3 * i + 1), ("x", 3 * i + 2)])
    R.dma("sp", "cst0", lambda e: e.dma_start(out=ccs[:], in_=ccT_d), writes=["ccs"])
    R.dma("sp", "cst1", lambda e: e.dma_start(out=gn[:], in_=gn_d), writes=["gn"])
    R.dma("sp", "cst2", lambda e: e.dma_start(out=bmodT[:], in_=bmodT_d), writes=["bmodT"])
    R.dma("sp", "cst3", lambda e: e.dma_start(out=convp[:], in_=convp_d), writes=["convp"])
    R.dma("sp", "cst4", lambda e: e.dma_start(out=glat[:], in_=glat_d), writes=["glat"])
    R.dma("pool", "cstA", lambda e: e.dma_start(out=ropeA[:], in_=ropeA_d), writes=["ropeA"])
    R.dma("pool", "cstM", lambda e: e.dma_start(out=ropeM[:], in_=ropeM_d), writes=["ropeM"])
    R.op("pool", lambda e: e.memset(identf[:], 0.0), writes=["identf"])
    R.op("pool", lambda e: e.affine_select(out=identf[:], in_=identf[:], pattern=[[-1, 128]],
                                           compare_op=ALU.not_equal, fill=1.0, base=0, channel_multiplier=1),
         reads=["identf"], writes=["identf"])
    R.op("pool", lambda e: e.tensor_copy(out=ident[:], in_=identf[:]), reads=["identf"], writes=["ident"])
    R.op("pool", lambda e: e.memset(ones_bf[:], 1.0), writes=["ones"])
    R.op("pool", lambda e: e.memset(epsb[:], EPS), writes=["eps"])
    R.op("act", lambda e: e.activation(out=s2[:], in_=ccs[:], func=AF.Silu), reads=["ccs"], writes=["s2"])

    def rstd_from_ss(n, scale, key):
        R.op("act", lambda e: e.activation(out=st_r[:, 0:n], in_=st_ss[:, 0:n], func=AF.Sqrt, scale=scale, bias=epsb[:]),
             reads=[("ss", key), "eps"], writes=[("sr", key)])
        R.op("dve", lambda e: e.reciprocal(out=st_r[:, 0:n], in_=st_r[:, 0:n]), reads=[("sr", key)], writes=[("sr", key)])

    def modulation(l):
        mark = Ar.pos
        ring = [anew(BF16, [KC, 512]) for _ in range(4)]
        for c in range(18):
            s = c % 4
            R.dma("pool", "wm%d" % s, lambda e, c=c, s=s: e.dma_start(out=ring[s][:], in_=wmod_d[l, c]),
                  writes=[("wmring", s)])

            def mm(e, c=c, s=s):
                inst = None
                for fc in range(4):
                    col = (c * 4 + fc) * 2
                    for kc in range(KC):
                        inst = e.matmul(ps[:, 0, col:col + 2], ring[s][:, kc, fc * 128:(fc + 1) * 128], s2[:, kc, :],
                                        start=(kc == 0), stop=(kc == KC - 1))
                return inst
            R.op("pe", mm, reads=[("wmring", s), "s2"], writes=[("ps", 0)])
        R.op("dve", lambda e: e.tensor_tensor(
            out=modT[:], in0=ps[:, 0, 0:144].rearrange("p (c r) -> p c r", r=2),
            in1=bmodT[:, l, :].unsqueeze(2).broadcast_to([128, 72, 2]), op=ALU.add),
            reads=[("ps", 0), "bmodT"], writes=["modT"])
        R.barrier()
        Ar.pos = mark

    def sub_modulation(l, j, gate_mult):
        c_shift, c_scale, c_gate = (3 * j) * 8, (3 * j + 1) * 8, (3 * j + 2) * 8
        R.op("dve", lambda e: e.tensor_scalar(out=Amod[:], in0=modT[:, c_scale:c_scale + 8, :], scalar1=1.0, scalar2=None,
                                              op0=ALU.add), reads=["modT"], writes=["Amod"])
        R.op("dve", lambda e: e.tensor_tensor(out=Amod[:], in0=Amod[:], in1=gn[:, l, j, :].unsqueeze(2).broadcast_to([128, KC, 2]),
                                              op=ALU.mult), reads=["Amod", "gn"], writes=["Amod"])
        R.op("dve", lambda e: e.tensor_copy(out=Bmod[:], in_=modT[:, c_shift:c_shift + 8, :]), reads=["modT"], writes=["Bmod"])
        mark = Ar.pos
        rep = anew(BF16, [KC, 128])
        for r in range(2):
            R.op("dve", lambda e, r=r: e.tensor_scalar(
                out=rep[:], in0=modT[:, c_gate:c_gate + 8, r:r + 1].broadcast_to([128, KC, 128]),
                scalar1=gate_mult, scalar2=None, op0=ALU.mult), reads=["modT"], writes=["rep"])

            def tr(e):
                inst = None
                for kc in range(KC):
                    inst = e.transpose(psb(1)[:, kc * 128:(kc + 1) * 128], rep[:, kc, :], ident[:])
                return inst
            R.op("pe", tr, reads=["rep", "ident"], writes=[("ps", 1)])
            R.op("act", lambda e, r=r: e.copy(out=gate_bc[:, r, :], in_=psb(1)[:, 0:1024]), reads=[("ps", 1)],
                 writes=[("gate", r)])
        R.barrier()
        Ar.pos = mark

    def emit_hT(t, dst, dst_key, xn, xn_key, psbank, stat_col):
        r = 0 if t < NLT else 1
        R.op("act", lambda e: e.activation(out=xn[:], in_=xs[:, t, :], func=AF.Square, accum_out=st_ss[:, stat_col:stat_col + 1]),
             reads=[("x", t)], writes=[xn_key, ("ss", "h%d" % stat_col)])
        R.op("act", lambda e: e.activation(out=st_r[:, stat_col:stat_col + 1], in_=st_ss[:, stat_col:stat_col + 1], func=AF.Sqrt,
                                           scale=1.0 / D, bias=epsb[:]),
             reads=[("ss", "h%d" % stat_col), "eps"], writes=[("sr", "h%d" % stat_col)], sync_self=True)
        R.op("dve", lambda e: e.reciprocal(out=st_r[:, stat_col:stat_col + 1], in_=st_r[:, stat_col:stat_col + 1]),
             reads=[("sr", "h%d" % stat_col)], writes=[("sr", "h%d" % stat_col)])
        R.op("dve", lambda e: e.tensor_scalar(out=xn[:], in0=xs[:, t, :], scalar1=st_r[:, stat_col:stat_col + 1], scalar2=None,
                                              op0=ALU.mult), reads=[("x", t), ("sr", "h%d" % stat_col), xn_key], writes=[xn_key], sync_self=True)

        def tr(e):
            inst = None
            for kc in range(KC):
                inst = e.transpose(psb(psbank)[:, kc * 128:(kc + 1) * 128], xn[:, kc * 128:(kc + 1) * 128], ident[:])
            return inst
        R.op("pe", tr, reads=[xn_key, "ident"], writes=[("ps", psbank)])
        for kc in range(KC):
            R.op("act", lambda e, kc=kc: e.activation(out=dst[:, kc, :], in_=psb(psbank)[:, kc * 128:(kc + 1) * 128],
                                                      func=AF.Identity, scale=Amod[:, kc, r:r + 1], bias=Bmod[:, kc, r:r + 1]),
                 reads=[("ps", psbank), "Amod", "Bmod"], writes=[dst_key])

    def ffn(l, f, j, tiles):
        ntl = len(tiles)
        sub_modulation(l, j, 0.5)
        mark = Ar.pos
        hT = anew(BF16, [KC, T])
        aT = anew(BF16, [4, T])
        gu_ring = [anew(BF16, [2, KC, 128]) for _ in range(3)]
        d_ring = [anew(BF16, [D]) for _ in range(8)]
        sil = [anew(BF16, [512]) for _ in range(2)]
        xn = [anew(BF16, [D]) for _ in range(2)]
        for i, t in enumerate(tiles):
            emit_hT(t, hT[:, :, t * 128:(t + 1) * 128], ("hT", t), xn[i % 2], ("xn", i % 2), i % 2, i % 2)
        tgs = []
        i = 0
        while i < ntl:
            n = min(4, ntl - i)
            tgs.append((tiles[i], n))
            i += n
        cbase = 0
        gu_cnt = 0
        d_cnt = 0
        ps_g = [2, 3]
        ps_u = [4, 5]
        gu_i = 0
        y_i = 0
        for gsz in FF_GROUPS:
            for ci in range(gsz):
                c = cbase + ci
                s = gu_cnt % 3
                R.dma("pool", "gu%d" % s, lambda e, c=c, s=s: e.dma_start(out=gu_ring[s][:], in_=wgu_d[l, f, c]),
                      writes=[("guring", s)])
                sd = d_cnt % 8
                R.dma("pool", "wd%d" % sd, lambda e, c=c, sd=sd: e.dma_start(out=d_ring[sd][:], in_=wd_d[l, f, c * 128:(c + 1) * 128, :]),
                      writes=[("dring", sd)])
                for (t0, n) in tgs:
                    ntok = n * 128
                    tok0 = t0 * 128
                    bg = ps_g[gu_i % 2]
                    bu = ps_u[gu_i % 2]
                    sl = sil[gu_i % 2]
                    gu_i += 1
                    hkeys = [("hT", t0 + k) for k in range(n)]

                    def mm(e, s=s, bg=bg, bu=bu, tok0=tok0, ntok=ntok):
                        inst = None
                        for which, bank in ((0, bg), (1, bu)):
                            for kc in range(KC):
                                inst = e.matmul(ps[:, bank, 0:ntok], gu_ring[s][:, which, kc, :], hT[:, kc, tok0:tok0 + ntok],
                                                start=(kc == 0), stop=(kc == KC - 1))
                        return inst
                    R.op("pe", mm, reads=[("guring", s)] + hkeys, writes=[("ps", bg), ("ps", bu)])
                    R.op("act", lambda e, bg=bg, sl=sl, ntok=ntok: e.activation(out=sl[:, 0:ntok], in_=ps[:, bg, 0:ntok], func=AF.Silu),
                         reads=[("ps", bg)], writes=[("sil", id(sl))])
                    R.op("dve", lambda e, bu=bu, sl=sl, ci=ci, tok0=tok0, ntok=ntok: e.tensor_tensor(
                        out=aT[:, ci, tok0:tok0 + ntok], in0=ps[:, bu, 0:ntok], in1=sl[:, 0:ntok], op=ALU.mult),
                        reads=[("ps", bu), ("sil", id(sl))], writes=[("aT", ci, t0 + k) for k in range(n)])
                gu_cnt += 1
                d_cnt += 1
            dslots = [(d_cnt - gsz + ci) % 8 for ci in range(gsz)]
            for t in tiles:
                r = 0 if t < NLT else 1
                b0 = 6 if (y_i % 2 == 0) else 0
                y_i += 1

                def mmd(e, t=t, b0=b0, dslots=dslots, gsz=gsz):
                    inst = None
                    for hd in range(2):
                        for ci in range(gsz):
                            inst = e.matmul(ps[:, b0 + hd, :], aT[:, ci, t * 128:(t + 1) * 128],
                                            d_ring[dslots[ci]][:, hd * 512:(hd + 1) * 512],
                                            start=(ci == 0), stop=(ci == gsz - 1))
                    return inst
                R.op("pe", mmd, reads=[("aT", ci, t) for ci in range(gsz)] + [("dring", sd) for sd in dslots],
                     writes=[("ps", b0), ("ps", b0 + 1)])
                yv = ps[:, b0:b0 + 2, :]
                R.op("dve", lambda e, yv=yv, r=r: e.tensor_tensor(out=yv, in0=yv, in1=gate_bc[:, r, :].rearrange("p (a b) -> p a b", a=2),
                                                                  op=ALU.mult),
                     reads=[("ps", b0), ("ps", b0 + 1), ("gate", r)], writes=[("ps", b0), ("ps", b0 + 1)])
                R.op("dve", lambda e, yv=yv, t=t: e.tensor_tensor(out=xs[:, t, :].rearrange("p (a b) -> p a b", a=2),
                                                                  in0=xs[:, t, :].rearrange("p (a b) -> p a b", a=2), in1=yv, op=ALU.add),
                     reads=[("ps", b0), ("ps", b0 + 1), ("x", t)], writes=[("x", t)])
            cbase += gsz
        R.barrier()
        Ar.pos = mark

    def rope_apply(eng, src, dst, cs, t, H, dim, key_r, key_w, tmp1, tmp2, k1, k2):
        q = dim // 4
        cosb = cs[:, 0, t, :].unsqueeze(1).broadcast_to([128, H, dim])
        R.op(eng, lambda e: e.tensor_tensor(out=tmp1, in0=src, in1=cosb, op=ALU.mult), reads=key_r + ["rope"], writes=[k1])
        s4 = src.rearrange("p h (a b c) -> p h a b c", a=2, b=2)
        t4 = tmp2.rearrange("p h (a b c) -> p h a b c", a=2, b=2)
        sn = cs[:, 1, t, :].rearrange("p (a b c) -> p a b c", a=2, b=2)
        for bsel in range(2):
            R.op(eng, lambda e, bsel=bsel: e.tensor_tensor(
                out=t4[:, :, :, bsel, :], in0=s4[:, :, :, 1 - bsel, :],
                in1=sn[:, :, bsel, :].unsqueeze(1).broadcast_to([128, H, 2, q]), op=ALU.mult),
                reads=key_r + ["rope"] + ([k2] if bsel == 1 else []), writes=[k2])
        R.op(eng, lambda e: e.tensor_tensor(out=dst, in0=tmp1, in1=tmp2, op=ALU.add),
             reads=[k1, k2], writes=key_w)

    def sumsq(src, H, dd, col0, scr, key_r, key_w):
        sv = scr[:, 0:H * dd].rearrange("p (h d) -> p h d", h=H)
        R.op("dve", lambda e: e.tensor_tensor(out=sv, in0=src, in1=src, op=ALU.mult), reads=key_r, writes=["sqscr"])
        R.op("dve", lambda e: e.tensor_reduce(out=st_ss[:, col0:col0 + H], in_=sv, axis=AX.X, op=ALU.add),
             reads=["sqscr"], writes=key_w)

    def mixer(l, need_ctx):
        sub_modulation(l, 1, 1.0)
        mark0 = Ar.pos
        kT_g = anew(BF16, [T])
        V_g = anew(BF16, [NT, 128])
        kT_m = anew(BF16, [6, T])
        V_m = anew(BF16, [NT, 384])
        NCV = 2050 + 258
        bT = anew(BF16, [2, NCV])
        gsc = anew(F32, [NGAIN])
        markU = Ar.pos
        uT = anew(BF16, [2, NCV])
        R.dma("sp", "gains", lambda e: e.dma_start(out=gains[:], in_=gains_d[l].partition_broadcast(128)), writes=["gains"])
        R.op("dve", lambda e: e.tensor_copy(out=gsc[:], in_=gains[:]), reads=["gains"], writes=["gsc"])
        R.op("dve", lambda e: e.tensor_scalar(out=gsc[:, 0:64], in0=gains[:, 0:64], scalar1=64.0 ** -0.5, scalar2=None, op0=ALU.mult),
             reads=["gains", "gsc"], writes=["gsc"])
        R.op("dve", lambda e: e.tensor_scalar(out=gsc[:, 128:192], in0=gains[:, 128:192], scalar1=96.0 ** -0.5, scalar2=None, op0=ALU.mult),
             reads=["gains", "gsc"], writes=["gsc"])
        R.op("dve", lambda e: e.tensor_scalar(out=gsc[:, 256:288], in0=gains[:, 256:288], scalar1=96.0 ** -0.5, scalar2=None, op0=ALU.mult),
             reads=["gains", "gsc"], writes=["gsc"])
        g_q, g_k, g_qn, g_kn, g_qr, g_kr = (gsc[:, 0:64], gsc[:, 64:128], gsc[:, 128:192], gsc[:, 192:256],
                                            gsc[:, 256:288], gsc[:, 288:320])
        R.op("pool", lambda e: e.memset(uT[:], 0.0), writes=["uT"])

        markK = Ar.pos
        wK = anew(BF16, [KC, 1312])
        wukv = anew(BF16, [2, 768])
        hTr = [anew(BF16, [KC, 128]) for _ in range(2)]
        xn = [anew(BF16, [D]) for _ in range(2)]
        kraws = [anew(F32, [544]) for _ in range(2)]
        kvraw = anew(F32, [768])
        scr = anew(F32, [768])
        tA = anew(F32, [384])
        tB = anew(F32, [384])
        tC = anew(F32, [384])
        kf = anew(BF16, [128])
        ckvb = anew(BF16, [256])
        ckvT = anew(BF16, [2, 128])
        kfull = anew(BF16, [6, 96])
        cgt = anew(F32, [2, 128])
        R.dma("pool", "wK", lambda e: e.dma_start(out=wK[:], in_=wink_d[l]), writes=["wK"])
        R.dma("pool", "wukv", lambda e: e.dma_start(out=wukv[:], in_=wukv_d[l]), writes=["wukv"])
        for c in range(2):
            R.op("dve", lambda e, c=c: e.tensor_scalar(out=wukv[:, c, :], in0=wukv[:, c, :], scalar1=glat[:, l, 3 + c:4 + c], scalar2=None,
                                                       op0=ALU.mult), reads=["wukv", "glat"], writes=["wukv"])
        def front(t):
            lat = t < NLT
            kraw = kraws[t % 2]
            krk = ("kraw", t % 2)
            h = hTr[t % 2]
            hk = ("hTr", t % 2)
            emit_hT(t, h, hk, xn[t % 2], ("xn", t % 2), 1, t % 2)

            def mmA(e, h=h):
                inst = None
                for kc in range(KC):
                    inst = e.matmul(ps[:, 0, :], h[:, kc, :], wK[:, kc, 0:512], start=(kc == 0), stop=(kc == KC - 1))
                for kc in range(KC):
                    inst = e.matmul(ps[:, 3, 256:288], h[:, kc, :], wK[:, kc, 512:544], start=(kc == 0), stop=(kc == KC - 1))
                return inst
            R.op("pe", mmA, reads=[hk, "wK"], writes=[("ps", 0), ("ps", 3)])

            def mmC(e, h=h):
                inst = None
                for cc in range(6):
                    bank, off = (2, cc * 128) if cc < 4 else (3, (cc - 4) * 128)
                    for kc in range(KC):
                        inst = e.matmul(ps[:, bank, off:off + 128], wK[:, kc, 544 + cc * 128:544 + (cc + 1) * 128], h[:, kc, :],
                                        start=(kc == 0), stop=(kc == KC - 1))
                return inst
            R.op("pe", mmC, reads=[hk, "wK"], writes=[("ps", 2), ("ps", 3)])
            R.op("act", lambda e: e.copy(out=kraw[:, 0:512], in_=ps[:, 0, :]), reads=[("ps", 0)], writes=[krk])
            R.op("act", lambda e: e.copy(out=kraw[:, 512:544], in_=ps[:, 3, 256:288]), reads=[("ps", 3), krk], writes=[krk])
            R.op("act", lambda e, t=t: e.copy(out=V_g[:, t, :], in_=kraw[:, 128:256]), reads=[krk], writes=[("Vg", t)])
            pos = (1 + t * 128) if lat else (2051 + (t - NLT) * 128)
            R.op("act", lambda e: e.copy(out=cgt[:].rearrange("p a b -> p (a b)"), in_=ps[:, 2, 256:512]), reads=[("ps", 2)], writes=["cgt"])
            R.op("dve", lambda e, pos=pos: e.tensor_tensor(out=uT[:, :, pos:pos + 128], in0=ps[:, 2, 0:256].rearrange("p (a b) -> p a b", a=2),
                                                           in1=cgt[:], op=ALU.mult), reads=[("ps", 2), "cgt", "uT"], writes=["uT"])
            R.op("act", lambda e, pos=pos: e.copy(out=bT[:, :, pos:pos + 128], in_=ps[:, 3, 0:256].rearrange("p (a b) -> p a b", a=2)),
                 reads=[("ps", 3)], writes=["bT"])

        def back(t):
            lat = t < NLT
            kraw = kraws[t % 2]
            krk = ("kraw", t % 2)
            R.op("act", lambda e: e.copy(out=ckvb[:], in_=kraw[:, 256:512]), reads=[krk], writes=["ckvb"])
            k3 = kraw[:, 0:128].rearrange("p (h d) -> p h d", h=2)
            sumsq(k3, 2, 64, 8, scr, [krk], [("ss", "k")])
            sumsq(kraw[:, 256:512].unsqueeze(1), 1, 256, 10, scr, [krk], [("ss", "ckv")])
            sumsq(kraw[:, 512:544].unsqueeze(1), 1, 32, 11, scr, [krk], [("ss", "kr")])
            R.op("act", lambda e: e.activation(out=st_r[:, 8:10], in_=st_ss[:, 8:10], func=AF.Sqrt, scale=1.0 / 64, bias=epsb[:]),
                 reads=[("ss", "k"), "eps"], writes=[("sr", "k")])
            R.op("act", lambda e: e.activation(out=st_r[:, 10:11], in_=st_ss[:, 10:11], func=AF.Sqrt, scale=1.0 / 256, bias=epsb[:]),
                 reads=[("ss", "ckv"), "eps"], writes=[("sr", "ckv")])
            R.op("act", lambda e: e.activation(out=st_r[:, 11:12], in_=st_ss[:, 11:12], func=AF.Sqrt, scale=1.0 / 32, bias=epsb[:]),
                 reads=[("ss", "kr"), "eps"], writes=[("sr", "kr")])
            R.op("dve", lambda e: e.reciprocal(out=st_r[:, 8:12], in_=st_r[:, 8:12]),
                 reads=[("sr", "k"), ("sr", "ckv"), ("sr", "kr")], writes=[("sr", "k"), ("sr", "ckv"), ("sr", "kr")])
            kn = tA[:, 0:128].rearrange("p (h d) -> p h d", h=2)
            for hh in range(2):
                R.op("dve", lambda e, hh=hh: e.scalar_tensor_tensor(out=kn[:, hh, :], in0=k3[:, hh, :], scalar=st_r[:, 8 + hh:9 + hh], in1=g_k,
                                                                    op0=ALU.mult, op1=ALU.mult),
                     reads=[krk, ("sr", "k"), "gsc"], writes=[("kn", hh)], sync_self=True)
            kf3 = kf[:].rearrange("p (h d) -> p h d", h=2)
            if lat:
                rope_apply("dve", kn, kf3, ropeA, t, 2, 64, [("kn", 0), ("kn", 1)], ["kf"],
                           tB[:, 0:128].rearrange("p (h d) -> p h d", h=2), tC[:, 0:128].rearrange("p (h d) -> p h d", h=2), ("tB", "k"), ("tC", "k"))
            else:
                R.op("dve", lambda e: e.tensor_copy(out=kf3, in_=kn), reads=[("kn", 0), ("kn", 1)], writes=["kf"])
            R.op("pe", lambda e: e.transpose(psb(7)[:, 0:128], kf[:], ident[:]), reads=["kf", "ident"], writes=[("ps", 7)])
            R.op("act", lambda e, t=t: e.copy(out=kT_g[:, t * 128:(t + 1) * 128], in_=psb(7)[:, 0:128]), reads=[("ps", 7)],
                 writes=[("kTg", t)])
            def trc(e):
                inst = None
                for c in range(2):
                    inst = e.transpose(psb(7)[:, 128 + c * 128:256 + c * 128], ckvb[:, c * 128:(c + 1) * 128], ident[:])
                return inst
            R.op("pe", trc, reads=["ckvb", "ident"], writes=[("ps", 7)])
            R.op("act", lambda e: e.copy(out=ckvT[:].rearrange("p a b -> p (a b)"), in_=psb(7)[:, 128:384]), reads=[("ps", 7)],
                 writes=["ckvT"])

            def mmkv(e):
                inst = None
                for (bank, c0, n) in ((4, 0, 512), (5, 512, 256)):
                    for c in range(2):
                        inst = e.matmul(ps[:, bank, 0:n], ckvT[:, c, :], wukv[:, c, c0:c0 + n], start=(c == 0), stop=(c == 1))
                return inst
            R.op("pe", mmkv, reads=["ckvT", "wukv"], writes=[("ps", 4), ("ps", 5)])
            R.op("act", lambda e: e.copy(out=kvraw[:, 0:512], in_=ps[:, 4, :]), reads=[("ps", 4)], writes=["kvraw"])
            R.op("act", lambda e: e.copy(out=kvraw[:, 512:768], in_=ps[:, 5, 0:256]), reads=[("ps", 5), "kvraw"], writes=["kvraw"])
            kv3 = kvraw[:].rearrange("p (h d) -> p h d", h=6)
            sumsq(kv3[:, :, 0:64], 6, 64, 16, scr, ["kvraw"], [("ss", "kn")])
            R.op("dve", lambda e: e.tensor_tensor(out=st_ss[:, 12:13], in0=st_r[:, 10:11], in1=st_r[:, 10:11], op=ALU.mult),
                 reads=[("sr", "ckv")], writes=[("ss", "b2")])
            R.op("dve", lambda e: e.tensor_scalar(out=st_ss[:, 16:22], in0=st_ss[:, 16:22], scalar1=st_ss[:, 12:13], scalar2=None, op0=ALU.mult),
                 reads=[("ss", "kn"), ("ss", "b2")], writes=[("ss", "kn")], sync_self=True)
            R.op("act", lambda e: e.activation(out=st_r[:, 16:22], in_=st_ss[:, 16:22], func=AF.Sqrt, scale=1.0 / 64, bias=epsb[:]),
                 reads=[("ss", "kn"), "eps"], writes=[("sr", "kn")])
            R.op("dve", lambda e: e.reciprocal(out=st_r[:, 16:22], in_=st_r[:, 16:22]), reads=[("sr", "kn")], writes=[("sr", "kn")])
            R.op("dve", lambda e: e.tensor_scalar(out=st_r[:, 16:22], in0=st_r[:, 16:22], scalar1=st_r[:, 10:11], scalar2=None, op0=ALU.mult),
                 reads=[("sr", "kn"), ("sr", "ckv")], writes=[("sr", "kn")], sync_self=True)
            for hh in range(6):
                R.op("dve", lambda e, hh=hh: e.scalar_tensor_tensor(out=kfull[:, hh, 0:64], in0=kv3[:, hh, 0:64], scalar=st_r[:, 16 + hh:17 + hh],
                                                                    in1=g_kn, op0=ALU.mult, op1=ALU.mult),
                     reads=["kvraw", ("sr", "kn"), "gsc"], writes=[("kfull", hh)], sync_self=(hh == 0))
            R.op("dve", lambda e, t=t: e.tensor_scalar(out=V_m[:, t, :].rearrange("p (h d) -> p h d", h=6), in0=kv3[:, :, 64:128],
                                                       scalar1=st_r[:, 10:11], scalar2=None, op0=ALU.mult),
                 reads=["kvraw", ("sr", "ckv")], writes=[("Vm", t)])
            krn = tA[:, 128:160].unsqueeze(1)
            R.op("dve", lambda e: e.scalar_tensor_tensor(out=tA[:, 128:160], in0=kraw[:, 512:544], scalar=st_r[:, 11:12], in1=g_kr,
                                                         op0=ALU.mult, op1=ALU.mult), reads=[krk, ("sr", "kr"), "gsc"], writes=["krn"])
            krf = tA[:, 160:192].unsqueeze(1)
            if lat:
                rope_apply("dve", krn, krf, ropeM, t, 1, 32, ["krn"], ["krf"], tB[:, 128:160].unsqueeze(1), tC[:, 128:160].unsqueeze(1), ("tB", "kr"), ("tC", "kr"))
                src_kr = krf
                krkey = "krf"
            else:
                src_kr = krn
                krkey = "krn"
            R.op("dve", lambda e, src_kr=src_kr: e.tensor_copy(out=kfull[:, :, 64:96], in_=src_kr.broadcast_to([128, 6, 32])),
                 reads=[krkey], writes=[("kfull", "r")])

            def trk(e):
                inst = None
                for hh in range(6):
                    inst = e.transpose(psb(6)[0:96, hh * 128:(hh + 1) * 128], kfull[:, hh, :], ident[:])
                return inst
            R.op("pe", trk, reads=[("kfull", hh) for hh in range(6)] + [("kfull", "r"), "ident"], writes=[("ps", 6)])
            R.op("act", lambda e, t=t: e.copy(out=kT_m[0:96, :, t * 128:(t + 1) * 128],
                                              in_=psb(6)[0:96, 0:768].rearrange("p (h n) -> p h n", h=6)),
                 reads=[("ps", 6)], writes=[("kTm", t)])
        streams = []
        for t in range(NT):
            R.capture_start()
            front(t)
            back(t)
            streams.append(R.capture_end())
        R.replay_zipped(streams, 0.5)
        cvt = scr[:, 0:512]
        for ch in range(2):
            segs = [(1 + 512 * i, 512) for i in range(4)] + [(2051, 256)]
            for (p0, n) in segs:
                R.op("dve", lambda e, ch=ch, p0=p0, n=n: e.tensor_scalar(out=cvt[:, 0:n], in0=uT[:, ch, p0:p0 + n], scalar1=convp[:, l, ch, 1:2],
                                                                         scalar2=convp[:, l, ch, 3:4], op0=ALU.mult, op1=ALU.add),
                     reads=["uT", "convp"], writes=["cvt"])
                R.op("dve", lambda e, ch=ch, p0=p0, n=n: e.scalar_tensor_tensor(out=cvt[:, 0:n], in0=uT[:, ch, p0 - 1:p0 - 1 + n],
                                                                                scalar=convp[:, l, ch, 0:1], in1=cvt[:, 0:n], op0=ALU.mult, op1=ALU.add),
                     reads=["uT", "convp", "cvt"], writes=["cvt"])
                R.op("dve", lambda e, ch=ch, p0=p0, n=n: e.scalar_tensor_tensor(out=cvt[:, 0:n], in0=uT[:, ch, p0 + 1:p0 + 1 + n],
                                                                                scalar=convp[:, l, ch, 2:3], in1=cvt[:, 0:n], op0=ALU.mult, op1=ALU.add),
                     reads=["uT", "convp", "cvt"], writes=["cvt"])
                R.op("dve", lambda e, ch=ch, p0=p0, n=n: e.tensor_tensor(out=bT[:, ch, p0:p0 + n], in0=bT[:, ch, p0:p0 + n], in1=cvt[:, 0:n],
                                                                         op=ALU.mult), reads=["bT", "cvt"], writes=["bT"])
        R.barrier()
        Ar.pos = markU

        wQ = anew(BF16, [KC, 768])
        wuq = anew(BF16, [3, 576])
        wo_b = anew(BF16, [KC, 512])
        hTq = [anew(BF16, [KC, 128])] * 2
        xnq = [anew(BF16, [D])] * 2
        qraw = anew(F32, [768])
        qmraw = qraw[:, 0:576]
        scrq = anew(F32, [576])
        tAq = anew(F32, [384])
        tBq = anew(F32, [384])
        tCq = anew(F32, [384])
        qf = anew(BF16, [384])
        cqb = anew(BF16, [384])
        cqT = anew(BF16, [3, 128])
        qmfull = anew(BF16, [6, 96])
        qT_g = anew(BF16, [3, 512])
        qT_m = anew(BF16, [6, 512])
        PT2 = [anew(BF16, [2, 512]) for _ in range(2)]
        mixT = anew(BF16, [6, 512])
        rden = scrq[:, 0:512]
        R.dma("pool", "wQ", lambda e: e.dma_start(out=wQ[:], in_=winq_d[l]), writes=["wQ"])
        R.dma("pool", "wuq", lambda e: e.dma_start(out=wuq[:], in_=wuq_d[l]), writes=["wuq"])
        for c in range(3):
            R.op("dve", lambda e, c=c: e.tensor_scalar(out=wuq[:, c, :], in0=wuq[:, c, :], scalar1=glat[:, l, c:c + 1], scalar2=None,
                                                       op0=ALU.mult), reads=["wuq", "glat"], writes=["wuq"])
        groups = [(4 * g, 4) for g in range(4)]
        if need_ctx:
            groups.append((16, 2))
        pt_i = 0
        st_i = 0
        o_i = 0
        for (t0, ntile) in groups:
            lat = t0 < NLT
            nq = ntile * 128
            r = 0 if lat else 1
            key_tiles = list(range(NT)) if lat else [16, 17]
            qstreams = []
            for lt in range(ntile):
                R.capture_start()
                t = t0 + lt
                h = hTq[0]
                hk = ("hTr", 0)
                emit_hT(t, h, hk, xnq[0], ("xn", 0), 6, t % 2)

                def mmQ(e, h=h):
                    inst = None
                    for (bank, c0) in ((4, 0), (5, 384)):
                        for kc in range(KC):
                            inst = e.matmul(ps[:, bank, 0:384], h[:, kc, :], wQ[:, kc, c0:c0 + 384], start=(kc == 0), stop=(kc == KC - 1))
                    return inst
                R.op("pe", mmQ, reads=[hk, "wQ"], writes=[("ps", 4), ("ps", 5)])
                R.op("act", lambda e: e.copy(out=qraw[:, 0:384], in_=ps[:, 4, 0:384]), reads=[("ps", 4)], writes=["qraw"])
                R.op("act", lambda e: e.copy(out=qraw[:, 384:768], in_=ps[:, 5, 0:384]), reads=[("ps", 5), "qraw"], writes=["qraw"])
                R.op("act", lambda e: e.copy(out=cqb[:], in_=qraw[:, 384:768]), reads=["qraw"], writes=["cqb"])
                q3 = qraw[:, 0:384].rearrange("p (h d) -> p h d", h=6)
                sumsq(q3, 6, 64, 24, scrq, ["qraw"], [("ss", "q")])
                sumsq(qraw[:, 384:768].unsqueeze(1), 1, 384, 30, scrq, ["qraw"], [("ss", "cq")])
                R.op("act", lambda e: e.activation(out=st_r[:, 24:30], in_=st_ss[:, 24:30], func=AF.Sqrt, scale=1.0 / 64, bias=epsb[:]),
                     reads=[("ss", "q"), "eps"], writes=[("sr", "q")])
                R.op("act", lambda e: e.activation(out=st_r[:, 30:31], in_=st_ss[:, 30:31], func=AF.Sqrt, scale=1.0 / 384, bias=epsb[:]),
                     reads=[("ss", "cq"), "eps"], writes=[("sr", "cq")])
                R.op("dve", lambda e: e.reciprocal(out=st_r[:, 24:31], in_=st_r[:, 24:31]), reads=[("sr", "q"), ("sr", "cq")],
                     writes=[("sr", "q"), ("sr", "cq")])
                qn = tAq[:].rearrange("p (h d) -> p h d", h=6)
                for hh in range(6):
                    R.op("dve", lambda e, hh=hh: e.scalar_tensor_tensor(out=qn[:, hh, :], in0=q3[:, hh, :], scalar=st_r[:, 24 + hh:25 + hh], in1=g_q,
                                                                        op0=ALU.mult, op1=ALU.mult),
                         reads=["qraw", ("sr", "q"), "gsc"], writes=[("qn", hh)], sync_self=(hh == 0))
                qf3 = qf[:].rearrange("p (h d) -> p h d", h=6)
                qnk = [("qn", hh) for hh in range(6)]
                if lat:
                    rope_apply("dve", qn, qf3, ropeA, t, 6, 64, qnk, ["qf"], tBq[:].rearrange("p (h d) -> p h d", h=6),
                               tCq[:].rearrange("p (h d) -> p h d", h=6), "tBq", "tCq")
                else:
                    R.op("dve", lambda e: e.tensor_copy(out=qf3, in_=qn), reads=qnk, writes=["qf"])

                def trq(e):
                    inst = None
                    for pr in range(3):
                        inst = e.transpose(psb(7)[:, pr * 128:(pr + 1) * 128], qf[:, pr * 128:(pr + 1) * 128], ident[:])
                    for c in range(3):
                        inst = e.transpose(psb(7)[:, 384 + c * 128:512 + c * 128], cqb[:, c * 128:(c + 1) * 128], ident[:])
                    return inst
                R.op("pe", trq, reads=["qf", "cqb", "ident"], writes=[("ps", 7)])
                R.op("act", lambda e, lt=lt: e.copy(out=qT_g[:, :, lt * 128:(lt + 1) * 128], in_=psb(7)[:, 0:384].rearrange("p (a b) -> p a b", a=3)),
                     reads=[("ps", 7)], writes=[("qTg", lt)])
                R.op("act", lambda e: e.copy(out=cqT[:].rearrange("p a b -> p (a b)"), in_=psb(7)[:, 384:768]), reads=[("ps", 7)], writes=["cqT"])

                def mmuq(e):
                    inst = None
                    for (bank, c0) in ((4, 0), (5, 288)):
                        for c in range(3):
                            inst = e.matmul(ps[:, bank, 0:288], cqT[:, c, :], wuq[:, c, c0:c0 + 288], start=(c == 0), stop=(c == 2))
                    return inst
                R.op("pe", mmuq, reads=["cqT", "wuq"], writes=[("ps", 4), ("ps", 5)])
                R.op("act", lambda e: e.copy(out=qmraw[:, 0:288], in_=ps[:, 4, 0:288]), reads=[("ps", 4)], writes=["qraw"])
                R.op("act", lambda e: e.copy(out=qmraw[:, 288:576], in_=ps[:, 5, 0:288]), reads=[("ps", 5), "qraw"], writes=["qraw"])
                qm3 = qmraw[:].rearrange("p (h d) -> p h d", h=6)
                sumsq(qm3[:, :, 0:64], 6, 64, 32, scrq, ["qraw"], [("ss", "qmn")])
                sumsq(qm3[:, :, 64:96], 6, 32, 38, scrq, ["qraw"], [("ss", "qmr")])
                R.op("dve", lambda e: e.tensor_tensor(out=st_ss[:, 31:32], in0=st_r[:, 30:31], in1=st_r[:, 30:31], op=ALU.mult),
                     reads=[("sr", "cq")], writes=[("ss", "a2")])
                R.op("dve", lambda e: e.tensor_scalar(out=st_ss[:, 32:44], in0=st_ss[:, 32:44], scalar1=st_ss[:, 31:32], scalar2=None, op0=ALU.mult),
                     reads=[("ss", "qmn"), ("ss", "qmr"), ("ss", "a2")], writes=[("ss", "qmn"), ("ss", "qmr")], sync_self=True)
                R.op("act", lambda e: e.activation(out=st_r[:, 32:38], in_=st_ss[:, 32:38], func=AF.Sqrt, scale=1.0 / 64, bias=epsb[:]),
                     reads=[("ss", "qmn"), "eps"], writes=[("sr", "qmn")])
                R.op("act", lambda e: e.activation(out=st_r[:, 38:44], in_=st_ss[:, 38:44], func=AF.Sqrt, scale=1.0 / 32, bias=epsb[:]),
                     reads=[("ss", "qmr"), "eps"], writes=[("sr", "qmr")])
                R.op("dve", lambda e: e.reciprocal(out=st_r[:, 32:44], in_=st_r[:, 32:44]), reads=[("sr", "qmn"), ("sr", "qmr")],
                     writes=[("sr", "qmn"), ("sr", "qmr")])
                R.op("dve", lambda e: e.tensor_scalar(out=st_r[:, 32:44], in0=st_r[:, 32:44], scalar1=st_r[:, 30:31], scalar2=None, op0=ALU.mult),
                     reads=[("sr", "qmn"), ("sr", "qmr"), ("sr", "cq")], writes=[("sr", "qmn"), ("sr", "qmr")], sync_self=True)
                qr = tAq[:, 0:192].rearrange("p (h d) -> p h d", h=6)
                for hh in range(6):
                    R.op("dve", lambda e, hh=hh: e.scalar_tensor_tensor(out=qmfull[:, hh, 0:64], in0=qm3[:, hh, 0:64], scalar=st_r[:, 32 + hh:33 + hh],
                                                                        in1=g_qn, op0=ALU.mult, op1=ALU.mult),
                         reads=["qraw", ("sr", "qmn"), "gsc"], writes=[("qmfull", hh)], sync_self=(hh == 0))
                    R.op("dve", lambda e, hh=hh: e.scalar_tensor_tensor(out=qr[:, hh, :], in0=qm3[:, hh, 64:96], scalar=st_r[:, 38 + hh:39 + hh],
                                                                        in1=g_qr, op0=ALU.mult, op1=ALU.mult),
                         reads=["qraw", ("sr", "qmr"), "gsc", "qf"] + qnk, writes=[("qr", hh)])
                qrk = [("qr", hh) for hh in range(6)]
                if lat:
                    rope_apply("dve", qr, qmfull[:, :, 64:96], ropeM, t, 6, 32, qrk, [("qmfull", "r")],
                               tBq[:, 0:192].rearrange("p (h d) -> p h d", h=6), tCq[:, 0:192].rearrange("p (h d) -> p h d", h=6), "tBq", "tCq")
                else:
                    R.op("dve", lambda e: e.tensor_copy(out=qmfull[:, :, 64:96], in_=qr), reads=qrk, writes=[("qmfull", "r")])

                def trqm(e):
                    inst = None
                    for hh in range(6):
                        inst = e.transpose(psb(7)[0:96, hh * 128:(hh + 1) * 128], qmfull[:, hh, :], ident[:])
                    return inst
                R.op("pe", trqm, reads=[("qmfull", hh) for hh in range(6)] + [("qmfull", "r"), "ident"], writes=[("ps", 7)])
                R.op("act", lambda e, lt=lt: e.copy(out=qT_m[0:96, :, lt * 128:(lt + 1) * 128],
                                                    in_=psb(7)[0:96, 0:768].rearrange("p (h n) -> p h n", h=6)),
                     reads=[("ps", 7)], writes=[("qTm", lt)])
                qstreams.append(R.capture_end())
            R.replay_zipped(qstreams, 0.7)
            qgk = [("qTg", lt) for lt in range(ntile)]
            qmk = [("qTm", lt) for lt in range(ntile)]
            npair = len(key_tiles) // 2
            items = [(slot, kj, key_tiles[2 * kj], key_tiles[2 * kj + 1]) for slot in range(12) for kj in range(npair)]
            LA = 1
            ST_PAIRS = [0, 6]

            def slot_info(slot):
                if slot < 6:
                    pr, hf = slot // 2, slot % 2
                    return True, pr, hf, pr, slot
                hm = slot - 6
                pr, hf = hm // 2, hm % 2
                return False, pr, hf, 3 + pr, hm

            def emit_st(i, nq=nq):
                slot, kj, kta, ktb = items[i]
                gqa, pr, hf, chunk, hm = slot_info(slot)
                b0 = ST_PAIRS[i % 2]
                pt = PT2[i % 2]
                ptk = ("PT", i % 2)

                def mmst(e, b0=b0, gqa=gqa, pr=pr, hf=hf, hm=hm, kta=kta, ktb=ktb, nq=nq):
                    inst = None
                    for k, kt in enumerate((kta, ktb)):
                        if gqa:
                            inst = e.matmul(ps[:, b0 + k, 0:nq], kT_g[64 * hf:64 * hf + 64, kt * 128:(kt + 1) * 128],
                                            qT_g[64 * hf:64 * hf + 64, pr, 0:nq], start=True, stop=True)
                        else:
                            inst = e.matmul(ps[:, b0 + k, 0:nq], kT_m[0:96, hm, kt * 128:(kt + 1) * 128], qT_m[0:96, hm, 0:nq],
                                            start=True, stop=True)
                    return inst
                kk = [("kTg", kta), ("kTg", ktb)] + qgk if gqa else [("kTm", kta), ("kTm", ktb)] + qmk
                R.op("pe", mmst, reads=kk, writes=[("ps", b0), ("ps", b0 + 1)])
                R.op("act", lambda e, b0=b0, pt=pt, nq=nq: e.activation(out=pt[:, :, 0:nq], in_=ps[:, b0:b0 + 2, 0:nq], func=AF.Exp),
                     reads=[("ps", b0), ("ps", b0 + 1)], writes=[ptk])

            def emit_pv(i, nq=nq):
                slot, kj, kta, ktb = items[i]
                gqa, pr, hf, chunk, hm = slot_info(slot)
                pt = PT2[i % 2]
                ptk = ("PT", i % 2)
                bo = 2 if (slot % 2 == 0) else 4
                bd = bo + 1
                if gqa:
                    vaps = [V_g[:, kta, :], V_g[:, ktb, :]]
                    vk = [("Vg", kta), ("Vg", ktb)]
                else:
                    vaps = [V_m[:, kta, pr * 128:(pr + 1) * 128], V_m[:, ktb, pr * 128:(pr + 1) * 128]]
                    vk = [("Vm", kta), ("Vm", ktb)]

                def mmpv(e, vaps=vaps, pt=pt, bo=bo, bd=bd, kj=kj, nq=nq, npair=npair):
                    inst = None
                    for k in range(2):
                        first = (kj == 0 and k == 0)
                        last = (kj == npair - 1 and k == 1)
                        e.matmul(ps[:, bo, 0:nq], vaps[k], pt[:, k, 0:nq], start=first, stop=last)
                        inst = e.matmul(ps[:, bd, 0:nq], ones_bf[:], pt[:, k, 0:nq], start=first, stop=last)
                    return inst
                R.op("pe", mmpv, reads=vk + [ptk, "ones"], writes=[("ps", bo), ("ps", bd)])
                if kj == npair - 1:
                    p0 = 64 * hf
                    R.op("dve", lambda e, bd=bd, p0=p0, nq=nq: e.reciprocal(out=rden[p0:p0 + 64, 0:nq], in_=ps[p0:p0 + 64, bd, 0:nq]),
                         reads=[("ps", bd)], writes=["rden"])
                    R.op("dve", lambda e, bo=bo, p0=p0, chunk=chunk, nq=nq: e.tensor_tensor(
                        out=mixT[p0:p0 + 64, chunk, 0:nq], in0=ps[p0:p0 + 64, bo, 0:nq], in1=rden[p0:p0 + 64, 0:nq], op=ALU.mult),
                        reads=[("ps", bo), "rden"], writes=[("mixT", chunk, hf)])

            for i in range(len(items) + LA):
                if i < len(items):
                    emit_st(i)
                if i >= LA:
                    emit_pv(i - LA)
            mixk = [("mixT", c, hf) for c in range(6) for hf in range(2)]
            for hd in range(2):
                R.dma("pool", "wo", lambda e, hd=hd: e.dma_start(out=wo_b[:], in_=wout_d[l, hd]), writes=["wo"])
                for lt in range(ntile):
                    t = t0 + lt
                    pos = (1 + t * 128) if lat else (2051 + (t - NLT) * 128)
                    bk = 6 + (lt % 2)

                    def mmo(e, lt=lt, pos=pos, bk=bk):
                        inst = None
                        for c in range(8):
                            if c < 3:
                                lh = mixT[:, c, lt * 128:(lt + 1) * 128]
                            elif c < 5:
                                lh = bT[:, c - 3, pos:pos + 128]
                            else:
                                lh = mixT[:, c - 2, lt * 128:(lt + 1) * 128]
                            inst = e.matmul(ps[:, bk, :], lh, wo_b[:, c, :], start=(c == 0), stop=(c == 7))
                        return inst
                    R.op("pe", mmo, reads=mixk + ["bT", "wo"], writes=[("ps", bk)])
                    R.op("dve", lambda e, bk=bk, r=r, hd=hd: e.tensor_tensor(out=ps[:, bk, :], in0=ps[:, bk, :],
                                                                             in1=gate_bc[:, r, hd * 512:(hd + 1) * 512], op=ALU.mult),
                         reads=[("ps", bk), ("gate", r)], writes=[("ps", bk)])
                    R.op("dve", lambda e, bk=bk, t=t, hd=hd: e.tensor_tensor(out=xs[:, t, hd * 512:(hd + 1) * 512],
                                                                             in0=xs[:, t, hd * 512:(hd + 1) * 512], in1=ps[:, bk, :], op=ALU.add),
                         reads=[("ps", bk), ("x", t)], writes=[("x", t)])
        R.barrier()
        Ar.pos = mark0

    done = False
    for l in range(L):
        if l > 0:
            R.new_epoch()
        need_ctx = l < DEPTH - 1
        modulation(l)
        ffn(l, 0, 0, list(range(NT)))
        if stop_after == (l, "ffn1"):
            break
        mixer(l, need_ctx)
        if stop_after == (l, "mix"):
            break
        ffn(l, 1, 2, list(range(NT)) if need_ctx else list(range(NLT)))
        if stop_after == (l, "ffn2"):
            break

    for i in range(4):
        R.dma("sp", "out", lambda e, i=i: e.dma_start(
            out=out_d[512 * i:512 * (i + 1), :].rearrange("(t p) d -> p t d", p=128), in_=xs[:, 4 * i:4 * i + 4, :]),
            reads=[("x", 4 * i + k) for k in range(4)])
    R.op("sp", lambda e: None, reads=[], writes=[("x", k) for k in range(NLT)])

    R.finalize()
    esems = {}
    for e in Rec.ENG:
        for ep in range(R.n_epochs):
            esems[(e, ep)] = es.enter_context(nc.semaphore("s_%s_%d" % (e, ep)))
    lsems = {ln: es.enter_context(nc.semaphore("l_%s" % ln)) for ln in R.lane_cnt}
    block = es.enter_context(nc.Block())

    @block.tensor
    def _(e):
        R.emit("pe", e, esems, lsems)

    @block.scalar
    def _(e):
        R.emit("act", e, esems, lsems)

    @block.vector
    def _(e):
        R.emit("dve", e, esems, lsems)

    @block.gpsimd
    def _(e):
        R.emit("pool", e, esems, lsems)

    @block.sync
    def _(e):
        R.emit("sp", e, esems, lsems)

    es.close()
    return nc


_CACHE = {}


def kernel(**inputs):
    inp = {k: np.asarray(v) for k, v in inputs.items()}
    if "nc" not in _CACHE:
        _CACHE["nc"] = build_program(DEPTH)
    nc = _CACHE["nc"]
    sh = prep_shared(inp, DEPTH)
    in_maps = []
    for b in range(8):
        m = dict(sh)
        m.update(prep_core(inp, b))
        in_maps.append(m)
    res = run_bass_kernel_spmd(nc, in_maps, core_ids=list(range(8)))
    out = np.stack([np.asarray(r["out"]) for r in res.results], axis=0)
    return out.astype(np.float32)
```

```python
import numpy as np
from contextlib import ExitStack
import concourse.bass as bass
import concourse.mybir as mybir
from concourse.bass_utils import run_bass_kernel_spmd

F32 = mybir.dt.float32
BF16 = mybir.dt.bfloat16
U8 = mybir.dt.uint8
ALU = mybir.AluOpType
AF = mybir.ActivationFunctionType
AX = mybir.AxisListType

D = 1024
DEPTH = 4
NT = 18
NLT = 16
T = NT * 128
DFF = 2816
NFC = 22
EPS = 1e-6
KC = 8
NGAIN = 320
FF_GROUPS = [4, 4, 4, 4, 4, 2]


class Rec:
    ENG = ("pe", "act", "dve", "pool", "sp")

    def __init__(self):
        self.ops = {e: [] for e in self.ENG}
        self.res = {}
        self.lane_cnt = {}
        self.epoch_starts = {e: [0] for e in self.ENG}
        self.pending = {e: [] for e in self.ENG}
        self.cap = None

    def capture_start(self):
        self.cap = []

    def capture_end(self):
        c = self.cap
        self.cap = None
        return c

    def replay(self, item):
        if item[0] == "op":
            self.op(item[1], item[2], item[3], item[4], item[5])
        else:
            self.dma(item[1], item[2], item[3], item[4], item[5])

    def replay_zipped(self, streams, frac=0.5):
        if not streams:
            return
        H = max(1, int(frac * max(len(st) for st in streams)))
        keyed = []
        for k, st in enumerate(streams):
            for j, it in enumerate(st):
                keyed.append((k * H + j, k, j, it))
        keyed.sort(key=lambda z: (z[0], z[1], z[2]))
        for _, _, _, it in keyed:
            self.replay(it)

    def new_epoch(self):
        for e in self.ENG:
            self.epoch_starts[e].append(len(self.ops[e]))

    def _deps(self, reads, writes):
        d = []
        for k in reads:
            st = self.res.get(k)
            if st is not None and st[0] is not None:
                d.append(st[0])
        for k in writes:
            st = self.res.get(k)
            if st is not None:
                if st[0] is not None:
                    d.append(st[0])
                for kk, v in st[1].items():
                    d.append(kk + (v,))
        return d

    def _commit(self, tok, reads, writes):
        for k in reads:
            st = self.res.get(k)
            if st is None:
                st = [None, {}]
                self.res[k] = st
            key = tok[:2]
            if st[1].get(key, -1) < tok[2]:
                st[1][key] = tok[2]
        for k in writes:
            self.res[k] = [tok, {}]

    def op(self, eng, fn, reads=(), writes=(), sync_self=False):
        if self.cap is not None:
            self.cap.append(("op", eng, fn, tuple(reads), tuple(writes), sync_self))
            return None
        deps = self._deps(reads, writes) + self.pending[eng]
        self.pending[eng] = []
        idx = len(self.ops[eng])
        tok = ("e", eng, idx)
        self.ops[eng].append({"fn": fn, "deps": deps, "lane": None, "ss": sync_self})
        self._commit(tok, reads, writes)
        return tok

    def dma(self, eng, lane, fn, reads=(), writes=()):
        if self.cap is not None:
            self.cap.append(("dma", eng, lane, fn, tuple(reads), tuple(writes)))
            return None
        deps = self._deps(reads, writes) + self.pending[eng]
        self.pending[eng] = []
        self.lane_cnt[lane] = self.lane_cnt.get(lane, 0) + 1
        tok = ("d", lane, self.lane_cnt[lane])
        self.ops[eng].append({"fn": fn, "deps": deps, "lane": lane, "ss": True})
        self._commit(tok, reads, writes)
        return tok

    def barrier(self):
        last = []
        for e in self.ENG:
            for i in range(len(self.ops[e]) - 1, -1, -1):
                if self.ops[e][i]["lane"] is None:
                    last.append(("e", e, i))
                    break
        for l, c in self.lane_cnt.items():
            last.append(("d", l, c))
        for e in self.ENG:
            self.pending[e] = self.pending[e] + list(last)

    def finalize(self):
        self.signal = {e: [False] * len(self.ops[e]) for e in self.ENG}
        for e in self.ENG:
            for i, op in enumerate(self.ops[e]):
                for tok in op["deps"]:
                    if tok[0] == "e" and (tok[1] != e or op["ss"] or tok[2] >= i - 2):
                        self.signal[tok[1]][tok[2]] = True
        self.sigval = {}
        self.epoch_of = {}
        for e in self.ENG:
            starts = self.epoch_starts[e]
            vals = [None] * len(self.ops[e])
            eps = [0] * len(self.ops[e])
            ep = 0
            cnt = 0
            for i in range(len(self.ops[e])):
                while ep + 1 < len(starts) and i >= starts[ep + 1]:
                    ep += 1
                    cnt = 0
                if self.signal[e][i]:
                    cnt += 1
                    vals[i] = cnt
                eps[i] = ep
            self.sigval[e] = vals
            self.epoch_of[e] = eps
        self.n_epochs = max(len(s) for s in self.epoch_starts.values())

    def emit(self, eng, e, esems, lsems):
        waited = {}
        for i, op in enumerate(self.ops[eng]):
            need = {}
            for tok in op["deps"]:
                if tok[0] == "e":
                    if tok[1] == eng and not op["ss"] and tok[2] < i - 2:
                        continue
                    key = ("e", tok[1], self.epoch_of[tok[1]][tok[2]])
                    val = self.sigval[tok[1]][tok[2]]
                else:
                    key = ("d", tok[1])
                    val = 16 * tok[2]
                if need.get(key, 0) < val:
                    need[key] = val
            for key, val in need.items():
                if waited.get(key, 0) >= val:
                    continue
                if key[0] == "e":
                    later = [k for k in waited if k[0] == "e" and k[1] == key[1] and k[2] > key[2]]
                    if later:
                        continue
                    e.wait_ge(esems[(key[1], key[2])], val)
                else:
                    e.wait_ge(lsems[key[1]], val)
                waited[key] = val
            inst = op["fn"](e)
            if inst is None:
                continue
            if op["lane"] is not None:
                inst.then_inc(lsems[op["lane"]], 16)
            elif self.signal[eng][i]:
                inst.then_inc(esems[(eng, self.epoch_of[eng][i])], 1)


Q_ORDER = [0, 3, 1, 4, 2, 5]


def _rope_np(dim):
    rows = 2048 // 64
    row = np.repeat(np.arange(rows, dtype=np.float64), 64)
    col = np.tile(np.arange(64, dtype=np.float64), rows)
    half = dim // 2
    inv = 1.0 / (10000.0 ** (np.arange(0, half, 2, dtype=np.float64) / half))
    ar = row[:, None] * inv[None, :]
    ac = col[:, None] * inv[None, :]
    ang = np.concatenate([ar, ar, ac, ac], axis=-1)
    cos = np.cos(ang).astype(np.float32)
    sin = np.sin(ang).astype(np.float32)
    q = dim // 4
    sgn = np.concatenate([-np.ones(q), np.ones(q), -np.ones(q), np.ones(q)]).astype(np.float32)
    sin = sin * sgn[None, :]
    cos = np.ascontiguousarray(cos.reshape(16, 128, dim).transpose(1, 0, 2))
    sin = np.ascontiguousarray(sin.reshape(16, 128, dim).transpose(1, 0, 2))
    return cos, sin


def prep_shared(inp, n_layers):
    L = n_layers
    f = lambda a: np.ascontiguousarray(a, dtype=np.float32)
    sh = {}
    wm = inp["w_mod"][:L]
    sh["wmod"] = f(wm.reshape(L, KC, 128, 18, 512).transpose(0, 3, 2, 1, 4))
    sh["bmodT"] = f(inp["b_mod"][:L].reshape(L, 72, 128).transpose(2, 0, 1))
    sh["gn"] = f(inp["g_norm"][:L].reshape(L, 3, KC, 128).transpose(3, 0, 1, 2))
    wg = inp["ffn_w_gate"][:L].reshape(L, 2, KC, 128, NFC, 128)
    wu = inp["ffn_w_up"][:L].reshape(L, 2, KC, 128, NFC, 128)
    wgu = np.stack([wg, wu], axis=0)
    sh["wgu"] = f(wgu.transpose(1, 2, 5, 4, 0, 3, 6))
    sh["wd"] = f(inp["ffn_w_down"][:L])
    win = inp["w_in"][:L]
    qcols = np.concatenate([np.arange(64 * h, 64 * h + 64) for h in Q_ORDER])
    kside = np.concatenate([np.arange(384, 512), np.arange(512, 640), np.arange(1792, 2048),
                            np.arange(2048, 2080),
                            np.arange(640, 896), np.arange(1152, 1408), np.arange(896, 1152)])
    qside = np.concatenate([qcols, np.arange(1408, 1792)])
    sh["wink"] = f(win[:, :, kside].reshape(L, KC, 128, 1312).transpose(0, 2, 1, 3))
    sh["winq"] = f(win[:, :, qside].reshape(L, KC, 128, 768).transpose(0, 2, 1, 3))
    sh["wuq"] = f(inp["mla_w_uq"][:L].reshape(L, 3, 128, 576).transpose(0, 2, 1, 3))
    sh["wukv"] = f(inp["mla_w_ukv"][:L].reshape(L, 2, 128, 768).transpose(0, 2, 1, 3))
    rows = []
    for pr in range(3):
        for hf in range(2):
            h = Q_ORDER[2 * pr + hf]
            rows.append(np.arange(64 * h, 64 * h + 64))
    rows.append(np.arange(384, 640))
    rows.append(np.arange(640, 1024))
    rows = np.concatenate(rows)
    wo = inp["w_out"][:L][:, rows, :]
    sh["wout"] = f(wo.reshape(L, KC, 128, 2, 512).transpose(0, 3, 2, 1, 4))
    sh["gains"] = f(np.concatenate([inp["gqa_g_q"][:L], inp["gqa_g_k"][:L], inp["mla_g_qn"][:L],
                                    inp["mla_g_kn"][:L], inp["mla_g_qr"][:L], inp["mla_g_kr"][:L]], axis=1))
    cw = inp["conv_w"][:L]
    cb = inp["conv_b"][:L]
    cp = np.concatenate([cw, cb[:, None, :]], axis=1)
    sh["convp"] = f(cp.reshape(L, 4, 2, 128).transpose(3, 0, 2, 1))
    gl = np.concatenate([inp["mla_g_cq"][:L].reshape(L, 3, 128), inp["mla_g_ckv"][:L].reshape(L, 2, 128)], axis=1)
    sh["glat"] = f(gl.transpose(2, 0, 1))
    ca, sa = _rope_np(64)
    cm, sm = _rope_np(32)
    sh["ropeA"] = f(np.stack([ca, sa], axis=1))
    sh["ropeM"] = f(np.stack([cm, sm], axis=1))
    return sh


def prep_core(inp, b):
    xin = np.concatenate([inp["x"][b], inp["ctx"][b]], axis=0)
    cc = np.stack([inp["c"][b], inp["c_ctx"]], axis=-1)
    ccT = np.ascontiguousarray(cc.reshape(KC, 128, 2).transpose(1, 0, 2), dtype=np.float32)
    return {"xin": np.ascontiguousarray(xin, dtype=np.float32), "ccT": ccT}


def build_program(n_layers=DEPTH, stop_after=None):
    L = n_layers
    nc = bass.Bass("TRN2", target_bir_lowering=False)
    dt_in = lambda name, shape: nc.dram_tensor(name, list(shape), F32, kind="ExternalInput").ap()
    xin = dt_in("xin", [T, D])
    ccT_d = dt_in("ccT", [128, KC, 2])
    wmod_d = dt_in("wmod", [L, 18, 128, KC, 512])
    bmodT_d = dt_in("bmodT", [128, L, 72])
    gn_d = dt_in("gn", [128, L, 3, KC])
    wgu_d = dt_in("wgu", [L, 2, NFC, 128, 2, KC, 128])
    wd_d = dt_in("wd", [L, 2, DFF, D])
    wink_d = dt_in("wink", [L, 128, KC, 1312])
    winq_d = dt_in("winq", [L, 128, KC, 768])
    wuq_d = dt_in("wuq", [L, 128, 3, 576])
    wukv_d = dt_in("wukv", [L, 128, 2, 768])
    wout_d = dt_in("wout", [L, 2, 128, KC, 512])
    gains_d = dt_in("gains", [L, NGAIN])
    convp_d = dt_in("convp", [128, L, 2, 4])
    glat_d = dt_in("glat", [128, L, 5])
    ropeA_d = dt_in("ropeA", [128, 2, 16, 64])
    ropeM_d = dt_in("ropeM", [128, 2, 16, 32])
    out_d = nc.dram_tensor("out", [2048, D], F32, kind="ExternalOutput").ap()

    R = Rec()
    es = ExitStack()
    sb = lambda name, shape, dt: es.enter_context(nc.sbuf_tensor(name, list(shape), dt))
    xs = sb("xs", [128, NT, D], F32)
    ropeA = sb("ropeA_s", [128, 2, 16, 64], BF16)
    ropeM = sb("ropeM_s", [128, 2, 16, 32], BF16)
    ident = sb("ident", [128, 128], BF16)
    identf = sb("identf", [128, 128], F32)
    ones_bf = sb("ones_bf", [128, 128], BF16)
    ccs = sb("ccs", [128, KC, 2], F32)
    s2 = sb("s2", [128, KC, 2], BF16)
    gn = sb("gn_s", [128, L, 3, KC], F32)
    bmodT = sb("bmodT_s", [128, L, 72], F32)
    convp = sb("convp_s", [128, L, 2, 4], F32)
    glat = sb("glat_s", [128, L, 5], F32)
    gains = sb("gains_s", [128, NGAIN], F32)
    modTs = [sb("modT0", [128, 72, 2], F32), sb("modT1", [128, 72, 2], F32)]
    Amod = sb("Amod", [128, KC, 2], F32)
    Bmod = sb("Bmod", [128, KC, 2], F32)
    gate_bc = sb("gate_bc", [128, 2, D], BF16)
    epsb = sb("epsb", [128, 1], F32)
    st_ss = sb("st_ss", [128, 64], F32)
    st_r = sb("st_r", [128, 64], F32)
    ARENA_BYTES = 120 * 1024
    arena = sb("arena", [128, ARENA_BYTES], U8)
    ps = es.enter_context(nc.psum_tensor("ps", [128, 8, 512], F32))

    class Ar:
        pos = 0

    def aalloc(nbytes):
        a0 = (Ar.pos + 31) // 32 * 32
        Ar.pos = a0 + nbytes
        assert Ar.pos <= ARENA_BYTES, ("arena overflow", Ar.pos)
        return a0

    def aview(a0, dt, shape):
        esz = 2 if dt == BF16 else 4
        n = int(np.prod(shape))
        v = arena[:, a0:a0 + n * esz].bitcast(dt)
        if len(shape) == 1:
            return v
        names = " ".join("a%d" % i for i in range(len(shape)))
        kw = {"a%d" % i: shape[i] for i in range(1, len(shape))}
        return v.rearrange("p (%s) -> p %s" % (names, names), **kw)

    def anew(dt, shape):
        esz = 2 if dt == BF16 else 4
        return aview(aalloc(int(np.prod(shape)) * esz), dt, shape)

    psb = lambda b: ps[:, b, :].bitcast(BF16)

    for i in range(6):
        R.dma("sp", "xin%d" % i, lambda e, i=i: e.dma_start(
            out=xs[:, 3 * i:3 * i + 3, :], in_=xin[384 * i:384 * (i + 1), :].rearrange("(t p) d -> p t d", p=128)),
            writes=[("x", 3 * i), ("x", 3 * i + 1), ("x", 3 * i + 2)])
    R.dma("sp", "cst0", lambda e: e.dma_start(out=ccs[:], in_=ccT_d), writes=["ccs"])
    R.dma("sp", "cst1", lambda e: e.dma_start(out=gn[:], in_=gn_d), writes=["gn"])
    R.dma("sp", "cst2", lambda e: e.dma_start(out=bmodT[:], in_=bmodT_d), writes=["bmodT"])
    R.dma("sp", "cst3", lambda e: e.dma_start(out=convp[:], in_=convp_d), writes=["convp"])
    R.dma("sp", "cst4", lambda e: e.dma_start(out=glat[:], in_=glat_d), writes=["glat"])
    R.dma("pool", "cstA", lambda e: e.dma_start(out=ropeA[:], in_=ropeA_d), writes=["ropeA"])
    R.dma("pool", "cstM", lambda e: e.dma_start(out=ropeM[:], in_=ropeM_d), writes=["ropeM"])
    R.op("pool", lambda e: e.memset(identf[:], 0.0), writes=["identf"])
    R.op("pool", lambda e: e.affine_select(out=identf[:], in_=identf[:], pattern=[[-1, 128]],
                                           compare_op=ALU.not_equal, fill=1.0, base=0, channel_multiplier=1),
         reads=["identf"], writes=["identf"])
    R.op("pool", lambda e: e.tensor_copy(out=ident[:], in_=identf[:]), reads=["identf"], writes=["ident"])
    R.op("pool", lambda e: e.memset(ones_bf[:], 1.0), writes=["ones"])
    R.op("pool", lambda e: e.memset(epsb[:], EPS), writes=["eps"])
    R.op("act", lambda e: e.activation(out=s2[:], in_=ccs[:], func=AF.Silu), reads=["ccs"], writes=["s2"])

    def rstd_from_ss(n, scale, key):
        R.op("act", lambda e: e.activation(out=st_r[:, 0:n], in_=st_ss[:, 0:n], func=AF.Sqrt, scale=scale, bias=epsb[:]),
             reads=[("ss", key), "eps"], writes=[("sr", key)])
        R.op("dve", lambda e: e.reciprocal(out=st_r[:, 0:n], in_=st_r[:, 0:n]), reads=[("sr", key)], writes=[("sr", key)])

    def modulation(l):
        mark = Ar.pos
        ring = [anew(BF16, [KC, 512]) for _ in range(4)]
        for c in range(18):
            s = c % 4
            R.dma("pool", "wm%d" % s, lambda e, c=c, s=s: e.dma_start(out=ring[s][:], in_=wmod_d[l, c]),
                  writes=[("wmring", s)])

            def mm(e, c=c, s=s):
                inst = None
                for fc in range(4):
                    col = (c * 4 + fc) * 2
                    for kc in range(KC):
                        inst = e.matmul(ps[:, 0, col:col + 2], ring[s][:, kc, fc * 128:(fc + 1) * 128], s2[:, kc, :],
                                        start=(kc == 0), stop=(kc == KC - 1))
                return inst
            R.op("pe", mm, reads=[("wmring", s), "s2"], writes=[("ps", 0)])
        modT = modTs[l % 2]
        R.op("dve", lambda e: e.tensor_tensor(
            out=modT[:], in0=ps[:, 0, 0:144].rearrange("p (c r) -> p c r", r=2),
            in1=bmodT[:, l, :].unsqueeze(2).broadcast_to([128, 72, 2]), op=ALU.add),
            reads=[("ps", 0), "bmodT"], writes=[("modT", l % 2)])
        R.barrier()
        Ar.pos = mark

    def sub_modulation(l, j, gate_mult):
        modT = modTs[l % 2]
        mk = ("modT", l % 2)
        c_shift, c_scale, c_gate = (3 * j) * 8, (3 * j + 1) * 8, (3 * j + 2) * 8
        R.op("dve", lambda e: e.tensor_scalar(out=Amod[:], in0=modT[:, c_scale:c_scale + 8, :], scalar1=1.0, scalar2=None,
                                              op0=ALU.add), reads=[mk], writes=["Amod"])
        R.op("dve", lambda e: e.tensor_tensor(out=Amod[:], in0=Amod[:], in1=gn[:, l, j, :].unsqueeze(2).broadcast_to([128, KC, 2]),
                                              op=ALU.mult), reads=["Amod", "gn"], writes=["Amod"])
        R.op("dve", lambda e: e.tensor_copy(out=Bmod[:], in_=modT[:, c_shift:c_shift + 8, :]), reads=[mk], writes=["Bmod"])
        mark = Ar.pos
        rep = anew(BF16, [KC, 128])
        for r in range(2):
            R.op("dve", lambda e, r=r: e.tensor_scalar(
                out=rep[:], in0=modT[:, c_gate:c_gate + 8, r:r + 1].broadcast_to([128, KC, 128]),
                scalar1=gate_mult, scalar2=None, op0=ALU.mult), reads=[mk], writes=["rep"])

            def tr(e):
                inst = None
                for kc in range(KC):
                    inst = e.transpose(psb(1)[:, kc * 128:(kc + 1) * 128], rep[:, kc, :], ident[:])
                return inst
            R.op("pe", tr, reads=["rep", "ident"], writes=[("ps", 1)])
            R.op("act", lambda e, r=r: e.copy(out=gate_bc[:, r, :], in_=psb(1)[:, 0:1024]), reads=[("ps", 1)],
                 writes=[("gate", r)])
        R.barrier()
        Ar.pos = mark

    def emit_hT(t, dst, dst_key, xn, xn_key, psbank, stat_col):
        r = 0 if t < NLT else 1
        R.op("act", lambda e: e.activation(out=xn[:], in_=xs[:, t, :], func=AF.Square, accum_out=st_ss[:, stat_col:stat_col + 1]),
             reads=[("x", t)], writes=[xn_key, ("ss", "h%d" % stat_col)])
        R.op("act", lambda e: e.activation(out=st_r[:, stat_col:stat_col + 1], in_=st_ss[:, stat_col:stat_col + 1], func=AF.Sqrt,
                                           scale=1.0 / D, bias=epsb[:]),
             reads=[("ss", "h%d" % stat_col), "eps"], writes=[("sr", "h%d" % stat_col)], sync_self=True)
        R.op("dve", lambda e: e.reciprocal(out=st_r[:, stat_col:stat_col + 1], in_=st_r[:, stat_col:stat_col + 1]),
             reads=[("sr", "h%d" % stat_col)], writes=[("sr", "h%d" % stat_col)])
        R.op("dve", lambda e: e.tensor_scalar(out=xn[:], in0=xs[:, t, :], scalar1=st_r[:, stat_col:stat_col + 1], scalar2=None,
                                              op0=ALU.mult), reads=[("x", t), ("sr", "h%d" % stat_col), xn_key], writes=[xn_key], sync_self=True)

        def tr(e):
            inst = None
            for kc in range(KC):
                inst = e.transpose(psb(psbank)[:, kc * 128:(kc + 1) * 128], xn[:, kc * 128:(kc + 1) * 128], ident[:])
            return inst
        R.op("pe", tr, reads=[xn_key, "ident"], writes=[("ps", psbank)])
        for kc in range(KC):
            R.op("act", lambda e, kc=kc: e.activation(out=dst[:, kc, :], in_=psb(psbank)[:, kc * 128:(kc + 1) * 128],
                                                      func=AF.Identity, scale=Amod[:, kc, r:r + 1], bias=Bmod[:, kc, r:r + 1]),
                 reads=[("ps", psbank), "Amod", "Bmod"], writes=[dst_key])

    def make_mod_hook(ln):
        state = {"ring": None}

        def hook(c):
            if c >= 18:
                return
            if state["ring"] is None:
                state["ring"] = [anew(BF16, [KC, 512]) for _ in range(3)]
            ring = state["ring"]
            s_ = c % 3
            modT = modTs[ln % 2]
            R.dma("pool", "wmh%d" % s_, lambda e: e.dma_start(out=ring[s_][:], in_=wmod_d[ln, c]), writes=[("wmhring", s_)])

            def mm(e):
                inst = None
                for fc in range(4):
                    for kc in range(KC):
                        inst = e.matmul(ps[:, 0, 2 * fc:2 * fc + 2], ring[s_][:, kc, fc * 128:(fc + 1) * 128], s2[:, kc, :],
                                        start=(kc == 0), stop=(kc == KC - 1))
                return inst
            R.op("pe", mm, reads=[("wmhring", s_), "s2"], writes=[("ps", 0)])
            R.op("dve", lambda e: e.tensor_tensor(
                out=modT[:, 4 * c:4 * c + 4, :], in0=ps[:, 0, 0:8].rearrange("p (c r) -> p c r", r=2),
                in1=bmodT[:, ln, 4 * c:4 * c + 4].unsqueeze(2).broadcast_to([128, 4, 2]), op=ALU.add),
                reads=[("ps", 0), "bmodT", ("modT", ln % 2)], writes=[("modT", ln % 2)])
        return hook

    def ffn(l, f, j, tiles, hook=None):
        ntl = len(tiles)
        sub_modulation(l, j, 0.5)
        mark = Ar.pos
        hT = anew(BF16, [KC, T])
        aT = anew(BF16, [4, T])
        gu_ring = [anew(BF16, [2, KC, 128]) for _ in range(3)]
        d_ring = [anew(BF16, [D]) for _ in range(8)]
        sil = [anew(BF16, [512]) for _ in range(2)]
        xn = [anew(BF16, [D]) for _ in range(2)]
        for i, t in enumerate(tiles):
            emit_hT(t, hT[:, :, t * 128:(t + 1) * 128], ("hT", t), xn[i % 2], ("xn", i % 2), i % 2, i % 2)
        tgs = []
        i = 0
        while i < ntl:
            n = min(4, ntl - i)
            tgs.append((tiles[i], n))
            i += n
        cbase = 0
        gu_cnt = 0
        d_cnt = 0
        ps_g = [2, 3]
        ps_u = [4, 5]
        gu_i = 0
        y_i = 0
        for gsz in FF_GROUPS:
            for ci in range(gsz):
                c = cbase + ci
                s = gu_cnt % 3
                R.dma("pool", "gu%d" % s, lambda e, c=c, s=s: e.dma_start(out=gu_ring[s][:], in_=wgu_d[l, f, c]),
                      writes=[("guring", s)])
                sd = d_cnt % 8
                R.dma("pool", "wd%d" % sd, lambda e, c=c, sd=sd: e.dma_start(out=d_ring[sd][:], in_=wd_d[l, f, c * 128:(c + 1) * 128, :]),
                      writes=[("dring", sd)])
                for (t0, n) in tgs:
                    ntok = n * 128
                    tok0 = t0 * 128
                    bg = ps_g[gu_i % 2]
                    bu = ps_u[gu_i % 2]
                    sl = sil[gu_i % 2]
                    gu_i += 1
                    hkeys = [("hT", t0 + k) for k in range(n)]

                    def mm(e, s=s, bg=bg, bu=bu, tok0=tok0, ntok=ntok):
                        inst = None
                        for which, bank in ((0, bg), (1, bu)):
                            for kc in range(KC):
                                inst = e.matmul(ps[:, bank, 0:ntok], gu_ring[s][:, which, kc, :], hT[:, kc, tok0:tok0 + ntok],
                                                start=(kc == 0), stop=(kc == KC - 1))
                        return inst
                    R.op("pe", mm, reads=[("guring", s)] + hkeys, writes=[("ps", bg), ("ps", bu)])
                    R.op("act", lambda e, bg=bg, sl=sl, ntok=ntok: e.activation(out=sl[:, 0:ntok], in_=ps[:, bg, 0:ntok], func=AF.Silu),
                         reads=[("ps", bg)], writes=[("sil", id(sl))])
                    R.op("dve", lambda e, bu=bu, sl=sl, ci=ci, tok0=tok0, ntok=ntok: e.tensor_tensor(
                        out=aT[:, ci, tok0:tok0 + ntok], in0=ps[:, bu, 0:ntok], in1=sl[:, 0:ntok], op=ALU.mult),
                        reads=[("ps", bu), ("sil", id(sl))], writes=[("aT", ci, t0 + k) for k in range(n)])
                gu_cnt += 1
                d_cnt += 1
                if hook is not None:
                    hook(c)
            dslots = [(d_cnt - gsz + ci) % 8 for ci in range(gsz)]
            for t in tiles:
                r = 0 if t < NLT else 1
                b0 = 6 if (y_i % 2 == 0) else 0
                y_i += 1

                def mmd(e, t=t, b0=b0, dslots=dslots, gsz=gsz):
                    inst = None
                    for hd in range(2):
                        for ci in range(gsz):
                            inst = e.matmul(ps[:, b0 + hd, :], aT[:, ci, t * 128:(t + 1) * 128],
                                            d_ring[dslots[ci]][:, hd * 512:(hd + 1) * 512],
                                            start=(ci == 0), stop=(ci == gsz - 1))
                    return inst
                R.op("pe", mmd, reads=[("aT", ci, t) for ci in range(gsz)] + [("dring", sd) for sd in dslots],
                     writes=[("ps", b0), ("ps", b0 + 1)])
                yv = ps[:, b0:b0 + 2, :]
                R.op("dve", lambda e, yv=yv, r=r: e.tensor_tensor(out=yv, in0=yv, in1=gate_bc[:, r, :].rearrange("p (a b) -> p a b", a=2),
                                                                  op=ALU.mult),
                     reads=[("ps", b0), ("ps", b0 + 1), ("gate", r)], writes=[("ps", b0), ("ps", b0 + 1)])
                R.op("dve", lambda e, yv=yv, t=t: e.tensor_tensor(out=xs[:, t, :].rearrange("p (a b) -> p a b", a=2),
                                                                  in0=xs[:, t, :].rearrange("p (a b) -> p a b", a=2), in1=yv, op=ALU.add),
                     reads=[("ps", b0), ("ps", b0 + 1), ("x", t)], writes=[("x", t)])
            cbase += gsz
        R.barrier()
        Ar.pos = mark

    def rope_apply(eng, src, dst, cs, t, H, dim, key_r, key_w, tmp1, tmp2, k1, k2):
        q = dim // 4
        cosb = cs[:, 0, t, :].unsqueeze(1).broadcast_to([128, H, dim])
        R.op(eng, lambda e: e.tensor_tensor(out=tmp1, in0=src, in1=cosb, op=ALU.mult), reads=key_r + ["rope"], writes=[k1])
        s4 = src.rearrange("p h (a b c) -> p h a b c", a=2, b=2)
        t4 = tmp2.rearrange("p h (a b c) -> p h a b c", a=2, b=2)
        sn = cs[:, 1, t, :].rearrange("p (a b c) -> p a b c", a=2, b=2)
        for bsel in range(2):
            R.op(eng, lambda e, bsel=bsel: e.tensor_tensor(
                out=t4[:, :, :, bsel, :], in0=s4[:, :, :, 1 - bsel, :],
                in1=sn[:, :, bsel, :].unsqueeze(1).broadcast_to([128, H, 2, q]), op=ALU.mult),
                reads=key_r + ["rope"] + ([k2] if bsel == 1 else []), writes=[k2])
        R.op(eng, lambda e: e.tensor_tensor(out=dst, in0=tmp1, in1=tmp2, op=ALU.add),
             reads=[k1, k2], writes=key_w)

    def sumsq(src, H, dd, col0, scr, key_r, key_w):
        sv = scr[:, 0:H * dd].rearrange("p (h d) -> p h d", h=H)
        R.op("dve", lambda e: e.tensor_tensor(out=sv, in0=src, in1=src, op=ALU.mult), reads=key_r, writes=["sqscr"])
        R.op("dve", lambda e: e.tensor_reduce(out=st_ss[:, col0:col0 + H], in_=sv, axis=AX.X, op=ALU.add),
             reads=["sqscr"], writes=key_w)

    def mixer(l, need_ctx):
        sub_modulation(l, 1, 1.0)
        mark0 = Ar.pos
        kT_g = anew(BF16, [T])
        V_g = anew(BF16, [NT, 128])
        kT_m = anew(BF16, [6, T])
        V_m = anew(BF16, [NT, 384])
        NCV = 2050 + 258
        bT = anew(BF16, [2, NCV])
        gsc = anew(F32, [NGAIN])
        markU = Ar.pos
        uT = anew(BF16, [2, NCV])
        R.dma("sp", "gains", lambda e: e.dma_start(out=gains[:], in_=gains_d[l].partition_broadcast(128)), writes=["gains"])
        R.op("dve", lambda e: e.tensor_copy(out=gsc[:], in_=gains[:]), reads=["gains"], writes=["gsc"])
        R.op("dve", lambda e: e.tensor_scalar(out=gsc[:, 0:64], in0=gains[:, 0:64], scalar1=64.0 ** -0.5, scalar2=None, op0=ALU.mult),
             reads=["gains", "gsc"], writes=["gsc"])
        R.op("dve", lambda e: e.tensor_scalar(out=gsc[:, 128:192], in0=gains[:, 128:192], scalar1=96.0 ** -0.5, scalar2=None, op0=ALU.mult),
             reads=["gains", "gsc"], writes=["gsc"])
        R.op("dve", lambda e: e.tensor_scalar(out=gsc[:, 256:288], in0=gains[:, 256:288], scalar1=96.0 ** -0.5, scalar2=None, op0=ALU.mult),
             reads=["gains", "gsc"], writes=["gsc"])
        g_q, g_k, g_qn, g_kn, g_qr, g_kr = (gsc[:, 0:64], gsc[:, 64:128], gsc[:, 128:192], gsc[:, 192:256],
                                            gsc[:, 256:288], gsc[:, 288:320])
        R.op("pool", lambda e: e.memset(uT[:], 0.0), writes=["uT"])

        markK = Ar.pos
        wK = anew(BF16, [KC, 1312])
        wukv = anew(BF16, [2, 768])
        hTr = [anew(BF16, [KC, 128]) for _ in range(2)]
        xn = [anew(BF16, [D]) for _ in range(2)]
        kraws = [anew(F32, [544]) for _ in range(2)]
        kvraw = anew(F32, [768])
        scr = anew(F32, [768])
        tA = anew(F32, [384])
        tB = anew(F32, [384])
        tC = anew(F32, [384])
        kf = anew(BF16, [128])
        ckvb = anew(BF16, [256])
        ckvT = anew(BF16, [2, 128])
        kfull = anew(BF16, [6, 96])
        cgt = anew(F32, [2, 128])
        R.dma("pool", "wK", lambda e: e.dma_start(out=wK[:], in_=wink_d[l]), writes=["wK"])
        R.dma("pool", "wukv", lambda e: e.dma_start(out=wukv[:], in_=wukv_d[l]), writes=["wukv"])
        for c in range(2):
            R.op("dve", lambda e, c=c: e.tensor_scalar(out=wukv[:, c, :], in0=wukv[:, c, :], scalar1=glat[:, l, 3 + c:4 + c], scalar2=None,
                                                       op0=ALU.mult), reads=["wukv", "glat"], writes=["wukv"])
        def front(t):
            lat = t < NLT
            kraw = kraws[t % 2]
            krk = ("kraw", t % 2)
            h = hTr[t % 2]
            hk = ("hTr", t % 2)
            emit_hT(t, h, hk, xn[t % 2], ("xn", t % 2), 1, t % 2)

            def mmA(e, h=h):
                inst = None
                for kc in range(KC):
                    inst = e.matmul(ps[:, 0, :], h[:, kc, :], wK[:, kc, 0:512], start=(kc == 0), stop=(kc == KC - 1))
                for kc in range(KC):
                    inst = e.matmul(ps[:, 3, 256:288], h[:, kc, :], wK[:, kc, 512:544], start=(kc == 0), stop=(kc == KC - 1))
                return inst
            R.op("pe", mmA, reads=[hk, "wK"], writes=[("ps", 0), ("ps", 3)])

            def mmC(e, h=h):
                inst = None
                for cc in range(6):
                    bank, off = (2, cc * 128) if cc < 4 else (3, (cc - 4) * 128)
                    for kc in range(KC):
                        inst = e.matmul(ps[:, bank, off:off + 128], wK[:, kc, 544 + cc * 128:544 + (cc + 1) * 128], h[:, kc, :],
                                        start=(kc == 0), stop=(kc == KC - 1))
                return inst
            R.op("pe", mmC, reads=[hk, "wK"], writes=[("ps", 2), ("ps", 3)])
            R.op("act", lambda e: e.copy(out=kraw[:, 0:512], in_=ps[:, 0, :]), reads=[("ps", 0)], writes=[krk])
            R.op("act", lambda e: e.copy(out=kraw[:, 512:544], in_=ps[:, 3, 256:288]), reads=[("ps", 3), krk], writes=[krk])
            R.op("act", lambda e, t=t: e.copy(out=V_g[:, t, :], in_=kraw[:, 128:256]), reads=[krk], writes=[("Vg", t)])
            pos = (1 + t * 128) if lat else (2051 + (t - NLT) * 128)
            R.op("act", lambda e: e.copy(out=cgt[:].rearrange("p a b -> p (a b)"), in_=ps[:, 2, 256:512]), reads=[("ps", 2)], writes=["cgt"])
            R.op("dve", lambda e, pos=pos: e.tensor_tensor(out=uT[:, :, pos:pos + 128], in0=ps[:, 2, 0:256].rearrange("p (a b) -> p a b", a=2),
                                                           in1=cgt[:], op=ALU.mult), reads=[("ps", 2), "cgt", "uT"], writes=["uT"])
            R.op("act", lambda e, pos=pos: e.copy(out=bT[:, :, pos:pos + 128], in_=ps[:, 3, 0:256].rearrange("p (a b) -> p a b", a=2)),
                 reads=[("ps", 3)], writes=["bT"])

        def back(t):
            lat = t < NLT
            kraw = kraws[t % 2]
            krk = ("kraw", t % 2)
            R.op("act", lambda e: e.copy(out=ckvb[:], in_=kraw[:, 256:512]), reads=[krk], writes=["ckvb"])
            k3 = kraw[:, 0:128].rearrange("p (h d) -> p h d", h=2)
            sumsq(k3, 2, 64, 8, scr, [krk], [("ss", "k")])
            sumsq(kraw[:, 256:512].unsqueeze(1), 1, 256, 10, scr, [krk], [("ss", "ckv")])
            sumsq(kraw[:, 512:544].unsqueeze(1), 1, 32, 11, scr, [krk], [("ss", "kr")])
            R.op("act", lambda e: e.activation(out=st_r[:, 8:10], in_=st_ss[:, 8:10], func=AF.Sqrt, scale=1.0 / 64, bias=epsb[:]),
                 reads=[("ss", "k"), "eps"], writes=[("sr", "k")])
            R.op("act", lambda e: e.activation(out=st_r[:, 10:11], in_=st_ss[:, 10:11], func=AF.Sqrt, scale=1.0 / 256, bias=epsb[:]),
                 reads=[("ss", "ckv"), "eps"], writes=[("sr", "ckv")])
            R.op("act", lambda e: e.activation(out=st_r[:, 11:12], in_=st_ss[:, 11:12], func=AF.Sqrt, scale=1.0 / 32, bias=epsb[:]),
                 reads=[("ss", "kr"), "eps"], writes=[("sr", "kr")])
            R.op("dve", lambda e: e.reciprocal(out=st_r[:, 8:12], in_=st_r[:, 8:12]),
                 reads=[("sr", "k"), ("sr", "ckv"), ("sr", "kr")], writes=[("sr", "k"), ("sr", "ckv"), ("sr", "kr")])
            kn = tA[:, 0:128].rearrange("p (h d) -> p h d", h=2)
            for hh in range(2):
                R.op("dve", lambda e, hh=hh: e.scalar_tensor_tensor(out=kn[:, hh, :], in0=k3[:, hh, :], scalar=st_r[:, 8 + hh:9 + hh], in1=g_k,
                                                                    op0=ALU.mult, op1=ALU.mult),
                     reads=[krk, ("sr", "k"), "gsc"], writes=[("kn", hh)], sync_self=True)
            kf3 = kf[:].rearrange("p (h d) -> p h d", h=2)
            if lat:
                rope_apply("dve", kn, kf3, ropeA, t, 2, 64, [("kn", 0), ("kn", 1)], ["kf"],
                           tB[:, 0:128].rearrange("p (h d) -> p h d", h=2), tC[:, 0:128].rearrange("p (h d) -> p h d", h=2), ("tB", "k"), ("tC", "k"))
            else:
                R.op("dve", lambda e: e.tensor_copy(out=kf3, in_=kn), reads=[("kn", 0), ("kn", 1)], writes=["kf"])
            R.op("pe", lambda e: e.transpose(psb(7)[:, 0:128], kf[:], ident[:]), reads=["kf", "ident"], writes=[("ps", 7)])
            R.op("act", lambda e, t=t: e.copy(out=kT_g[:, t * 128:(t + 1) * 128], in_=psb(7)[:, 0:128]), reads=[("ps", 7)],
                 writes=[("kTg", t)])
            def trc(e):
                inst = None
                for c in range(2):
                    inst = e.transpose(psb(7)[:, 128 + c * 128:256 + c * 128], ckvb[:, c * 128:(c + 1) * 128], ident[:])
                return inst
            R.op("pe", trc, reads=["ckvb", "ident"], writes=[("ps", 7)])
            R.op("act", lambda e: e.copy(out=ckvT[:].rearrange("p a b -> p (a b)"), in_=psb(7)[:, 128:384]), reads=[("ps", 7)],
                 writes=["ckvT"])

            def mmkv(e):
                inst = None
                for (bank, c0, n) in ((4, 0, 512), (5, 512, 256)):
                    for c in range(2):
                        inst = e.matmul(ps[:, bank, 0:n], ckvT[:, c, :], wukv[:, c, c0:c0 + n], start=(c == 0), stop=(c == 1))
                return inst
            R.op("pe", mmkv, reads=["ckvT", "wukv"], writes=[("ps", 4), ("ps", 5)])
            R.op("act", lambda e: e.copy(out=kvraw[:, 0:512], in_=ps[:, 4, :]), reads=[("ps", 4)], writes=["kvraw"])
            R.op("act", lambda e: e.copy(out=kvraw[:, 512:768], in_=ps[:, 5, 0:256]), reads=[("ps", 5), "kvraw"], writes=["kvraw"])
            kv3 = kvraw[:].rearrange("p (h d) -> p h d", h=6)
            sumsq(kv3[:, :, 0:64], 6, 64, 16, scr, ["kvraw"], [("ss", "kn")])
            R.op("dve", lambda e: e.tensor_tensor(out=st_ss[:, 12:13], in0=st_r[:, 10:11], in1=st_r[:, 10:11], op=ALU.mult),
                 reads=[("sr", "ckv")], writes=[("ss", "b2")])
            R.op("dve", lambda e: e.tensor_scalar(out=st_ss[:, 16:22], in0=st_ss[:, 16:22], scalar1=st_ss[:, 12:13], scalar2=None, op0=ALU.mult),
                 reads=[("ss", "kn"), ("ss", "b2")], writes=[("ss", "kn")], sync_self=True)
            R.op("act", lambda e: e.activation(out=st_r[:, 16:22], in_=st_ss[:, 16:22], func=AF.Sqrt, scale=1.0 / 64, bias=epsb[:]),
                 reads=[("ss", "kn"), "eps"], writes=[("sr", "kn")])
            R.op("dve", lambda e: e.reciprocal(out=st_r[:, 16:22], in_=st_r[:, 16:22]), reads=[("sr", "kn")], writes=[("sr", "kn")])
            R.op("dve", lambda e: e.tensor_scalar(out=st_r[:, 16:22], in0=st_r[:, 16:22], scalar1=st_r[:, 10:11], scalar2=None, op0=ALU.mult),
                 reads=[("sr", "kn"), ("sr", "ckv")], writes=[("sr", "kn")], sync_self=True)
            for hh in range(6):
                R.op("dve", lambda e, hh=hh: e.scalar_tensor_tensor(out=kfull[:, hh, 0:64], in0=kv3[:, hh, 0:64], scalar=st_r[:, 16 + hh:17 + hh],
                                                                    in1=g_kn, op0=ALU.mult, op1=ALU.mult),
                     reads=["kvraw", ("sr", "kn"), "gsc"], writes=[("kfull", hh)], sync_self=(hh == 0))
            R.op("dve", lambda e, t=t: e.tensor_scalar(out=V_m[:, t, :].rearrange("p (h d) -> p h d", h=6), in0=kv3[:, :, 64:128],
                                                       scalar1=st_r[:, 10:11], scalar2=None, op0=ALU.mult),
                 reads=["kvraw", ("sr", "ckv")], writes=[("Vm", t)])
            krn = tA[:, 128:160].unsqueeze(1)
            R.op("dve", lambda e: e.scalar_tensor_tensor(out=tA[:, 128:160], in0=kraw[:, 512:544], scalar=st_r[:, 11:12], in1=g_kr,
                                                         op0=ALU.mult, op1=ALU.mult), reads=[krk, ("sr", "kr"), "gsc"], writes=["krn"])
            krf = tA[:, 160:192].unsqueeze(1)
            if lat:
                rope_apply("dve", krn, krf, ropeM, t, 1, 32, ["krn"], ["krf"], tB[:, 128:160].unsqueeze(1), tC[:, 128:160].unsqueeze(1), ("tB", "kr"), ("tC", "kr"))
                src_kr = krf
                krkey = "krf"
            else:
                src_kr = krn
                krkey = "krn"
            R.op("dve", lambda e, src_kr=src_kr: e.tensor_copy(out=kfull[:, :, 64:96], in_=src_kr.broadcast_to([128, 6, 32])),
                 reads=[krkey], writes=[("kfull", "r")])

            def trk(e):
                inst = None
                for hh in range(6):
                    inst = e.transpose(psb(6)[0:96, hh * 128:(hh + 1) * 128], kfull[:, hh, :], ident[:])
                return inst
            R.op("pe", trk, reads=[("kfull", hh) for hh in range(6)] + [("kfull", "r"), "ident"], writes=[("ps", 6)])
            R.op("act", lambda e, t=t: e.copy(out=kT_m[0:96, :, t * 128:(t + 1) * 128],
                                              in_=psb(6)[0:96, 0:768].rearrange("p (h n) -> p h n", h=6)),
                 reads=[("ps", 6)], writes=[("kTm", t)])
        streams = []
        for t in range(NT):
            R.capture_start()
            front(t)
            back(t)
            streams.append(R.capture_end())
        R.replay_zipped(streams, 0.5)
        cvt = scr[:, 0:512]
        for ch in range(2):
            segs = [(1 + 512 * i, 512) for i in range(4)] + [(2051, 256)]
            for (p0, n) in segs:
                R.op("dve", lambda e, ch=ch, p0=p0, n=n: e.tensor_scalar(out=cvt[:, 0:n], in0=uT[:, ch, p0:p0 + n], scalar1=convp[:, l, ch, 1:2],
                                                                         scalar2=convp[:, l, ch, 3:4], op0=ALU.mult, op1=ALU.add),
                     reads=["uT", "convp"], writes=["cvt"])
                R.op("dve", lambda e, ch=ch, p0=p0, n=n: e.scalar_tensor_tensor(out=cvt[:, 0:n], in0=uT[:, ch, p0 - 1:p0 - 1 + n],
                                                                                scalar=convp[:, l, ch, 0:1], in1=cvt[:, 0:n], op0=ALU.mult, op1=ALU.add),
                     reads=["uT", "convp", "cvt"], writes=["cvt"])
                R.op("dve", lambda e, ch=ch, p0=p0, n=n: e.scalar_tensor_tensor(out=cvt[:, 0:n], in0=uT[:, ch, p0 + 1:p0 + 1 + n],
                                                                                scalar=convp[:, l, ch, 2:3], in1=cvt[:, 0:n], op0=ALU.mult, op1=ALU.add),
                     reads=["uT", "convp", "cvt"], writes=["cvt"])
                R.op("dve", lambda e, ch=ch, p0=p0, n=n: e.tensor_tensor(out=bT[:, ch, p0:p0 + n], in0=bT[:, ch, p0:p0 + n], in1=cvt[:, 0:n],
                                                                         op=ALU.mult), reads=["bT", "cvt"], writes=["bT"])
        R.barrier()
        Ar.pos = markU

        wQ = anew(BF16, [KC, 768])
        wuq = anew(BF16, [3, 576])
        wo_b = anew(BF16, [KC, 512])
        hTq = [anew(BF16, [KC, 128])] * 2
        xnq = [anew(BF16, [D])] * 2
        qraw = anew(F32, [768])
        qmraw = qraw[:, 0:576]
        scrq = anew(F32, [576])
        tAq = anew(F32, [384])
        tBq = anew(F32, [384])
        tCq = anew(F32, [384])
        qf = anew(BF16, [384])
        cqb = anew(BF16, [384])
        cqT = anew(BF16, [3, 128])
        qmfull = anew(BF16, [6, 96])
        qT_g = anew(BF16, [3, 512])
        qT_m = anew(BF16, [6, 512])
        PT2 = [anew(BF16, [2, 512]) for _ in range(2)]
        mixT = anew(BF16, [6, 512])
        rden = scrq[:, 0:512]
        R.dma("pool", "wQ", lambda e: e.dma_start(out=wQ[:], in_=winq_d[l]), writes=["wQ"])
        R.dma("pool", "wuq", lambda e: e.dma_start(out=wuq[:], in_=wuq_d[l]), writes=["wuq"])
        for c in range(3):
            R.op("dve", lambda e, c=c: e.tensor_scalar(out=wuq[:, c, :], in0=wuq[:, c, :], scalar1=glat[:, l, c:c + 1], scalar2=None,
                                                       op0=ALU.mult), reads=["wuq", "glat"], writes=["wuq"])
        groups = [(4 * g, 4) for g in range(4)]
        if need_ctx:
            groups.append((16, 2))
        pt_i = 0
        st_i = 0
        o_i = 0
        for (t0, ntile) in groups:
            lat = t0 < NLT
            nq = ntile * 128
            r = 0 if lat else 1
            key_tiles = list(range(NT)) if lat else [16, 17]
            qstreams = []
            for lt in range(ntile):
                R.capture_start()
                t = t0 + lt
                h = hTq[0]
                hk = ("hTr", 0)
                emit_hT(t, h, hk, xnq[0], ("xn", 0), 6, t % 2)

                def mmQ(e, h=h):
                    inst = None
                    for (bank, c0) in ((4, 0), (5, 384)):
                        for kc in range(KC):
                            inst = e.matmul(ps[:, bank, 0:384], h[:, kc, :], wQ[:, kc, c0:c0 + 384], start=(kc == 0), stop=(kc == KC - 1))
                    return inst
                R.op("pe", mmQ, reads=[hk, "wQ"], writes=[("ps", 4), ("ps", 5)])
                R.op("act", lambda e: e.copy(out=qraw[:, 0:384], in_=ps[:, 4, 0:384]), reads=[("ps", 4)], writes=["qraw"])
                R.op("act", lambda e: e.copy(out=qraw[:, 384:768], in_=ps[:, 5, 0:384]), reads=[("ps", 5), "qraw"], writes=["qraw"])
                R.op("act", lambda e: e.copy(out=cqb[:], in_=qraw[:, 384:768]), reads=["qraw"], writes=["cqb"])
                q3 = qraw[:, 0:384].rearrange("p (h d) -> p h d", h=6)
                sumsq(q3, 6, 64, 24, scrq, ["qraw"], [("ss", "q")])
                sumsq(qraw[:, 384:768].unsqueeze(1), 1, 384, 30, scrq, ["qraw"], [("ss", "cq")])
                R.op("act", lambda e: e.activation(out=st_r[:, 24:30], in_=st_ss[:, 24:30], func=AF.Sqrt, scale=1.0 / 64, bias=epsb[:]),
                     reads=[("ss", "q"), "eps"], writes=[("sr", "q")])
                R.op("act", lambda e: e.activation(out=st_r[:, 30:31], in_=st_ss[:, 30:31], func=AF.Sqrt, scale=1.0 / 384, bias=epsb[:]),
                     reads=[("ss", "cq"), "eps"], writes=[("sr", "cq")])
                R.op("dve", lambda e: e.reciprocal(out=st_r[:, 24:31], in_=st_r[:, 24:31]), reads=[("sr", "q"), ("sr", "cq")],
                     writes=[("sr", "q"), ("sr", "cq")])
                qn = tAq[:].rearrange("p (h d) -> p h d", h=6)
                for hh in range(6):
                    R.op("dve", lambda e, hh=hh: e.scalar_tensor_tensor(out=qn[:, hh, :], in0=q3[:, hh, :], scalar=st_r[:, 24 + hh:25 + hh], in1=g_q,
                                                                        op0=ALU.mult, op1=ALU.mult),
                         reads=["qraw", ("sr", "q"), "gsc"], writes=[("qn", hh)], sync_self=(hh == 0))
                qf3 = qf[:].rearrange("p (h d) -> p h d", h=6)
                qnk = [("qn", hh) for hh in range(6)]
                if lat:
                    rope_apply("dve", qn, qf3, ropeA, t, 6, 64, qnk, ["qf"], tBq[:].rearrange("p (h d) -> p h d", h=6),
                               tCq[:].rearrange("p (h d) -> p h d", h=6), "tBq", "tCq")
                else:
                    R.op("dve", lambda e: e.tensor_copy(out=qf3, in_=qn), reads=qnk, writes=["qf"])

                def trq(e):
                    inst = None
                    for pr in range(3):
                        inst = e.transpose(psb(7)[:, pr * 128:(pr + 1) * 128], qf[:, pr * 128:(pr + 1) * 128], ident[:])
                    for c in range(3):
                        inst = e.transpose(psb(7)[:, 384 + c * 128:512 + c * 128], cqb[:, c * 128:(c + 1) * 128], ident[:])
                    return inst
                R.op("pe", trq, reads=["qf", "cqb", "ident"], writes=[("ps", 7)])
                R.op("act", lambda e, lt=lt: e.copy(out=qT_g[:, :, lt * 128:(lt + 1) * 128], in_=psb(7)[:, 0:384].rearrange("p (a b) -> p a b", a=3)),
                     reads=[("ps", 7)], writes=[("qTg", lt)])
                R.op("act", lambda e: e.copy(out=cqT[:].rearrange("p a b -> p (a b)"), in_=psb(7)[:, 384:768]), reads=[("ps", 7)], writes=["cqT"])

                def mmuq(e):
                    inst = None
                    for (bank, c0) in ((4, 0), (5, 288)):
                        for c in range(3):
                            inst = e.matmul(ps[:, bank, 0:288], cqT[:, c, :], wuq[:, c, c0:c0 + 288], start=(c == 0), stop=(c == 2))
                    return inst
                R.op("pe", mmuq, reads=["cqT", "wuq"], writes=[("ps", 4), ("ps", 5)])
                R.op("act", lambda e: e.copy(out=qmraw[:, 0:288], in_=ps[:, 4, 0:288]), reads=[("ps", 4)], writes=["qraw"])
                R.op("act", lambda e: e.copy(out=qmraw[:, 288:576], in_=ps[:, 5, 0:288]), reads=[("ps", 5), "qraw"], writes=["qraw"])
                qm3 = qmraw[:].rearrange("p (h d) -> p h d", h=6)
                sumsq(qm3[:, :, 0:64], 6, 64, 32, scrq, ["qraw"], [("ss", "qmn")])
                sumsq(qm3[:, :, 64:96], 6, 32, 38, scrq, ["qraw"], [("ss", "qmr")])
                R.op("dve", lambda e: e.tensor_tensor(out=st_ss[:, 31:32], in0=st_r[:, 30:31], in1=st_r[:, 30:31], op=ALU.mult),
                     reads=[("sr", "cq")], writes=[("ss", "a2")])
                R.op("dve", lambda e: e.tensor_scalar(out=st_ss[:, 32:44], in0=st_ss[:, 32:44], scalar1=st_ss[:, 31:32], scalar2=None, op0=ALU.mult),
                     reads=[("ss", "qmn"), ("ss", "qmr"), ("ss", "a2")], writes=[("ss", "qmn"), ("ss", "qmr")], sync_self=True)
                R.op("act", lambda e: e.activation(out=st_r[:, 32:38], in_=st_ss[:, 32:38], func=AF.Sqrt, scale=1.0 / 64, bias=epsb[:]),
                     reads=[("ss", "qmn"), "eps"], writes=[("sr", "qmn")])
                R.op("act", lambda e: e.activation(out=st_r[:, 38:44], in_=st_ss[:, 38:44], func=AF.Sqrt, scale=1.0 / 32, bias=epsb[:]),
                     reads=[("ss", "qmr"), "eps"], writes=[("sr", "qmr")])
                R.op("dve", lambda e: e.reciprocal(out=st_r[:, 32:44], in_=st_r[:, 32:44]), reads=[("sr", "qmn"), ("sr", "qmr")],
                     writes=[("sr", "qmn"), ("sr", "qmr")])
                R.op("dve", lambda e: e.tensor_scalar(out=st_r[:, 32:44], in0=st_r[:, 32:44], scalar1=st_r[:, 30:31], scalar2=None, op0=ALU.mult),
                     reads=[("sr", "qmn"), ("sr", "qmr"), ("sr", "cq")], writes=[("sr", "qmn"), ("sr", "qmr")], sync_self=True)
                qr = tAq[:, 0:192].rearrange("p (h d) -> p h d", h=6)
                for hh in range(6):
                    R.op("dve", lambda e, hh=hh: e.scalar_tensor_tensor(out=qmfull[:, hh, 0:64], in0=qm3[:, hh, 0:64], scalar=st_r[:, 32 + hh:33 + hh],
                                                                        in1=g_qn, op0=ALU.mult, op1=ALU.mult),
                         reads=["qraw", ("sr", "qmn"), "gsc"], writes=[("qmfull", hh)], sync_self=(hh == 0))
                    R.op("dve", lambda e, hh=hh: e.scalar_tensor_tensor(out=qr[:, hh, :], in0=qm3[:, hh, 64:96], scalar=st_r[:, 38 + hh:39 + hh],
                                                                        in1=g_qr, op0=ALU.mult, op1=ALU.mult),
                         reads=["qraw", ("sr", "qmr"), "gsc", "qf"] + qnk, writes=[("qr", hh)])
                qrk = [("qr", hh) for hh in range(6)]
                if lat:
                    rope_apply("dve", qr, qmfull[:, :, 64:96], ropeM, t, 6, 32, qrk, [("qmfull", "r")],
                               tBq[:, 0:192].rearrange("p (h d) -> p h d", h=6), tCq[:, 0:192].rearrange("p (h d) -> p h d", h=6), "tBq", "tCq")
                else:
                    R.op("dve", lambda e: e.tensor_copy(out=qmfull[:, :, 64:96], in_=qr), reads=qrk, writes=[("qmfull", "r")])

                def trqm(e):
                    inst = None
                    for hh in range(6):
                        inst = e.transpose(psb(7)[0:96, hh * 128:(hh + 1) * 128], qmfull[:, hh, :], ident[:])
                    return inst
                R.op("pe", trqm, reads=[("qmfull", hh) for hh in range(6)] + [("qmfull", "r"), "ident"], writes=[("ps", 7)])
                R.op("act", lambda e, lt=lt: e.copy(out=qT_m[0:96, :, lt * 128:(lt + 1) * 128],
                                                    in_=psb(7)[0:96, 0:768].rearrange("p (h n) -> p h n", h=6)),
                     reads=[("ps", 7)], writes=[("qTm", lt)])
                qstreams.append(R.capture_end())
            R.replay_zipped(qstreams, 0.7)
            qgk = [("qTg", lt) for lt in range(ntile)]
            qmk = [("qTm", lt) for lt in range(ntile)]
            npair = len(key_tiles) // 2
            items = [(slot, kj, key_tiles[2 * kj], key_tiles[2 * kj + 1]) for slot in range(12) for kj in range(npair)]
            LA = 1
            ST_PAIRS = [0, 6]

            def slot_info(slot):
                if slot < 6:
                    pr, hf = slot // 2, slot % 2
                    return True, pr, hf, pr, slot
                hm = slot - 6
                pr, hf = hm // 2, hm % 2
                return False, pr, hf, 3 + pr, hm

            def emit_st(i, nq=nq):
                slot, kj, kta, ktb = items[i]
                gqa, pr, hf, chunk, hm = slot_info(slot)
                b0 = ST_PAIRS[i % 2]
                pt = PT2[i % 2]
                ptk = ("PT", i % 2)

                def mmst(e, b0=b0, gqa=gqa, pr=pr, hf=hf, hm=hm, kta=kta, ktb=ktb, nq=nq):
                    inst = None
                    for k, kt in enumerate((kta, ktb)):
                        if gqa:
                            inst = e.matmul(ps[:, b0 + k, 0:nq], kT_g[64 * hf:64 * hf + 64, kt * 128:(kt + 1) * 128],
                                            qT_g[64 * hf:64 * hf + 64, pr, 0:nq], start=True, stop=True)
                        else:
                            inst = e.matmul(ps[:, b0 + k, 0:nq], kT_m[0:96, hm, kt * 128:(kt + 1) * 128], qT_m[0:96, hm, 0:nq],
                                            start=True, stop=True)
                    return inst
                kk = [("kTg", kta), ("kTg", ktb)] + qgk if gqa else [("kTm", kta), ("kTm", ktb)] + qmk
                R.op("pe", mmst, reads=kk, writes=[("ps", b0), ("ps", b0 + 1)])
                R.op("act", lambda e, b0=b0, pt=pt, nq=nq: e.activation(out=pt[:, :, 0:nq], in_=ps[:, b0:b0 + 2, 0:nq], func=AF.Exp),
                     reads=[("ps", b0), ("ps", b0 + 1)], writes=[ptk])

            def emit_pv(i, nq=nq):
                slot, kj, kta, ktb = items[i]
                gqa, pr, hf, chunk, hm = slot_info(slot)
                pt = PT2[i % 2]
                ptk = ("PT", i % 2)
                bo = 2 if (slot % 2 == 0) else 4
                bd = bo + 1
                if gqa:
                    vaps = [V_g[:, kta, :], V_g[:, ktb, :]]
                    vk = [("Vg", kta), ("Vg", ktb)]
                else:
                    vaps = [V_m[:, kta, pr * 128:(pr + 1) * 128], V_m[:, ktb, pr * 128:(pr + 1) * 128]]
                    vk = [("Vm", kta), ("Vm", ktb)]

                def mmpv(e, vaps=vaps, pt=pt, bo=bo, bd=bd, kj=kj, nq=nq, npair=npair):
                    inst = None
                    for k in range(2):
                        first = (kj == 0 and k == 0)
                        last = (kj == npair - 1 and k == 1)
                        e.matmul(ps[:, bo, 0:nq], vaps[k], pt[:, k, 0:nq], start=first, stop=last)
                        inst = e.matmul(ps[:, bd, 0:nq], ones_bf[:], pt[:, k, 0:nq], start=first, stop=last)
                    return inst
                R.op("pe", mmpv, reads=vk + [ptk, "ones"], writes=[("ps", bo), ("ps", bd)])
                if kj == npair - 1:
                    p0 = 64 * hf
                    R.op("dve", lambda e, bd=bd, p0=p0, nq=nq: e.reciprocal(out=rden[p0:p0 + 64, 0:nq], in_=ps[p0:p0 + 64, bd, 0:nq]),
                         reads=[("ps", bd)], writes=["rden"])
                    R.op("dve", lambda e, bo=bo, p0=p0, chunk=chunk, nq=nq: e.tensor_tensor(
                        out=mixT[p0:p0 + 64, chunk, 0:nq], in0=ps[p0:p0 + 64, bo, 0:nq], in1=rden[p0:p0 + 64, 0:nq], op=ALU.mult),
                        reads=[("ps", bo), "rden"], writes=[("mixT", chunk, hf)])

            for i in range(len(items) + LA):
                if i < len(items):
                    emit_st(i)
                if i >= LA:
                    emit_pv(i - LA)
            mixk = [("mixT", c, hf) for c in range(6) for hf in range(2)]
            for hd in range(2):
                R.dma("pool", "wo", lambda e, hd=hd: e.dma_start(out=wo_b[:], in_=wout_d[l, hd]), writes=["wo"])
                for lt in range(ntile):
                    t = t0 + lt
                    pos = (1 + t * 128) if lat else (2051 + (t - NLT) * 128)
                    bk = 6 + (lt % 2)

                    def mmo(e, lt=lt, pos=pos, bk=bk):
                        inst = None
                        for c in range(8):
                            if c < 3:
                                lh = mixT[:, c, lt * 128:(lt + 1) * 128]
                            elif c < 5:
                                lh = bT[:, c - 3, pos:pos + 128]
                            else:
                                lh = mixT[:, c - 2, lt * 128:(lt + 1) * 128]
                            inst = e.matmul(ps[:, bk, :], lh, wo_b[:, c, :], start=(c == 0), stop=(c == 7))
                        return inst
                    R.op("pe", mmo, reads=mixk + ["bT", "wo"], writes=[("ps", bk)])
                    R.op("dve", lambda e, bk=bk, r=r, hd=hd: e.tensor_tensor(out=ps[:, bk, :], in0=ps[:, bk, :],
                                                                             in1=gate_bc[:, r, hd * 512:(hd + 1) * 512], op=ALU.mult),
                         reads=[("ps", bk), ("gate", r)], writes=[("ps", bk)])
                    R.op("dve", lambda e, bk=bk, t=t, hd=hd: e.tensor_tensor(out=xs[:, t, hd * 512:(hd + 1) * 512],
                                                                             in0=xs[:, t, hd * 512:(hd + 1) * 512], in1=ps[:, bk, :], op=ALU.add),
                         reads=[("ps", bk), ("x", t)], writes=[("x", t)])
        R.barrier()
        Ar.pos = mark0

    done = False
    for l in range(L):
        if l > 0:
            R.new_epoch()
        need_ctx = l < DEPTH - 1
        if l == 0:
            modulation(l)
        ffn(l, 0, 0, list(range(NT)))
        if stop_after == (l, "ffn1"):
            break
        mixer(l, need_ctx)
        if stop_after == (l, "mix"):
            break
        ffn(l, 1, 2, list(range(NT)) if need_ctx else list(range(NLT)), hook=(make_mod_hook(l + 1) if l + 1 < L else None))
        if stop_after == (l, "ffn2"):
            break

    for i in range(4):
        R.dma("sp", "out", lambda e, i=i: e.dma_start(
            out=out_d[512 * i:512 * (i + 1), :].rearrange("(t p) d -> p t d", p=128), in_=xs[:, 4 * i:4 * i + 4, :]),
            reads=[("x", 4 * i + k) for k in range(4)])
    R.op("sp", lambda e: None, reads=[], writes=[("x", k) for k in range(NLT)])

    R.finalize()
    esems = {}
    for e in Rec.ENG:
        for ep in range(R.n_epochs):
            esems[(e, ep)] = es.enter_context(nc.semaphore("s_%s_%d" % (e, ep)))
    lsems = {ln: es.enter_context(nc.semaphore("l_%s" % ln)) for ln in R.lane_cnt}
    block = es.enter_context(nc.Block())

    @block.tensor
    def _(e):
        R.emit("pe", e, esems, lsems)

    @block.scalar
    def _(e):
        R.emit("act", e, esems, lsems)

    @block.vector
    def _(e):
        R.emit("dve", e, esems, lsems)

    @block.gpsimd
    def _(e):
        R.emit("pool", e, esems, lsems)

    @block.sync
    def _(e):
        R.emit("sp", e, esems, lsems)

    es.close()
    return nc


_CACHE = {}


def kernel(**inputs):
    inp = {k: np.asarray(v) for k, v in inputs.items()}
    if "nc" not in _CACHE:
        _CACHE["nc"] = build_program(DEPTH)
    nc = _CACHE["nc"]
    sh = prep_shared(inp, DEPTH)
    in_maps = []
    for b in range(8):
        m = dict(sh)
        m.update(prep_core(inp, b))
        in_maps.append(m)
    res = run_bass_kernel_spmd(nc, in_maps, core_ids=list(range(8)))
    out = np.stack([np.asarray(r["out"]) for r in res.results], axis=0)
    return out.astype(np.float32)
```

```python
import numpy as np
from contextlib import ExitStack
import concourse.bass as bass
import concourse.mybir as mybir
from concourse.bass_utils import run_bass_kernel_spmd

F32 = mybir.dt.float32
BF16 = mybir.dt.bfloat16
U8 = mybir.dt.uint8
ALU = mybir.AluOpType
AF = mybir.ActivationFunctionType
AX = mybir.AxisListType

D = 1024
DEPTH = 4
NT = 18
NLT = 16
T = NT * 128
DFF = 2816
NFC = 22
EPS = 1e-6
KC = 8
NGAIN = 320
FF_GROUPS = [4, 4, 4, 4, 4, 2]


class Rec:
    ENG = ("pe", "act", "dve", "pool", "sp")

    def __init__(self):
        self.ops = {e: [] for e in self.ENG}
        self.res = {}
        self.lane_cnt = {}
        self.epoch_starts = {e: [0] for e in self.ENG}
        self.pending = {e: [] for e in self.ENG}
        self.cap = None

    def capture_start(self):
        self.cap = []

    def capture_end(self):
        c = self.cap
        self.cap = None
        return c

    def replay(self, item):
        if item[0] == "op":
            self.op(item[1], item[2], item[3], item[4], item[5])
        else:
            self.dma(item[1], item[2], item[3], item[4], item[5])

    def replay_zipped(self, streams, frac=0.5):
        if not streams:
            return
        H = max(1, int(frac * max(len(st) for st in streams)))
        keyed = []
        for k, st in enumerate(streams):
            for j, it in enumerate(st):
                keyed.append((k * H + j, k, j, it))
        keyed.sort(key=lambda z: (z[0], z[1], z[2]))
        for _, _, _, it in keyed:
            self.replay(it)

    def new_epoch(self):
        for e in self.ENG:
            self.epoch_starts[e].append(len(self.ops[e]))

    def _deps(self, reads, writes):
        d = []
        for k in reads:
            st = self.res.get(k)
            if st is not None and st[0] is not None:
                d.append(st[0])
        for k in writes:
            st = self.res.get(k)
            if st is not None:
                if st[0] is not None:
                    d.append(st[0])
                for kk, v in st[1].items():
                    d.append(kk + (v,))
        return d

    def _commit(self, tok, reads, writes):
        for k in reads:
            st = self.res.get(k)
            if st is None:
                st = [None, {}]
                self.res[k] = st
            key = tok[:2]
            if st[1].get(key, -1) < tok[2]:
                st[1][key] = tok[2]
        for k in writes:
            self.res[k] = [tok, {}]

    def op(self, eng, fn, reads=(), writes=(), sync_self=False):
        if self.cap is not None:
            self.cap.append(("op", eng, fn, tuple(reads), tuple(writes), sync_self))
            return None
        deps = self._deps(reads, writes) + self.pending[eng]
        self.pending[eng] = []
        idx = len(self.ops[eng])
        tok = ("e", eng, idx)
        self.ops[eng].append({"fn": fn, "deps": deps, "lane": None, "ss": sync_self})
        self._commit(tok, reads, writes)
        return tok

    def dma(self, eng, lane, fn, reads=(), writes=()):
        if self.cap is not None:
            self.cap.append(("dma", eng, lane, fn, tuple(reads), tuple(writes)))
            return None
        deps = self._deps(reads, writes) + self.pending[eng]
        self.pending[eng] = []
        self.lane_cnt[lane] = self.lane_cnt.get(lane, 0) + 1
        tok = ("d", lane, self.lane_cnt[lane])
        self.ops[eng].append({"fn": fn, "deps": deps, "lane": lane, "ss": True})
        self._commit(tok, reads, writes)
        return tok

    def barrier(self):
        last = []
        for e in self.ENG:
            for i in range(len(self.ops[e]) - 1, -1, -1):
                if self.ops[e][i]["lane"] is None:
                    last.append(("e", e, i))
                    break
        for l, c in self.lane_cnt.items():
            last.append(("d", l, c))
        for e in self.ENG:
            self.pending[e] = self.pending[e] + list(last)

    def finalize(self):
        self.signal = {e: [False] * len(self.ops[e]) for e in self.ENG}
        for e in self.ENG:
            for i, op in enumerate(self.ops[e]):
                for tok in op["deps"]:
                    if tok[0] == "e" and (tok[1] != e or op["ss"] or tok[2] >= i - 2):
                        self.signal[tok[1]][tok[2]] = True
        self.sigval = {}
        self.epoch_of = {}
        for e in self.ENG:
            starts = self.epoch_starts[e]
            vals = [None] * len(self.ops[e])
            eps = [0] * len(self.ops[e])
            ep = 0
            cnt = 0
            for i in range(len(self.ops[e])):
                while ep + 1 < len(starts) and i >= starts[ep + 1]:
                    ep += 1
                    cnt = 0
                if self.signal[e][i]:
                    cnt += 1
                    vals[i] = cnt
                eps[i] = ep
            self.sigval[e] = vals
            self.epoch_of[e] = eps
        self.n_epochs = max(len(s) for s in self.epoch_starts.values())

    def emit(self, eng, e, esems, lsems):
        waited = {}
        for i, op in enumerate(self.ops[eng]):
            need = {}
            for tok in op["deps"]:
                if tok[0] == "e":
                    if tok[1] == eng and not op["ss"] and tok[2] < i - 2:
                        continue
                    key = ("e", tok[1], self.epoch_of[tok[1]][tok[2]])
                    val = self.sigval[tok[1]][tok[2]]
                else:
                    key = ("d", tok[1])
                    val = 16 * tok[2]
                if need.get(key, 0) < val:
                    need[key] = val
            for key, val in need.items():
                if waited.get(key, 0) >= val:
                    continue
                if key[0] == "e":
                    later = [k for k in waited if k[0] == "e" and k[1] == key[1] and k[2] > key[2]]
                    if later:
                        continue
                    e.wait_ge(esems[(key[1], key[2])], val)
                else:
                    e.wait_ge(lsems[key[1]], val)
                waited[key] = val
            inst = op["fn"](e)
            if inst is None:
                continue
            if op["lane"] is not None:
                inst.then_inc(lsems[op["lane"]], 16)
            elif self.signal[eng][i]:
                inst.then_inc(esems[(eng, self.epoch_of[eng][i])], 1)


Q_ORDER = [0, 3, 1, 4, 2, 5]


def _rope_np(dim):
    rows = 2048 // 64
    row = np.repeat(np.arange(rows, dtype=np.float64), 64)
    col = np.tile(np.arange(64, dtype=np.float64), rows)
    half = dim // 2
    inv = 1.0 / (10000.0 ** (np.arange(0, half, 2, dtype=np.float64) / half))
    ar = row[:, None] * inv[None, :]
    ac = col[:, None] * inv[None, :]
    ang = np.concatenate([ar, ar, ac, ac], axis=-1)
    cos = np.cos(ang).astype(np.float32)
    sin = np.sin(ang).astype(np.float32)
    q = dim // 4
    sgn = np.concatenate([-np.ones(q), np.ones(q), -np.ones(q), np.ones(q)]).astype(np.float32)
    sin = sin * sgn[None, :]
    cos = np.ascontiguousarray(cos.reshape(16, 128, dim).transpose(1, 0, 2))
    sin = np.ascontiguousarray(sin.reshape(16, 128, dim).transpose(1, 0, 2))
    return cos, sin


def prep_shared(inp, n_layers):
    L = n_layers
    f = lambda a: np.ascontiguousarray(a, dtype=np.float32)
    sh = {}
    wm = inp["w_mod"][:L]
    sh["wmod"] = f(wm.reshape(L, KC, 128, 18, 512).transpose(0, 3, 2, 1, 4))
    sh["bmodT"] = f(inp["b_mod"][:L].reshape(L, 72, 128).transpose(2, 0, 1))
    sh["gn"] = f(inp["g_norm"][:L].reshape(L, 3, KC, 128).transpose(3, 0, 1, 2))
    wg = inp["ffn_w_gate"][:L].reshape(L, 2, KC, 128, NFC, 128)
    wu = inp["ffn_w_up"][:L].reshape(L, 2, KC, 128, NFC, 128)
    wgu = np.stack([wg, wu], axis=0)
    sh["wgu"] = f(wgu.transpose(1, 2, 5, 4, 0, 3, 6))
    sh["wd"] = f(inp["ffn_w_down"][:L])
    win = inp["w_in"][:L]
    qcols = np.concatenate([np.arange(64 * h, 64 * h + 64) for h in Q_ORDER])
    kside = np.concatenate([np.arange(384, 512), np.arange(512, 640), np.arange(1792, 2048),
                            np.arange(2048, 2080),
                            np.arange(640, 896), np.arange(1152, 1408), np.arange(896, 1152)])
    qside = np.concatenate([qcols, np.arange(1408, 1792)])
    sh["wink"] = f(win[:, :, kside].reshape(L, KC, 128, 1312).transpose(0, 2, 1, 3))
    sh["winq"] = f(win[:, :, qside].reshape(L, KC, 128, 768).transpose(0, 2, 1, 3))
    sh["wuq"] = f(inp["mla_w_uq"][:L].reshape(L, 3, 128, 576).transpose(0, 2, 1, 3))
    sh["wukv"] = f(inp["mla_w_ukv"][:L].reshape(L, 2, 128, 768).transpose(0, 2, 1, 3))
    rows = []
    for pr in range(3):
        for hf in range(2):
            h = Q_ORDER[2 * pr + hf]
            rows.append(np.arange(64 * h, 64 * h + 64))
    rows.append(np.arange(384, 640))
    rows.append(np.arange(640, 1024))
    rows = np.concatenate(rows)
    wo = inp["w_out"][:L][:, rows, :]
    sh["wout"] = f(wo.reshape(L, KC, 128, 2, 512).transpose(0, 3, 2, 1, 4))
    sh["gains"] = f(np.concatenate([inp["gqa_g_q"][:L], inp["gqa_g_k"][:L], inp["mla_g_qn"][:L],
                                    inp["mla_g_kn"][:L], inp["mla_g_qr"][:L], inp["mla_g_kr"][:L]], axis=1))
    cw = inp["conv_w"][:L]
    cb = inp["conv_b"][:L]
    cp = np.concatenate([cw, cb[:, None, :]], axis=1)
    sh["convp"] = f(cp.reshape(L, 4, 2, 128).transpose(3, 0, 2, 1))
    gl = np.concatenate([inp["mla_g_cq"][:L].reshape(L, 3, 128), inp["mla_g_ckv"][:L].reshape(L, 2, 128)], axis=1)
    sh["glat"] = f(gl.transpose(2, 0, 1))
    ca, sa = _rope_np(64)
    cm, sm = _rope_np(32)
    sh["ropeA"] = f(np.stack([ca, sa], axis=1))
    sh["ropeM"] = f(np.stack([cm, sm], axis=1))
    return sh


def prep_core(inp, b):
    xin = np.concatenate([inp["x"][b], inp["ctx"][b]], axis=0)
    cc = np.stack([inp["c"][b], inp["c_ctx"]], axis=-1)
    ccT = np.ascontiguousarray(cc.reshape(KC, 128, 2).transpose(1, 0, 2), dtype=np.float32)
    return {"xin": np.ascontiguousarray(xin, dtype=np.float32), "ccT": ccT}


def build_program(n_layers=DEPTH, stop_after=None):
    L = n_layers
    nc = bass.Bass("TRN2", target_bir_lowering=False)
    dt_in = lambda name, shape: nc.dram_tensor(name, list(shape), F32, kind="ExternalInput").ap()
    xin = dt_in("xin", [T, D])
    ccT_d = dt_in("ccT", [128, KC, 2])
    wmod_d = dt_in("wmod", [L, 18, 128, KC, 512])
    bmodT_d = dt_in("bmodT", [128, L, 72])
    gn_d = dt_in("gn", [128, L, 3, KC])
    wgu_d = dt_in("wgu", [L, 2, NFC, 128, 2, KC, 128])
    wd_d = dt_in("wd", [L, 2, DFF, D])
    wink_d = dt_in("wink", [L, 128, KC, 1312])
    winq_d = dt_in("winq", [L, 128, KC, 768])
    wuq_d = dt_in("wuq", [L, 128, 3, 576])
    wukv_d = dt_in("wukv", [L, 128, 2, 768])
    wout_d = dt_in("wout", [L, 2, 128, KC, 512])
    gains_d = dt_in("gains", [L, NGAIN])
    convp_d = dt_in("convp", [128, L, 2, 4])
    glat_d = dt_in("glat", [128, L, 5])
    ropeA_d = dt_in("ropeA", [128, 2, 16, 64])
    ropeM_d = dt_in("ropeM", [128, 2, 16, 32])
    out_d = nc.dram_tensor("out", [2048, D], F32, kind="ExternalOutput").ap()

    R = Rec()
    es = ExitStack()
    sb = lambda name, shape, dt: es.enter_context(nc.sbuf_tensor(name, list(shape), dt))
    xs = sb("xs", [128, NT, D], F32)
    ropeA = sb("ropeA_s", [128, 2, 16, 64], BF16)
    ropeM = sb("ropeM_s", [128, 2, 16, 32], BF16)
    ident = sb("ident", [128, 128], BF16)
    identf = sb("identf", [128, 128], F32)
    ones_bf = sb("ones_bf", [128, 128], BF16)
    ccs = sb("ccs", [128, KC, 2], F32)
    s2 = sb("s2", [128, KC, 2], BF16)
    gn = sb("gn_s", [128, L, 3, KC], F32)
    bmodT = sb("bmodT_s", [128, L, 72], F32)
    convp = sb("convp_s", [128, L, 2, 4], F32)
    glat = sb("glat_s", [128, L, 5], F32)
    gains = sb("gains_s", [128, NGAIN], F32)
    modTs = [sb("modT0", [128, 72, 2], F32), sb("modT1", [128, 72, 2], F32)]
    Amod = sb("Amod", [128, KC, 2], F32)
    Bmod = sb("Bmod", [128, KC, 2], F32)
    gate_bc = sb("gate_bc", [128, 2, D], BF16)
    epsb = sb("epsb", [128, 1], F32)
    st_ss = sb("st_ss", [128, 64], F32)
    st_r = sb("st_r", [128, 64], F32)
    ARENA_BYTES = 120 * 1024
    arena = sb("arena", [128, ARENA_BYTES], U8)
    ps = es.enter_context(nc.psum_tensor("ps", [128, 8, 512], F32))

    class Ar:
        pos = 0

    def aalloc(nbytes):
        a0 = (Ar.pos + 31) // 32 * 32
        Ar.pos = a0 + nbytes
        assert Ar.pos <= ARENA_BYTES, ("arena overflow", Ar.pos)
        return a0

    def aview(a0, dt, shape):
        esz = 2 if dt == BF16 else 4
        n = int(np.prod(shape))
        v = arena[:, a0:a0 + n * esz].bitcast(dt)
        if len(shape) == 1:
            return v
        names = " ".join("a%d" % i for i in range(len(shape)))
        kw = {"a%d" % i: shape[i] for i in range(1, len(shape))}
        return v.rearrange("p (%s) -> p %s" % (names, names), **kw)

    def anew(dt, shape):
        esz = 2 if dt == BF16 else 4
        return aview(aalloc(int(np.prod(shape)) * esz), dt, shape)

    psb = lambda b: ps[:, b, :].bitcast(BF16)

    for i in range(6):
        R.dma("sp", "xin%d" % i, lambda e, i=i: e.dma_start(
            out=xs[:, 3 * i:3 * i + 3, :], in_=xin[384 * i:384 * (i + 1), :].rearrange("(t p) d -> p t d", p=128)),
            writes=[("x", 3 * i), ("x", 3 * i + 1), ("x", 3 * i + 2)])
    R.dma("sp", "cst0", lambda e: e.dma_start(out=ccs[:], in_=ccT_d), writes=["ccs"])
    R.dma("sp", "cst1", lambda e: e.dma_start(out=gn[:], in_=gn_d), writes=["gn"])
    R.dma("sp", "cst2", lambda e: e.dma_start(out=bmodT[:], in_=bmodT_d), writes=["bmodT"])
    R.dma("sp", "cst3", lambda e: e.dma_start(out=convp[:], in_=convp_d), writes=["convp"])
    R.dma("sp", "cst4", lambda e: e.dma_start(out=glat[:], in_=glat_d), writes=["glat"])
    R.dma("pool", "cstA", lambda e: e.dma_start(out=ropeA[:], in_=ropeA_d), writes=["ropeA"])
    R.dma("pool", "cstM", lambda e: e.dma_start(out=ropeM[:], in_=ropeM_d), writes=["ropeM"])
    R.op("pool", lambda e: e.memset(identf[:], 0.0), writes=["identf"])
    R.op("pool", lambda e: e.affine_select(out=identf[:], in_=identf[:], pattern=[[-1, 128]],
                                           compare_op=ALU.not_equal, fill=1.0, base=0, channel_multiplier=1),
         reads=["identf"], writes=["identf"])
    R.op("pool", lambda e: e.tensor_copy(out=ident[:], in_=identf[:]), reads=["identf"], writes=["ident"])
    R.op("pool", lambda e: e.memset(ones_bf[:], 1.0), writes=["ones"])
    R.op("pool", lambda e: e.memset(epsb[:], EPS), writes=["eps"])
    R.op("act", lambda e: e.activation(out=s2[:], in_=ccs[:], func=AF.Silu), reads=["ccs"], writes=["s2"])

    def rstd_from_ss(n, scale, key):
        R.op("act", lambda e: e.activation(out=st_r[:, 0:n], in_=st_ss[:, 0:n], func=AF.Sqrt, scale=scale, bias=epsb[:]),
             reads=[("ss", key), "eps"], writes=[("sr", key)])
        R.op("dve", lambda e: e.reciprocal(out=st_r[:, 0:n], in_=st_r[:, 0:n]), reads=[("sr", key)], writes=[("sr", key)])

    def modulation(l):
        mark = Ar.pos
        ring = [anew(BF16, [KC, 512]) for _ in range(4)]
        for c in range(18):
            s = c % 4
            R.dma("pool", "wm%d" % s, lambda e, c=c, s=s: e.dma_start(out=ring[s][:], in_=wmod_d[l, c]),
                  writes=[("wmring", s)])

            def mm(e, c=c, s=s):
                inst = None
                for fc in range(4):
                    col = (c * 4 + fc) * 2
                    for kc in range(KC):
                        inst = e.matmul(ps[:, 0, col:col + 2], ring[s][:, kc, fc * 128:(fc + 1) * 128], s2[:, kc, :],
                                        start=(kc == 0), stop=(kc == KC - 1))
                return inst
            R.op("pe", mm, reads=[("wmring", s), "s2"], writes=[("ps", 0)])
        modT = modTs[l % 2]
        R.op("dve", lambda e: e.tensor_tensor(
            out=modT[:], in0=ps[:, 0, 0:144].rearrange("p (c r) -> p c r", r=2),
            in1=bmodT[:, l, :].unsqueeze(2).broadcast_to([128, 72, 2]), op=ALU.add),
            reads=[("ps", 0), "bmodT"], writes=[("modT", l % 2)])
        R.barrier()
        Ar.pos = mark

    def sub_modulation(l, j, gate_mult):
        modT = modTs[l % 2]
        mk = ("modT", l % 2)
        c_shift, c_scale, c_gate = (3 * j) * 8, (3 * j + 1) * 8, (3 * j + 2) * 8
        R.op("dve", lambda e: e.tensor_scalar(out=Amod[:], in0=modT[:, c_scale:c_scale + 8, :], scalar1=1.0, scalar2=None,
                                              op0=ALU.add), reads=[mk], writes=["Amod"])
        R.op("dve", lambda e: e.tensor_tensor(out=Amod[:], in0=Amod[:], in1=gn[:, l, j, :].unsqueeze(2).broadcast_to([128, KC, 2]),
                                              op=ALU.mult), reads=["Amod", "gn"], writes=["Amod"])
        R.op("dve", lambda e: e.tensor_copy(out=Bmod[:], in_=modT[:, c_shift:c_shift + 8, :]), reads=[mk], writes=["Bmod"])
        mark = Ar.pos
        rep = anew(BF16, [KC, 128])
        for r in range(2):
            R.op("dve", lambda e, r=r: e.tensor_scalar(
                out=rep[:], in0=modT[:, c_gate:c_gate + 8, r:r + 1].broadcast_to([128, KC, 128]),
                scalar1=gate_mult, scalar2=None, op0=ALU.mult), reads=[mk], writes=["rep"])

            def tr(e):
                inst = None
                for kc in range(KC):
                    inst = e.transpose(psb(1)[:, kc * 128:(kc + 1) * 128], rep[:, kc, :], ident[:])
                return inst
            R.op("pe", tr, reads=["rep", "ident"], writes=[("ps", 1)])
            R.op("act", lambda e, r=r: e.copy(out=gate_bc[:, r, :], in_=psb(1)[:, 0:1024]), reads=[("ps", 1)],
                 writes=[("gate", r)])
        R.barrier()
        Ar.pos = mark

    def emit_hT(t, dst, dst_key, xn, xn_key, psbank, stat_col):
        r = 0 if t < NLT else 1
        R.op("act", lambda e: e.activation(out=xn[:], in_=xs[:, t, :], func=AF.Square, accum_out=st_ss[:, stat_col:stat_col + 1]),
             reads=[("x", t)], writes=[xn_key, ("ss", "h%d" % stat_col)])
        R.op("act", lambda e: e.activation(out=st_r[:, stat_col:stat_col + 1], in_=st_ss[:, stat_col:stat_col + 1], func=AF.Sqrt,
                                           scale=1.0 / D, bias=epsb[:]),
             reads=[("ss", "h%d" % stat_col), "eps"], writes=[("sr", "h%d" % stat_col)], sync_self=True)
        R.op("dve", lambda e: e.reciprocal(out=st_r[:, stat_col:stat_col + 1], in_=st_r[:, stat_col:stat_col + 1]),
             reads=[("sr", "h%d" % stat_col)], writes=[("sr", "h%d" % stat_col)])
        R.op("dve", lambda e: e.tensor_scalar(out=xn[:], in0=xs[:, t, :], scalar1=st_r[:, stat_col:stat_col + 1], scalar2=None,
                                              op0=ALU.mult), reads=[("x", t), ("sr", "h%d" % stat_col), xn_key], writes=[xn_key], sync_self=True)

        def tr(e):
            inst = None
            for kc in range(KC):
                inst = e.transpose(psb(psbank)[:, kc * 128:(kc + 1) * 128], xn[:, kc * 128:(kc + 1) * 128], ident[:])
            return inst
        R.op("pe", tr, reads=[xn_key, "ident"], writes=[("ps", psbank)])
        for kc in range(KC):
            R.op("act", lambda e, kc=kc: e.activation(out=dst[:, kc, :], in_=psb(psbank)[:, kc * 128:(kc + 1) * 128],
                                                      func=AF.Identity, scale=Amod[:, kc, r:r + 1], bias=Bmod[:, kc, r:r + 1]),
                 reads=[("ps", psbank), "Amod", "Bmod"], writes=[dst_key])

    def make_mod_hook(ln):
        state = {"ring": None}

        def hook(c):
            if c >= 18:
                return
            if state["ring"] is None:
                state["ring"] = [anew(BF16, [KC, 512]) for _ in range(3)]
            ring = state["ring"]
            s_ = c % 3
            modT = modTs[ln % 2]
            R.dma("pool", "wmh%d" % s_, lambda e: e.dma_start(out=ring[s_][:], in_=wmod_d[ln, c]), writes=[("wmhring", s_)])

            def mm(e):
                inst = None
                for fc in range(4):
                    for kc in range(KC):
                        inst = e.matmul(ps[:, 0, 2 * fc:2 * fc + 2], ring[s_][:, kc, fc * 128:(fc + 1) * 128], s2[:, kc, :],
                                        start=(kc == 0), stop=(kc == KC - 1))
                return inst
            R.op("pe", mm, reads=[("wmhring", s_), "s2"], writes=[("ps", 0)])
            R.op("dve", lambda e: e.tensor_tensor(
                out=modT[:, 4 * c:4 * c + 4, :], in0=ps[:, 0, 0:8].rearrange("p (c r) -> p c r", r=2),
                in1=bmodT[:, ln, 4 * c:4 * c + 4].unsqueeze(2).broadcast_to([128, 4, 2]), op=ALU.add),
                reads=[("ps", 0), "bmodT", ("modT", ln % 2)], writes=[("modT", ln % 2)])
        return hook

    def ffn(l, f, j, tiles, hook=None):
        ntl = len(tiles)
        sub_modulation(l, j, 0.5)
        mark = Ar.pos
        hT = anew(BF16, [KC, T])
        aT = anew(BF16, [4, T])
        gu_ring = [anew(BF16, [2, KC, 128]) for _ in range(3)]
        d_ring = [anew(BF16, [D]) for _ in range(8)]
        sil = [anew(BF16, [512]) for _ in range(2)]
        xn = [anew(BF16, [D]) for _ in range(2)]
        junk = anew(BF16, [D])
        NB = 44
        for i, t in enumerate(tiles):
            R.op("act", lambda e, i=i, t=t: e.activation(out=junk[:], in_=xs[:, t, :], func=AF.Square, accum_out=st_ss[:, NB + i:NB + i + 1]),
                 reads=[("x", t)], writes=["junkN", ("ss", "N")])
        R.op("act", lambda e: e.activation(out=st_r[:, NB:NB + ntl], in_=st_ss[:, NB:NB + ntl], func=AF.Sqrt, scale=1.0 / D, bias=epsb[:]),
             reads=[("ss", "N"), "eps"], writes=[("sr", "N")], sync_self=True)
        R.op("dve", lambda e: e.reciprocal(out=st_r[:, NB:NB + ntl], in_=st_r[:, NB:NB + ntl]), reads=[("sr", "N")], writes=[("sr", "N")])
        for i, t in enumerate(tiles):
            r_ = 0 if t < NLT else 1
            xb = xn[i % 2]
            xk = ("xn", i % 2)
            pb = i % 2
            R.op("dve", lambda e, i=i, t=t, xb=xb: e.tensor_scalar(out=xb[:], in0=xs[:, t, :], scalar1=st_r[:, NB + i:NB + i + 1], scalar2=None,
                                                                 op0=ALU.mult), reads=[("x", t), ("sr", "N"), xk], writes=[xk], sync_self=(i == 0))

            def trN(e, xb=xb, pb=pb):
                inst = None
                for kc in range(KC):
                    inst = e.transpose(psb(pb)[:, kc * 128:(kc + 1) * 128], xb[:, kc * 128:(kc + 1) * 128], ident[:])
                return inst
            R.op("pe", trN, reads=[xk, "ident"], writes=[("ps", pb)])
            for kc in range(KC):
                R.op("act", lambda e, kc=kc, t=t, pb=pb, r_=r_: e.activation(
                    out=hT[:, kc, t * 128:(t + 1) * 128], in_=psb(pb)[:, kc * 128:(kc + 1) * 128],
                    func=AF.Identity, scale=Amod[:, kc, r_:r_ + 1], bias=Bmod[:, kc, r_:r_ + 1]),
                    reads=[("ps", pb), "Amod", "Bmod"], writes=[("hT", t)])
        tgs = []
        i = 0
        while i < ntl:
            n = min(4, ntl - i)
            tgs.append((tiles[i], n))
            i += n
        cbase = 0
        gu_cnt = 0
        d_cnt = 0
        ps_g = [2, 3]
        ps_u = [4, 5]
        gu_i = 0
        y_i = 0
        for gsz in FF_GROUPS:
            for ci in range(gsz):
                c = cbase + ci
                s = gu_cnt % 3
                R.dma("pool", "gu%d" % s, lambda e, c=c, s=s: e.dma_start(out=gu_ring[s][:], in_=wgu_d[l, f, c]),
                      writes=[("guring", s)])
                sd = d_cnt % 8
                R.dma("pool", "wd%d" % sd, lambda e, c=c, sd=sd: e.dma_start(out=d_ring[sd][:], in_=wd_d[l, f, c * 128:(c + 1) * 128, :]),
                      writes=[("dring", sd)])
                for (t0, n) in tgs:
                    ntok = n * 128
                    tok0 = t0 * 128
                    bg = ps_g[gu_i % 2]
                    bu = ps_u[gu_i % 2]
                    sl = sil[gu_i % 2]
                    gu_i += 1
                    hkeys = [("hT", t0 + k) for k in range(n)]

                    def mm(e, s=s, bg=bg, bu=bu, tok0=tok0, ntok=ntok):
                        inst = None
                        for which, bank in ((0, bg), (1, bu)):
                            for kc in range(KC):
                                inst = e.matmul(ps[:, bank, 0:ntok], gu_ring[s][:, which, kc, :], hT[:, kc, tok0:tok0 + ntok],
                                                start=(kc == 0), stop=(kc == KC - 1))
                        return inst
                    R.op("pe", mm, reads=[("guring", s)] + hkeys, writes=[("ps", bg), ("ps", bu)])
                    R.op("act", lambda e, bg=bg, sl=sl, ntok=ntok: e.activation(out=sl[:, 0:ntok], in_=ps[:, bg, 0:ntok], func=AF.Silu),
                         reads=[("ps", bg)], writes=[("sil", id(sl))])
                    R.op("dve", lambda e, bu=bu, sl=sl, ci=ci, tok0=tok0, ntok=ntok: e.tensor_tensor(
                        out=aT[:, ci, tok0:tok0 + ntok], in0=ps[:, bu, 0:ntok], in1=sl[:, 0:ntok], op=ALU.mult),
                        reads=[("ps", bu), ("sil", id(sl))], writes=[("aT", ci, t0 + k) for k in range(n)])
                gu_cnt += 1
                d_cnt += 1
                if hook is not None:
                    hook(c)
            dslots = [(d_cnt - gsz + ci) % 8 for ci in range(gsz)]
            for t in tiles:
                r = 0 if t < NLT else 1
                b0 = 6 if (y_i % 2 == 0) else 0
                y_i += 1

                def mmd(e, t=t, b0=b0, dslots=dslots, gsz=gsz):
                    inst = None
                    for hd in range(2):
                        for ci in range(gsz):
                            inst = e.matmul(ps[:, b0 + hd, :], aT[:, ci, t * 128:(t + 1) * 128],
                                            d_ring[dslots[ci]][:, hd * 512:(hd + 1) * 512],
                                            start=(ci == 0), stop=(ci == gsz - 1))
                    return inst
                R.op("pe", mmd, reads=[("aT", ci, t) for ci in range(gsz)] + [("dring", sd) for sd in dslots],
                     writes=[("ps", b0), ("ps", b0 + 1)])
                yv = ps[:, b0:b0 + 2, :]
                R.op("dve", lambda e, yv=yv, r=r: e.tensor_tensor(out=yv, in0=yv, in1=gate_bc[:, r, :].rearrange("p (a b) -> p a b", a=2),
                                                                  op=ALU.mult),
                     reads=[("ps", b0), ("ps", b0 + 1), ("gate", r)], writes=[("ps", b0), ("ps", b0 + 1)])
                R.op("dve", lambda e, yv=yv, t=t: e.tensor_tensor(out=xs[:, t, :].rearrange("p (a b) -> p a b", a=2),
                                                                  in0=xs[:, t, :].rearrange("p (a b) -> p a b", a=2), in1=yv, op=ALU.add),
                     reads=[("ps", b0), ("ps", b0 + 1), ("x", t)], writes=[("x", t)])
            cbase += gsz
        R.barrier()
        Ar.pos = mark

    def rope_apply(eng, src, dst, cs, t, H, dim, key_r, key_w, tmp1, tmp2, k1, k2):
        q = dim // 4
        cosb = cs[:, 0, t, :].unsqueeze(1).broadcast_to([128, H, dim])
        R.op(eng, lambda e: e.tensor_tensor(out=tmp1, in0=src, in1=cosb, op=ALU.mult), reads=key_r + ["rope"], writes=[k1])
        s4 = src.rearrange("p h (a b c) -> p h a b c", a=2, b=2)
        t4 = tmp2.rearrange("p h (a b c) -> p h a b c", a=2, b=2)
        sn = cs[:, 1, t, :].rearrange("p (a b c) -> p a b c", a=2, b=2)
        for bsel in range(2):
            R.op(eng, lambda e, bsel=bsel: e.tensor_tensor(
                out=t4[:, :, :, bsel, :], in0=s4[:, :, :, 1 - bsel, :],
                in1=sn[:, :, bsel, :].unsqueeze(1).broadcast_to([128, H, 2, q]), op=ALU.mult),
                reads=key_r + ["rope"] + ([k2] if bsel == 1 else []), writes=[k2])
        R.op(eng, lambda e: e.tensor_tensor(out=dst, in0=tmp1, in1=tmp2, op=ALU.add),
             reads=[k1, k2], writes=key_w)

    def sumsq(src, H, dd, col0, scr, key_r, key_w):
        sv = scr[:, 0:H * dd].rearrange("p (h d) -> p h d", h=H)
        R.op("dve", lambda e: e.tensor_tensor(out=sv, in0=src, in1=src, op=ALU.mult), reads=key_r, writes=["sqscr"])
        R.op("dve", lambda e: e.tensor_reduce(out=st_ss[:, col0:col0 + H], in_=sv, axis=AX.X, op=ALU.add),
             reads=["sqscr"], writes=key_w)

    def mixer(l, need_ctx):
        sub_modulation(l, 1, 1.0)
        mark0 = Ar.pos
        kT_g = anew(BF16, [T])
        V_g = anew(BF16, [NT, 128])
        kT_m = anew(BF16, [6, T])
        V_m = anew(BF16, [NT, 384])
        NCV = 2050 + 258
        bT = anew(BF16, [2, NCV])
        gsc = anew(F32, [NGAIN])
        markU = Ar.pos
        uT = anew(BF16, [2, NCV])
        R.dma("sp", "gains", lambda e: e.dma_start(out=gains[:], in_=gains_d[l].partition_broadcast(128)), writes=["gains"])
        R.op("dve", lambda e: e.tensor_copy(out=gsc[:], in_=gains[:]), reads=["gains"], writes=["gsc"])
        R.op("dve", lambda e: e.tensor_scalar(out=gsc[:, 0:64], in0=gains[:, 0:64], scalar1=64.0 ** -0.5, scalar2=None, op0=ALU.mult),
             reads=["gains", "gsc"], writes=["gsc"])
        R.op("dve", lambda e: e.tensor_scalar(out=gsc[:, 128:192], in0=gains[:, 128:192], scalar1=96.0 ** -0.5, scalar2=None, op0=ALU.mult),
             reads=["gains", "gsc"], writes=["gsc"])
        R.op("dve", lambda e: e.tensor_scalar(out=gsc[:, 256:288], in0=gains[:, 256:288], scalar1=96.0 ** -0.5, scalar2=None, op0=ALU.mult),
             reads=["gains", "gsc"], writes=["gsc"])
        g_q, g_k, g_qn, g_kn, g_qr, g_kr = (gsc[:, 0:64], gsc[:, 64:128], gsc[:, 128:192], gsc[:, 192:256],
                                            gsc[:, 256:288], gsc[:, 288:320])
        R.op("pool", lambda e: e.memset(uT[:], 0.0), writes=["uT"])

        markK = Ar.pos
        wK = anew(BF16, [KC, 1312])
        wukv = anew(BF16, [2, 768])
        hTr = [anew(BF16, [KC, 128]) for _ in range(2)]
        xn = [anew(BF16, [D]) for _ in range(2)]
        kraws = [anew(F32, [544]) for _ in range(2)]
        kvraw = anew(F32, [768])
        scr = anew(F32, [768])
        tA = anew(F32, [384])
        tB = anew(F32, [384])
        tC = anew(F32, [384])
        kf = anew(BF16, [128])
        ckvb = anew(BF16, [256])
        ckvT = anew(BF16, [2, 128])
        kfull = anew(BF16, [6, 96])
        cgt = anew(F32, [2, 128])
        R.dma("pool", "wK", lambda e: e.dma_start(out=wK[:], in_=wink_d[l]), writes=["wK"])
        R.dma("pool", "wukv", lambda e: e.dma_start(out=wukv[:], in_=wukv_d[l]), writes=["wukv"])
        for c in range(2):
            R.op("dve", lambda e, c=c: e.tensor_scalar(out=wukv[:, c, :], in0=wukv[:, c, :], scalar1=glat[:, l, 3 + c:4 + c], scalar2=None,
                                                       op0=ALU.mult), reads=["wukv", "glat"], writes=["wukv"])
        def front(t):
            lat = t < NLT
            kraw = kraws[t % 2]
            krk = ("kraw", t % 2)
            h = hTr[t % 2]
            hk = ("hTr", t % 2)
            emit_hT(t, h, hk, xn[t % 2], ("xn", t % 2), 1, t % 2)

            def mmA(e, h=h):
                inst = None
                for kc in range(KC):
                    inst = e.matmul(ps[:, 0, :], h[:, kc, :], wK[:, kc, 0:512], start=(kc == 0), stop=(kc == KC - 1))
                for kc in range(KC):
                    inst = e.matmul(ps[:, 3, 256:288], h[:, kc, :], wK[:, kc, 512:544], start=(kc == 0), stop=(kc == KC - 1))
                return inst
            R.op("pe", mmA, reads=[hk, "wK"], writes=[("ps", 0), ("ps", 3)])

            def mmC(e, h=h):
                inst = None
                for cc in range(6):
                    bank, off = (2, cc * 128) if cc < 4 else (3, (cc - 4) * 128)
                    for kc in range(KC):
                        inst = e.matmul(ps[:, bank, off:off + 128], wK[:, kc, 544 + cc * 128:544 + (cc + 1) * 128], h[:, kc, :],
                                        start=(kc == 0), stop=(kc == KC - 1))
                return inst
            R.op("pe", mmC, reads=[hk, "wK"], writes=[("ps", 2), ("ps", 3)])
            R.op("act", lambda e: e.copy(out=kraw[:, 0:512], in_=ps[:, 0, :]), reads=[("ps", 0)], writes=[krk])
            R.op("act", lambda e: e.copy(out=kraw[:, 512:544], in_=ps[:, 3, 256:288]), reads=[("ps", 3), krk], writes=[krk])
            R.op("act", lambda e, t=t: e.copy(out=V_g[:, t, :], in_=kraw[:, 128:256]), reads=[krk], writes=[("Vg", t)])
            pos = (1 + t * 128) if lat else (2051 + (t - NLT) * 128)
            R.op("act", lambda e: e.copy(out=cgt[:].rearrange("p a b -> p (a b)"), in_=ps[:, 2, 256:512]), reads=[("ps", 2)], writes=["cgt"])
            R.op("dve", lambda e, pos=pos: e.tensor_tensor(out=uT[:, :, pos:pos + 128], in0=ps[:, 2, 0:256].rearrange("p (a b) -> p a b", a=2),
                                                           in1=cgt[:], op=ALU.mult), reads=[("ps", 2), "cgt", "uT"], writes=["uT"])
            R.op("act", lambda e, pos=pos: e.copy(out=bT[:, :, pos:pos + 128], in_=ps[:, 3, 0:256].rearrange("p (a b) -> p a b", a=2)),
                 reads=[("ps", 3)], writes=["bT"])

        def back(t):
            lat = t < NLT
            kraw = kraws[t % 2]
            krk = ("kraw", t % 2)
            R.op("act", lambda e: e.copy(out=ckvb[:], in_=kraw[:, 256:512]), reads=[krk], writes=["ckvb"])
            k3 = kraw[:, 0:128].rearrange("p (h d) -> p h d", h=2)
            sumsq(k3, 2, 64, 8, scr, [krk], [("ss", "k")])
            sumsq(kraw[:, 256:512].unsqueeze(1), 1, 256, 10, scr, [krk], [("ss", "ckv")])
            sumsq(kraw[:, 512:544].unsqueeze(1), 1, 32, 11, scr, [krk], [("ss", "kr")])
            R.op("act", lambda e: e.activation(out=st_r[:, 8:10], in_=st_ss[:, 8:10], func=AF.Sqrt, scale=1.0 / 64, bias=epsb[:]),
                 reads=[("ss", "k"), "eps"], writes=[("sr", "k")])
            R.op("act", lambda e: e.activation(out=st_r[:, 10:11], in_=st_ss[:, 10:11], func=AF.Sqrt, scale=1.0 / 256, bias=epsb[:]),
                 reads=[("ss", "ckv"), "eps"], writes=[("sr", "ckv")])
            R.op("act", lambda e: e.activation(out=st_r[:, 11:12], in_=st_ss[:, 11:12], func=AF.Sqrt, scale=1.0 / 32, bias=epsb[:]),
                 reads=[("ss", "kr"), "eps"], writes=[("sr", "kr")])
            R.op("dve", lambda e: e.reciprocal(out=st_r[:, 8:12], in_=st_r[:, 8:12]),
                 reads=[("sr", "k"), ("sr", "ckv"), ("sr", "kr")], writes=[("sr", "k"), ("sr", "ckv"), ("sr", "kr")])
            kn = tA[:, 0:128].rearrange("p (h d) -> p h d", h=2)
            for hh in range(2):
                R.op("dve", lambda e, hh=hh: e.scalar_tensor_tensor(out=kn[:, hh, :], in0=k3[:, hh, :], scalar=st_r[:, 8 + hh:9 + hh], in1=g_k,
                                                                    op0=ALU.mult, op1=ALU.mult),
                     reads=[krk, ("sr", "k"), "gsc"], writes=[("kn", hh)], sync_self=True)
            kf3 = kf[:].rearrange("p (h d) -> p h d", h=2)
            if lat:
                rope_apply("dve", kn, kf3, ropeA, t, 2, 64, [("kn", 0), ("kn", 1)], ["kf"],
                           tB[:, 0:128].rearrange("p (h d) -> p h d", h=2), tC[:, 0:128].rearrange("p (h d) -> p h d", h=2), ("tB", "k"), ("tC", "k"))
            else:
                R.op("dve", lambda e: e.tensor_copy(out=kf3, in_=kn), reads=[("kn", 0), ("kn", 1)], writes=["kf"])
            R.op("pe", lambda e: e.transpose(psb(7)[:, 0:128], kf[:], ident[:]), reads=["kf", "ident"], writes=[("ps", 7)])
            R.op("act", lambda e, t=t: e.copy(out=kT_g[:, t * 128:(t + 1) * 128], in_=psb(7)[:, 0:128]), reads=[("ps", 7)],
                 writes=[("kTg", t)])
            def trc(e):
                inst = None
                for c in range(2):
                    inst = e.transpose(psb(7)[:, 128 + c * 128:256 + c * 128], ckvb[:, c * 128:(c + 1) * 128], ident[:])
                return inst
            R.op("pe", trc, reads=["ckvb", "ident"], writes=[("ps", 7)])
            R.op("act", lambda e: e.copy(out=ckvT[:].rearrange("p a b -> p (a b)"), in_=psb(7)[:, 128:384]), reads=[("ps", 7)],
                 writes=["ckvT"])

            def mmkv(e):
                inst = None
                for (bank, c0, n) in ((4, 0, 512), (5, 512, 256)):
                    for c in range(2):
                        inst = e.matmul(ps[:, bank, 0:n], ckvT[:, c, :], wukv[:, c, c0:c0 + n], start=(c == 0), stop=(c == 1))
                return inst
            R.op("pe", mmkv, reads=["ckvT", "wukv"], writes=[("ps", 4), ("ps", 5)])
            R.op("act", lambda e: e.copy(out=kvraw[:, 0:512], in_=ps[:, 4, :]), reads=[("ps", 4)], writes=["kvraw"])
            R.op("act", lambda e: e.copy(out=kvraw[:, 512:768], in_=ps[:, 5, 0:256]), reads=[("ps", 5), "kvraw"], writes=["kvraw"])
            kv3 = kvraw[:].rearrange("p (h d) -> p h d", h=6)
            sumsq(kv3[:, :, 0:64], 6, 64, 16, scr, ["kvraw"], [("ss", "kn")])
            R.op("dve", lambda e: e.tensor_tensor(out=st_ss[:, 12:13], in0=st_r[:, 10:11], in1=st_r[:, 10:11], op=ALU.mult),
                 reads=[("sr", "ckv")], writes=[("ss", "b2")])
            R.op("dve", lambda e: e.tensor_scalar(out=st_ss[:, 16:22], in0=st_ss[:, 16:22], scalar1=st_ss[:, 12:13], scalar2=None, op0=ALU.mult),
                 reads=[("ss", "kn"), ("ss", "b2")], writes=[("ss", "kn")], sync_self=True)
            R.op("act", lambda e: e.activation(out=st_r[:, 16:22], in_=st_ss[:, 16:22], func=AF.Sqrt, scale=1.0 / 64, bias=epsb[:]),
                 reads=[("ss", "kn"), "eps"], writes=[("sr", "kn")])
            R.op("dve", lambda e: e.reciprocal(out=st_r[:, 16:22], in_=st_r[:, 16:22]), reads=[("sr", "kn")], writes=[("sr", "kn")])
            R.op("dve", lambda e: e.tensor_scalar(out=st_r[:, 16:22], in0=st_r[:, 16:22], scalar1=st_r[:, 10:11], scalar2=None, op0=ALU.mult),
                 reads=[("sr", "kn"), ("sr", "ckv")], writes=[("sr", "kn")], sync_self=True)
            for hh in range(6):
                R.op("dve", lambda e, hh=hh: e.scalar_tensor_tensor(out=kfull[:, hh, 0:64], in0=kv3[:, hh, 0:64], scalar=st_r[:, 16 + hh:17 + hh],
                                                                    in1=g_kn, op0=ALU.mult, op1=ALU.mult),
                     reads=["kvraw", ("sr", "kn"), "gsc"], writes=[("kfull", hh)], sync_self=(hh == 0))
            R.op("dve", lambda e, t=t: e.tensor_scalar(out=V_m[:, t, :].rearrange("p (h d) -> p h d", h=6), in0=kv3[:, :, 64:128],
                                                       scalar1=st_r[:, 10:11], scalar2=None, op0=ALU.mult),
                 reads=["kvraw", ("sr", "ckv")], writes=[("Vm", t)])
            krn = tA[:, 128:160].unsqueeze(1)
            R.op("dve", lambda e: e.scalar_tensor_tensor(out=tA[:, 128:160], in0=kraw[:, 512:544], scalar=st_r[:, 11:12], in1=g_kr,
                                                         op0=ALU.mult, op1=ALU.mult), reads=[krk, ("sr", "kr"), "gsc"], writes=["krn"])
            krf = tA[:, 160:192].unsqueeze(1)
            if lat:
                rope_apply("dve", krn, krf, ropeM, t, 1, 32, ["krn"], ["krf"], tB[:, 128:160].unsqueeze(1), tC[:, 128:160].unsqueeze(1), ("tB", "kr"), ("tC", "kr"))
                src_kr = krf
                krkey = "krf"
            else:
                src_kr = krn
                krkey = "krn"
            R.op("dve", lambda e, src_kr=src_kr: e.tensor_copy(out=kfull[:, :, 64:96], in_=src_kr.broadcast_to([128, 6, 32])),
                 reads=[krkey], writes=[("kfull", "r")])

            def trk(e):
                inst = None
                for hh in range(6):
                    inst = e.transpose(psb(6)[0:96, hh * 128:(hh + 1) * 128], kfull[:, hh, :], ident[:])
                return inst
            R.op("pe", trk, reads=[("kfull", hh) for hh in range(6)] + [("kfull", "r"), "ident"], writes=[("ps", 6)])
            R.op("act", lambda e, t=t: e.copy(out=kT_m[0:96, :, t * 128:(t + 1) * 128],
                                              in_=psb(6)[0:96, 0:768].rearrange("p (h n) -> p h n", h=6)),
                 reads=[("ps", 6)], writes=[("kTm", t)])
        streams = []
        for t in range(NT):
            R.capture_start()
            front(t)
            back(t)
            streams.append(R.capture_end())
        R.replay_zipped(streams, 0.5)
        cvt = scr[:, 0:512]
        for ch in range(2):
            segs = [(1 + 512 * i, 512) for i in range(4)] + [(2051, 256)]
            for (p0, n) in segs:
                R.op("dve", lambda e, ch=ch, p0=p0, n=n: e.tensor_scalar(out=cvt[:, 0:n], in0=uT[:, ch, p0:p0 + n], scalar1=convp[:, l, ch, 1:2],
                                                                         scalar2=convp[:, l, ch, 3:4], op0=ALU.mult, op1=ALU.add),
                     reads=["uT", "convp"], writes=["cvt"])
                R.op("dve", lambda e, ch=ch, p0=p0, n=n: e.scalar_tensor_tensor(out=cvt[:, 0:n], in0=uT[:, ch, p0 - 1:p0 - 1 + n],
                                                                                scalar=convp[:, l, ch, 0:1], in1=cvt[:, 0:n], op0=ALU.mult, op1=ALU.add),
                     reads=["uT", "convp", "cvt"], writes=["cvt"])
                R.op("dve", lambda e, ch=ch, p0=p0, n=n: e.scalar_tensor_tensor(out=cvt[:, 0:n], in0=uT[:, ch, p0 + 1:p0 + 1 + n],
                                                                                scalar=convp[:, l, ch, 2:3], in1=cvt[:, 0:n], op0=ALU.mult, op1=ALU.add),
                     reads=["uT", "convp", "cvt"], writes=["cvt"])
                R.op("dve", lambda e, ch=ch, p0=p0, n=n: e.tensor_tensor(out=bT[:, ch, p0:p0 + n], in0=bT[:, ch, p0:p0 + n], in1=cvt[:, 0:n],
                                                                         op=ALU.mult), reads=["bT", "cvt"], writes=["bT"])
        R.barrier()
        Ar.pos = markU

        wQ = anew(BF16, [KC, 768])
        wuq = anew(BF16, [3, 576])
        wo_b = anew(BF16, [KC, 512])
        hTq = [anew(BF16, [KC, 128])] * 2
        xnq = [anew(BF16, [D])] * 2
        qraw = anew(F32, [768])
        qmraw = qraw[:, 0:576]
        scrq = anew(F32, [576])
        tAq = anew(F32, [384])
        tBq = anew(F32, [384])
        tCq = anew(F32, [384])
        qf = anew(BF16, [384])
        cqb = anew(BF16, [384])
        cqT = anew(BF16, [3, 128])
        qmfull = anew(BF16, [6, 96])
        qT_g = anew(BF16, [3, 512])
        qT_m = anew(BF16, [6, 512])
        PT2 = [anew(BF16, [2, 512]) for _ in range(2)]
        mixT = anew(BF16, [6, 512])
        rden = scrq[:, 0:512]
        R.dma("pool", "wQ", lambda e: e.dma_start(out=wQ[:], in_=winq_d[l]), writes=["wQ"])
        R.dma("pool", "wuq", lambda e: e.dma_start(out=wuq[:], in_=wuq_d[l]), writes=["wuq"])
        for c in range(3):
            R.op("dve", lambda e, c=c: e.tensor_scalar(out=wuq[:, c, :], in0=wuq[:, c, :], scalar1=glat[:, l, c:c + 1], scalar2=None,
                                                       op0=ALU.mult), reads=["wuq", "glat"], writes=["wuq"])
        groups = [(4 * g, 4) for g in range(4)]
        if need_ctx:
            groups.append((16, 2))
        pt_i = 0
        st_i = 0
        o_i = 0
        for (t0, ntile) in groups:
            lat = t0 < NLT
            nq = ntile * 128
            r = 0 if lat else 1
            key_tiles = list(range(NT)) if lat else [16, 17]
            qstreams = []
            for lt in range(ntile):
                R.capture_start()
                t = t0 + lt
                h = hTq[0]
                hk = ("hTr", 0)
                emit_hT(t, h, hk, xnq[0], ("xn", 0), 6, t % 2)

                def mmQ(e, h=h):
                    inst = None
                    for (bank, c0) in ((4, 0), (5, 384)):
                        for kc in range(KC):
                            inst = e.matmul(ps[:, bank, 0:384], h[:, kc, :], wQ[:, kc, c0:c0 + 384], start=(kc == 0), stop=(kc == KC - 1))
                    return inst
                R.op("pe", mmQ, reads=[hk, "wQ"], writes=[("ps", 4), ("ps", 5)])
                R.op("act", lambda e: e.copy(out=qraw[:, 0:384], in_=ps[:, 4, 0:384]), reads=[("ps", 4)], writes=["qraw"])
                R.op("act", lambda e: e.copy(out=qraw[:, 384:768], in_=ps[:, 5, 0:384]), reads=[("ps", 5), "qraw"], writes=["qraw"])
                R.op("act", lambda e: e.copy(out=cqb[:], in_=qraw[:, 384:768]), reads=["qraw"], writes=["cqb"])
                q3 = qraw[:, 0:384].rearrange("p (h d) -> p h d", h=6)
                sumsq(q3, 6, 64, 24, scrq, ["qraw"], [("ss", "q")])
                sumsq(qraw[:, 384:768].unsqueeze(1), 1, 384, 30, scrq, ["qraw"], [("ss", "cq")])
                R.op("act", lambda e: e.activation(out=st_r[:, 24:30], in_=st_ss[:, 24:30], func=AF.Sqrt, scale=1.0 / 64, bias=epsb[:]),
                     reads=[("ss", "q"), "eps"], writes=[("sr", "q")])
                R.op("act", lambda e: e.activation(out=st_r[:, 30:31], in_=st_ss[:, 30:31], func=AF.Sqrt, scale=1.0 / 384, bias=epsb[:]),
                     reads=[("ss", "cq"), "eps"], writes=[("sr", "cq")])
                R.op("dve", lambda e: e.reciprocal(out=st_r[:, 24:31], in_=st_r[:, 24:31]), reads=[("sr", "q"), ("sr", "cq")],
                     writes=[("sr", "q"), ("sr", "cq")])
                qn = tAq[:].rearrange("p (h d) -> p h d", h=6)
                for hh in range(6):
                    R.op("dve", lambda e, hh=hh: e.scalar_tensor_tensor(out=qn[:, hh, :], in0=q3[:, hh, :], scalar=st_r[:, 24 + hh:25 + hh], in1=g_q,
                                                                        op0=ALU.mult, op1=ALU.mult),
                         reads=["qraw", ("sr", "q"), "gsc"], writes=[("qn", hh)], sync_self=(hh == 0))
                qf3 = qf[:].rearrange("p (h d) -> p h d", h=6)
                qnk = [("qn", hh) for hh in range(6)]
                if lat:
                    rope_apply("dve", qn, qf3, ropeA, t, 6, 64, qnk, ["qf"], tBq[:].rearrange("p (h d) -> p h d", h=6),
                               tCq[:].rearrange("p (h d) -> p h d", h=6), "tBq", "tCq")
                else:
                    R.op("dve", lambda e: e.tensor_copy(out=qf3, in_=qn), reads=qnk, writes=["qf"])

                def trq(e):
                    inst = None
                    for pr in range(3):
                        inst = e.transpose(psb(7)[:, pr * 128:(pr + 1) * 128], qf[:, pr * 128:(pr + 1) * 128], ident[:])
                    for c in range(3):
                        inst = e.transpose(psb(7)[:, 384 + c * 128:512 + c * 128], cqb[:, c * 128:(c + 1) * 128], ident[:])
                    return inst
                R.op("pe", trq, reads=["qf", "cqb", "ident"], writes=[("ps", 7)])
                R.op("act", lambda e, lt=lt: e.copy(out=qT_g[:, :, lt * 128:(lt + 1) * 128], in_=psb(7)[:, 0:384].rearrange("p (a b) -> p a b", a=3)),
                     reads=[("ps", 7)], writes=[("qTg", lt)])
                R.op("act", lambda e: e.copy(out=cqT[:].rearrange("p a b -> p (a b)"), in_=psb(7)[:, 384:768]), reads=[("ps", 7)], writes=["cqT"])

                def mmuq(e):
                    inst = None
                    for (bank, c0) in ((4, 0), (5, 288)):
                        for c in range(3):
                            inst = e.matmul(ps[:, bank, 0:288], cqT[:, c, :], wuq[:, c, c0:c0 + 288], start=(c == 0), stop=(c == 2))
                    return inst
                R.op("pe", mmuq, reads=["cqT", "wuq"], writes=[("ps", 4), ("ps", 5)])
                R.op("act", lambda e: e.copy(out=qmraw[:, 0:288], in_=ps[:, 4, 0:288]), reads=[("ps", 4)], writes=["qraw"])
                R.op("act", lambda e: e.copy(out=qmraw[:, 288:576], in_=ps[:, 5, 0:288]), reads=[("ps", 5), "qraw"], writes=["qraw"])
                qm3 = qmraw[:].rearrange("p (h d) -> p h d", h=6)
                sumsq(qm3[:, :, 0:64], 6, 64, 32, scrq, ["qraw"], [("ss", "qmn")])
                sumsq(qm3[:, :, 64:96], 6, 32, 38, scrq, ["qraw"], [("ss", "qmr")])
                R.op("dve", lambda e: e.tensor_tensor(out=st_ss[:, 31:32], in0=st_r[:, 30:31], in1=st_r[:, 30:31], op=ALU.mult),
                     reads=[("sr", "cq")], writes=[("ss", "a2")])
                R.op("dve", lambda e: e.tensor_scalar(out=st_ss[:, 32:44], in0=st_ss[:, 32:44], scalar1=st_ss[:, 31:32], scalar2=None, op0=ALU.mult),
                     reads=[("ss", "qmn"), ("ss", "qmr"), ("ss", "a2")], writes=[("ss", "qmn"), ("ss", "qmr")], sync_self=True)
                R.op("act", lambda e: e.activation(out=st_r[:, 32:38], in_=st_ss[:, 32:38], func=AF.Sqrt, scale=1.0 / 64, bias=epsb[:]),
                     reads=[("ss", "qmn"), "eps"], writes=[("sr", "qmn")])
                R.op("act", lambda e: e.activation(out=st_r[:, 38:44], in_=st_ss[:, 38:44], func=AF.Sqrt, scale=1.0 / 32, bias=epsb[:]),
                     reads=[("ss", "qmr"), "eps"], writes=[("sr", "qmr")])
                R.op("dve", lambda e: e.reciprocal(out=st_r[:, 32:44], in_=st_r[:, 32:44]), reads=[("sr", "qmn"), ("sr", "qmr")],
                     writes=[("sr", "qmn"), ("sr", "qmr")])
                R.op("dve", lambda e: e.tensor_scalar(out=st_r[:, 32:44], in0=st_r[:, 32:44], scalar1=st_r[:, 30:31], scalar2=None, op0=ALU.mult),
                     reads=[("sr", "qmn"), ("sr", "qmr"), ("sr", "cq")], writes=[("sr", "qmn"), ("sr", "qmr")], sync_self=True)
                qr = tAq[:, 0:192].rearrange("p (h d) -> p h d", h=6)
                for hh in range(6):
                    R.op("dve", lambda e, hh=hh: e.scalar_tensor_tensor(out=qmfull[:, hh, 0:64], in0=qm3[:, hh, 0:64], scalar=st_r[:, 32 + hh:33 + hh],
                                                                        in1=g_qn, op0=ALU.mult, op1=ALU.mult),
                         reads=["qraw", ("sr", "qmn"), "gsc"], writes=[("qmfull", hh)], sync_self=(hh == 0))
                    R.op("dve", lambda e, hh=hh: e.scalar_tensor_tensor(out=qr[:, hh, :], in0=qm3[:, hh, 64:96], scalar=st_r[:, 38 + hh:39 + hh],
                                                                        in1=g_qr, op0=ALU.mult, op1=ALU.mult),
                         reads=["qraw", ("sr", "qmr"), "gsc", "qf"] + qnk, writes=[("qr", hh)])
                qrk = [("qr", hh) for hh in range(6)]
                if lat:
                    rope_apply("dve", qr, qmfull[:, :, 64:96], ropeM, t, 6, 32, qrk, [("qmfull", "r")],
                               tBq[:, 0:192].rearrange("p (h d) -> p h d", h=6), tCq[:, 0:192].rearrange("p (h d) -> p h d", h=6), "tBq", "tCq")
                else:
                    R.op("dve", lambda e: e.tensor_copy(out=qmfull[:, :, 64:96], in_=qr), reads=qrk, writes=[("qmfull", "r")])

                def trqm(e):
                    inst = None
                    for hh in range(6):
                        inst = e.transpose(psb(7)[0:96, hh * 128:(hh + 1) * 128], qmfull[:, hh, :], ident[:])
                    return inst
                R.op("pe", trqm, reads=[("qmfull", hh) for hh in range(6)] + [("qmfull", "r"), "ident"], writes=[("ps", 7)])
                R.op("act", lambda e, lt=lt: e.copy(out=qT_m[0:96, :, lt * 128:(lt + 1) * 128],
                                                    in_=psb(7)[0:96, 0:768].rearrange("p (h n) -> p h n", h=6)),
                     reads=[("ps", 7)], writes=[("qTm", lt)])
                qstreams.append(R.capture_end())
            R.replay_zipped(qstreams, 0.7)
            qgk = [("qTg", lt) for lt in range(ntile)]
            qmk = [("qTm", lt) for lt in range(ntile)]
            npair = len(key_tiles) // 2
            items = [(slot, kj, key_tiles[2 * kj], key_tiles[2 * kj + 1]) for slot in range(12) for kj in range(npair)]
            LA = 1
            ST_PAIRS = [0, 6]

            def slot_info(slot):
                if slot < 6:
                    pr, hf = slot // 2, slot % 2
                    return True, pr, hf, pr, slot
                hm = slot - 6
                pr, hf = hm // 2, hm % 2
                return False, pr, hf, 3 + pr, hm

            def emit_st(i, nq=nq):
                slot, kj, kta, ktb = items[i]
                gqa, pr, hf, chunk, hm = slot_info(slot)
                b0 = ST_PAIRS[i % 2]
                pt = PT2[i % 2]
                ptk = ("PT", i % 2)

                def mmst(e, b0=b0, gqa=gqa, pr=pr, hf=hf, hm=hm, kta=kta, ktb=ktb, nq=nq):
                    inst = None
                    for k, kt in enumerate((kta, ktb)):
                        if gqa:
                            inst = e.matmul(ps[:, b0 + k, 0:nq], kT_g[64 * hf:64 * hf + 64, kt * 128:(kt + 1) * 128],
                                            qT_g[64 * hf:64 * hf + 64, pr, 0:nq], start=True, stop=True)
                        else:
                            inst = e.matmul(ps[:, b0 + k, 0:nq], kT_m[0:96, hm, kt * 128:(kt + 1) * 128], qT_m[0:96, hm, 0:nq],
                                            start=True, stop=True)
                    return inst
                kk = [("kTg", kta), ("kTg", ktb)] + qgk if gqa else [("kTm", kta), ("kTm", ktb)] + qmk
                R.op("pe", mmst, reads=kk, writes=[("ps", b0), ("ps", b0 + 1)])
                R.op("act", lambda e, b0=b0, pt=pt, nq=nq: e.activation(out=pt[:, :, 0:nq], in_=ps[:, b0:b0 + 2, 0:nq], func=AF.Exp),
                     reads=[("ps", b0), ("ps", b0 + 1)], writes=[ptk])

            def emit_pv(i, nq=nq):
                slot, kj, kta, ktb = items[i]
                gqa, pr, hf, chunk, hm = slot_info(slot)
                pt = PT2[i % 2]
                ptk = ("PT", i % 2)
                bo = 2 if (slot % 2 == 0) else 4
                bd = bo + 1
                if gqa:
                    vaps = [V_g[:, kta, :], V_g[:, ktb, :]]
                    vk = [("Vg", kta), ("Vg", ktb)]
                else:
                    vaps = [V_m[:, kta, pr * 128:(pr + 1) * 128], V_m[:, ktb, pr * 128:(pr + 1) * 128]]
                    vk = [("Vm", kta), ("Vm", ktb)]

                def mmpv(e, vaps=vaps, pt=pt, bo=bo, bd=bd, kj=kj, nq=nq, npair=npair):
                    inst = None
                    for k in range(2):
                        first = (kj == 0 and k == 0)
                        last = (kj == npair - 1 and k == 1)
                        e.matmul(ps[:, bo, 0:nq], vaps[k], pt[:, k, 0:nq], start=first, stop=last)
                        inst = e.matmul(ps[:, bd, 0:nq], ones_bf[:], pt[:, k, 0:nq], start=first, stop=last)
                    return inst
                R.op("pe", mmpv, reads=vk + [ptk, "ones"], writes=[("ps", bo), ("ps", bd)])
                if kj == npair - 1:
                    p0 = 64 * hf
                    R.op("dve", lambda e, bd=bd, p0=p0, nq=nq: e.reciprocal(out=rden[p0:p0 + 64, 0:nq], in_=ps[p0:p0 + 64, bd, 0:nq]),
                         reads=[("ps", bd)], writes=["rden"])
                    R.op("dve", lambda e, bo=bo, p0=p0, chunk=chunk, nq=nq: e.tensor_tensor(
                        out=mixT[p0:p0 + 64, chunk, 0:nq], in0=ps[p0:p0 + 64, bo, 0:nq], in1=rden[p0:p0 + 64, 0:nq], op=ALU.mult),
                        reads=[("ps", bo), "rden"], writes=[("mixT", chunk, hf)])

            for i in range(len(items) + LA):
                if i < len(items):
                    emit_st(i)
                if i >= LA:
                    emit_pv(i - LA)
            mixk = [("mixT", c, hf) for c in range(6) for hf in range(2)]
            for hd in range(2):
                R.dma("pool", "wo", lambda e, hd=hd: e.dma_start(out=wo_b[:], in_=wout_d[l, hd]), writes=["wo"])
                for lt in range(ntile):
                    t = t0 + lt
                    pos = (1 + t * 128) if lat else (2051 + (t - NLT) * 128)
                    bk = 6 + (lt % 2)

                    def mmo(e, lt=lt, pos=pos, bk=bk):
                        inst = None
                        for c in range(8):
                            if c < 3:
                                lh = mixT[:, c, lt * 128:(lt + 1) * 128]
                            elif c < 5:
                                lh = bT[:, c - 3, pos:pos + 128]
                            else:
                                lh = mixT[:, c - 2, lt * 128:(lt + 1) * 128]
                            inst = e.matmul(ps[:, bk, :], lh, wo_b[:, c, :], start=(c == 0), stop=(c == 7))
                        return inst
                    R.op("pe", mmo, reads=mixk + ["bT", "wo"], writes=[("ps", bk)])
                    R.op("dve", lambda e, bk=bk, r=r, hd=hd: e.tensor_tensor(out=ps[:, bk, :], in0=ps[:, bk, :],
                                                                             in1=gate_bc[:, r, hd * 512:(hd + 1) * 512], op=ALU.mult),
                         reads=[("ps", bk), ("gate", r)], writes=[("ps", bk)])
                    R.op("dve", lambda e, bk=bk, t=t, hd=hd: e.tensor_tensor(out=xs[:, t, hd * 512:(hd + 1) * 512],
                                                                             in0=xs[:, t, hd * 512:(hd + 1) * 512], in1=ps[:, bk, :], op=ALU.add),
                         reads=[("ps", bk), ("x", t)], writes=[("x", t)])
        R.barrier()
        Ar.pos = mark0

    done = False
    for l in range(L):
        if l > 0:
            R.new_epoch()
        need_ctx = l < DEPTH - 1
        if l == 0:
            modulation(l)
        ffn(l, 0, 0, list(range(NT)))
        if stop_after == (l, "ffn1"):
            break
        mixer(l, need_ctx)
        if stop_after == (l, "mix"):
            break
        ffn(l, 1, 2, list(range(NT)) if need_ctx else list(range(NLT)), hook=(make_mod_hook(l + 1) if l + 1 < L else None))
        if stop_after == (l, "ffn2"):
            break

    for i in range(4):
        R.dma("sp", "out", lambda e, i=i: e.dma_start(
            out=out_d[512 * i:512 * (i + 1), :].rearrange("(t p) d -> p t d", p=128), in_=xs[:, 4 * i:4 * i + 4, :]),
            reads=[("x", 4 * i + k) for k in range(4)])
    R.op("sp", lambda e: None, reads=[], writes=[("x", k) for k in range(NLT)])

    R.finalize()
    esems = {}
    for e in Rec.ENG:
        for ep in range(R.n_epochs):
            esems[(e, ep)] = es.enter_context(nc.semaphore("s_%s_%d" % (e, ep)))
    lsems = {ln: es.enter_context(nc.semaphore("l_%s" % ln)) for ln in R.lane_cnt}
    block = es.enter_context(nc.Block())

    @block.tensor
    def _(e):
        R.emit("pe", e, esems, lsems)

    @block.scalar
    def _(e):
        R.emit("act", e, esems, lsems)

    @block.vector
    def _(e):
        R.emit("dve", e, esems, lsems)

    @block.gpsimd
    def _(e):
        R.emit("pool", e, esems, lsems)

    @block.sync
    def _(e):
        R.emit("sp", e, esems, lsems)

    es.close()
    return nc


_CACHE = {}


def kernel(**inputs):
    inp = {k: np.asarray(v) for k, v in inputs.items()}
    if "nc" not in _CACHE:
        _CACHE["nc"] = build_program(DEPTH)
    nc = _CACHE["nc"]
    sh = prep_shared(inp, DEPTH)
    in_maps = []
    for b in range(8):
        m = dict(sh)
        m.update(prep_core(inp, b))
        in_maps.append(m)
    res = run_bass_kernel_spmd(nc, in_maps, core_ids=list(range(8)))
    out = np.stack([np.asarray(r["out"]) for r in res.results], axis=0)
    return out.astype(np.float32)
```

```python
import numpy as np
from contextlib import ExitStack
import concourse.bass as bass
import concourse.mybir as mybir
from concourse.bass_utils import run_bass_kernel_spmd

F32 = mybir.dt.float32
BF16 = mybir.dt.bfloat16
U8 = mybir.dt.uint8
ALU = mybir.AluOpType
AF = mybir.ActivationFunctionType
AX = mybir.AxisListType

D = 1024
DEPTH = 4
NT = 18
NLT = 16
T = NT * 128
DFF = 2816
NFC = 22
EPS = 1e-6
KC = 8
NGAIN = 320
FF_GROUPS = [4, 4, 4, 4, 4, 2]


class Rec:
    ENG = ("pe", "act", "dve", "pool", "sp")

    def __init__(self):
        self.ops = {e: [] for e in self.ENG}
        self.res = {}
        self.lane_cnt = {}
        self.epoch_starts = {e: [0] for e in self.ENG}
        self.pending = {e: [] for e in self.ENG}
        self.cap = None

    def capture_start(self):
        self.cap = []

    def capture_end(self):
        c = self.cap
        self.cap = None
        return c

    def replay(self, item):
        if item[0] == "op":
            self.op(item[1], item[2], item[3], item[4], item[5])
        else:
            self.dma(item[1], item[2], item[3], item[4], item[5])

    def replay_zipped(self, streams, frac=0.5):
        if not streams:
            return
        H = max(1, int(frac * max(len(st) for st in streams)))
        keyed = []
        for k, st in enumerate(streams):
            for j, it in enumerate(st):
                keyed.append((k * H + j, k, j, it))
        keyed.sort(key=lambda z: (z[0], z[1], z[2]))
        for _, _, _, it in keyed:
            self.replay(it)

    def new_epoch(self):
        for e in self.ENG:
            self.epoch_starts[e].append(len(self.ops[e]))

    def _deps(self, reads, writes):
        d = []
        for k in reads:
            st = self.res.get(k)
            if st is not None and st[0] is not None:
                d.append(st[0])
        for k in writes:
            st = self.res.get(k)
            if st is not None:
                if st[0] is not None:
                    d.append(st[0])
                for kk, v in st[1].items():
                    d.append(kk + (v,))
        return d

    def _commit(self, tok, reads, writes):
        for k in reads:
            st = self.res.get(k)
            if st is None:
                st = [None, {}]
                self.res[k] = st
            key = tok[:2]
            if st[1].get(key, -1) < tok[2]:
                st[1][key] = tok[2]
        for k in writes:
            self.res[k] = [tok, {}]

    def op(self, eng, fn, reads=(), writes=(), sync_self=False):
        if self.cap is not None:
            self.cap.append(("op", eng, fn, tuple(reads), tuple(writes), sync_self))
            return None
        deps = self._deps(reads, writes) + self.pending[eng]
        self.pending[eng] = []
        idx = len(self.ops[eng])
        tok = ("e", eng, idx)
        self.ops[eng].append({"fn": fn, "deps": deps, "lane": None, "ss": sync_self})
        self._commit(tok, reads, writes)
        return tok

    def dma(self, eng, lane, fn, reads=(), writes=()):
        if self.cap is not None:
            self.cap.append(("dma", eng, lane, fn, tuple(reads), tuple(writes)))
            return None
        deps = self._deps(reads, writes) + self.pending[eng]
        self.pending[eng] = []
        self.lane_cnt[lane] = self.lane_cnt.get(lane, 0) + 1
        tok = ("d", lane, self.lane_cnt[lane])
        self.ops[eng].append({"fn": fn, "deps": deps, "lane": lane, "ss": True})
        self._commit(tok, reads, writes)
        return tok

    def barrier(self):
        last = []
        for e in self.ENG:
            for i in range(len(self.ops[e]) - 1, -1, -1):
                if self.ops[e][i]["lane"] is None:
                    last.append(("e", e, i))
                    break
        for l, c in self.lane_cnt.items():
            last.append(("d", l, c))
        for e in self.ENG:
            self.pending[e] = self.pending[e] + list(last)

    def finalize(self):
        self.signal = {e: [False] * len(self.ops[e]) for e in self.ENG}
        for e in self.ENG:
            for i, op in enumerate(self.ops[e]):
                for tok in op["deps"]:
                    if tok[0] == "e" and (tok[1] != e or op["ss"] or tok[2] >= i - 2):
                        self.signal[tok[1]][tok[2]] = True
        self.sigval = {}
        self.epoch_of = {}
        for e in self.ENG:
            starts = self.epoch_starts[e]
            vals = [None] * len(self.ops[e])
            eps = [0] * len(self.ops[e])
            ep = 0
            cnt = 0
            for i in range(len(self.ops[e])):
                while ep + 1 < len(starts) and i >= starts[ep + 1]:
                    ep += 1
                    cnt = 0
                if self.signal[e][i]:
                    cnt += 1
                    vals[i] = cnt
                eps[i] = ep
            self.sigval[e] = vals
            self.epoch_of[e] = eps
        self.n_epochs = max(len(s) for s in self.epoch_starts.values())

    def emit(self, eng, e, esems, lsems):
        waited = {}
        for i, op in enumerate(self.ops[eng]):
            need = {}
            for tok in op["deps"]:
                if tok[0] == "e":
                    if tok[1] == eng and not op["ss"] and tok[2] < i - 2:
                        continue
                    key = ("e", tok[1], self.epoch_of[tok[1]][tok[2]])
                    val = self.sigval[tok[1]][tok[2]]
                else:
                    key = ("d", tok[1])
                    val = 16 * tok[2]
                if need.get(key, 0) < val:
                    need[key] = val
            for key, val in need.items():
                if waited.get(key, 0) >= val:
                    continue
                if key[0] == "e":
                    later = [k for k in waited if k[0] == "e" and k[1] == key[1] and k[2] > key[2]]
                    if later:
                        continue
                    e.wait_ge(esems[(key[1], key[2])], val)
                else:
                    e.wait_ge(lsems[key[1]], val)
                waited[key] = val
            inst = op["fn"](e)
            if inst is None:
                continue
            if op["lane"] is not None:
                inst.then_inc(lsems[op["lane"]], 16)
            elif self.signal[eng][i]:
                inst.then_inc(esems[(eng, self.epoch_of[eng][i])], 1)


Q_ORDER = [0, 3, 1, 4, 2, 5]


def _rope_np(dim):
    rows = 2048 // 64
    row = np.repeat(np.arange(rows, dtype=np.float64), 64)
    col = np.tile(np.arange(64, dtype=np.float64), rows)
    half = dim // 2
    inv = 1.0 / (10000.0 ** (np.arange(0, half, 2, dtype=np.float64) / half))
    ar = row[:, None] * inv[None, :]
    ac = col[:, None] * inv[None, :]
    ang = np.concatenate([ar, ar, ac, ac], axis=-1)
    cos = np.cos(ang).astype(np.float32)
    sin = np.sin(ang).astype(np.float32)
    q = dim // 4
    sgn = np.concatenate([-np.ones(q), np.ones(q), -np.ones(q), np.ones(q)]).astype(np.float32)
    sin = sin * sgn[None, :]
    cos = np.ascontiguousarray(cos.reshape(16, 128, dim).transpose(1, 0, 2))
    sin = np.ascontiguousarray(sin.reshape(16, 128, dim).transpose(1, 0, 2))
    return cos, sin


def prep_shared(inp, n_layers):
    L = n_layers
    f = lambda a: np.ascontiguousarray(a, dtype=np.float32)
    sh = {}
    wm = inp["w_mod"][:L]
    sh["wmod"] = f(wm.reshape(L, KC, 128, 18, 512).transpose(0, 3, 2, 1, 4))
    sh["bmodT"] = f(inp["b_mod"][:L].reshape(L, 72, 128).transpose(2, 0, 1))
    sh["gn"] = f(inp["g_norm"][:L].reshape(L, 3, KC, 128).transpose(3, 0, 1, 2))
    wg = inp["ffn_w_gate"][:L].reshape(L, 2, KC, 128, NFC, 128)
    wu = inp["ffn_w_up"][:L].reshape(L, 2, KC, 128, NFC, 128)
    wgu = np.stack([wg, wu], axis=0)
    sh["wgu"] = f(wgu.transpose(1, 2, 5, 4, 0, 3, 6))
    sh["wd"] = f(inp["ffn_w_down"][:L])
    win = inp["w_in"][:L]
    qcols = np.concatenate([np.arange(64 * h, 64 * h + 64) for h in Q_ORDER])
    kside = np.concatenate([np.arange(384, 512), np.arange(512, 640), np.arange(1792, 2048),
                            np.arange(2048, 2080),
                            np.arange(640, 896), np.arange(1152, 1408), np.arange(896, 1152)])
    qside = np.concatenate([qcols, np.arange(1408, 1792)])
    sh["wink"] = f(win[:, :, kside].reshape(L, KC, 128, 1312).transpose(0, 2, 1, 3))
    sh["winq"] = f(win[:, :, qside].reshape(L, KC, 128, 768).transpose(0, 2, 1, 3))
    sh["wuq"] = f(inp["mla_w_uq"][:L].reshape(L, 3, 128, 576).transpose(0, 2, 1, 3))
    sh["wukv"] = f(inp["mla_w_ukv"][:L].reshape(L, 2, 128, 768).transpose(0, 2, 1, 3))
    rows = []
    for pr in range(3):
        for hf in range(2):
            h = Q_ORDER[2 * pr + hf]
            rows.append(np.arange(64 * h, 64 * h + 64))
    rows.append(np.arange(384, 640))
    rows.append(np.arange(640, 1024))
    rows = np.concatenate(rows)
    wo = inp["w_out"][:L][:, rows, :]
    sh["wout"] = f(wo.reshape(L, KC, 128, 2, 512).transpose(0, 3, 2, 1, 4))
    sh["gains"] = f(np.concatenate([inp["gqa_g_q"][:L], inp["gqa_g_k"][:L], inp["mla_g_qn"][:L],
                                    inp["mla_g_kn"][:L], inp["mla_g_qr"][:L], inp["mla_g_kr"][:L]], axis=1))
    cw = inp["conv_w"][:L]
    cb = inp["conv_b"][:L]
    cp = np.concatenate([cw, cb[:, None, :]], axis=1)
    sh["convp"] = f(cp.reshape(L, 4, 2, 128).transpose(3, 0, 2, 1))
    gl = np.concatenate([inp["mla_g_cq"][:L].reshape(L, 3, 128), inp["mla_g_ckv"][:L].reshape(L, 2, 128)], axis=1)
    sh["glat"] = f(gl.transpose(2, 0, 1))
    ca, sa = _rope_np(64)
    cm, sm = _rope_np(32)
    sh["ropeA"] = f(np.stack([ca, sa], axis=1))
    sh["ropeM"] = f(np.stack([cm, sm], axis=1))
    return sh


def prep_core(inp, b):
    xin = np.concatenate([inp["x"][b], inp["ctx"][b]], axis=0)
    cc = np.stack([inp["c"][b], inp["c_ctx"]], axis=-1)
    ccT = np.ascontiguousarray(cc.reshape(KC, 128, 2).transpose(1, 0, 2), dtype=np.float32)
    return {"xin": np.ascontiguousarray(xin, dtype=np.float32), "ccT": ccT}


def build_program(n_layers=DEPTH, stop_after=None):
    L = n_layers
    nc = bass.Bass("TRN2", target_bir_lowering=False)
    dt_in = lambda name, shape: nc.dram_tensor(name, list(shape), F32, kind="ExternalInput").ap()
    xin = dt_in("xin", [T, D])
    ccT_d = dt_in("ccT", [128, KC, 2])
    wmod_d = dt_in("wmod", [L, 18, 128, KC, 512])
    bmodT_d = dt_in("bmodT", [128, L, 72])
    gn_d = dt_in("gn", [128, L, 3, KC])
    wgu_d = dt_in("wgu", [L, 2, NFC, 128, 2, KC, 128])
    wd_d = dt_in("wd", [L, 2, DFF, D])
    wink_d = dt_in("wink", [L, 128, KC, 1312])
    winq_d = dt_in("winq", [L, 128, KC, 768])
    wuq_d = dt_in("wuq", [L, 128, 3, 576])
    wukv_d = dt_in("wukv", [L, 128, 2, 768])
    wout_d = dt_in("wout", [L, 2, 128, KC, 512])
    gains_d = dt_in("gains", [L, NGAIN])
    convp_d = dt_in("convp", [128, L, 2, 4])
    glat_d = dt_in("glat", [128, L, 5])
    ropeA_d = dt_in("ropeA", [128, 2, 16, 64])
    ropeM_d = dt_in("ropeM", [128, 2, 16, 32])
    out_d = nc.dram_tensor("out", [2048, D], F32, kind="ExternalOutput").ap()

    R = Rec()
    es = ExitStack()
    sb = lambda name, shape, dt: es.enter_context(nc.sbuf_tensor(name, list(shape), dt))
    xs = sb("xs", [128, NT, D], F32)
    ropeA = sb("ropeA_s", [128, 2, 16, 64], BF16)
    ropeM = sb("ropeM_s", [128, 2, 16, 32], BF16)
    ident = sb("ident", [128, 128], BF16)
    identf = sb("identf", [128, 128], F32)
    ones_bf = sb("ones_bf", [128, 128], BF16)
    ccs = sb("ccs", [128, KC, 2], F32)
    s2 = sb("s2", [128, KC, 2], BF16)
    gn = sb("gn_s", [128, L, 3, KC], F32)
    bmodT = sb("bmodT_s", [128, L, 72], F32)
    convp = sb("convp_s", [128, L, 2, 4], F32)
    glat = sb("glat_s", [128, L, 5], F32)
    gains = sb("gains_s", [128, NGAIN], F32)
    modTs = [sb("modT0", [128, 72, 2], F32), sb("modT1", [128, 72, 2], F32)]
    Amod = sb("Amod", [128, KC, 2], F32)
    Bmod = sb("Bmod", [128, KC, 2], F32)
    gate_bc = sb("gate_bc", [128, 2, D], BF16)
    epsb = sb("epsb", [128, 1], F32)
    st_ss = sb("st_ss", [128, 64], F32)
    st_r = sb("st_r", [128, 64], F32)
    ARENA_BYTES = 120 * 1024
    arena = sb("arena", [128, ARENA_BYTES], U8)
    ps = es.enter_context(nc.psum_tensor("ps", [128, 8, 512], F32))

    class Ar:
        pos = 0

    def aalloc(nbytes):
        a0 = (Ar.pos + 31) // 32 * 32
        Ar.pos = a0 + nbytes
        assert Ar.pos <= ARENA_BYTES, ("arena overflow", Ar.pos)
        return a0

    def aview(a0, dt, shape):
        esz = 2 if dt == BF16 else 4
        n = int(np.prod(shape))
        v = arena[:, a0:a0 + n * esz].bitcast(dt)
        if len(shape) == 1:
            return v
        names = " ".join("a%d" % i for i in range(len(shape)))
        kw = {"a%d" % i: shape[i] for i in range(1, len(shape))}
        return v.rearrange("p (%s) -> p %s" % (names, names), **kw)

    def anew(dt, shape):
        esz = 2 if dt == BF16 else 4
        return aview(aalloc(int(np.prod(shape)) * esz), dt, shape)

    psb = lambda b: ps[:, b, :].bitcast(BF16)

    for i in range(6):
        R.dma("sp", "xin%d" % i, lambda e, i=i: e.dma_start(
            out=xs[:, 3 * i:3 * i + 3, :], in_=xin[384 * i:384 * (i + 1), :].rearrange("(t p) d -> p t d", p=128)),
            writes=[("x", 3 * i), ("x", 3 * i + 1), ("x", 3 * i + 2)])
    R.dma("sp", "cst0", lambda e: e.dma_start(out=ccs[:], in_=ccT_d), writes=["ccs"])
    R.dma("sp", "cst1", lambda e: e.dma_start(out=gn[:], in_=gn_d), writes=["gn"])
    R.dma("sp", "cst2", lambda e: e.dma_start(out=bmodT[:], in_=bmodT_d), writes=["bmodT"])
    R.dma("sp", "cst3", lambda e: e.dma_start(out=convp[:], in_=convp_d), writes=["convp"])
    R.dma("sp", "cst4", lambda e: e.dma_start(out=glat[:], in_=glat_d), writes=["glat"])
    R.dma("pool", "cstA", lambda e: e.dma_start(out=ropeA[:], in_=ropeA_d), writes=["ropeA"])
    R.dma("pool", "cstM", lambda e: e.dma_start(out=ropeM[:], in_=ropeM_d), writes=["ropeM"])
    R.op("pool", lambda e: e.memset(identf[:], 0.0), writes=["identf"])
    R.op("pool", lambda e: e.affine_select(out=identf[:], in_=identf[:], pattern=[[-1, 128]],
                                           compare_op=ALU.not_equal, fill=1.0, base=0, channel_multiplier=1),
         reads=["identf"], writes=["identf"])
    R.op("pool", lambda e: e.tensor_copy(out=ident[:], in_=identf[:]), reads=["identf"], writes=["ident"])
    R.op("pool", lambda e: e.memset(ones_bf[:], 1.0), writes=["ones"])
    R.op("pool", lambda e: e.memset(epsb[:], EPS), writes=["eps"])
    R.op("act", lambda e: e.activation(out=s2[:], in_=ccs[:], func=AF.Silu), reads=["ccs"], writes=["s2"])

    def rstd_from_ss(n, scale, key):
        R.op("act", lambda e: e.activation(out=st_r[:, 0:n], in_=st_ss[:, 0:n], func=AF.Sqrt, scale=scale, bias=epsb[:]),
             reads=[("ss", key), "eps"], writes=[("sr", key)])
        R.op("dve", lambda e: e.reciprocal(out=st_r[:, 0:n], in_=st_r[:, 0:n]), reads=[("sr", key)], writes=[("sr", key)])

    def modulation(l):
        mark = Ar.pos
        ring = [anew(BF16, [KC, 512]) for _ in range(4)]
        for c in range(18):
            s = c % 4
            R.dma("pool", "wm%d" % s, lambda e, c=c, s=s: e.dma_start(out=ring[s][:], in_=wmod_d[l, c]),
                  writes=[("wmring", s)])

            def mm(e, c=c, s=s):
                inst = None
                for fc in range(4):
                    col = (c * 4 + fc) * 2
                    for kc in range(KC):
                        inst = e.matmul(ps[:, 0, col:col + 2], ring[s][:, kc, fc * 128:(fc + 1) * 128], s2[:, kc, :],
                                        start=(kc == 0), stop=(kc == KC - 1))
                return inst
            R.op("pe", mm, reads=[("wmring", s), "s2"], writes=[("ps", 0)])
        modT = modTs[l % 2]
        R.op("dve", lambda e: e.tensor_tensor(
            out=modT[:], in0=ps[:, 0, 0:144].rearrange("p (c r) -> p c r", r=2),
            in1=bmodT[:, l, :].unsqueeze(2).broadcast_to([128, 72, 2]), op=ALU.add),
            reads=[("ps", 0), "bmodT"], writes=[("modT", l % 2)])
        R.barrier()
        Ar.pos = mark

    def sub_modulation(l, j, gate_mult):
        modT = modTs[l % 2]
        mk = ("modT", l % 2)
        c_shift, c_scale, c_gate = (3 * j) * 8, (3 * j + 1) * 8, (3 * j + 2) * 8
        R.op("dve", lambda e: e.tensor_scalar(out=Amod[:], in0=modT[:, c_scale:c_scale + 8, :], scalar1=1.0, scalar2=None,
                                              op0=ALU.add), reads=[mk], writes=["Amod"])
        R.op("dve", lambda e: e.tensor_tensor(out=Amod[:], in0=Amod[:], in1=gn[:, l, j, :].unsqueeze(2).broadcast_to([128, KC, 2]),
                                              op=ALU.mult), reads=["Amod", "gn"], writes=["Amod"])
        R.op("dve", lambda e: e.tensor_copy(out=Bmod[:], in_=modT[:, c_shift:c_shift + 8, :]), reads=[mk], writes=["Bmod"])
        mark = Ar.pos
        rep = anew(BF16, [KC, 128])
        for r in range(2):
            R.op("dve", lambda e, r=r: e.tensor_scalar(
                out=rep[:], in0=modT[:, c_gate:c_gate + 8, r:r + 1].broadcast_to([128, KC, 128]),
                scalar1=gate_mult, scalar2=None, op0=ALU.mult), reads=[mk], writes=["rep"])

            def tr(e):
                inst = None
                for kc in range(KC):
                    inst = e.transpose(psb(1)[:, kc * 128:(kc + 1) * 128], rep[:, kc, :], ident[:])
                return inst
            R.op("pe", tr, reads=["rep", "ident"], writes=[("ps", 1)])
            R.op("act", lambda e, r=r: e.copy(out=gate_bc[:, r, :], in_=psb(1)[:, 0:1024]), reads=[("ps", 1)],
                 writes=[("gate", r)])
        R.barrier()
        Ar.pos = mark

    def emit_hT(t, dst, dst_key, xn, xn_key, psbank, stat_col):
        r = 0 if t < NLT else 1
        R.op("act", lambda e: e.activation(out=xn[:], in_=xs[:, t, :], func=AF.Square, accum_out=st_ss[:, stat_col:stat_col + 1]),
             reads=[("x", t)], writes=[xn_key, ("ss", "h%d" % stat_col)])
        R.op("act", lambda e: e.activation(out=st_r[:, stat_col:stat_col + 1], in_=st_ss[:, stat_col:stat_col + 1], func=AF.Sqrt,
                                           scale=1.0 / D, bias=epsb[:]),
             reads=[("ss", "h%d" % stat_col), "eps"], writes=[("sr", "h%d" % stat_col)], sync_self=True)
        R.op("dve", lambda e: e.reciprocal(out=st_r[:, stat_col:stat_col + 1], in_=st_r[:, stat_col:stat_col + 1]),
             reads=[("sr", "h%d" % stat_col)], writes=[("sr", "h%d" % stat_col)])
        R.op("dve", lambda e: e.tensor_scalar(out=xn[:], in0=xs[:, t, :], scalar1=st_r[:, stat_col:stat_col + 1], scalar2=None,
                                              op0=ALU.mult), reads=[("x", t), ("sr", "h%d" % stat_col), xn_key], writes=[xn_key], sync_self=True)

        def tr(e):
            inst = None
            for kc in range(KC):
                inst = e.transpose(psb(psbank)[:, kc * 128:(kc + 1) * 128], xn[:, kc * 128:(kc + 1) * 128], ident[:])
            return inst
        R.op("pe", tr, reads=[xn_key, "ident"], writes=[("ps", psbank)])
        for kc in range(KC):
            R.op("act", lambda e, kc=kc: e.activation(out=dst[:, kc, :], in_=psb(psbank)[:, kc * 128:(kc + 1) * 128],
                                                      func=AF.Identity, scale=Amod[:, kc, r:r + 1], bias=Bmod[:, kc, r:r + 1]),
                 reads=[("ps", psbank), "Amod", "Bmod"], writes=[dst_key])

    def make_mod_hook(ln):
        state = {"ring": None}

        def hook(c):
            if c >= 18:
                return
            if state["ring"] is None:
                state["ring"] = [anew(BF16, [KC, 512]) for _ in range(3)]
            ring = state["ring"]
            s_ = c % 3
            modT = modTs[ln % 2]
            R.dma("pool", "wmh%d" % s_, lambda e: e.dma_start(out=ring[s_][:], in_=wmod_d[ln, c]), writes=[("wmhring", s_)])

            def mm(e):
                inst = None
                for fc in range(4):
                    for kc in range(KC):
                        inst = e.matmul(ps[:, 0, 2 * fc:2 * fc + 2], ring[s_][:, kc, fc * 128:(fc + 1) * 128], s2[:, kc, :],
                                        start=(kc == 0), stop=(kc == KC - 1))
                return inst
            R.op("pe", mm, reads=[("wmhring", s_), "s2"], writes=[("ps", 0)])
            R.op("dve", lambda e: e.tensor_tensor(
                out=modT[:, 4 * c:4 * c + 4, :], in0=ps[:, 0, 0:8].rearrange("p (c r) -> p c r", r=2),
                in1=bmodT[:, ln, 4 * c:4 * c + 4].unsqueeze(2).broadcast_to([128, 4, 2]), op=ALU.add),
                reads=[("ps", 0), "bmodT", ("modT", ln % 2)], writes=[("modT", ln % 2)])
        return hook

    def ffn(l, f, j, tiles, hook=None):
        ntl = len(tiles)
        sub_modulation(l, j, 0.5)
        mark = Ar.pos
        hT = anew(BF16, [KC, T])
        aT = anew(BF16, [4, T])
        gu_ring = [anew(BF16, [2, KC, 128]) for _ in range(4)]
        d_ring = [anew(BF16, [D]) for _ in range(8)]
        sil = [anew(BF16, [512]) for _ in range(2)]
        xn = [anew(BF16, [D]) for _ in range(2)]
        junk = anew(BF16, [D])
        NB = 44
        for i, t in enumerate(tiles):
            R.op("act", lambda e, i=i, t=t: e.activation(out=junk[:], in_=xs[:, t, :], func=AF.Square, accum_out=st_ss[:, NB + i:NB + i + 1]),
                 reads=[("x", t)], writes=["junkN", ("ss", "N")])
        R.op("act", lambda e: e.activation(out=st_r[:, NB:NB + ntl], in_=st_ss[:, NB:NB + ntl], func=AF.Sqrt, scale=1.0 / D, bias=epsb[:]),
             reads=[("ss", "N"), "eps"], writes=[("sr", "N")], sync_self=True)
        R.op("dve", lambda e: e.reciprocal(out=st_r[:, NB:NB + ntl], in_=st_r[:, NB:NB + ntl]), reads=[("sr", "N")], writes=[("sr", "N")])
        for i, t in enumerate(tiles):
            r_ = 0 if t < NLT else 1
            xb = xn[i % 2]
            xk = ("xn", i % 2)
            pb = i % 2
            R.op("dve", lambda e, i=i, t=t, xb=xb: e.tensor_scalar(out=xb[:], in0=xs[:, t, :], scalar1=st_r[:, NB + i:NB + i + 1], scalar2=None,
                                                                 op0=ALU.mult), reads=[("x", t), ("sr", "N"), xk], writes=[xk], sync_self=(i == 0))

            def trN(e, xb=xb, pb=pb):
                inst = None
                for kc in range(KC):
                    inst = e.transpose(psb(pb)[:, kc * 128:(kc + 1) * 128], xb[:, kc * 128:(kc + 1) * 128], ident[:])
                return inst
            R.op("pe", trN, reads=[xk, "ident"], writes=[("ps", pb)])
            for kc in range(KC):
                R.op("act", lambda e, kc=kc, t=t, pb=pb, r_=r_: e.activation(
                    out=hT[:, kc, t * 128:(t + 1) * 128], in_=psb(pb)[:, kc * 128:(kc + 1) * 128],
                    func=AF.Identity, scale=Amod[:, kc, r_:r_ + 1], bias=Bmod[:, kc, r_:r_ + 1]),
                    reads=[("ps", pb), "Amod", "Bmod"], writes=[("hT", t)])
        tgs = []
        i = 0
        while i < ntl:
            n = min(4, ntl - i)
            tgs.append((tiles[i], n))
            i += n
        cbase = 0
        gu_cnt = 0
        d_cnt = 0
        ps_g = [2, 3]
        ps_u = [4, 5]
        gu_i = 0
        y_i = 0
        for gsz in FF_GROUPS:
            for ci in range(gsz):
                c = cbase + ci
                s = gu_cnt % 4
                R.dma("pool", "gu%d" % s, lambda e, c=c, s=s: e.dma_start(out=gu_ring[s][:], in_=wgu_d[l, f, c]),
                      writes=[("guring", s)])
                sd = d_cnt % 8
                R.dma("pool", "wd%d" % sd, lambda e, c=c, sd=sd: e.dma_start(out=d_ring[sd][:], in_=wd_d[l, f, c * 128:(c + 1) * 128, :]),
                      writes=[("dring", sd)])
                for (t0, n) in tgs:
                    ntok = n * 128
                    tok0 = t0 * 128
                    bg = ps_g[gu_i % 2]
                    bu = ps_u[gu_i % 2]
                    sl = sil[gu_i % 2]
                    gu_i += 1
                    hkeys = [("hT", t0 + k) for k in range(n)]

                    def mm(e, s=s, bg=bg, bu=bu, tok0=tok0, ntok=ntok):
                        inst = None
                        for which, bank in ((0, bg), (1, bu)):
                            for kc in range(KC):
                                inst = e.matmul(ps[:, bank, 0:ntok], gu_ring[s][:, which, kc, :], hT[:, kc, tok0:tok0 + ntok],
                                                start=(kc == 0), stop=(kc == KC - 1))
                        return inst
                    R.op("pe", mm, reads=[("guring", s)] + hkeys, writes=[("ps", bg), ("ps", bu)])
                    R.op("act", lambda e, bg=bg, sl=sl, ntok=ntok: e.activation(out=sl[:, 0:ntok], in_=ps[:, bg, 0:ntok], func=AF.Silu),
                         reads=[("ps", bg)], writes=[("sil", id(sl))])
                    R.op("dve", lambda e, bu=bu, sl=sl, ci=ci, tok0=tok0, ntok=ntok: e.tensor_tensor(
                        out=aT[:, ci, tok0:tok0 + ntok], in0=ps[:, bu, 0:ntok], in1=sl[:, 0:ntok], op=ALU.mult),
                        reads=[("ps", bu), ("sil", id(sl))], writes=[("aT", ci, t0 + k) for k in range(n)])
                gu_cnt += 1
                d_cnt += 1
                if hook is not None:
                    hook(c)
            dslots = [(d_cnt - gsz + ci) % 8 for ci in range(gsz)]
            for t in tiles:
                r = 0 if t < NLT else 1
                b0 = 6 if (y_i % 2 == 0) else 0
                y_i += 1

                def mmd(e, t=t, b0=b0, dslots=dslots, gsz=gsz):
                    inst = None
                    for hd in range(2):
                        for ci in range(gsz):
                            inst = e.matmul(ps[:, b0 + hd, :], aT[:, ci, t * 128:(t + 1) * 128],
                                            d_ring[dslots[ci]][:, hd * 512:(hd + 1) * 512],
                                            start=(ci == 0), stop=(ci == gsz - 1))
                    return inst
                R.op("pe", mmd, reads=[("aT", ci, t) for ci in range(gsz)] + [("dring", sd) for sd in dslots],
                     writes=[("ps", b0), ("ps", b0 + 1)])
                yv = ps[:, b0:b0 + 2, :]
                R.op("dve", lambda e, yv=yv, r=r: e.tensor_tensor(out=yv, in0=yv, in1=gate_bc[:, r, :].rearrange("p (a b) -> p a b", a=2),
                                                                  op=ALU.mult),
                     reads=[("ps", b0), ("ps", b0 + 1), ("gate", r)], writes=[("ps", b0), ("ps", b0 + 1)])
                R.op("dve", lambda e, yv=yv, t=t: e.tensor_tensor(out=xs[:, t, :].rearrange("p (a b) -> p a b", a=2),
                                                                  in0=xs[:, t, :].rearrange("p (a b) -> p a b", a=2), in1=yv, op=ALU.add),
                     reads=[("ps", b0), ("ps", b0 + 1), ("x", t)], writes=[("x", t)])
            cbase += gsz
        R.barrier()
        Ar.pos = mark

    def rope_apply(eng, src, dst, cs, t, H, dim, key_r, key_w, tmp1, tmp2, k1, k2):
        q = dim // 4
        cosb = cs[:, 0, t, :].unsqueeze(1).broadcast_to([128, H, dim])
        R.op(eng, lambda e: e.tensor_tensor(out=tmp1, in0=src, in1=cosb, op=ALU.mult), reads=key_r + ["rope"], writes=[k1])
        s4 = src.rearrange("p h (a b c) -> p h a b c", a=2, b=2)
        t4 = tmp2.rearrange("p h (a b c) -> p h a b c", a=2, b=2)
        sn = cs[:, 1, t, :].rearrange("p (a b c) -> p a b c", a=2, b=2)
        for bsel in range(2):
            R.op(eng, lambda e, bsel=bsel: e.tensor_tensor(
                out=t4[:, :, :, bsel, :], in0=s4[:, :, :, 1 - bsel, :],
                in1=sn[:, :, bsel, :].unsqueeze(1).broadcast_to([128, H, 2, q]), op=ALU.mult),
                reads=key_r + ["rope"] + ([k2] if bsel == 1 else []), writes=[k2])
        R.op(eng, lambda e: e.tensor_tensor(out=dst, in0=tmp1, in1=tmp2, op=ALU.add),
             reads=[k1, k2], writes=key_w)

    def sumsq(src, H, dd, col0, scr, key_r, key_w):
        sv = scr[:, 0:H * dd].rearrange("p (h d) -> p h d", h=H)
        R.op("dve", lambda e: e.tensor_tensor(out=sv, in0=src, in1=src, op=ALU.mult), reads=key_r, writes=["sqscr"])
        R.op("dve", lambda e: e.tensor_reduce(out=st_ss[:, col0:col0 + H], in_=sv, axis=AX.X, op=ALU.add),
             reads=["sqscr"], writes=key_w)

    def mixer(l, need_ctx):
        sub_modulation(l, 1, 1.0)
        mark0 = Ar.pos
        kT_g = anew(BF16, [T])
        V_g = anew(BF16, [NT, 128])
        kT_m = anew(BF16, [6, T])
        V_m = anew(BF16, [NT, 384])
        NCV = 2050 + 258
        bT = anew(BF16, [2, NCV])
        gsc = anew(F32, [NGAIN])
        markU = Ar.pos
        uT = anew(BF16, [2, NCV])
        R.dma("sp", "gains", lambda e: e.dma_start(out=gains[:], in_=gains_d[l].partition_broadcast(128)), writes=["gains"])
        R.op("dve", lambda e: e.tensor_copy(out=gsc[:], in_=gains[:]), reads=["gains"], writes=["gsc"])
        R.op("dve", lambda e: e.tensor_scalar(out=gsc[:, 0:64], in0=gains[:, 0:64], scalar1=64.0 ** -0.5, scalar2=None, op0=ALU.mult),
             reads=["gains", "gsc"], writes=["gsc"])
        R.op("dve", lambda e: e.tensor_scalar(out=gsc[:, 128:192], in0=gains[:, 128:192], scalar1=96.0 ** -0.5, scalar2=None, op0=ALU.mult),
             reads=["gains", "gsc"], writes=["gsc"])
        R.op("dve", lambda e: e.tensor_scalar(out=gsc[:, 256:288], in0=gains[:, 256:288], scalar1=96.0 ** -0.5, scalar2=None, op0=ALU.mult),
             reads=["gains", "gsc"], writes=["gsc"])
        g_q, g_k, g_qn, g_kn, g_qr, g_kr = (gsc[:, 0:64], gsc[:, 64:128], gsc[:, 128:192], gsc[:, 192:256],
                                            gsc[:, 256:288], gsc[:, 288:320])
        R.op("pool", lambda e: e.memset(uT[:], 0.0), writes=["uT"])

        markK = Ar.pos
        wK = anew(BF16, [KC, 1312])
        wukv = anew(BF16, [2, 768])
        hTr = [anew(BF16, [KC, 128]) for _ in range(2)]
        xn = [anew(BF16, [D]) for _ in range(2)]
        kraws = [anew(F32, [544]) for _ in range(2)]
        kvraw = anew(F32, [768])
        scr = anew(F32, [768])
        tA = anew(F32, [384])
        tB = anew(F32, [384])
        tC = anew(F32, [384])
        kf = anew(BF16, [128])
        ckvb = anew(BF16, [256])
        ckvT = anew(BF16, [2, 128])
        kfull = anew(BF16, [6, 96])
        cgt = anew(F32, [2, 128])
        R.dma("pool", "wK", lambda e: e.dma_start(out=wK[:], in_=wink_d[l]), writes=["wK"])
        R.dma("pool", "wukv", lambda e: e.dma_start(out=wukv[:], in_=wukv_d[l]), writes=["wukv"])
        for c in range(2):
            R.op("dve", lambda e, c=c: e.tensor_scalar(out=wukv[:, c, :], in0=wukv[:, c, :], scalar1=glat[:, l, 3 + c:4 + c], scalar2=None,
                                                       op0=ALU.mult), reads=["wukv", "glat"], writes=["wukv"])
        def front(t):
            lat = t < NLT
            kraw = kraws[t % 2]
            krk = ("kraw", t % 2)
            h = hTr[t % 2]
            hk = ("hTr", t % 2)
            emit_hT(t, h, hk, xn[t % 2], ("xn", t % 2), 1, t % 2)

            def mmA(e, h=h):
                inst = None
                for kc in range(KC):
                    inst = e.matmul(ps[:, 0, :], h[:, kc, :], wK[:, kc, 0:512], start=(kc == 0), stop=(kc == KC - 1))
                for kc in range(KC):
                    inst = e.matmul(ps[:, 3, 256:288], h[:, kc, :], wK[:, kc, 512:544], start=(kc == 0), stop=(kc == KC - 1))
                return inst
            R.op("pe", mmA, reads=[hk, "wK"], writes=[("ps", 0), ("ps", 3)])

            def mmC(e, h=h):
                inst = None
                for cc in range(6):
                    bank, off = (2, cc * 128) if cc < 4 else (3, (cc - 4) * 128)
                    for kc in range(KC):
                        inst = e.matmul(ps[:, bank, off:off + 128], wK[:, kc, 544 + cc * 128:544 + (cc + 1) * 128], h[:, kc, :],
                                        start=(kc == 0), stop=(kc == KC - 1))
                return inst
            R.op("pe", mmC, reads=[hk, "wK"], writes=[("ps", 2), ("ps", 3)])
            R.op("act", lambda e: e.copy(out=kraw[:, 0:512], in_=ps[:, 0, :]), reads=[("ps", 0)], writes=[krk])
            R.op("act", lambda e: e.copy(out=kraw[:, 512:544], in_=ps[:, 3, 256:288]), reads=[("ps", 3), krk], writes=[krk])
            R.op("act", lambda e, t=t: e.copy(out=V_g[:, t, :], in_=kraw[:, 128:256]), reads=[krk], writes=[("Vg", t)])
            pos = (1 + t * 128) if lat else (2051 + (t - NLT) * 128)
            R.op("act", lambda e: e.copy(out=cgt[:].rearrange("p a b -> p (a b)"), in_=ps[:, 2, 256:512]), reads=[("ps", 2)], writes=["cgt"])
            R.op("dve", lambda e, pos=pos: e.tensor_tensor(out=uT[:, :, pos:pos + 128], in0=ps[:, 2, 0:256].rearrange("p (a b) -> p a b", a=2),
                                                           in1=cgt[:], op=ALU.mult), reads=[("ps", 2), "cgt", "uT"], writes=["uT"])
            R.op("act", lambda e, pos=pos: e.copy(out=bT[:, :, pos:pos + 128], in_=ps[:, 3, 0:256].rearrange("p (a b) -> p a b", a=2)),
                 reads=[("ps", 3)], writes=["bT"])

        def back(t):
            lat = t < NLT
            kraw = kraws[t % 2]
            krk = ("kraw", t % 2)
            R.op("act", lambda e: e.copy(out=ckvb[:], in_=kraw[:, 256:512]), reads=[krk], writes=["ckvb"])
            k3 = kraw[:, 0:128].rearrange("p (h d) -> p h d", h=2)
            sumsq(k3, 2, 64, 8, scr, [krk], [("ss", "k")])
            sumsq(kraw[:, 256:512].unsqueeze(1), 1, 256, 10, scr, [krk], [("ss", "ckv")])
            sumsq(kraw[:, 512:544].unsqueeze(1), 1, 32, 11, scr, [krk], [("ss", "kr")])
            R.op("act", lambda e: e.activation(out=st_r[:, 8:10], in_=st_ss[:, 8:10], func=AF.Sqrt, scale=1.0 / 64, bias=epsb[:]),
                 reads=[("ss", "k"), "eps"], writes=[("sr", "k")])
            R.op("act", lambda e: e.activation(out=st_r[:, 10:11], in_=st_ss[:, 10:11], func=AF.Sqrt, scale=1.0 / 256, bias=epsb[:]),
                 reads=[("ss", "ckv"), "eps"], writes=[("sr", "ckv")])
            R.op("act", lambda e: e.activation(out=st_r[:, 11:12], in_=st_ss[:, 11:12], func=AF.Sqrt, scale=1.0 / 32, bias=epsb[:]),
                 reads=[("ss", "kr"), "eps"], writes=[("sr", "kr")])
            R.op("dve", lambda e: e.reciprocal(out=st_r[:, 8:12], in_=st_r[:, 8:12]),
                 reads=[("sr", "k"), ("sr", "ckv"), ("sr", "kr")], writes=[("sr", "k"), ("sr", "ckv"), ("sr", "kr")])
            kn = tA[:, 0:128].rearrange("p (h d) -> p h d", h=2)
            for hh in range(2):
                R.op("dve", lambda e, hh=hh: e.scalar_tensor_tensor(out=kn[:, hh, :], in0=k3[:, hh, :], scalar=st_r[:, 8 + hh:9 + hh], in1=g_k,
                                                                    op0=ALU.mult, op1=ALU.mult),
                     reads=[krk, ("sr", "k"), "gsc"], writes=[("kn", hh)], sync_self=True)
            kf3 = kf[:].rearrange("p (h d) -> p h d", h=2)
            if lat:
                rope_apply("dve", kn, kf3, ropeA, t, 2, 64, [("kn", 0), ("kn", 1)], ["kf"],
                           tB[:, 0:128].rearrange("p (h d) -> p h d", h=2), tC[:, 0:128].rearrange("p (h d) -> p h d", h=2), ("tB", "k"), ("tC", "k"))
            else:
                R.op("dve", lambda e: e.tensor_copy(out=kf3, in_=kn), reads=[("kn", 0), ("kn", 1)], writes=["kf"])
            R.op("pe", lambda e: e.transpose(psb(7)[:, 0:128], kf[:], ident[:]), reads=["kf", "ident"], writes=[("ps", 7)])
            R.op("act", lambda e, t=t: e.copy(out=kT_g[:, t * 128:(t + 1) * 128], in_=psb(7)[:, 0:128]), reads=[("ps", 7)],
                 writes=[("kTg", t)])
            def trc(e):
                inst = None
                for c in range(2):
                    inst = e.transpose(psb(7)[:, 128 + c * 128:256 + c * 128], ckvb[:, c * 128:(c + 1) * 128], ident[:])
                return inst
            R.op("pe", trc, reads=["ckvb", "ident"], writes=[("ps", 7)])
            R.op("act", lambda e: e.copy(out=ckvT[:].rearrange("p a b -> p (a b)"), in_=psb(7)[:, 128:384]), reads=[("ps", 7)],
                 writes=["ckvT"])

            def mmkv(e):
                inst = None
                for (bank, c0, n) in ((4, 0, 512), (5, 512, 256)):
                    for c in range(2):
                        inst = e.matmul(ps[:, bank, 0:n], ckvT[:, c, :], wukv[:, c, c0:c0 + n], start=(c == 0), stop=(c == 1))
                return inst
            R.op("pe", mmkv, reads=["ckvT", "wukv"], writes=[("ps", 4), ("ps", 5)])
            R.op("act", lambda e: e.copy(out=kvraw[:, 0:512], in_=ps[:, 4, :]), reads=[("ps", 4)], writes=["kvraw"])
            R.op("act", lambda e: e.copy(out=kvraw[:, 512:768], in_=ps[:, 5, 0:256]), reads=[("ps", 5), "kvraw"], writes=["kvraw"])
            kv3 = kvraw[:].rearrange("p (h d) -> p h d", h=6)
            sumsq(kv3[:, :, 0:64], 6, 64, 16, scr, ["kvraw"], [("ss", "kn")])
            R.op("dve", lambda e: e.tensor_tensor(out=st_ss[:, 12:13], in0=st_r[:, 10:11], in1=st_r[:, 10:11], op=ALU.mult),
                 reads=[("sr", "ckv")], writes=[("ss", "b2")])
            R.op("dve", lambda e: e.tensor_scalar(out=st_ss[:, 16:22], in0=st_ss[:, 16:22], scalar1=st_ss[:, 12:13], scalar2=None, op0=ALU.mult),
                 reads=[("ss", "kn"), ("ss", "b2")], writes=[("ss", "kn")], sync_self=True)
            R.op("act", lambda e: e.activation(out=st_r[:, 16:22], in_=st_ss[:, 16:22], func=AF.Sqrt, scale=1.0 / 64, bias=epsb[:]),
                 reads=[("ss", "kn"), "eps"], writes=[("sr", "kn")])
            R.op("dve", lambda e: e.reciprocal(out=st_r[:, 16:22], in_=st_r[:, 16:22]), reads=[("sr", "kn")], writes=[("sr", "kn")])
            R.op("dve", lambda e: e.tensor_scalar(out=st_r[:, 16:22], in0=st_r[:, 16:22], scalar1=st_r[:, 10:11], scalar2=None, op0=ALU.mult),
                 reads=[("sr", "kn"), ("sr", "ckv")], writes=[("sr", "kn")], sync_self=True)
            for hh in range(6):
                R.op("dve", lambda e, hh=hh: e.scalar_tensor_tensor(out=kfull[:, hh, 0:64], in0=kv3[:, hh, 0:64], scalar=st_r[:, 16 + hh:17 + hh],
                                                                    in1=g_kn, op0=ALU.mult, op1=ALU.mult),
                     reads=["kvraw", ("sr", "kn"), "gsc"], writes=[("kfull", hh)], sync_self=(hh == 0))
            R.op("dve", lambda e, t=t: e.tensor_scalar(out=V_m[:, t, :].rearrange("p (h d) -> p h d", h=6), in0=kv3[:, :, 64:128],
                                                       scalar1=st_r[:, 10:11], scalar2=None, op0=ALU.mult),
                 reads=["kvraw", ("sr", "ckv")], writes=[("Vm", t)])
            krn = tA[:, 128:160].unsqueeze(1)
            R.op("dve", lambda e: e.scalar_tensor_tensor(out=tA[:, 128:160], in0=kraw[:, 512:544], scalar=st_r[:, 11:12], in1=g_kr,
                                                         op0=ALU.mult, op1=ALU.mult), reads=[krk, ("sr", "kr"), "gsc"], writes=["krn"])
            krf = tA[:, 160:192].unsqueeze(1)
            if lat:
                rope_apply("dve", krn, krf, ropeM, t, 1, 32, ["krn"], ["krf"], tB[:, 128:160].unsqueeze(1), tC[:, 128:160].unsqueeze(1), ("tB", "kr"), ("tC", "kr"))
                src_kr = krf
                krkey = "krf"
            else:
                src_kr = krn
                krkey = "krn"
            R.op("dve", lambda e, src_kr=src_kr: e.tensor_copy(out=kfull[:, :, 64:96], in_=src_kr.broadcast_to([128, 6, 32])),
                 reads=[krkey], writes=[("kfull", "r")])

            def trk(e):
                inst = None
                for hh in range(6):
                    inst = e.transpose(psb(6)[0:96, hh * 128:(hh + 1) * 128], kfull[:, hh, :], ident[:])
                return inst
            R.op("pe", trk, reads=[("kfull", hh) for hh in range(6)] + [("kfull", "r"), "ident"], writes=[("ps", 6)])
            R.op("act", lambda e, t=t: e.copy(out=kT_m[0:96, :, t * 128:(t + 1) * 128],
                                              in_=psb(6)[0:96, 0:768].rearrange("p (h n) -> p h n", h=6)),
                 reads=[("ps", 6)], writes=[("kTm", t)])
        streams = []
        for t in range(NT):
            R.capture_start()
            front(t)
            back(t)
            streams.append(R.capture_end())
        R.replay_zipped(streams, 0.5)
        cvt = scr[:, 0:512]
        for ch in range(2):
            segs = [(1 + 512 * i, 512) for i in range(4)] + [(2051, 256)]
            for (p0, n) in segs:
                R.op("dve", lambda e, ch=ch, p0=p0, n=n: e.tensor_scalar(out=cvt[:, 0:n], in0=uT[:, ch, p0:p0 + n], scalar1=convp[:, l, ch, 1:2],
                                                                         scalar2=convp[:, l, ch, 3:4], op0=ALU.mult, op1=ALU.add),
                     reads=["uT", "convp"], writes=["cvt"])
                R.op("dve", lambda e, ch=ch, p0=p0, n=n: e.scalar_tensor_tensor(out=cvt[:, 0:n], in0=uT[:, ch, p0 - 1:p0 - 1 + n],
                                                                                scalar=convp[:, l, ch, 0:1], in1=cvt[:, 0:n], op0=ALU.mult, op1=ALU.add),
                     reads=["uT", "convp", "cvt"], writes=["cvt"])
                R.op("dve", lambda e, ch=ch, p0=p0, n=n: e.scalar_tensor_tensor(out=cvt[:, 0:n], in0=uT[:, ch, p0 + 1:p0 + 1 + n],
                                                                                scalar=convp[:, l, ch, 2:3], in1=cvt[:, 0:n], op0=ALU.mult, op1=ALU.add),
                     reads=["uT", "convp", "cvt"], writes=["cvt"])
                R.op("dve", lambda e, ch=ch, p0=p0, n=n: e.tensor_tensor(out=bT[:, ch, p0:p0 + n], in0=bT[:, ch, p0:p0 + n], in1=cvt[:, 0:n],
                                                                         op=ALU.mult), reads=["bT", "cvt"], writes=["bT"])
        R.barrier()
        Ar.pos = markU

        wQ = anew(BF16, [KC, 768])
        wuq = anew(BF16, [3, 576])
        wo_b = anew(BF16, [KC, 512])
        hTq = [anew(BF16, [KC, 128])] * 2
        xnq = [anew(BF16, [D])] * 2
        qraw = anew(F32, [768])
        qmraw = qraw[:, 0:576]
        scrq = anew(F32, [576])
        tAq = anew(F32, [384])
        tBq = anew(F32, [384])
        tCq = anew(F32, [384])
        qf = anew(BF16, [384])
        cqb = anew(BF16, [384])
        cqT = anew(BF16, [3, 128])
        qmfull = anew(BF16, [6, 96])
        qT_g = anew(BF16, [3, 512])
        qT_m = anew(BF16, [6, 512])
        PT2 = [anew(BF16, [2, 512]) for _ in range(2)]
        mixT = anew(BF16, [6, 512])
        rden = scrq[:, 0:512]
        R.dma("pool", "wQ", lambda e: e.dma_start(out=wQ[:], in_=winq_d[l]), writes=["wQ"])
        R.dma("pool", "wuq", lambda e: e.dma_start(out=wuq[:], in_=wuq_d[l]), writes=["wuq"])
        for c in range(3):
            R.op("dve", lambda e, c=c: e.tensor_scalar(out=wuq[:, c, :], in0=wuq[:, c, :], scalar1=glat[:, l, c:c + 1], scalar2=None,
                                                       op0=ALU.mult), reads=["wuq", "glat"], writes=["wuq"])
        groups = [(4 * g, 4) for g in range(4)]
        if need_ctx:
            groups.append((16, 2))
        pt_i = 0
        st_i = 0
        o_i = 0
        for (t0, ntile) in groups:
            lat = t0 < NLT
            nq = ntile * 128
            r = 0 if lat else 1
            key_tiles = list(range(NT)) if lat else [16, 17]
            qstreams = []
            for lt in range(ntile):
                R.capture_start()
                t = t0 + lt
                h = hTq[0]
                hk = ("hTr", 0)
                emit_hT(t, h, hk, xnq[0], ("xn", 0), 6, t % 2)

                def mmQ(e, h=h):
                    inst = None
                    for (bank, c0) in ((4, 0), (5, 384)):
                        for kc in range(KC):
                            inst = e.matmul(ps[:, bank, 0:384], h[:, kc, :], wQ[:, kc, c0:c0 + 384], start=(kc == 0), stop=(kc == KC - 1))
                    return inst
                R.op("pe", mmQ, reads=[hk, "wQ"], writes=[("ps", 4), ("ps", 5)])
                R.op("act", lambda e: e.copy(out=qraw[:, 0:384], in_=ps[:, 4, 0:384]), reads=[("ps", 4)], writes=["qraw"])
                R.op("act", lambda e: e.copy(out=qraw[:, 384:768], in_=ps[:, 5, 0:384]), reads=[("ps", 5), "qraw"], writes=["qraw"])
                R.op("act", lambda e: e.copy(out=cqb[:], in_=qraw[:, 384:768]), reads=["qraw"], writes=["cqb"])
                q3 = qraw[:, 0:384].rearrange("p (h d) -> p h d", h=6)
                sumsq(q3, 6, 64, 24, scrq, ["qraw"], [("ss", "q")])
                sumsq(qraw[:, 384:768].unsqueeze(1), 1, 384, 30, scrq, ["qraw"], [("ss", "cq")])
                R.op("act", lambda e: e.activation(out=st_r[:, 24:30], in_=st_ss[:, 24:30], func=AF.Sqrt, scale=1.0 / 64, bias=epsb[:]),
                     reads=[("ss", "q"), "eps"], writes=[("sr", "q")])
                R.op("act", lambda e: e.activation(out=st_r[:, 30:31], in_=st_ss[:, 30:31], func=AF.Sqrt, scale=1.0 / 384, bias=epsb[:]),
                     reads=[("ss", "cq"), "eps"], writes=[("sr", "cq")])
                R.op("dve", lambda e: e.reciprocal(out=st_r[:, 24:31], in_=st_r[:, 24:31]), reads=[("sr", "q"), ("sr", "cq")],
                     writes=[("sr", "q"), ("sr", "cq")])
                qn = tAq[:].rearrange("p (h d) -> p h d", h=6)
                for hh in range(6):
                    R.op("dve", lambda e, hh=hh: e.scalar_tensor_tensor(out=qn[:, hh, :], in0=q3[:, hh, :], scalar=st_r[:, 24 + hh:25 + hh], in1=g_q,
                                                                        op0=ALU.mult, op1=ALU.mult),
                         reads=["qraw", ("sr", "q"), "gsc"], writes=[("qn", hh)], sync_self=(hh == 0))
                qf3 = qf[:].rearrange("p (h d) -> p h d", h=6)
                qnk = [("qn", hh) for hh in range(6)]
                if lat:
                    rope_apply("dve", qn, qf3, ropeA, t, 6, 64, qnk, ["qf"], tBq[:].rearrange("p (h d) -> p h d", h=6),
                               tCq[:].rearrange("p (h d) -> p h d", h=6), "tBq", "tCq")
                else:
                    R.op("dve", lambda e: e.tensor_copy(out=qf3, in_=qn), reads=qnk, writes=["qf"])

                def trq(e):
                    inst = None
                    for pr in range(3):
                        inst = e.transpose(psb(7)[:, pr * 128:(pr + 1) * 128], qf[:, pr * 128:(pr + 1) * 128], ident[:])
                    for c in range(3):
                        inst = e.transpose(psb(7)[:, 384 + c * 128:512 + c * 128], cqb[:, c * 128:(c + 1) * 128], ident[:])
                    return inst
                R.op("pe", trq, reads=["qf", "cqb", "ident"], writes=[("ps", 7)])
                R.op("act", lambda e, lt=lt: e.copy(out=qT_g[:, :, lt * 128:(lt + 1) * 128], in_=psb(7)[:, 0:384].rearrange("p (a b) -> p a b", a=3)),
                     reads=[("ps", 7)], writes=[("qTg", lt)])
                R.op("act", lambda e: e.copy(out=cqT[:].rearrange("p a b -> p (a b)"), in_=psb(7)[:, 384:768]), reads=[("ps", 7)], writes=["cqT"])

                def mmuq(e):
                    inst = None
                    for (bank, c0) in ((4, 0), (5, 288)):
                        for c in range(3):
                            inst = e.matmul(ps[:, bank, 0:288], cqT[:, c, :], wuq[:, c, c0:c0 + 288], start=(c == 0), stop=(c == 2))
                    return inst
                R.op("pe", mmuq, reads=["cqT", "wuq"], writes=[("ps", 4), ("ps", 5)])
                R.op("act", lambda e: e.copy(out=qmraw[:, 0:288], in_=ps[:, 4, 0:288]), reads=[("ps", 4)], writes=["qraw"])
                R.op("act", lambda e: e.copy(out=qmraw[:, 288:576], in_=ps[:, 5, 0:288]), reads=[("ps", 5), "qraw"], writes=["qraw"])
                qm3 = qmraw[:].rearrange("p (h d) -> p h d", h=6)
                sumsq(qm3[:, :, 0:64], 6, 64, 32, scrq, ["qraw"], [("ss", "qmn")])
                sumsq(qm3[:, :, 64:96], 6, 32, 38, scrq, ["qraw"], [("ss", "qmr")])
                R.op("dve", lambda e: e.tensor_tensor(out=st_ss[:, 31:32], in0=st_r[:, 30:31], in1=st_r[:, 30:31], op=ALU.mult),
                     reads=[("sr", "cq")], writes=[("ss", "a2")])
                R.op("dve", lambda e: e.tensor_scalar(out=st_ss[:, 32:44], in0=st_ss[:, 32:44], scalar1=st_ss[:, 31:32], scalar2=None, op0=ALU.mult),
                     reads=[("ss", "qmn"), ("ss", "qmr"), ("ss", "a2")], writes=[("ss", "qmn"), ("ss", "qmr")], sync_self=True)
                R.op("act", lambda e: e.activation(out=st_r[:, 32:38], in_=st_ss[:, 32:38], func=AF.Sqrt, scale=1.0 / 64, bias=epsb[:]),
                     reads=[("ss", "qmn"), "eps"], writes=[("sr", "qmn")])
                R.op("act", lambda e: e.activation(out=st_r[:, 38:44], in_=st_ss[:, 38:44], func=AF.Sqrt, scale=1.0 / 32, bias=epsb[:]),
                     reads=[("ss", "qmr"), "eps"], writes=[("sr", "qmr")])
                R.op("dve", lambda e: e.reciprocal(out=st_r[:, 32:44], in_=st_r[:, 32:44]), reads=[("sr", "qmn"), ("sr", "qmr")],
                     writes=[("sr", "qmn"), ("sr", "qmr")])
                R.op("dve", lambda e: e.tensor_scalar(out=st_r[:, 32:44], in0=st_r[:, 32:44], scalar1=st_r[:, 30:31], scalar2=None, op0=ALU.mult),
                     reads=[("sr", "qmn"), ("sr", "qmr"), ("sr", "cq")], writes=[("sr", "qmn"), ("sr", "qmr")], sync_self=True)
                qr = tAq[:, 0:192].rearrange("p (h d) -> p h d", h=6)
                for hh in range(6):
                    R.op("dve", lambda e, hh=hh: e.scalar_tensor_tensor(out=qmfull[:, hh, 0:64], in0=qm3[:, hh, 0:64], scalar=st_r[:, 32 + hh:33 + hh],
                                                                        in1=g_qn, op0=ALU.mult, op1=ALU.mult),
                         reads=["qraw", ("sr", "qmn"), "gsc"], writes=[("qmfull", hh)], sync_self=(hh == 0))
                    R.op("dve", lambda e, hh=hh: e.scalar_tensor_tensor(out=qr[:, hh, :], in0=qm3[:, hh, 64:96], scalar=st_r[:, 38 + hh:39 + hh],
                                                                        in1=g_qr, op0=ALU.mult, op1=ALU.mult),
                         reads=["qraw", ("sr", "qmr"), "gsc", "qf"] + qnk, writes=[("qr", hh)])
                qrk = [("qr", hh) for hh in range(6)]
                if lat:
                    rope_apply("dve", qr, qmfull[:, :, 64:96], ropeM, t, 6, 32, qrk, [("qmfull", "r")],
                               tBq[:, 0:192].rearrange("p (h d) -> p h d", h=6), tCq[:, 0:192].rearrange("p (h d) -> p h d", h=6), "tBq", "tCq")
                else:
                    R.op("dve", lambda e: e.tensor_copy(out=qmfull[:, :, 64:96], in_=qr), reads=qrk, writes=[("qmfull", "r")])

                def trqm(e):
                    inst = None
                    for hh in range(6):
                        inst = e.transpose(psb(7)[0:96, hh * 128:(hh + 1) * 128], qmfull[:, hh, :], ident[:])
                    return inst
                R.op("pe", trqm, reads=[("qmfull", hh) for hh in range(6)] + [("qmfull", "r"), "ident"], writes=[("ps", 7)])
                R.op("act", lambda e, lt=lt: e.copy(out=qT_m[0:96, :, lt * 128:(lt + 1) * 128],
                                                    in_=psb(7)[0:96, 0:768].rearrange("p (h n) -> p h n", h=6)),
                     reads=[("ps", 7)], writes=[("qTm", lt)])
                qstreams.append(R.capture_end())
            R.replay_zipped(qstreams, 0.7)
            qgk = [("qTg", lt) for lt in range(ntile)]
            qmk = [("qTm", lt) for lt in range(ntile)]
            npair = len(key_tiles) // 2
            items = [(slot, kj, key_tiles[2 * kj], key_tiles[2 * kj + 1]) for slot in range(12) for kj in range(npair)]
            LA = 1
            ST_PAIRS = [0, 6]

            def slot_info(slot):
                if slot < 6:
                    pr, hf = slot // 2, slot % 2
                    return True, pr, hf, pr, slot
                hm = slot - 6
                pr, hf = hm // 2, hm % 2
                return False, pr, hf, 3 + pr, hm

            def emit_st(i, nq=nq):
                slot, kj, kta, ktb = items[i]
                gqa, pr, hf, chunk, hm = slot_info(slot)
                b0 = ST_PAIRS[i % 2]
                pt = PT2[i % 2]
                ptk = ("PT", i % 2)

                def mmst(e, b0=b0, gqa=gqa, pr=pr, hf=hf, hm=hm, kta=kta, ktb=ktb, nq=nq):
                    inst = None
                    for k, kt in enumerate((kta, ktb)):
                        if gqa:
                            inst = e.matmul(ps[:, b0 + k, 0:nq], kT_g[64 * hf:64 * hf + 64, kt * 128:(kt + 1) * 128],
                                            qT_g[64 * hf:64 * hf + 64, pr, 0:nq], start=True, stop=True)
                        else:
                            inst = e.matmul(ps[:, b0 + k, 0:nq], kT_m[0:96, hm, kt * 128:(kt + 1) * 128], qT_m[0:96, hm, 0:nq],
                                            start=True, stop=True)
                    return inst
                kk = [("kTg", kta), ("kTg", ktb)] + qgk if gqa else [("kTm", kta), ("kTm", ktb)] + qmk
                R.op("pe", mmst, reads=kk, writes=[("ps", b0), ("ps", b0 + 1)])
                R.op("act", lambda e, b0=b0, pt=pt, nq=nq: e.activation(out=pt[:, :, 0:nq], in_=ps[:, b0:b0 + 2, 0:nq], func=AF.Exp),
                     reads=[("ps", b0), ("ps", b0 + 1)], writes=[ptk])

            def emit_pv(i, nq=nq):
                slot, kj, kta, ktb = items[i]
                gqa, pr, hf, chunk, hm = slot_info(slot)
                pt = PT2[i % 2]
                ptk = ("PT", i % 2)
                bo = 2 if (slot % 2 == 0) else 4
                bd = bo + 1
                if gqa:
                    vaps = [V_g[:, kta, :], V_g[:, ktb, :]]
                    vk = [("Vg", kta), ("Vg", ktb)]
                else:
                    vaps = [V_m[:, kta, pr * 128:(pr + 1) * 128], V_m[:, ktb, pr * 128:(pr + 1) * 128]]
                    vk = [("Vm", kta), ("Vm", ktb)]

                def mmpv(e, vaps=vaps, pt=pt, bo=bo, bd=bd, kj=kj, nq=nq, npair=npair):
                    inst = None
                    for k in range(2):
                        first = (kj == 0 and k == 0)
                        last = (kj == npair - 1 and k == 1)
                        e.matmul(ps[:, bo, 0:nq], vaps[k], pt[:, k, 0:nq], start=first, stop=last)
                        inst = e.matmul(ps[:, bd, 0:nq], ones_bf[:], pt[:, k, 0:nq], start=first, stop=last)
                    return inst
                R.op("pe", mmpv, reads=vk + [ptk, "ones"], writes=[("ps", bo), ("ps", bd)])
                if kj == npair - 1:
                    p0 = 64 * hf
                    R.op("dve", lambda e, bd=bd, p0=p0, nq=nq: e.reciprocal(out=rden[p0:p0 + 64, 0:nq], in_=ps[p0:p0 + 64, bd, 0:nq]),
                         reads=[("ps", bd)], writes=["rden"])
                    R.op("dve", lambda e, bo=bo, p0=p0, chunk=chunk, nq=nq: e.tensor_tensor(
                        out=mixT[p0:p0 + 64, chunk, 0:nq], in0=ps[p0:p0 + 64, bo, 0:nq], in1=rden[p0:p0 + 64, 0:nq], op=ALU.mult),
                        reads=[("ps", bo), "rden"], writes=[("mixT", chunk, hf)])

            for i in range(len(items) + LA):
                if i < len(items):
                    emit_st(i)
                if i >= LA:
                    emit_pv(i - LA)
            mixk = [("mixT", c, hf) for c in range(6) for hf in range(2)]
            for hd in range(2):
                R.dma("pool", "wo", lambda e, hd=hd: e.dma_start(out=wo_b[:], in_=wout_d[l, hd]), writes=["wo"])
                for lt in range(ntile):
                    t = t0 + lt
                    pos = (1 + t * 128) if lat else (2051 + (t - NLT) * 128)
                    bk = 6 + (lt % 2)

                    def mmo(e, lt=lt, pos=pos, bk=bk):
                        inst = None
                        for c in range(8):
                            if c < 3:
                                lh = mixT[:, c, lt * 128:(lt + 1) * 128]
                            elif c < 5:
                                lh = bT[:, c - 3, pos:pos + 128]
                            else:
                                lh = mixT[:, c - 2, lt * 128:(lt + 1) * 128]
                            inst = e.matmul(ps[:, bk, :], lh, wo_b[:, c, :], start=(c == 0), stop=(c == 7))
                        return inst
                    R.op("pe", mmo, reads=mixk + ["bT", "wo"], writes=[("ps", bk)])
                    R.op("dve", lambda e, bk=bk, r=r, hd=hd: e.tensor_tensor(out=ps[:, bk, :], in0=ps[:, bk, :],
                                                                             in1=gate_bc[:, r, hd * 512:(hd + 1) * 512], op=ALU.mult),
                         reads=[("ps", bk), ("gate", r)], writes=[("ps", bk)])
                    R.op("dve", lambda e, bk=bk, t=t, hd=hd: e.tensor_tensor(out=xs[:, t, hd * 512:(hd + 1) * 512],
                                                                             in0=xs[:, t, hd * 512:(hd + 1) * 512], in1=ps[:, bk, :], op=ALU.add),
                         reads=[("ps", bk), ("x", t)], writes=[("x", t)])
        R.barrier()
        Ar.pos = mark0

    done = False
    for l in range(L):
        if l > 0:
            R.new_epoch()
        need_ctx = l < DEPTH - 1
        if l == 0:
            modulation(l)
        ffn(l, 0, 0, list(range(NT)))
        if stop_after == (l, "ffn1"):
            break
        mixer(l, need_ctx)
        if stop_after == (l, "mix"):
            break
        ffn(l, 1, 2, list(range(NT)) if need_ctx else list(range(NLT)), hook=(make_mod_hook(l + 1) if l + 1 < L else None))
        if stop_after == (l, "ffn2"):
            break

    for i in range(4):
        R.dma("sp", "out", lambda e, i=i: e.dma_start(
            out=out_d[512 * i:512 * (i + 1), :].rearrange("(t p) d -> p t d", p=128), in_=xs[:, 4 * i:4 * i + 4, :]),
            reads=[("x", 4 * i + k) for k in range(4)])
    R.op("sp", lambda e: None, reads=[], writes=[("x", k) for k in range(NLT)])

    R.finalize()
    esems = {}
    for e in Rec.ENG:
        for ep in range(R.n_epochs):
            esems[(e, ep)] = es.enter_context(nc.semaphore("s_%s_%d" % (e, ep)))
    lsems = {ln: es.enter_context(nc.semaphore("l_%s" % ln)) for ln in R.lane_cnt}
    block = es.enter_context(nc.Block())

    @block.tensor
    def _(e):
        R.emit("pe", e, esems, lsems)

    @block.scalar
    def _(e):
        R.emit("act", e, esems, lsems)

    @block.vector
    def _(e):
        R.emit("dve", e, esems, lsems)

    @block.gpsimd
    def _(e):
        R.emit("pool", e, esems, lsems)

    @block.sync
    def _(e):
        R.emit("sp", e, esems, lsems)

    es.close()
    return nc


_CACHE = {}


def kernel(**inputs):
    inp = {k: np.asarray(v) for k, v in inputs.items()}
    if "nc" not in _CACHE:
        _CACHE["nc"] = build_program(DEPTH)
    nc = _CACHE["nc"]
    sh = prep_shared(inp, DEPTH)
    in_maps = []
    for b in range(8):
        m = dict(sh)
        m.update(prep_core(inp, b))
        in_maps.append(m)
    res = run_bass_kernel_spmd(nc, in_maps, core_ids=list(range(8)))
    out = np.stack([np.asarray(r["out"]) for r in res.results], axis=0)
    return out.astype(np.float32)
```
